# Optimizing a Trainium2 kernel written in Bass

```python
import math
import jax
import jax.numpy as jnp
from jax import lax
import numpy as np

D_MODEL = 1024
BATCH = 2
SEQ = 8192
DEPTH = 4

GRID_W = 64
CTX_LEN = 256

HEAD_DIM = 64
ATTN_HEADS = 8
ATTN_KV_HEADS = 2
ATTN_GROUP = ATTN_HEADS // ATTN_KV_HEADS
ATTN_WIDTH = ATTN_HEADS * HEAD_DIM
KV_WIDTH = ATTN_KV_HEADS * HEAD_DIM
ATTN_SCALE = HEAD_DIM ** -0.5
Q_BLOCK = 128
ROPE_THETA = 10000.0

HYENA_WIDTH = 256
HYENA_ORDER = 2
SHORT_CONV = 3
FILTER_BANDS = 16
FILTER_EMB = 1 + 2 * FILTER_BANDS
FILTER_HIDDEN = 64
HYENA_TARGET = 1e-2
HYENA_FAST_DECAY_PCT = 0.3
HYENA_SLOW_DECAY_PCT = 1.5
HYENA_MIN_DECAY = math.log(HYENA_TARGET) / HYENA_SLOW_DECAY_PCT
HYENA_MAX_DECAY = math.log(HYENA_TARGET) / HYENA_FAST_DECAY_PCT

GLA_HEADS = 4
GLA_DK = 32
GLA_DV = 64
GLA_K_WIDTH = GLA_HEADS * GLA_DK
GLA_V_WIDTH = GLA_HEADS * GLA_DV
GLA_GATE_RANK = 16
GLA_GATE_TAU = 16.0
GLA_CHUNK = 64

MIX_WIDTH = ATTN_WIDTH + HYENA_WIDTH + GLA_V_WIDTH
FFN_HIDDEN = -(-8 * D_MODEL // (3 * 256)) * 256
N_MOD = 6
NORM_EPS = 1e-6

IN_SPLITS = (ATTN_WIDTH, KV_WIDTH, KV_WIDTH, 3 * HYENA_WIDTH, GLA_K_WIDTH, GLA_K_WIDTH,
             GLA_V_WIDTH, GLA_V_WIDTH, GLA_GATE_RANK, GLA_GATE_RANK)
IN_WIDTH = sum(IN_SPLITS)

kernel_name = "hybrid_gqa_hyena_gla_prefix_dit"


def rms_norm(x, g):
    xf = x.astype(jnp.float32)
    y = xf * lax.rsqrt(jnp.mean(jnp.square(xf), axis=-1, keepdims=True) + NORM_EPS)
    return (y * g.astype(jnp.float32)).astype(x.dtype)


def head_rms_norm(x, g):
    shp = x.shape
    xh = x.reshape(shp[:-1] + (shp[-1] // HEAD_DIM, HEAD_DIM))
    return rms_norm(xh, g.reshape(-1, HEAD_DIM)).reshape(shp)


def modulate(x, shift, scale):
    return x * (1.0 + scale) + shift


def split_columns(p):
    parts, start = [], 0
    for width in IN_SPLITS:
        parts.append(p[..., start:start + width])
        start += width
    return parts


def axial_rope_tables(n_tokens):
    rows = n_tokens // GRID_W
    row = jnp.broadcast_to(jnp.arange(rows, dtype=jnp.float32)[:, None], (rows, GRID_W)).reshape(-1)
    col = jnp.broadcast_to(jnp.arange(GRID_W, dtype=jnp.float32)[None, :], (rows, GRID_W)).reshape(-1)
    n_freq = HEAD_DIM // 4
    inv_freq = jnp.power(ROPE_THETA, -jnp.arange(n_freq, dtype=jnp.float32) / n_freq)
    ang = jnp.concatenate([row[:, None] * inv_freq, col[:, None] * inv_freq], axis=-1)
    return jnp.cos(ang), jnp.sin(ang)


def apply_rope(x, cos, sin):
    half = HEAD_DIM // 2
    xf = x.astype(jnp.float32)
    x1, x2 = xf[..., :half], xf[..., half:]
    c, s = cos[None, :, None, :], sin[None, :, None, :]
    return jnp.concatenate([x1 * c - x2 * s, x1 * s + x2 * c], axis=-1).astype(x.dtype)


def gqa_attend(q, k, v):
    b, lq = q.shape[:2]
    qg = q.reshape(b, lq, ATTN_KV_HEADS, ATTN_GROUP, HEAD_DIM)
    s = jnp.einsum('bqkgd,bskd->bkgqs', qg, k, preferred_element_type=jnp.float32)
    p = jax.nn.softmax(s, axis=-1).astype(v.dtype)
    o = jnp.einsum('bkgqs,bskd->bqkgd', p, v)
    return o.reshape(b, lq, ATTN_WIDTH)


def latent_block_attention(q_lat, k_all, v_all):
    b, n = q_lat.shape[:2]
    nb = n // Q_BLOCK
    qb = q_lat.reshape(b, nb, Q_BLOCK, ATTN_HEADS, HEAD_DIM).swapaxes(0, 1)
    ob = lax.map(lambda blk: gqa_attend(blk, k_all, v_all), qb)
    return ob.swapaxes(0, 1).reshape(b, n, ATTN_WIDTH)


def short_conv(u, w, b):
    pad = SHORT_CONV // 2
    y = lax.conv_general_dilated(u, w[:, None, :].astype(u.dtype), window_strides=(1,), padding=[(pad, pad)],
                                 dimension_numbers=('NWC', 'WIO', 'NWC'), feature_group_count=u.shape[-1])
    return y + b


def hyena_filter_spectra(n, w1, b1, w2, b2, w3, freq):
    f32 = jnp.float32
    t = jnp.linspace(0.0, 1.0, n, dtype=f32)[:, None]
    omega = (2.0 * math.pi / n) * jnp.arange(n, dtype=f32)
    bands = jnp.linspace(1e-4, FILTER_BANDS - 1, FILTER_BANDS, dtype=f32)
    phase = omega[:, None] * bands[None, :]
    z = jnp.concatenate([t, jnp.cos(phase), -jnp.sin(phase)], axis=-1)
    fr = freq.astype(f32)
    h = jnp.sin(fr * (z @ w1.astype(f32) + b1.astype(f32)))
    h = jnp.sin(fr * (h @ w2.astype(f32) + b2.astype(f32)))
    h = (h @ w3.astype(f32)).reshape(n, HYENA_ORDER, 2, HYENA_WIDTH)
    deltas = jnp.abs(jnp.linspace(HYENA_MIN_DECAY, HYENA_MAX_DECAY, HYENA_WIDTH, dtype=f32))
    window = jnp.exp(-t * deltas[None, :])
    h = h * window[:, None, None, :]
    h_pos, h_neg = h[:, :, 0], h[:, :, 1]
    h_circ = jnp.concatenate([h_pos, jnp.zeros((1, HYENA_ORDER, HYENA_WIDTH), f32), h_neg[:0:-1]], axis=0)
    return jnp.fft.rfft(h_circ, axis=0)


def long_conv(z, spec, bias):
    n = z.shape[1]
    y = jnp.fft.irfft(jnp.fft.rfft(z, n=2 * n, axis=1) * spec[None], n=2 * n, axis=1)[:, :n]
    return y + z * bias


def hyena_mix(proj, conv_w, conv_b, spec, bias):
    u = short_conv(proj, conv_w, conv_b).astype(jnp.float32)
    v, x1, x2 = u[..., :HYENA_WIDTH], u[..., HYENA_WIDTH:2 * HYENA_WIDTH], u[..., 2 * HYENA_WIDTH:]
    bias = bias.astype(jnp.float32)
    z = x1 * long_conv(v, spec[:, 0], bias[0])
    z = x2 * long_conv(z, spec[:, 1], bias[1])
    return z.astype(proj.dtype)


def gla_chunked(q, k, v, log_a, s0):
    b, n, h, _ = q.shape
    dv = v.shape[-1]
    nc = n // GLA_CHUNK

    def to_chunks(t):
        return t.reshape(b, nc, GLA_CHUNK, h, t.shape[-1]).transpose(0, 1, 3, 2, 4)

    qc, kc, vc, gc = to_chunks(q), to_chunks(k), to_chunks(v), to_chunks(log_a)
    g_cum = jnp.cumsum(gc, axis=3)
    g_last = g_cum[:, :, :, -1:, :]
    q_dec = qc * jnp.exp(g_cum)
    k_dec = kc * jnp.exp(-g_cum)
    k_tail = kc * jnp.exp(g_last - g_cum)
    lower = jnp.tril(jnp.ones((GLA_CHUNK, GLA_CHUNK), dtype=bool))
    scores = jnp.where(lower, jnp.einsum('bnhid,bnhjd->bnhij', q_dec, k_dec), 0.0)
    o_intra = jnp.einsum('bnhij,bnhjv->bnhiv', scores, vc)
    kv = jnp.einsum('bnhjd,bnhjv->bnhdv', k_tail, vc)
    decay = jnp.exp(g_last[:, :, :, 0, :])

    def step(state, inp):
        d, kv_c = inp
        return d[..., None] * state + kv_c, state

    s_final, s_prev = lax.scan(step, s0, (jnp.moveaxis(decay, 1, 0), jnp.moveaxis(kv, 1, 0)))
    s_prev = jnp.moveaxis(s_prev, 0, 1)
    o_inter = jnp.einsum('bnhid,bnhdv->bnhiv', q_dec, s_prev)
    o = (o_intra + o_inter).transpose(0, 1, 3, 2, 4).reshape(b, n, h, dv)
    return o, s_final


def gla_bidirectional(q_l, k_l, v_l, af_l, ab_l, q_c, k_c, v_c, af_c, ab_c):
    b = q_l.shape[0]
    s0 = jnp.zeros((b, GLA_HEADS, GLA_DK, GLA_DV), jnp.float32)
    rev = lambda t: t[:, ::-1]
    oc_f, sc_f = gla_chunked(q_c, k_c, v_c, af_c, s0)
    oc_b, sc_b = gla_chunked(rev(q_c), rev(k_c), rev(v_c), rev(ab_c), s0)
    ol_f, _ = gla_chunked(q_l, k_l, v_l, af_l, sc_f)
    ol_b, _ = gla_chunked(rev(q_l), rev(k_l), rev(v_l), rev(ab_l), sc_b)
    return ol_f + rev(ol_b), oc_f + rev(oc_b)


def mixer_inputs(u, w_in, q_norm_g, k_norm_g, gla_gate_w, gla_gate_b):
    b, n = u.shape[:2]
    f32 = jnp.float32
    aq, ak, av, hy, gq, gk, gv, gr, gaf, gab = split_columns(u @ w_in)
    q = rms_norm(aq.reshape(b, n, ATTN_HEADS, HEAD_DIM), q_norm_g)
    k = rms_norm(ak.reshape(b, n, ATTN_KV_HEADS, HEAD_DIM), k_norm_g)
    v = av.reshape(b, n, ATTN_KV_HEADS, HEAD_DIM)
    gla_q = gq.reshape(b, n, GLA_HEADS, GLA_DK).astype(f32) * (GLA_DK ** -0.5)
    gla_k = gk.reshape(b, n, GLA_HEADS, GLA_DK).astype(f32)
    gla_v = gv.reshape(b, n, GLA_HEADS, GLA_DV).astype(f32)

    def log_gate(low_rank, d):
        zg = (low_rank @ gla_gate_w[d] + gla_gate_b[d]).astype(f32)
        return (jax.nn.log_sigmoid(zg) / GLA_GATE_TAU).reshape(b, n, GLA_HEADS, GLA_DK)

    gla = (gla_q, gla_k, gla_v, log_gate(gaf, 0), log_gate(gab, 1))
    return q, k, v, hy, gla, gr


def merge_heads(attn_o, hy_o, gla_o, gate_r, out_norm_g, w_out):
    dt = attn_o.dtype
    gla_flat = gla_o.reshape(gla_o.shape[:2] + (GLA_V_WIDTH,))
    y = jnp.concatenate([
        head_rms_norm(attn_o, out_norm_g[:ATTN_WIDTH]),
        head_rms_norm(hy_o, out_norm_g[ATTN_WIDTH:ATTN_WIDTH + HYENA_WIDTH]),
        (head_rms_norm(gla_flat, out_norm_g[ATTN_WIDTH + HYENA_WIDTH:]) * jax.nn.silu(gate_r.astype(jnp.float32))).astype(dt),
    ], axis=-1)
    return y @ w_out


def swiglu(u, w1, w3, w2):
    return (jax.nn.silu(u @ w1) * (u @ w3)) @ w2


def hybrid_layer(h_lat, h_ctx, mod_lat, mod_ctx, rope_cos, rope_sin, norm1_g, w_in, q_norm_g, k_norm_g,
                 hy_conv_w, hy_conv_b, filt_w1, filt_b1, filt_w2, filt_b2, filt_w3, filt_freq, hy_bias,
                 gla_gate_w, gla_gate_b, out_norm_g, w_out, norm2_g, ffn_w1, ffn_w3, ffn_w2, update_ctx):
    n_lat, n_ctx = h_lat.shape[1], h_ctx.shape[1]
    sh1_l, sc1_l, g1_l, sh2_l, sc2_l, g2_l = jnp.split(mod_lat, N_MOD, axis=-1)
    sh1_c, sc1_c, g1_c, sh2_c, sc2_c, g2_c = jnp.split(mod_ctx, N_MOD, axis=-1)
    u_lat = modulate(rms_norm(h_lat, norm1_g), sh1_l, sc1_l)
    u_ctx = modulate(rms_norm(h_ctx, norm1_g), sh1_c, sc1_c)
    q_l, k_l, v_l, hy_l, gla_l, r_l = mixer_inputs(u_lat, w_in, q_norm_g, k_norm_g, gla_gate_w, gla_gate_b)
    q_c, k_c, v_c, hy_c, gla_c, r_c = mixer_inputs(u_ctx, w_in, q_norm_g, k_norm_g, gla_gate_w, gla_gate_b)

    q_l = apply_rope(q_l, rope_cos, rope_sin) * ATTN_SCALE
    k_l = apply_rope(k_l, rope_cos, rope_sin)
    q_c = q_c * ATTN_SCALE
    attn_l = latent_block_attention(q_l, jnp.concatenate([k_l, k_c], axis=1), jnp.concatenate([v_l, v_c], axis=1))

    filt = (filt_w1, filt_b1, filt_w2, filt_b2, filt_w3, filt_freq)
    hy_out_l = hyena_mix(hy_l, hy_conv_w, hy_conv_b, hyena_filter_spectra(n_lat, *filt), hy_bias)

    gla_out_l, gla_out_c = gla_bidirectional(*gla_l, *gla_c)

    h_lat = h_lat + g1_l * merge_heads(attn_l, hy_out_l, gla_out_l, r_l, out_norm_g, w_out)
    h_lat = h_lat + g2_l * swiglu(modulate(rms_norm(h_lat, norm2_g), sh2_l, sc2_l), ffn_w1, ffn_w3, ffn_w2)

    if update_ctx:
        attn_c = gqa_attend(q_c, k_c, v_c)
        hy_out_c = hyena_mix(hy_c, hy_conv_w, hy_conv_b, hyena_filter_spectra(n_ctx, *filt), hy_bias)
        h_ctx = h_ctx + g1_c * merge_heads(attn_c, hy_out_c, gla_out_c, r_c, out_norm_g, w_out)
        h_ctx = h_ctx + g2_c * swiglu(modulate(rms_norm(h_ctx, norm2_g), sh2_c, sc2_c), ffn_w1, ffn_w3, ffn_w2)
    return h_lat, h_ctx


def setup_inputs(seed: int = 0) -> dict:
    key = jax.random.key(seed)
    ks = jax.random.split(key, 28)
    f32 = jnp.float32

    def nrm(k, shape, std):
        return std * jax.random.normal(k, shape, f32)

    d = D_MODEL
    return {
        "x": nrm(ks[0], (BATCH, SEQ, d), 1.0),
        "c": nrm(ks[1], (BATCH, d), 1.0),
        "ctx": nrm(ks[2], (BATCH, CTX_LEN, d), 1.0),
        "c_ctx": nrm(ks[3], (d,), 1.0),
        "ada_w": nrm(ks[4], (DEPTH, d, N_MOD * d), 0.5 * d ** -0.5),
        "ada_b": nrm(ks[5], (DEPTH, N_MOD * d), 0.02),
        "norm1_g": 1.0 + nrm(ks[6], (DEPTH, d), 0.02),
        "w_in": nrm(ks[7], (DEPTH, d, IN_WIDTH), d ** -0.5),
        "q_norm_g": 1.0 + nrm(ks[8], (DEPTH, HEAD_DIM), 0.02),
        "k_norm_g": 1.0 + nrm(ks[9], (DEPTH, HEAD_DIM), 0.02),
        "hy_conv_w": nrm(ks[10], (DEPTH, SHORT_CONV, 3 * HYENA_WIDTH), SHORT_CONV ** -0.5),
        "hy_conv_b": nrm(ks[11], (DEPTH, 3 * HYENA_WIDTH), 0.02),
        "filt_w1": nrm(ks[12], (DEPTH, FILTER_EMB, FILTER_HIDDEN), FILTER_EMB ** -0.5),
        "filt_b1": nrm(ks[13], (DEPTH, FILTER_HIDDEN), 0.02),
        "filt_w2": nrm(ks[14], (DEPTH, FILTER_HIDDEN, FILTER_HIDDEN), FILTER_HIDDEN ** -0.5),
        "filt_b2": nrm(ks[15], (DEPTH, FILTER_HIDDEN), 0.02),
        "filt_w3": nrm(ks[16], (DEPTH, FILTER_HIDDEN, HYENA_ORDER * 2 * HYENA_WIDTH), 0.02),
        "filt_freq": 1.0 + nrm(ks[17], (DEPTH, FILTER_HIDDEN), 0.02),
        "hy_bias": nrm(ks[18], (DEPTH, HYENA_ORDER, HYENA_WIDTH), 0.5),
        "gla_gate_w": nrm(ks[19], (DEPTH, 2, GLA_GATE_RANK, GLA_K_WIDTH), GLA_GATE_RANK ** -0.5),
        "gla_gate_b": nrm(ks[20], (DEPTH, 2, GLA_K_WIDTH), 0.02),
        "out_norm_g": 1.0 + nrm(ks[21], (DEPTH, MIX_WIDTH), 0.02),
        "w_out": nrm(ks[22], (DEPTH, MIX_WIDTH, d), MIX_WIDTH ** -0.5),
        "norm2_g": 1.0 + nrm(ks[23], (DEPTH, d), 0.02),
        "ffn_w1": nrm(ks[24], (DEPTH, d, FFN_HIDDEN), d ** -0.5),
        "ffn_w3": nrm(ks[25], (DEPTH, d, FFN_HIDDEN), d ** -0.5),
        "ffn_w2": nrm(ks[26], (DEPTH, FFN_HIDDEN, d), FFN_HIDDEN ** -0.5),
        "final_norm_g": 1.0 + nrm(ks[27], (d,), 0.02),
    }


def reference(x, c, ctx, c_ctx, ada_w, ada_b, norm1_g, w_in, q_norm_g, k_norm_g, hy_conv_w, hy_conv_b,
              filt_w1, filt_b1, filt_w2, filt_b2, filt_w3, filt_freq, hy_bias, gla_gate_w, gla_gate_b,
              out_norm_g, w_out, norm2_g, ffn_w1, ffn_w3, ffn_w2, final_norm_g):
    rope_cos, rope_sin = axial_rope_tables(x.shape[1])
    silu_c = jax.nn.silu(c)
    silu_cc = jax.nn.silu(c_ctx)
    h_lat, h_ctx = x, ctx
    for i in range(DEPTH):
        mod_lat = (silu_c @ ada_w[i] + ada_b[i])[:, None, :]
        mod_ctx = (silu_cc @ ada_w[i] + ada_b[i])[None, None, :]
        h_lat, h_ctx = hybrid_layer(
            h_lat, h_ctx, mod_lat, mod_ctx, rope_cos, rope_sin, norm1_g[i], w_in[i], q_norm_g[i], k_norm_g[i],
            hy_conv_w[i], hy_conv_b[i], filt_w1[i], filt_b1[i], filt_w2[i], filt_b2[i], filt_w3[i], filt_freq[i],
            hy_bias[i], gla_gate_w[i], gla_gate_b[i], out_norm_g[i], w_out[i], norm2_g[i],
            ffn_w1[i], ffn_w3[i], ffn_w2[i], i < DEPTH - 1)
    return rms_norm(h_lat, final_norm_g)
```

```python
import math
import numpy as np
import concourse.bass as bass
import concourse.mybir as mybir
from concourse.bass_utils import run_bass_kernel_spmd

F32 = mybir.dt.float32
BF16 = mybir.dt.bfloat16
AF = mybir.ActivationFunctionType
ALU = mybir.AluOpType
AX = mybir.AxisListType

NCORES = 8
D = 1024
SEQ = 8192
CTX = 256
DEPTH = 4
IN_W = 2336
FFN = 2816
EPS = 1e-6
NSLOT = 8


class Tile:
    def __init__(self, h):
        self.h = h
        self.w = None
        self.r = {}

    def __getitem__(self, idx):
        return self.h[idx]


class Prog:
    def __init__(self, self_sync=True):
        self.nc = bass.Bass("TRN2", target_bir_lowering=False)
        nc = self.nc
        self.eng = {"pe": nc.tensor, "dve": nc.vector, "act": nc.scalar, "pool": nc.gpsimd, "sp": nc.sync}
        self.sems = {}
        self.cnt = {}
        self.known = {e: {} for e in self.eng}
        self.dslot = {"sp": 0, "act": 0, "pool": 0}
        self.self_sync = self_sync
        self._sem_cms = []
        self.n_inst = 0

    def _sem(self, sid):
        if sid not in self.sems:
            cm = self.nc.semaphore(sid)
            self._sem_cms.append(cm)
            self.sems[sid] = cm.__enter__()
            self.cnt[sid] = 0
        return self.sems[sid]

    def sb(self, name, shape, dtype=F32):
        return Tile(self.nc.alloc_sbuf_tensor("sb_" + name, list(shape), dtype))

    def ps(self, name, shape=(128, 512), dtype=F32):
        return Tile(self.nc.alloc_psum_tensor("ps_" + name, list(shape), dtype))

    def dram_in(self, name, shape, dtype=F32):
        return Tile(self.nc.dram_tensor(name, list(shape), dtype, kind="ExternalInput").ap())

    def dram_out(self, name, shape, dtype=F32):
        return Tile(self.nc.dram_tensor(name, list(shape), dtype, kind="ExternalOutput").ap())

    def _deps(self, eng, reads, writes):
        deps = {}

        def add(sid, v):
            if deps.get(sid, 0) < v:
                deps[sid] = v

        for t in reads:
            if t.w:
                add(*t.w)
        for t in writes:
            if t.w:
                add(*t.w)
            for sid, v in t.r.items():
                add(sid, v)
        if eng == "pe" or not self.self_sync:
            deps.pop("c_" + eng, None)
        return deps

    def _wait(self, eng, deps):
        e = self.eng[eng]
        kn = self.known[eng]
        for sid, val in deps.items():
            if kn.get(sid, 0) >= val:
                continue
            e.wait_ge(self.sems[sid], val)
            kn[sid] = val

    def _mark(self, tok, reads, writes):
        sid, v = tok
        for t in reads:
            if t.r.get(sid, 0) < v:
                t.r[sid] = v
        for t in writes:
            t.w = tok
            t.r = {}

    def op(self, eng, fn, reads=(), writes=()):
        self._wait(eng, self._deps(eng, reads, writes))
        inst = fn(self.eng[eng])
        sid = "c_" + eng
        sem = self._sem(sid)
        self.cnt[sid] += 1
        inst.then_inc(sem, 1)
        self._mark((sid, self.cnt[sid]), reads, writes)
        self.n_inst += 1
        return inst

    def dma(self, q, out, in_, reads=(), writes=(), **kw):
        deps = self._deps(q, reads, writes)
        slot = self.dslot[q]
        self.dslot[q] = (slot + 1) % NSLOT
        sid = "d_%s_%d" % (q, slot)
        sem = self._sem(sid)
        if self.cnt[sid] > 0 and deps.get(sid, 0) < self.cnt[sid]:
            deps[sid] = self.cnt[sid]
        self._wait(q, deps)
        inst = self.eng[q].dma_start(out=out, in_=in_, **kw)
        self.cnt[sid] += 16
        inst.then_inc(sem, 16)
        self._mark((sid, self.cnt[sid]), reads, writes)
        self.n_inst += 1
        return inst

    def finish(self):
        deps = {sid: c for sid, c in self.cnt.items() if sid.startswith("d_") and c > 0}
        self._wait("sp", deps)
        return self.nc

    def tt(self, eng, out, in0, in1, op, reads, writes):
        return self.op(eng, lambda e: e.tensor_tensor(out=out, in0=in0, in1=in1, op=op), reads, writes)

    def ts(self, eng, out, in0, s1, op0, reads, writes, s2=None, op1=None):
        if op1 is None:
            return self.op(eng, lambda e: e.tensor_scalar(out=out, in0=in0, scalar1=s1, scalar2=None, op0=op0), reads, writes)
        return self.op(eng, lambda e: e.tensor_scalar(out=out, in0=in0, scalar1=s1, scalar2=s2, op0=op0, op1=op1), reads, writes)

    def stt(self, out, in0, scalar, in1, op0, op1, reads, writes):
        return self.op("dve", lambda e: e.scalar_tensor_tensor(out=out, in0=in0, scalar=scalar, in1=in1, op0=op0, op1=op1), reads, writes)

    def act(self, out, in_, func, reads, writes, bias=None, scale=None, accum_out=None):
        kw = {}
        if bias is not None:
            kw["bias"] = bias
        if scale is not None:
            kw["scale"] = scale
        if accum_out is not None:
            kw["accum_out"] = accum_out
        return self.op("act", lambda e: e.activation(out=out, in_=in_, func=func, **kw), reads, writes)

    def mm(self, out, lhsT, rhs, start, stop, reads, writes):
        return self.op("pe", lambda e: e.matmul(out, lhsT, rhs, start=start, stop=stop), reads, writes)

    def tr(self, out, in_, ident, reads, writes):
        return self.op("pe", lambda e: e.transpose(out, in_, ident), reads, writes)

    def rsqrt(self, out_t, out_ap, in_t, in_ap, scale, eps_t):
        self.act(out_ap, in_ap, AF.Sqrt, [in_t, eps_t], [out_t], bias=eps_t[0:in_ap.shape[0], 0:1], scale=scale)
        self.op("dve", lambda e: e.reciprocal(out=out_ap, in_=out_ap), [out_t], [out_t])


def run(prog, in_maps):
    nc = prog.finish()
    res = run_bass_kernel_spmd(nc, in_maps, core_ids=list(range(len(in_maps))))
    return res.results


ADA_COLS = DEPTH * 6 * D // NCORES


def build_ada():
    p = Prog()
    cT = p.dram_in("cT", [128, 8, 3])
    w = p.dram_in("w", [D, ADA_COLS])
    b = p.dram_in("b", [1, ADA_COLS])
    out = p.dram_out("mod", [3, ADA_COLS])
    W = p.sb("W", [128, 8, ADA_COLS])
    cs = p.sb("cs", [128, 8, 3])
    bb = p.sb("bb", [3, ADA_COLS])
    res = p.sb("res", [3, ADA_COLS])
    p.dma("sp", cs[:], cT[:], [cT], [cs])
    p.dma("sp", bb[:], b[:].partition_broadcast(3)[:, 0, :], [b], [bb])
    for c in range(8):
        p.dma("sp" if c % 2 == 0 else "pool", W[:, c, :], w[c * 128:(c + 1) * 128, :], [w], [W])
    p.act(cs[:], cs[:], AF.Silu, [cs], [cs])
    pss = [p.ps("ps%d" % i) for i in range(2)]
    for j in range(ADA_COLS // 512):
        ps = pss[j % 2]
        for c in range(8):
            p.mm(ps[0:3, :], cs[:, c, :], W[:, c, j * 512:(j + 1) * 512], c == 0, c == 7, [cs, W], [ps])
        p.tt("dve", res[:, j * 512:(j + 1) * 512], ps[0:3, :], bb[:, j * 512:(j + 1) * 512], ALU.add, [ps, bb], [res])
    p.dma("sp", out[:], res[:], [res], [out])
    return p


def run_ada(c, c_ctx, ada_w, ada_b):
    cvec = np.concatenate([c, c_ctx[None, :]], axis=0)
    cT = np.ascontiguousarray(cvec.T.reshape(8, 128, 3).transpose(1, 0, 2))
    wall = np.ascontiguousarray(ada_w.transpose(1, 0, 2).reshape(D, DEPTH * 6 * D))
    ball = ada_b.reshape(1, DEPTH * 6 * D)
    in_maps = []
    for i in range(NCORES):
        sl = slice(i * ADA_COLS, (i + 1) * ADA_COLS)
        in_maps.append({"cT": cT, "w": np.ascontiguousarray(wall[:, sl]), "b": np.ascontiguousarray(ball[:, sl])})
    res = run(build_ada(), in_maps)
    mod = np.concatenate([r["mod"] for r in res], axis=1)
    return mod.reshape(3, DEPTH, 6, D)


NT1 = 17
OUT1 = 2560
GROUPS1 = [(0, 512), (512, 1024), (1024, 1536), (1536, 2048), (2048, 2336)]


def build_l1():
    p = Prog()
    ntok = NT1 * 128
    x_tm = p.dram_in("x_tm", [ntok, D])
    x_fm = p.dram_in("x_fm", [128, 8, ntok])
    w_in = p.dram_in("w_in", [D, IN_W])
    vecs = p.dram_in("vecs", [128, 8, 5])
    qkg = p.dram_in("qkg", [1, 128])
    cs_d = p.dram_in("cs", [ntok, 64])
    gw = p.dram_in("gw", [2, 16, 128])
    gb = p.dram_in("gb", [1, 256])
    ident_d = p.dram_in("ident", [128, 128])
    out = p.dram_out("out", [ntok, OUT1])

    W = p.sb("W", [128, 8, IN_W], BF16)
    V = p.sb("V", [128, 8, 5])
    A = p.sb("A", [128, 8, 2])
    Brep = p.sb("Brep", [128, 8, 128], BF16)
    bW = [p.sb("bW%d" % k, [128, IN_W]) for k in range(2)]
    gains = p.sb("gains", [128, 10, 64])
    Wblk = p.sb("Wblk", [32, 256])
    gbb = p.sb("gbb", [128, 256])
    ident = p.sb("ident", [128, 128])
    epst = p.sb("eps", [128, 1])
    pss = [p.ps("ps%d" % i) for i in range(6)]
    pst = p.ps("pst")
    psg = p.ps("psg")

    p.op("dve", lambda e: e.memset(epst[:], EPS), [], [epst])
    p.op("dve", lambda e: e.memset(Wblk[:], 0.0), [], [Wblk])
    p.dma("sp", V[:], vecs[:], [vecs], [V])
    p.dma("sp", ident[:], ident_d[:], [ident_d], [ident])
    p.dma("sp", gbb[:], gb[:].partition_broadcast(128)[:, 0, :], [gb], [gbb])
    p.dma("sp", Wblk[0:16, 0:128], gw[0], [gw], [Wblk])
    p.dma("sp", Wblk[16:32, 128:256], gw[1], [gw], [Wblk])
    for h in range(10):
        src = qkg[:, 0:64] if h < 8 else qkg[:, 64:128]
        p.dma("sp", gains[:, h, :], src.partition_broadcast(128)[:, 0, :], [qkg], [gains])
    p.ts("dve", gains[:, 0:8, :], gains[:, 0:8, :], 0.125, ALU.mult, [gains], [gains])
    for c in range(8):
        p.dma("pool", W[:, c, :], w_in[c * 128:(c + 1) * 128, :], [w_in], [W], max_dma_last_dim=4096)
    for k in range(2):
        p.stt(A[:, :, k], V[:, :, 1 + 2 * k], 1.0, V[:, :, 0], ALU.add, ALU.mult, [V], [A])
        p.op("dve", lambda e: e.tensor_copy(out=Brep[:], in_=V[:, :, 2 + 2 * k].unsqueeze(2).broadcast_to([128, 8, 128])), [V], [Brep])
        for gi, (a, b) in enumerate(GROUPS1):
            ps = pss[gi]
            for c in range(8):
                p.mm(ps[:, 0:b - a], Brep[:, c, :], W[:, c, a:b], c == 0, c == 7, [Brep, W], [ps])
            p.act(bW[k][:, a:b], ps[:, 0:b - a], AF.Identity, [ps], [bW[k]])

    NB = 2
    xt = [p.sb("xt%d" % i, [128, D]) for i in range(NB)]
    xf = [p.sb("xf%d" % i, [128, 8, 128]) for i in range(NB)]
    xa = [p.sb("xa%d" % i, [128, 8, 128], BF16) for i in range(NB)]
    cst = [p.sb("cst%d" % i, [128, 64]) for i in range(NB)]
    O = [p.sb("O%d" % i, [128, OUT1]) for i in range(NB)]
    junk = p.sb("junk", [128, D])
    sq = p.sb("sq", [128, 640])
    tmp = p.sb("tmp", [128, 10, 32])
    st = [p.sb("st%d" % i, [128, 16]) for i in range(NB)]
    lr = p.sb("lr", [128, 32])
    lrT = p.sb("lrT", [32, 128])
    gz = p.sb("gz", [128, 256])
    ga = p.sb("ga", [128, 256])
    pi = 0
    for i in range(NT1):
        kind = 0 if i < 16 else 1
        bi = i % NB
        rows = slice(i * 128, (i + 1) * 128)
        p.dma("sp", xt[bi][:], x_tm[rows, :], [x_tm], [xt[bi]])
        p.dma("sp", xf[bi][:], x_fm[:, :, rows], [x_fm], [xf[bi]])
        p.dma("sp", cst[bi][:], cs_d[rows, :], [cs_d], [cst[bi]])
        s = st[bi]
        p.act(junk[:], xt[bi][:], AF.Square, [xt[bi]], [junk, s], accum_out=s[:, 0:1])
        p.rsqrt(s, s[:, 1:2], s, s[:, 0:1], 1.0 / D, epst)
        p.tt("dve", xa[bi][:], xf[bi][:], A[:, :, kind].unsqueeze(2).broadcast_to([128, 8, 128]), ALU.mult, [xf[bi], A], [xa[bi]])
        o = O[bi]
        for gi, (a, b) in enumerate(GROUPS1):
            ps = pss[pi % 6]
            pi += 1
            for c in range(8):
                p.mm(ps[:, 0:b - a], xa[bi][:, c, :], W[:, c, a:b], c == 0, c == 7, [xa[bi], W], [ps])
            if gi < 4:
                p.stt(o[:, a:b], ps[:, 0:b - a], s[:, 1:2], bW[kind][:, a:b], ALU.mult, ALU.add, [ps, s, bW[kind]], [o])
            else:
                p.stt(o[:, 2048:2304], ps[:, 0:256], s[:, 1:2], bW[kind][:, 2048:2304], ALU.mult, ALU.add, [ps, s, bW[kind]], [o])
                p.stt(lr[:], ps[:, 256:288], s[:, 1:2], bW[kind][:, 2304:2336], ALU.mult, ALU.add, [ps, s, bW[kind]], [lr])
        qk = o[:, 0:640]
        p.tt("pool", sq[:], qk, qk, ALU.mult, [o], [sq])
        p.op("dve", lambda e: e.tensor_reduce(out=s[:, 2:12], in_=sq[:].rearrange("p (h d) -> p h d", d=64), axis=AX.X, op=ALU.add), [sq], [s])
        p.rsqrt(s, s[:, 2:12], s, s[:, 2:12], 1.0 / 64, epst)
        qk3 = qk.rearrange("p (h d) -> p h d", d=64)
        p.tt("dve", qk3, qk3, s[:, 2:12].unsqueeze(2).broadcast_to([128, 10, 64]), ALU.mult, [o, s], [o])
        p.tt("pool", qk3, qk3, gains[:], ALU.mult, [o, gains], [o])
        x1 = qk3[:, :, 0:32]
        x2 = qk3[:, :, 32:64]
        cb = cst[bi][:, 0:32].unsqueeze(1).broadcast_to([128, 10, 32])
        sb_ = cst[bi][:, 32:64].unsqueeze(1).broadcast_to([128, 10, 32])
        t3 = sq[:, 0:320].rearrange("p (h d) -> p h d", d=32)
        t4 = sq[:, 320:640].rearrange("p (h d) -> p h d", d=32)
        p.tt("dve", tmp[:], x2, sb_, ALU.mult, [o, cst[bi]], [tmp])
        p.tt("pool", t3, x1, sb_, ALU.mult, [o, cst[bi]], [sq])
        p.tt("dve", x1, x1, cb, ALU.mult, [o, cst[bi]], [o])
        p.tt("dve", x1, x1, tmp[:], ALU.subtract, [o, tmp], [o])
        p.tt("dve", x2, x2, cb, ALU.mult, [o, cst[bi]], [o])
        p.tt("dve", x2, x2, t3, ALU.add, [o, sq], [o])
        p.act(o[:, 1536:1664], o[:, 1536:1664], AF.Identity, [o], [o], scale=32 ** -0.5)
        p.act(o[:, 2048:2304], o[:, 2048:2304], AF.Silu, [o], [o])
        p.tr(pst[0:32, 0:128], lr[:], ident[:], [lr, ident], [pst])
        p.act(lrT[:], pst[0:32, 0:128], AF.Identity, [pst], [lrT])
        p.mm(psg[:, 0:256], lrT[:], Wblk[:], True, True, [lrT, Wblk], [psg])
        p.tt("dve", gz[:], psg[:, 0:256], gbb[:], ALU.add, [psg, gbb], [gz])
        p.stt(ga[:], gz[:], -1.0, gz[:], ALU.mult, ALU.min, [gz], [ga])
        p.act(ga[:], ga[:], AF.Exp, [ga], [ga])
        p.act(ga[:], ga[:], AF.Ln, [ga], [ga], bias=1.0)
        p.ts("dve", gz[:], gz[:], 0.0, ALU.min, [gz], [gz], s2=1.0 / 16, op1=ALU.mult)
        p.stt(o[:, 2304:2560], ga[:], -1.0 / 16, gz[:], ALU.mult, ALU.add, [ga, gz], [o])
        p.dma("sp", out[rows, :], o[:], [o], [out])
    return p


def fm(v):
    return np.ascontiguousarray(v.reshape(-1, 128).T)


def tok_split(lat, ctx):
    F = lat.shape[-1]
    outs = []
    for i in range(NCORES):
        b, j = divmod(i, 4)
        a = np.zeros((NT1 * 128, F), lat.dtype)
        a[:2048] = lat[b, j * 2048:(j + 1) * 2048]
        if j < 2:
            a[2048:] = ctx[b, j * 128:(j + 1) * 128]
        outs.append(a)
    return outs


def tok_merge(per_core):
    F = per_core[0].shape[-1]
    lat = np.zeros((2, SEQ, F), per_core[0].dtype)
    ctx = np.zeros((2, CTX, F), per_core[0].dtype)
    for i in range(NCORES):
        b, j = divmod(i, 4)
        lat[b, j * 2048:(j + 1) * 2048] = per_core[i][:2048]
        if j < 2:
            ctx[b, j * 128:(j + 1) * 128] = per_core[i][2048:]
    return lat, ctx


def rope_table():
    t = np.arange(SEQ)
    row = (t // 64).astype(np.float32)
    col = (t % 64).astype(np.float32)
    inv = np.power(np.float32(10000.0), -np.arange(16, dtype=np.float32) / np.float32(16)).astype(np.float32)
    ang = np.concatenate([row[:, None] * inv, col[:, None] * inv], axis=-1).astype(np.float32)
    return np.concatenate([np.cos(ang), np.sin(ang)], axis=-1).astype(np.float32)


_PROGS = {}


def get_prog(name, builder):
    return builder()


def run_l1(h_lat, h_ctx, mod, layer, w):
    xs = tok_split(h_lat, h_ctx)
    rope = rope_table()
    one = np.concatenate([np.ones((128, 32), np.float32), np.zeros((128, 32), np.float32)], axis=1)
    ident = np.eye(128, dtype=np.float32)
    in_maps = []
    for i in range(NCORES):
        b, j = divmod(i, 4)
        x = xs[i]
        vecs = np.stack([fm(w["norm1_g"][layer]), fm(mod[b, layer, 1]), fm(mod[b, layer, 0]),
                         fm(mod[2, layer, 1]), fm(mod[2, layer, 0])], axis=-1)
        cs = np.concatenate([rope[j * 2048:(j + 1) * 2048], one], axis=0)
        in_maps.append({
            "x_tm": x,
            "x_fm": np.ascontiguousarray(x.T.reshape(8, 128, NT1 * 128).transpose(1, 0, 2)),
            "w_in": w["w_in"][layer],
            "vecs": np.ascontiguousarray(vecs),
            "qkg": np.concatenate([w["q_norm_g"][layer], w["k_norm_g"][layer]])[None, :],
            "cs": cs,
            "gw": w["gla_gate_w"][layer],
            "gb": w["gla_gate_b"][layer].reshape(1, 256),
            "ident": ident,
        })
    res = run(build_l1(), in_maps)
    return tok_merge([r["out"] for r in res])


NQT = 33
NKT = 66


def build_l2():
    p = Prog()
    qT_d = p.dram_in("qT", [64, NQT, 512])
    kT_d = p.dram_in("kT", [64, NKT * 128])
    vA_d = p.dram_in("vA", [128, NKT, 65])
    ident_d = p.dram_in("ident", [128, 128])
    out = p.dram_out("out", [NQT * 128, 256])

    qT = p.sb("qT", [64, NQT, 512], BF16)
    kT = p.sb("kT", [64, NKT * 128], BF16)
    vA = p.sb("vA", [128, NKT, 65], BF16)
    ident = p.sb("ident", [128, 128])
    p.dma("sp", ident[:], ident_d[:], [ident_d], [ident])
    p.dma("pool", kT[:, 0:4096], kT_d[:, 0:4096], [kT_d], [kT], max_dma_last_dim=4096)
    p.dma("pool", kT[:, 4096:], kT_d[:, 4096:], [kT_d], [kT], max_dma_last_dim=4096)
    p.dma("pool", vA[:], vA_d[:], [vA_d], [vA], max_dma_last_dim=4096)
    for a in range(0, NQT, 3):
        p.dma("pool", qT[:, a:a + 3, :], qT_d[:, a:a + 3, :], [qT_d], [qT], max_dma_last_dim=4096)

    psS = [p.ps("psS%d" % i) for i in range(3)]
    psO = [p.ps("psO%d" % i) for i in range(2)]
    psT = [p.ps("psT%d" % i) for i in range(2)]
    pT = [p.sb("pT%d" % i, [128, 512], BF16) for i in range(3)]
    oT = [p.sb("oT%d" % i, [65, 512]) for i in range(2)]
    ot = [p.sb("ot%d" % i, [128, 256]) for i in range(2)]
    rec = p.sb("rec", [128, 8])
    it = 0
    for qt in range(NQT):
        kts = list(range(NKT)) if qt < 32 else [64, 65]
        po = psO[qt % 2]
        for j, kt in enumerate(kts):
            ps = psS[it % 3]
            pt = pT[it % 3]
            it += 1
            p.mm(ps[:], kT[:, kt * 128:(kt + 1) * 128], qT[:, qt, :], True, True, [kT, qT], [ps])
            p.act(pt[:], ps[:], AF.Exp, [ps], [pt])
            p.mm(po[0:65, :], vA[:, kt, :], pt[:], j == 0, j == len(kts) - 1, [vA, pt], [po])
        o_sb = oT[qt % 2]
        p.op("dve", lambda e: e.tensor_copy(out=o_sb[:], in_=po[0:65, :]), [po], [o_sb])
        o_t = ot[qt % 2]
        for h in range(4):
            pst = psT[h % 2]
            p.tr(pst[:, 0:65], o_sb[:, h * 128:(h + 1) * 128], ident[0:65, 0:65], [o_sb, ident], [pst])
            rc = rec[:, (qt % 2) * 4 + h:(qt % 2) * 4 + h + 1]
            p.op("dve", lambda e: e.reciprocal(out=rc, in_=pst[:, 64:65]), [pst], [rec])
            p.ts("dve", o_t[:, h * 64:(h + 1) * 64], pst[:, 0:64], rc, ALU.mult, [pst, rec], [o_t])
        p.dma("sp", out[qt * 128:(qt + 1) * 128, :], o_t[:], [o_t], [out])
    return p


def run_l2(lat, ctx):
    ident = np.eye(128, dtype=np.float32)
    in_maps = []
    for i in range(NCORES):
        b, g, qh = i // 4, (i // 2) % 2, i % 2
        ql = lat[b, qh * 4096:(qh + 1) * 4096, 0:512].reshape(32, 128, 8, 64)[:, :, 4 * g:4 * g + 4, :]
        qc = ctx[b, qh * 128:(qh + 1) * 128, 0:512].reshape(1, 128, 8, 64)[:, :, 4 * g:4 * g + 4, :]
        q = np.concatenate([ql, qc], axis=0)
        qT = np.ascontiguousarray(q.transpose(3, 0, 2, 1)).reshape(64, NQT, 512)
        k_all = np.concatenate([lat[b, :, 512:640], ctx[b, :, 512:640]], axis=0)[:, g * 64:(g + 1) * 64]
        kT = np.ascontiguousarray(k_all.T)
        v_all = np.concatenate([lat[b, :, 640:768], ctx[b, :, 640:768]], axis=0)[:, g * 64:(g + 1) * 64]
        vA = np.ones((128, NKT, 65), np.float32)
        vA[:, :, 0:64] = v_all.reshape(NKT, 128, 64).transpose(1, 0, 2)
        in_maps.append({"qT": qT, "kT": kT, "vA": vA, "ident": ident})
    res = run(build_l2(), in_maps)
    a_lat = np.zeros((2, SEQ, 512), np.float32)
    a_ctx = np.zeros((2, CTX, 512), np.float32)
    for i in range(NCORES):
        b, g, qh = i // 4, (i // 2) % 2, i % 2
        o = res[i]["out"]
        a_lat[b, qh * 4096:(qh + 1) * 4096, g * 256:(g + 1) * 256] = o[:4096]
        a_ctx[b, qh * 128:(qh + 1) * 128, g * 256:(g + 1) * 256] = o[4096:]
    return a_lat, a_ctx


def build_l5a():
    p = Prog()
    ntok = NT1 * 128
    y_d = p.dram_in("y", [ntok, D])
    sr_d = p.dram_in("sr", [ntok, 256])
    yb_d = p.dram_in("yb", [ntok, 256])
    h_d = p.dram_in("h", [ntok, D])
    w_d = p.dram_in("w_out", [D, D])
    og_d = p.dram_in("og", [128, 8])
    g1_d = p.dram_in("g1", [2, D])
    ident_d = p.dram_in("ident", [128, 128])
    out = p.dram_out("out", [ntok, D])

    W = p.sb("W", [128, 8, D], BF16)
    og = p.sb("og", [128, 8])
    g1 = [p.sb("g1_%d" % k, [128, D]) for k in range(2)]
    ident = p.sb("ident", [128, 128])
    epst = p.sb("eps", [128, 1])
    p.op("dve", lambda e: e.memset(epst[:], EPS), [], [epst])
    p.dma("sp", og[:], og_d[:], [og_d], [og])
    p.dma("sp", ident[:], ident_d[:], [ident_d], [ident])
    for k in range(2):
        p.dma("sp", g1[k][:], g1_d[k:k + 1, :].partition_broadcast(128)[:, 0, :], [g1_d], [g1[k]])
    for c in range(8):
        p.dma("pool", W[:, c, :], w_d[c * 128:(c + 1) * 128, :], [w_d], [W], max_dma_last_dim=4096)
    NB = 2
    yt = [p.sb("yt%d" % i, [128, D]) for i in range(NB)]
    ht = [p.sb("ht%d" % i, [128, D]) for i in range(NB)]
    srt = [p.sb("srt%d" % i, [128, 256]) for i in range(NB)]
    ybt = [p.sb("ybt%d" % i, [128, 256]) for i in range(NB)]
    ho = [p.sb("ho%d" % i, [128, D]) for i in range(NB)]
    yT = [p.sb("yT%d" % i, [128, 8, 128], BF16) for i in range(NB)]
    sq = p.sb("sq", [128, D])
    st = [p.sb("st%d" % i, [128, 16]) for i in range(NB)]
    pst = [p.ps("pst%d" % i) for i in range(2)]
    pso = [p.ps("pso%d" % i) for i in range(4)]
    for i in range(NT1):
        kind = 0 if i < 16 else 1
        bi = i % NB
        rows = slice(i * 128, (i + 1) * 128)
        y, h, sr, s = yt[bi], ht[bi], srt[bi], st[bi]
        p.dma("sp", y[:], y_d[rows, :], [y_d], [y])
        p.dma("sp", sr[:], sr_d[rows, :], [sr_d], [sr])
        p.dma("sp", h[:], h_d[rows, :], [h_d], [h])
        p.dma("sp", ybt[bi][:], yb_d[rows, :], [yb_d], [ybt[bi]])
        p.tt("pool", y[:, 768:1024], y[:, 768:1024], ybt[bi][:], ALU.add, [y, ybt[bi]], [y])
        p.tt("pool", sq[:], y[:], y[:], ALU.mult, [y], [sq])
        p.op("dve", lambda e: e.tensor_reduce(out=s[:], in_=sq[:].rearrange("p (h d) -> p h d", d=64), axis=AX.X, op=ALU.add), [sq], [s])
        p.rsqrt(s, s[:], s, s[:], 1.0 / 64, epst)
        y3 = y[:].rearrange("p (h d) -> p h d", d=64)
        p.tt("dve", y3, y3, s[:].unsqueeze(2).broadcast_to([128, 16, 64]), ALU.mult, [y, s], [y])
        p.tt("pool", y[:, 768:1024], y[:, 768:1024], sr[:], ALU.mult, [y, sr], [y])
        for c in range(8):
            ps = pst[c // 4]
            blk = ps[:, (c % 4) * 128:(c % 4 + 1) * 128]
            p.tr(blk, y[:, c * 128:(c + 1) * 128], ident[:], [y, ident], [ps])
            p.act(yT[bi][:, c, :], blk, AF.Identity, [ps, og], [yT[bi]], scale=og[:, c:c + 1])
        for hf in range(2):
            ps = pso[(2 * i + hf) % 4]
            cols = slice(hf * 512, (hf + 1) * 512)
            for c in range(8):
                p.mm(ps[:], yT[bi][:, c, :], W[:, c, cols], c == 0, c == 7, [yT[bi], W], [ps])
            p.tt("dve", ho[bi][:, cols], ps[:], g1[kind][:, cols], ALU.mult, [ps, g1[kind]], [ho[bi]])
            p.tt("pool", ho[bi][:, cols], ho[bi][:, cols], h[:, cols], ALU.add, [ho[bi], h], [ho[bi]])
        p.dma("sp", out[rows, :], ho[bi][:], [ho[bi]], [out])
    return p


def run_l5a(y_lat, y_ctx, yb_lat, yb_ctx, sr_lat, sr_ctx, h_lat, h_ctx, mod, layer, w):
    ys = tok_split(y_lat, y_ctx)
    ybs = tok_split(yb_lat, yb_ctx)
    srs = tok_split(sr_lat, sr_ctx)
    hs = tok_split(h_lat, h_ctx)
    ident = np.eye(128, dtype=np.float32)
    in_maps = []
    for i in range(NCORES):
        b = i // 4
        in_maps.append({"y": ys[i], "yb": ybs[i], "sr": srs[i], "h": hs[i], "w_out": w["w_out"][layer],
                        "og": fm(w["out_norm_g"][layer]),
                        "g1": np.ascontiguousarray(np.stack([mod[b, layer, 2], mod[2, layer, 2]])),
                        "ident": ident})
    res = run(build_l5a(), in_maps)
    return tok_merge([r["out"] for r in res])


NF = FFN // 128


def build_l5b(final):
    p = Prog()
    ntok = NT1 * 128
    h_d = p.dram_in("h", [ntok, D])
    w1_d = p.dram_in("w1", [D, FFN])
    w3_d = p.dram_in("w3", [D, FFN])
    w2_d = p.dram_in("w2", [FFN, D])
    vec_d = p.dram_in("vecs", [128, 8, 5])
    g2_d = p.dram_in("g2", [3, D])
    ident_d = p.dram_in("ident", [128, 128])
    out = p.dram_out("out", [ntok, D])

    W1 = p.sb("W1", [128, 8, FFN], BF16)
    W3 = p.sb("W3", [128, 8, FFN], BF16)
    W2 = p.sb("W2", [128, NF, D], BF16)
    V = p.sb("V", [128, 8, 5])
    A = p.sb("A", [128, 8, 2])
    g2 = [p.sb("g2_%d" % k, [128, D]) for k in range(3 if final else 2)]
    ident = p.sb("ident", [128, 128])
    epst = p.sb("eps", [128, 1])
    p.op("dve", lambda e: e.memset(epst[:], EPS), [], [epst])
    p.dma("sp", V[:], vec_d[:], [vec_d], [V])
    p.dma("sp", ident[:], ident_d[:], [ident_d], [ident])
    for k in range(len(g2)):
        p.dma("sp", g2[k][:], g2_d[k:k + 1, :].partition_broadcast(128)[:, 0, :], [g2_d], [g2[k]])
    for c in range(8):
        p.dma("pool", W1[:, c, :], w1_d[c * 128:(c + 1) * 128, :], [w1_d], [W1], max_dma_last_dim=4096)
        p.dma("pool", W3[:, c, :], w3_d[c * 128:(c + 1) * 128, :], [w3_d], [W3], max_dma_last_dim=4096)
    for f in range(NF):
        p.dma("pool", W2[:, f, :], w2_d[f * 128:(f + 1) * 128, :], [w2_d], [W2], max_dma_last_dim=4096)
    for k in range(2):
        p.stt(A[:, :, k], V[:, :, 1 + 2 * k], 1.0, V[:, :, 0], ALU.add, ALU.mult, [V], [A])

    G = 2
    hts = [p.sb("ht%d" % i, [128, D]) for i in range(2 * G)]
    xs = p.sb("xs", [128, D])
    junk = p.sb("junk", [128, D])
    u2T = [p.sb("u2T%d" % i, [128, 8, G * 128], BF16) for i in range(2)]
    hidT = p.sb("hidT", [128, NF, G * 128], BF16)
    sil = [p.sb("sil%d" % i, [128, G * 128]) for i in range(2)]
    ho = [p.sb("ho%d" % i, [128, D]) for i in range(2)]
    st = p.sb("st", [128, 8])
    pst = [p.ps("pst%d" % i) for i in range(2)]
    psu = [p.ps("psu%d" % i) for i in range(4)]
    psd = [p.ps("psd%d" % i) for i in range(2)]
    groups = [list(range(a, min(a + G, 16))) for a in range(0, 16, G)] + [[16]]
    nd = 0
    for gi, tiles in enumerate(groups):
        kind = 0 if tiles[0] < 16 else 1
        N = len(tiles) * 128
        u = u2T[gi % 2]
        for j, i in enumerate(tiles):
            h = hts[(gi % 2) * G + j]
            rows = slice(i * 128, (i + 1) * 128)
            p.dma("sp", h[:], h_d[rows, :], [h_d], [h])
            sc = st[:, 2 * j:2 * j + 1]
            sr_ = st[:, 2 * j + 1:2 * j + 2]
            p.act(junk[:], h[:], AF.Square, [h], [junk, st], accum_out=sc)
            p.rsqrt(st, sr_, st, sc, 1.0 / D, epst)
            p.act(xs[:], h[:], AF.Identity, [h, st], [xs], scale=sr_)
            for c in range(8):
                ps = pst[c // 4]
                blk = ps[:, (c % 4) * 128:(c % 4 + 1) * 128]
                p.tr(blk, xs[:, c * 128:(c + 1) * 128], ident[:], [xs, ident], [ps])
                p.act(u[:, c, j * 128:(j + 1) * 128], blk, AF.Identity, [ps, A, V], [u],
                      scale=A[:, c, kind:kind + 1], bias=V[:, c, 2 + 2 * kind:3 + 2 * kind])
        for f in range(NF):
            ps1 = psu[(2 * f) % 4]
            ps3 = psu[(2 * f + 1) % 4]
            fc = slice(f * 128, (f + 1) * 128)
            for c in range(8):
                p.mm(ps1[:, 0:N], W1[:, c, fc], u[:, c, 0:N], c == 0, c == 7, [W1, u], [ps1])
            for c in range(8):
                p.mm(ps3[:, 0:N], W3[:, c, fc], u[:, c, 0:N], c == 0, c == 7, [W3, u], [ps3])
            s_ = sil[f % 2]
            p.act(s_[:, 0:N], ps1[:, 0:N], AF.Silu, [ps1], [s_])
            p.tt("dve", hidT[:, f, 0:N], s_[:, 0:N], ps3[:, 0:N], ALU.mult, [s_, ps3], [hidT])
        for j, i in enumerate(tiles):
            h = hts[(gi % 2) * G + j]
            rows = slice(i * 128, (i + 1) * 128)
            o = ho[nd % 2]
            nd += 1
            for hf in range(2):
                ps = psd[hf]
                cols = slice(hf * 512, (hf + 1) * 512)
                for f in range(NF):
                    p.mm(ps[:], hidT[:, f, j * 128:(j + 1) * 128], W2[:, f, cols], f == 0, f == NF - 1, [hidT, W2], [ps])
                p.tt("dve", o[:, cols], ps[:], g2[kind][:, cols], ALU.mult, [ps, g2[kind]], [o])
                p.tt("pool", o[:, cols], o[:, cols], h[:, cols], ALU.add, [o, h], [o])
            if final:
                sc = st[:, 4:5]
                sr_ = st[:, 5:6]
                p.act(junk[:], o[:], AF.Square, [o], [junk, st], accum_out=sc)
                p.rsqrt(st, sr_, st, sc, 1.0 / D, epst)
                p.stt(o[:], o[:], sr_, g2[2][:], ALU.mult, ALU.mult, [o, st, g2[2]], [o])
            p.dma("sp", out[rows, :], o[:], [o], [out])
    return p


def run_l5b(h_lat, h_ctx, mod, layer, w, final):
    hs = tok_split(h_lat, h_ctx)
    ident = np.eye(128, dtype=np.float32)
    in_maps = []
    for i in range(NCORES):
        b = i // 4
        vecs = np.stack([fm(w["norm2_g"][layer]), fm(mod[b, layer, 4]), fm(mod[b, layer, 3]),
                         fm(mod[2, layer, 4]), fm(mod[2, layer, 3])], axis=-1)
        in_maps.append({"h": hs[i], "w1": w["ffn_w1"][layer], "w3": w["ffn_w3"][layer], "w2": w["ffn_w2"][layer],
                        "vecs": np.ascontiguousarray(vecs),
                        "g2": np.ascontiguousarray(np.stack([mod[b, layer, 5], mod[2, layer, 5], w["final_norm_g"]])),
                        "ident": ident})
    res = run(build_l5b(final), in_maps)
    return tok_merge([r["out"] for r in res])


TG = SEQ + CTX
NCH = TG // 64
NPR = TG // 128
SEGL = 1408
NSEG = TG // SEGL


def build_l4(stage=4):
    p = Prog()
    qkg_d = p.dram_in("qkg", [2, 3, 32, TG])
    v_d = p.dram_in("v", [2, 128, NPR, 64])
    mask_d = p.dram_in("mask", [128, 128])
    ident_d = p.dram_in("ident", [128, 128])
    out = p.dram_out("out", [2, 64, TG])

    ident = p.sb("ident", [128, 128])
    mask = p.sb("mask", [128, 128])
    ones = p.sb("ones", [32, 1])
    p.dma("sp", ident[:], ident_d[:], [ident_d], [ident])
    p.dma("sp", mask[:], mask_d[:], [mask_d], [mask])
    p.op("dve", lambda e: e.memset(ones[:], 1.0), [], [ones])

    qd = p.sb("qd", [32, TG], BF16)
    kd = p.sb("kd", [32, TG], BF16)
    ktT = [p.sb("ktT%d" % i, [128, NPR, 32], BF16) for i in range(2)]
    pm_d = p.dram_in("pm", [128, 2])
    pm = p.sb("pm", [128, 2])
    p.dma("sp", pm[:], pm_d[:], [pm_d], [pm])
    vt = p.sb("vt", [128, NPR, 64], BF16)
    STm = p.sb("STm", [128, NPR, 128], BF16)
    Sbf = p.sb("Sbf", [32, NCH + 1, 64], BF16)
    dc = p.sb("dc", [32, NCH])
    oT = p.sb("oT", [64, TG])
    Scur = [p.sb("Scur%d" % i, [32, 64]) for i in range(2)]
    gseg = p.sb("gseg", [32, SEGL])
    qseg = p.sb("qseg", [32, SEGL])
    kseg = p.sb("kseg", [32, SEGL])
    Gs = [p.sb("Gs%d" % i, [32, SEGL]) for i in range(2)]
    At = p.sb("At", [32, SEGL])
    Bt = p.sb("Bt", [32, SEGL])
    Sc = p.sb("Sc", [32, 22])
    tmpc = p.sb("tmpc", [32, 22])
    psT = [p.ps("psT%d" % i) for i in range(2)]
    psS = [p.ps("psS%d" % i) for i in range(2)]
    psK = [p.ps("psK%d" % i) for i in range(2)]
    psO = [p.ps("psO%d" % i) for i in range(2)]

    for d in range(2):
        p.dma("pool", vt[:], v_d[d], [v_d], [vt], max_dma_last_dim=4096)
        for s in range(NSEG):
            seg = slice(s * SEGL, (s + 1) * SEGL)
            G = Gs[s % 2]
            Gp = Gs[(s + 1) % 2]
            p.dma("sp", qseg[:], qkg_d[d, 0, :, seg], [qkg_d], [qseg])
            p.dma("sp", kseg[:], qkg_d[d, 1, :, seg], [qkg_d], [kseg])
            p.dma("sp", gseg[:], qkg_d[d, 2, :, seg], [qkg_d], [gseg])
            init = 0.0 if s == 0 else Gp[:, SEGL - 1:SEGL]
            rd = [gseg, ones] + ([] if s == 0 else [Gp])
            p.op("dve", lambda e: e.tensor_tensor_scan(out=G[:], data0=ones[:, 0:1].broadcast_to([32, SEGL]), data1=gseg[:],
                                                       initial=init, op0=ALU.mult, op1=ALU.add), rd, [G])
            G3 = G[:].rearrange("p (c j) -> p c j", j=64)
            Ec = G3[:, :, 63]
            if s == 0:
                p.op("dve", lambda e: e.memset(Sc[:, 0:1], 0.0), [], [Sc])
            else:
                p.op("dve", lambda e: e.tensor_copy(out=Sc[:, 0:1], in_=Gp[:, SEGL - 1:SEGL]), [Gp], [Sc])
            p.op("dve", lambda e: e.tensor_copy(out=Sc[:, 1:22], in_=G3[:, 0:21, 63]), [G], [Sc])
            A3 = At[:].rearrange("p (c j) -> p c j", j=64)
            p.tt("dve", A3, G3, Sc[:].unsqueeze(2).broadcast_to([32, 22, 64]), ALU.subtract, [G, Sc], [At])
            p.act(Bt[:], At[:], AF.Exp, [At], [Bt])
            p.tt("dve", qd[:, seg], qseg[:], Bt[:], ALU.mult, [qseg, Bt], [qd])
            p.act(Bt[:], At[:], AF.Exp, [At], [Bt], scale=-1.0)
            p.tt("dve", kd[:, seg], kseg[:], Bt[:], ALU.mult, [kseg, Bt], [kd])
            p.tt("dve", tmpc[:], Ec, Sc[:], ALU.subtract, [G, Sc], [tmpc])
            p.act(dc[:, s * 22:(s + 1) * 22], tmpc[:], AF.Exp, [tmpc], [dc])
            p.tt("dve", A3, Ec.unsqueeze(2).broadcast_to([32, 22, 64]), G3, ALU.subtract, [G], [At])
            p.act(At[:], At[:], AF.Exp, [At], [At])
            p.tt("dve", Bt[:], kseg[:], At[:], ALU.mult, [kseg, At], [Bt])
            ps = psT[s % 2]
            for j in range(11):
                p.tr(ps[:, j * 32:(j + 1) * 32], Bt[:, j * 128:(j + 1) * 128], ident[0:32, 0:32], [Bt, ident], [ps])
            for hf in range(2):
                p.act(ktT[hf][:, s * 11:(s + 1) * 11, :], ps[:, 0:352].rearrange("p (a b) -> p a b", b=32), AF.Identity,
                      [ps, pm], [ktT[hf]], scale=pm[:, hf:hf + 1])
        if stage < 4:
            p.op("dve", lambda e: e.memset(oT[:], 0.0), [], [oT])
        for g0 in (range(0, NPR, 4) if stage >= 2 else []):
            n = min(4, NPR - g0)
            ps = psS[(g0 // 4) % 2]
            for j in range(n):
                pr = g0 + j
                tok = slice(pr * 128, (pr + 1) * 128)
                p.mm(ps[:, j * 128:(j + 1) * 128], kd[:, tok], qd[:, tok], True, True, [kd, qd], [ps])
            p.tt("dve", STm[:, g0:g0 + n, :], ps[:, 0:n * 128].rearrange("p (a b) -> p a b", b=128),
                 mask[:].unsqueeze(1).broadcast_to([128, n, 128]), ALU.mult, [ps, mask], [STm])
        p.op("dve", lambda e: e.memset(Scur[0][:], 0.0), [], [Scur[0]])
        p.op("dve", lambda e: e.memset(Sbf[:, 0, :], 0.0), [], [Sbf])
        for c0 in (range(0, NCH, 8) if stage >= 3 else []):
            n = min(8, NCH - c0)
            ps = psK[(c0 // 8) % 2]
            for j in range(n):
                c = c0 + j
                pr, hf = divmod(c, 2)
                p.mm(ps[0:32, j * 64:(j + 1) * 64], ktT[hf][:, pr, :], vt[:, pr, :], True, True, [ktT[hf], vt], [ps])
            for j in range(n):
                c = c0 + j
                sa, sb_ = Scur[c % 2], Scur[(c + 1) % 2]
                p.stt(sb_[:], sa[:], dc[:, c:c + 1], ps[0:32, j * 64:(j + 1) * 64], ALU.mult, ALU.add, [sa, dc, ps], [sb_])
                p.act(Sbf[:, c + 1, :], sb_[:], AF.Identity, [sb_], [Sbf])
        for g0 in (range(0, NPR, 4) if stage >= 4 else []):
            n = min(4, NPR - g0)
            ps = psO[(g0 // 4) % 2]
            for j in range(n):
                pr = g0 + j
                cs_ = slice(j * 128, (j + 1) * 128)
                p.mm(ps[0:64, cs_], vt[:, pr, :], STm[:, pr, :], True, False, [vt, STm], [ps])
                for hf in range(2):
                    c = 2 * pr + hf
                    tok = slice(c * 64, (c + 1) * 64)
                    p.mm(ps[0:64, j * 128 + hf * 64:j * 128 + (hf + 1) * 64], Sbf[:, c, :], qd[:, tok], False, hf == 1, [Sbf, qd], [ps])
            p.act(oT[:, g0 * 128:(g0 + n) * 128], ps[0:64, 0:n * 128], AF.Identity, [ps], [oT])
        p.dma("sp", out[d], oT[:], [oT], [out])
    return p


L4_PM = np.stack([(np.arange(128) < 64), (np.arange(128) >= 64)], axis=1).astype(np.float32)


def run_l4(lat, ctx):
    ident = np.eye(128, dtype=np.float32)
    j = np.arange(128)
    mask = ((j[:, None] // 64 == j[None, :] // 64) & (j[None, :] >= j[:, None])).astype(np.float32)
    in_maps = []
    for i in range(NCORES):
        b, hd = divmod(i, 4)
        qkg = np.zeros((2, 3, 32, TG), np.float32)
        v = np.zeros((2, 128, NPR, 64), np.float32)
        for d in range(2):
            def seq(c0, w):
                a, l = ctx[b, :, c0:c0 + w], lat[b, :, c0:c0 + w]
                if d == 1:
                    a, l = a[::-1], l[::-1]
                return np.concatenate([a, l], axis=0)
            qkg[d, 0] = seq(1536 + hd * 32, 32).T
            qkg[d, 1] = seq(1664 + hd * 32, 32).T
            qkg[d, 2] = seq((2304 if d == 0 else 2432) + hd * 32, 32).T
            v[d] = seq(1792 + hd * 64, 64).reshape(NPR, 128, 64).transpose(1, 0, 2)
        in_maps.append({"qkg": qkg, "v": v, "mask": mask, "ident": ident, "pm": L4_PM})
    res = run(build_l4(), in_maps)
    f_lat = np.zeros((2, SEQ, 256), np.float32)
    b_lat = np.zeros((2, SEQ, 256), np.float32)
    f_ctx = np.zeros((2, CTX, 256), np.float32)
    b_ctx = np.zeros((2, CTX, 256), np.float32)
    for i in range(NCORES):
        b, hd = divmod(i, 4)
        o = res[i]["out"]
        cols = slice(hd * 64, (hd + 1) * 64)
        f_ctx[b, :, cols] = o[0].T[:CTX]
        f_lat[b, :, cols] = o[0].T[CTX:]
        b_ctx[b, :, cols] = o[1].T[:CTX][::-1]
        b_lat[b, :, cols] = o[1].T[CTX:][::-1]
    return f_lat, b_lat, f_ctx, b_ctx


NFFT = 16384
PI = math.pi


def hy_consts():
    n1 = np.arange(64)[:, None]
    k = np.arange(128)[None, :]
    th = 2 * np.pi * n1 * k / 128
    F1 = np.concatenate([np.cos(th), -np.sin(th)], axis=1)
    n2 = np.arange(128)[:, None]
    tw = 2 * np.pi * n2 * k / NFFT
    TWf = np.stack([np.cos(tw), -np.sin(tw)], axis=1)
    th2 = 2 * np.pi * n2 * k / 128
    F2 = np.stack([np.cos(th2), -np.sin(th2), np.sin(th2)], axis=1)
    G2 = np.stack([np.concatenate([np.cos(th2), np.sin(th2)], axis=1),
                   np.concatenate([-np.sin(th2), np.cos(th2)], axis=1)], axis=1)
    TWi = np.stack([np.cos(tw.T), np.sin(tw.T)], axis=1)
    k1 = np.arange(128)[:, None]
    n1r = np.arange(64)[None, :]
    th1 = 2 * np.pi * k1 * n1r / 128
    G1 = np.stack([np.cos(th1), -np.sin(th1)], axis=1) / NFFT
    f32 = lambda a: np.ascontiguousarray(a, dtype=np.float32)
    return {"F1": f32(F1), "TWf": f32(TWf), "F2": f32(F2), "G2": f32(G2), "TWi": f32(TWi), "G1": f32(G1)}


def hy_features(n):
    t = np.linspace(0.0, 1.0, n, dtype=np.float32)[:, None]
    omega = (np.float32(2.0 * math.pi / n) * np.arange(n, dtype=np.float32)).astype(np.float32)
    bands = np.linspace(1e-4, 15, 16, dtype=np.float32)
    phase = (omega[:, None] * bands[None, :]).astype(np.float32)
    z = np.concatenate([t, np.cos(phase), -np.sin(phase)], axis=-1).astype(np.float32)
    mn, mx = math.log(1e-2) / 1.5, math.log(1e-2) / 0.3
    deltas = np.abs(np.linspace(mn, mx, 256, dtype=np.float32))
    window = np.exp(-t * deltas[None, :]).astype(np.float32)
    return np.ascontiguousarray(z.T), window


def build_l3(debug=False):
    p = Prog()
    x_d = p.dram_in("x", [3, 64, SEQ])
    xc_d = p.dram_in("xc", [3, 64, CTX])
    cw_d = p.dram_in("cw", [64, 3, 4])
    hb_d = p.dram_in("hb", [2, 64])
    hbc_d = p.dram_in("hbc", [64, 2])
    w1_d = p.dram_in("fw1", [33, 64])
    w2_d = p.dram_in("fw2", [64, 64])
    w3_d = p.dram_in("fw3", [64, 4, 64])
    fb_d = p.dram_in("fb", [64, 3])
    zl_d = p.dram_in("zl", [33, SEQ])
    zc_d = p.dram_in("zc", [33, CTX])
    wl_d = p.dram_in("wl", [64, 64, 128])
    wc_d = p.dram_in("wc", [64, CTX])
    cd = {k: p.dram_in(k, list(v.shape)) for k, v in hy_consts().items()}
    scr = Tile(p.nc.dram_tensor("scr", [3, 64, SEQ], F32, kind="Internal").ap())
    yl_d = p.dram_out("yl", [64, 64, 128])
    yc_d = p.dram_out("yc", [64, CTX])

    BG = [p.sb("BG%d" % i, [128, SEQ]) for i in range(3)]
    Hs = p.sb("Hs", [128, 2, 64, 128], BF16)
    F1 = p.sb("F1", [64, 256], BF16)
    TWf = p.sb("TWf", [128, 2, 128])
    F2 = p.sb("F2", [128, 3, 128], BF16)
    G2 = p.sb("G2", [128, 2, 256], BF16)
    TWi = p.sb("TWi", [128, 2, 128])
    G1 = p.sb("G1", [128, 2, 64], BF16)
    for nm, t in (("F1", F1), ("F2", F2), ("G2", G2), ("G1", G1)):
        p.dma("pool", t[:], cd[nm][:], [cd[nm]], [t])
    p.dma("sp", TWf[:], cd["TWf"][:], [cd["TWf"]], [TWf])
    p.dma("sp", TWi[:], cd["TWi"][:], [cd["TWi"]], [TWi])
    cw = p.sb("cw", [64, 3, 4])
    hbr = p.sb("hbr", [64, 2, 64])
    hbc = p.sb("hbc", [64, 2])
    w1 = p.sb("fw1", [33, 64])
    w2 = p.sb("fw2", [64, 64])
    w3 = p.sb("fw3", [64, 4, 64], BF16)
    w3f = p.sb("fw3f", [64, 4, 64])
    fb = p.sb("fb", [64, 3])
    frb = p.sb("frb", [64, 2])
    wc = p.sb("wc", [64, CTX])
    p.dma("sp", cw[:], cw_d[:], [cw_d], [cw])
    p.dma("sp", hbc[:], hbc_d[:], [hbc_d], [hbc])
    for o in range(2):
        p.dma("sp", hbr[:, o, :], hb_d[o:o + 1, :].partition_broadcast(64)[:, 0, :], [hb_d], [hbr])
    p.dma("sp", w1[:], w1_d[:], [w1_d], [w1])
    p.dma("sp", w2[:], w2_d[:], [w2_d], [w2])
    p.dma("sp", w3f[:], w3_d[:], [w3_d], [w3f])
    p.dma("pool", w3[:], w3_d[:], [w3_d], [w3])
    p.dma("sp", fb[:], fb_d[:], [fb_d], [fb])
    p.dma("sp", wc[:], wc_d[:], [wc_d], [wc])
    for j in range(2):
        p.tt("dve", frb[:, j:j + 1], fb[:, 0:1], fb[:, 1 + j:2 + j], ALU.mult, [fb], [frb])
    PS = [p.ps("P%d" % i) for i in range(8)]
    PA, PX, PB, PY = PS[0:2], PS[2:4], PS[4:6], PS[6:8]

    def short_conv(xt, xap, ut, uap, g, n):
        p.act(uap, xap, AF.Identity, [xt, cw], [ut], scale=cw[:, g, 1:2], bias=cw[:, g, 3:4])
        p.stt(uap[:, 1:n], xap[:, 0:n - 1], cw[:, g, 0:1], uap[:, 1:n], ALU.mult, ALU.add, [xt, cw, ut], [ut])
        p.stt(uap[:, 0:n - 1], xap[:, 1:n], cw[:, g, 2:3], uap[:, 0:n - 1], ALU.mult, ALU.add, [xt, cw, ut], [ut])

    for g in range(3):
        p.dma("sp", BG[0][0:64, :], x_d[g], [x_d], [BG[0]])
        short_conv(BG[0], BG[0][0:64, :], BG[1], BG[1][0:64, :], g, SEQ)
        p.dma("sp", scr[g], BG[1][0:64, :], [BG[1]], [scr])
    uc = p.sb("uc", [64, 3, CTX])
    xct = p.sb("xct", [64, 3, CTX])
    p.dma("sp", xct[:], xc_d[:].rearrange("g c t -> c g t"), [xc_d], [xct])
    for g in range(3):
        short_conv(xct, xct[:, g, :], uc, uc[:, g, :], g, CTX)

    def wrap_sin(dst_t, dst_ap, ps_ap, ps_t, j, n, arg_t):
        a = arg_t[0:64, 0:n]
        p.ts("dve", a, ps_ap, fb[:, 0:1], ALU.mult, [ps_t, fb, frb], [arg_t], s2=frb[:, j:j + 1], op1=ALU.add)
        w_ = wrp[0:64, 0:n]
        for bound, period in ((3 * PI, 4 * PI), (PI, 2 * PI)):
            p.ts("dve", w_, a, bound, ALU.is_gt, [arg_t], [wrp], s2=-period, op1=ALU.mult)
            p.tt("dve", a, a, w_, ALU.add, [arg_t, wrp], [arg_t])
            p.ts("dve", w_, a, -bound, ALU.is_lt, [arg_t], [wrp], s2=period, op1=ALU.mult)
            p.tt("dve", a, a, w_, ALU.add, [arg_t, wrp], [arg_t])
        p.act(dst_ap, a, AF.Sin, [arg_t], [dst_t])

    argt = p.sb("argt", [64, 512])
    wrp = p.sb("wrp", [64, 512])
    h1c = p.sb("h1c", [64, 512])
    zch = [p.sb("zch%d" % i, [33, 512]) for i in range(2)]

    def mlp(z_dram, n, dst_t, dst_fn):
        for q in range(0, n, 512):
            m = min(512, n - q)
            zt = zch[(q // 512) % 2]
            p.dma("sp", zt[:, 0:m], z_dram[:, q:q + m], [z_dram], [zt])
            p.mm(PA[0][0:64, 0:m], w1[:], zt[:, 0:m], True, True, [w1, zt], [PA[0]])
            wrap_sin(h1c, h1c[:, 0:m], PA[0][0:64, 0:m], PA[0], 0, m, argt)
            p.mm(PA[1][0:64, 0:m], w2[:], h1c[:, 0:m], True, True, [w2, h1c], [PA[1]])
            wrap_sin(dst_t, dst_fn(q, m), PA[1][0:64, 0:m], PA[1], 1, m, argt)

    h2 = BG[1][:].bitcast(BF16)
    mlp(zl_d, SEQ, BG[1], lambda q, m: h2[0:64, q:q + m])
    h2c = p.sb("h2c", [64, CTX])
    mlp(zc_d, CTX, h2c, lambda q, m: h2c[:, q:q + m])

    hfc = p.sb("hfc", [64, 4, CTX])
    for blk in range(4):
        p.mm(PX[0][0:64, 0:CTX], w3f[:, blk, :], h2c[:], True, True, [w3f, h2c], [PX[0]])
        p.tt("dve", hfc[:, blk, :], PX[0][0:64, 0:CTX], wc[:], ALU.mult, [PX[0], wc], [hfc])
    zc1 = p.sb("zc1", [64, CTX])
    yct = p.sb("yct", [64, CTX])

    def ctx_conv(zt, zap, o, gate_ap, out_t, out_ap):
        p.ts("dve", yct[:], zap, hbc[:, o:o + 1], ALU.mult, [zt, hbc], [yct])
        for q in range(CTX):
            p.stt(yct[:, q:CTX], zap[:, 0:CTX - q], hfc[:, 2 * o, q:q + 1], yct[:, q:CTX], ALU.mult, ALU.add, [zt, hfc, yct], [yct])
        for q in range(1, CTX):
            p.stt(yct[:, 0:CTX - q], zap[:, q:CTX], hfc[:, 2 * o + 1, q:q + 1], yct[:, 0:CTX - q], ALU.mult, ALU.add, [zt, hfc, yct], [yct])
        p.tt("dve", out_ap, yct[:], gate_ap, ALU.mult, [yct, uc], [out_t])

    ctx_conv(uc, uc[:, 0, :], 0, uc[:, 1, :], zc1, zc1[:])
    ctx_conv(zc1, zc1[:], 1, uc[:, 2, :], zc1, zc1[:])
    p.dma("sp", yc_d[:], zc1[:], [zc1], [yc_d])

    Af = [p.sb("Af%d" % i, [128, 512]) for i in range(2)]
    tm = [p.sb("tm%d" % i, [128, 512]) for i in range(4)]
    Apr = [p.sb("Apr%d" % i, [128, 4, 128], BF16) for i in range(2)]
    Api = [p.sb("Api%d" % i, [128, 4, 128], BF16) for i in range(2)]
    Xf = [p.sb("Xf%d" % i, [128, 512]) for i in range(4)]
    Yr = [p.sb("Yr%d" % i, [128, 4, 128], BF16) for i in range(2)]
    Yi = [p.sb("Yi%d" % i, [128, 4, 128], BF16) for i in range(2)]
    cnt = {"f": 0, "e": 0}

    def eng():
        cnt["e"] += 1
        return "dve" if cnt["e"] % 2 else "pool"

    def cmul(src_t, sr, si, tw_t, twr, twi, dr_t, dr, di_t, di, t_a, t_b):
        e1, e2 = eng(), eng()
        p.tt(e1, t_a[0], sr, twr, ALU.mult, [src_t, tw_t], [t_a[1]])
        p.tt(e2, t_b[0], si, twi, ALU.mult, [src_t, tw_t], [t_b[1]])
        p.tt(e1, dr, t_a[0], t_b[0], ALU.subtract, [t_a[1], t_b[1]], [dr_t])
        p.tt(e1, t_a[0], sr, twi, ALU.mult, [src_t, tw_t], [t_a[1]])
        p.tt(e2, t_b[0], si, twr, ALU.mult, [src_t, tw_t], [t_b[1]])
        p.tt(e2, di, t_a[0], t_b[0], ALU.add, [t_a[1], t_b[1]], [di_t])

    def stage12(src_t, lhs_fn, rhs_t, rhs0, rhs1, lhs2_fn, TW, dr_t, di_t, src2_t=None):
        b = cnt["f"] % 2
        cnt["f"] += 1
        for hf in range(2):
            ps = (PA if rhs1 is None else PB)[hf]
            for k in range(2):
                c = 2 * hf + k
                cols = slice(k * 256, (k + 1) * 256)
                if rhs1 is None:
                    p.mm(ps[:, cols], lhs_fn(c), rhs0, True, True, [src_t, rhs_t], [ps])
                else:
                    p.mm(ps[:, cols], lhs_fn(c), rhs0, True, False, [src_t, rhs_t], [ps])
                    p.mm(ps[:, cols], lhs2_fn(c), rhs1, False, True, [src2_t, rhs_t], [ps])
            af = Af[hf]
            p.act(af[:], ps[:], AF.Identity, [ps], [af])
            a4 = af[:].rearrange("p (k r j) -> p k r j", k=2, r=2)
            twr = TW[:, 0, :].unsqueeze(1).broadcast_to([128, 2, 128])
            twi = TW[:, 1, :].unsqueeze(1).broadcast_to([128, 2, 128])
            ta = tm[2 * hf][:, 0:256].rearrange("p (k j) -> p k j", k=2)
            tb = tm[2 * hf + 1][:, 0:256].rearrange("p (k j) -> p k j", k=2)
            cmul(af, a4[:, :, 0, :], a4[:, :, 1, :], TW, twr, twi,
                 dr_t, dr_t[:, 2 * hf:2 * hf + 2, :], di_t, di_t[:, 2 * hf:2 * hf + 2, :],
                 (ta, tm[2 * hf]), (tb, tm[2 * hf + 1]))

    def fwd_batch(src_t, lhs_fn):
        b = cnt["f"] % 2
        ar, ai = Apr[b], Api[b]
        stage12(src_t, lhs_fn, F1, F1[:], None, None, TWf, ar, ai)
        arf = ar[:].rearrange("p c j -> p (c j)")
        aif = ai[:].rearrange("p c j -> p (c j)")
        p.mm(PX[0][:], F2[:, 0, :], arf, True, False, [F2, ar], [PX[0]])
        p.mm(PX[0][:], F2[:, 2, :], aif, False, True, [F2, ai], [PX[0]])
        p.mm(PX[1][:], F2[:, 0, :], aif, True, False, [F2, ai], [PX[1]])
        p.mm(PX[1][:], F2[:, 1, :], arf, False, True, [F2, ar], [PX[1]])

    def inv_batch(yr, yi, py):
        b = cnt["f"] % 2
        br, bi = Apr[b], Api[b]
        stage12(yr, lambda c: yr[:, c, :], G2, G2[:, 0, :], G2[:, 1, :], lambda c: yi[:, c, :], TWi, br, bi, src2_t=yi)
        p.mm(py[0:64, :], G1[:, 0, :], br[:].rearrange("p c j -> p (c j)"), True, False, [G1, br], [py])
        p.mm(py[0:64, :], G1[:, 1, :], bi[:].rearrange("p c j -> p (c j)"), False, True, [G1, bi], [py])

    win = BG[0]
    hf_t = BG[2]
    hfv = BG[2][:].bitcast(BF16)[0:64, :].rearrange("p (s c j) -> p s c j", s=2, c=64)
    p.dma("sp", win[0:64, :], wl_d[:].rearrange("p c j -> p (c j)"), [wl_d], [win])
    win3 = win[0:64, :].rearrange("p (c j) -> p c j", j=128)
    scr2 = Tile(p.nc.dram_tensor("scr2", [64, 64, 128], F32, kind="Internal").ap())
    zf = [p.sb("zf%d" % i, [64, 4, 128]) for i in range(2)]
    zb = [p.sb("zb%d" % i, [64, 4, 128], BF16) for i in range(2)]
    xgb = [p.sb("xgb%d" % i, [64, 4, 128]) for i in range(2)]
    for o in range(2):
        for n2 in range(128):
            ps = PY[n2 % 2]
            p.mm(ps[0:64, 0:128], h2[0:64, n2:SEQ:128], w3[:, 2 * o:2 * o + 2, :].rearrange("p s c -> p (s c)"), True, True, [BG[1], w3], [ps])
            p.tt("dve", hfv[:, :, :, n2], ps[0:64, 0:128].rearrange("p (s c) -> p s c", s=2),
                 win3[:, :, n2].unsqueeze(1).broadcast_to([64, 2, 64]), ALU.mult, [ps, win], [hf_t])
        p.op("dve", lambda e: e.memset(hfv[0:1, 1, :, 0], 0.0), [], [hf_t])
        for bt in range(16):
            c0 = 4 * bt
            fwd_batch(hf_t, lambda c: hfv[:, 0, c0 + c, :])
            xr, xi = Xf[0], Xf[1]
            p.act(xr[:], PX[0][:], AF.Identity, [PX[0]], [xr])
            p.act(xi[:], PX[1][:], AF.Identity, [PX[1]], [xi])
            fwd_batch(hf_t, lambda c: hfv[:, 1, c0 + c, :])
            p.tt("dve", Hs[:, 0, c0:c0 + 4, :], xr[:].rearrange("p (c j) -> p c j", c=4),
                 PX[0][:].rearrange("p (c j) -> p c j", c=4), ALU.add, [xr, PX[0]], [Hs])
            p.tt("dve", Hs[:, 1, c0:c0 + 4, :], xi[:].rearrange("p (c j) -> p c j", c=4),
                 PX[1][:].rearrange("p (c j) -> p c j", c=4), ALU.subtract, [xi, PX[1]], [Hs])
        if debug:
            dh = p.dram_out("dbg_hf", [64, 2, 64, 128])
            dH = p.dram_out("dbg_H", [128, 2, 64, 128])
            dh2 = p.dram_out("dbg_h2", [64, SEQ])
            p.dma("pool", dh[:], hfv, [hf_t], [dh])
            p.dma("pool", dH[:], Hs[:], [Hs], [dH])
            p.dma("pool", dh2[:], h2[0:64, 0:SEQ], [BG[1]], [dh2], max_dma_last_dim=2048)
            return p
        for bt in range(16):
            c0 = 4 * bt
            b = bt % 2
            z_, zb_, g_ = zf[b], zb[b], xgb[b]
            if o == 0:
                p.dma("sp", z_[:], scr[0, c0:c0 + 4, :].rearrange("c (p j) -> p c j", j=128), [scr], [z_])
            else:
                p.dma("sp", z_[:], scr2[:, c0:c0 + 4, :], [scr2], [z_])
            p.dma("sp", g_[:], scr[1 + o, c0:c0 + 4, :].rearrange("c (p j) -> p c j", j=128), [scr], [g_])
            p.op("pool", lambda e: e.tensor_copy(out=zb_[:], in_=z_[:]), [z_], [zb_])
            p.tt("pool", z_[:], z_[:], hbr[:, o, c0:c0 + 4].unsqueeze(2).broadcast_to([64, 4, 128]), ALU.mult, [z_, hbr], [z_])
            fwd_batch(zb_, lambda c: zb_[:, c, :])
            xr, xi = Xf[2], Xf[3]
            p.act(xr[:], PX[0][:], AF.Identity, [PX[0]], [xr])
            p.act(xi[:], PX[1][:], AF.Identity, [PX[1]], [xi])
            x4r = xr[:].rearrange("p (c j) -> p c j", c=4)
            x4i = xi[:].rearrange("p (c j) -> p c j", c=4)
            ta = tm[0][:].rearrange("p (c j) -> p c j", c=4)
            tb = tm[1][:].rearrange("p (c j) -> p c j", c=4)
            e1, e2 = eng(), eng()
            p.tt(e1, ta, x4r, Hs[:, 0, c0:c0 + 4, :], ALU.mult, [xr, Hs], [tm[0]])
            p.tt(e2, tb, x4i, Hs[:, 1, c0:c0 + 4, :], ALU.mult, [xi, Hs], [tm[1]])
            p.tt(e1, Yr[b][:], ta, tb, ALU.subtract, [tm[0], tm[1]], [Yr[b]])
            p.tt(e1, ta, x4r, Hs[:, 1, c0:c0 + 4, :], ALU.mult, [xr, Hs], [tm[0]])
            p.tt(e2, tb, x4i, Hs[:, 0, c0:c0 + 4, :], ALU.mult, [xi, Hs], [tm[1]])
            p.tt(e2, Yi[b][:], ta, tb, ALU.add, [tm[0], tm[1]], [Yi[b]])
            py = PY[bt % 2]
            inv_batch(Yr[b], Yi[b], py)
            p.tt("dve", z_[:], py[0:64, :].rearrange("p (c j) -> p c j", c=4), z_[:], ALU.add, [py, z_], [z_])
            p.tt("pool", z_[:], z_[:], g_[:], ALU.mult, [z_, g_], [z_])
            if o == 0:
                p.dma("sp", scr2[:, c0:c0 + 4, :], z_[:], [z_], [scr2])
            else:
                p.dma("sp", yl_d[:, c0:c0 + 4, :], z_[:], [z_], [yl_d])
    return p


def run_l3(lat, ctx, layer, w):
    consts = hy_consts()
    zl, win_l = hy_features(SEQ)
    zc, win_c = hy_features(CTX)
    in_maps = []
    for i in range(NCORES):
        b, cg = divmod(i, 4)
        ch = slice(cg * 64, (cg + 1) * 64)
        cols = [768 + g * 256 + cg * 64 for g in range(3)]
        x = np.stack([lat[b, :, c:c + 64].T for c in cols])
        xc = np.stack([ctx[b, :, c:c + 64].T for c in cols])
        cwf = w["hy_conv_w"][layer].reshape(3, 3, 256)[:, :, ch]
        cbf = w["hy_conv_b"][layer].reshape(3, 256)[:, ch]
        cw = np.concatenate([cwf.transpose(2, 1, 0), cbf.T[:, :, None]], axis=2)
        hb = w["hy_bias"][layer][:, ch]
        w3 = w["filt_w3"][layer].reshape(64, 4, 256)[:, :, ch]
        fb = np.stack([w["filt_freq"][layer], w["filt_b1"][layer], w["filt_b2"][layer]], axis=1)
        wl = win_l[:, ch].reshape(64, 128, 64).transpose(0, 2, 1)
        m = {"x": x, "xc": xc, "cw": cw, "hb": hb, "hbc": hb.T, "fw1": w["filt_w1"][layer], "fw2": w["filt_w2"][layer],
             "fw3": w3, "fb": fb, "zl": zl, "zc": zc, "wl": wl, "wc": win_c[:, ch].T}
        m.update(consts)
        in_maps.append({k: np.ascontiguousarray(v, dtype=np.float32) for k, v in m.items()})
    res = run(build_l3(), in_maps)
    y_lat = np.zeros((2, SEQ, 256), np.float32)
    y_ctx = np.zeros((2, CTX, 256), np.float32)
    for i in range(NCORES):
        b, cg = divmod(i, 4)
        ch = slice(cg * 64, (cg + 1) * 64)
        y_lat[b, :, ch] = res[i]["yl"].transpose(0, 2, 1).reshape(SEQ, 64)
        y_ctx[b, :, ch] = res[i]["yc"].T
    return y_lat, y_ctx


def kernel(**inputs):
    w = {k: np.ascontiguousarray(np.asarray(v, dtype=np.float32)) for k, v in inputs.items()}
    mod = run_ada(w["c"], w["c_ctx"], w["ada_w"], w["ada_b"])
    h_lat, h_ctx = w["x"], w["ctx"]
    for layer in range(DEPTH):
        l1_lat, l1_ctx = run_l1(h_lat, h_ctx, mod, layer, w)
        a_lat, a_ctx = run_l2(l1_lat, l1_ctx)
        hy_lat, hy_ctx = run_l3(l1_lat, l1_ctx, layer, w)
        f_lat, b_lat, f_ctx, b_ctx = run_l4(l1_lat, l1_ctx)
        y_lat = np.concatenate([a_lat, hy_lat, f_lat], axis=-1)
        y_ctx = np.concatenate([a_ctx, hy_ctx, f_ctx], axis=-1)
        h_lat, h_ctx = run_l5a(y_lat, y_ctx, b_lat, b_ctx, l1_lat[:, :, 2048:2304], l1_ctx[:, :, 2048:2304],
                               h_lat, h_ctx, mod, layer, w)
        h_lat, h_ctx = run_l5b(h_lat, h_ctx, mod, layer, w, layer == DEPTH - 1)
    return np.ascontiguousarray(h_lat, dtype=np.float32)


TT = SEQ + CTX
NTT = TT // 128


class Stage:
    def __init__(self, p):
        self.p = p
        self.cms = []

    def sb(self, name, shape, dtype=F32):
        p = self.p
        p.uid = getattr(p, "uid", 0) + 1
        cm = p.nc.sbuf_tensor("%s_%d" % (name, p.uid), list(shape), dtype)
        h = cm.__enter__()
        self.cms.append(cm)
        return Tile(h)

    def close(self):
        p = self.p
        tot = {sid: c for sid, c in p.cnt.items() if c > 0}
        for e in p.eng:
            p._wait(e, dict(tot))
        for cm in reversed(self.cms):
            cm.__exit__(None, None, None)
        self.cms = []


def scratch(p, name, shape):
    return Tile(p.nc.dram_tensor(name, list(shape), F32, kind="Internal").ap())


def emit_ada(p, PS, cT_d, adaw_d, adab_d, MOD):
    st = Stage(p)
    cs = st.sb("cs", [128, 8, 2])
    Wc = [st.sb("Wc%d" % i, [128, 8, 512]) for i in range(3)]
    bb = st.sb("bb", [2, 6 * D])
    res = st.sb("res", [2, 6 * D])
    p.dma("sp", cs[:], cT_d[:], [cT_d], [cs])
    p.act(cs[:], cs[:], AF.Silu, [cs], [cs])
    k = 0
    for l in range(DEPTH):
        p.dma("sp", bb[:], adab_d[l:l + 1, :].partition_broadcast(2)[:, 0, :], [adab_d], [bb])
        for j in range(6 * D // 512):
            W = Wc[k % 3]
            q = "sp" if k % 2 == 0 else "act"
            p.dma(q, W[:], adaw_d[l, :, j * 512:(j + 1) * 512].rearrange("(c p) n -> p c n", p=128), [adaw_d], [W])
            ps = PS[k % 2]
            k += 1
            for c in range(8):
                p.mm(ps[0:2, :], cs[:, c, :], W[:, c, :], c == 0, c == 7, [cs, W], [ps])
            p.tt("dve", res[:, j * 512:(j + 1) * 512], ps[0:2, :], bb[:, j * 512:(j + 1) * 512], ALU.add, [ps, bb], [res])
        p.dma("sp", MOD[l], res[:], [res], [MOD])
    st.close()


def load_modT(p, st, PS, MOD, l, ident):
    raw = st.sb("modraw", [96, 128])
    modT = st.sb("modT", [128, 96])
    p.dma("sp", raw[:], MOD[l].rearrange("r (s p) -> (r s) p", p=128), [MOD], [raw])
    p.tr(PS[7][:, 0:96], raw[:], ident[0:96, 0:96], [raw, ident], [PS[7]])
    p.act(modT[:], PS[7][:, 0:96], AF.Identity, [PS[7]], [modT])
    return modT


def emit_l1(p, PS, l, Hin, S1, MOD, wd, cs_d, ident_d):
    st = Stage(p)
    sb = st.sb
    W = sb("W", [128, 8, IN_W], BF16)
    A = sb("A", [128, 8, 2])
    Brep = sb("Brep", [128, 8, 128], BF16)
    bW = [sb("bW%d" % k, [128, IN_W]) for k in range(2)]
    gains = sb("gains", [128, 10, 64])
    Wblk = sb("Wblk", [32, 256])
    gbb = sb("gbb", [128, 256])
    ident = sb("ident", [128, 128])
    epst = sb("eps", [128, 1])
    ng = sb("ng", [128, 8])
    pss, pst, psg = PS[0:5], PS[5:7], PS[7]
    p.op("dve", lambda e: e.memset(epst[:], EPS), [], [epst])
    p.op("dve", lambda e: e.memset(Wblk[:], 0.0), [], [Wblk])
    p.dma("sp", ident[:], ident_d[:], [ident_d], [ident])
    p.dma("sp", ng[:], wd["norm1_g"][l], [wd["norm1_g"]], [ng])
    p.dma("sp", gbb[:], wd["gla_gate_b"][l:l + 1, :].partition_broadcast(128)[:, 0, :], [wd["gla_gate_b"]], [gbb])
    p.dma("sp", Wblk[0:16, 0:128], wd["gla_gate_w"][l, 0], [wd["gla_gate_w"]], [Wblk])
    p.dma("sp", Wblk[16:32, 128:256], wd["gla_gate_w"][l, 1], [wd["gla_gate_w"]], [Wblk])
    for h in range(10):
        src = wd["qkg"][l:l + 1, 0:64] if h < 8 else wd["qkg"][l:l + 1, 64:128]
        p.dma("sp", gains[:, h, :], src.partition_broadcast(128)[:, 0, :], [wd["qkg"]], [gains])
    p.ts("dve", gains[:, 0:8, :], gains[:, 0:8, :], 0.125, ALU.mult, [gains], [gains])
    for c in range(8):
        p.dma("pool", W[:, c, :], wd["w_in"][l, c * 128:(c + 1) * 128, :], [wd["w_in"]], [W], max_dma_last_dim=4096)
    modT = load_modT(p, st, PS, MOD, l, ident)
    for k in range(2):
        sc = modT[:, k * 48 + 8:k * 48 + 16]
        sh = modT[:, k * 48 + 0:k * 48 + 8]
        p.stt(A[:, :, k], sc, 1.0, ng[:], ALU.add, ALU.mult, [modT, ng], [A])
        p.op("dve", lambda e: e.tensor_copy(out=Brep[:], in_=sh.unsqueeze(2).broadcast_to([128, 8, 128])), [modT], [Brep])
        for gi, (a, b) in enumerate(GROUPS1):
            ps = pss[gi]
            for c in range(8):
                p.mm(ps[:, 0:b - a], Brep[:, c, :], W[:, c, a:b], c == 0, c == 7, [Brep, W], [ps])
            p.act(bW[k][:, a:b], ps[:, 0:b - a], AF.Identity, [ps], [bW[k]])
    NB = 2
    xt = [sb("xt%d" % i, [128, D]) for i in range(NB)]
    xa = [sb("xa%d" % i, [128, 8, 128], BF16) for i in range(NB)]
    cst = [sb("cst%d" % i, [128, 64]) for i in range(NB)]
    O = [sb("O%d" % i, [128, OUT1]) for i in range(NB)]
    junk = sb("junk", [128, D])
    sq = sb("sq", [128, 640])
    tmp = sb("tmp", [128, 10, 32])
    stt_ = [sb("st%d" % i, [128, 16]) for i in range(NB)]
    lrs = [sb("lr%d" % i, [128, 32]) for i in range(NB)]
    lrT = sb("lrT", [32, 128])
    gz = sb("gz", [128, 256])
    ga = sb("ga", [128, 256])
    pic = [0]

    def phaseA(i):
        kind = 0 if i < 64 else 1
        bi = i % NB
        rows = slice(i * 128, (i + 1) * 128)
        p.dma("sp", xt[bi][:], Hin[rows, :], [Hin], [xt[bi]])
        p.dma("sp", cst[bi][:], cs_d[rows, :], [cs_d], [cst[bi]])
        s = stt_[bi]
        p.act(junk[:], xt[bi][:], AF.Square, [xt[bi]], [junk, s], accum_out=s[:, 0:1])
        p.rsqrt(s, s[:, 1:2], s, s[:, 0:1], 1.0 / D, epst)
        for c in range(8):
            pt = pst[c // 4]
            blk = pt[:, (c % 4) * 128:(c % 4 + 1) * 128]
            p.tr(blk, xt[bi][:, c * 128:(c + 1) * 128], ident[:], [xt[bi], ident], [pt])
            p.act(xa[bi][:, c, :], blk, AF.Identity, [pt, A], [xa[bi]], scale=A[:, c, kind:kind + 1])
        o = O[bi]
        for gi, (a, b) in enumerate(GROUPS1):
            ps = pss[pic[0] % 5]
            pic[0] += 1
            for c in range(8):
                p.mm(ps[:, 0:b - a], xa[bi][:, c, :], W[:, c, a:b], c == 0, c == 7, [xa[bi], W], [ps])
            if gi < 4:
                p.stt(o[:, a:b], ps[:, 0:b - a], s[:, 1:2], bW[kind][:, a:b], ALU.mult, ALU.add, [ps, s, bW[kind]], [o])
            else:
                p.stt(o[:, 2048:2304], ps[:, 0:256], s[:, 1:2], bW[kind][:, 2048:2304], ALU.mult, ALU.add, [ps, s, bW[kind]], [o])
                p.stt(lrs[bi][:], ps[:, 256:288], s[:, 1:2], bW[kind][:, 2304:2336], ALU.mult, ALU.add, [ps, s, bW[kind]], [lrs[bi]])

    def phaseB(i):
        bi = i % NB
        rows = slice(i * 128, (i + 1) * 128)
        s = stt_[bi]
        o = O[bi]
        lr = lrs[bi]
        qk = o[:, 0:640]
        p.tt("pool", sq[:], qk, qk, ALU.mult, [o], [sq])
        p.op("dve", lambda e: e.tensor_reduce(out=s[:, 2:12], in_=sq[:].rearrange("p (h d) -> p h d", d=64), axis=AX.X, op=ALU.add), [sq], [s])
        p.rsqrt(s, s[:, 2:12], s, s[:, 2:12], 1.0 / 64, epst)
        qk3 = qk.rearrange("p (h d) -> p h d", d=64)
        p.tt("dve", qk3, qk3, s[:, 2:12].unsqueeze(2).broadcast_to([128, 10, 64]), ALU.mult, [o, s], [o])
        p.tt("pool", qk3, qk3, gains[:], ALU.mult, [o, gains], [o])
        x1 = qk3[:, :, 0:32]
        x2 = qk3[:, :, 32:64]
        cb = cst[bi][:, 0:32].unsqueeze(1).broadcast_to([128, 10, 32])
        sb_ = cst[bi][:, 32:64].unsqueeze(1).broadcast_to([128, 10, 32])
        t3 = sq[:, 0:320].rearrange("p (h d) -> p h d", d=32)
        p.tt("dve", tmp[:], x2, sb_, ALU.mult, [o, cst[bi]], [tmp])
        p.tt("pool", t3, x1, sb_, ALU.mult, [o, cst[bi]], [sq])
        p.tt("dve", x1, x1, cb, ALU.mult, [o, cst[bi]], [o])
        p.tt("dve", x1, x1, tmp[:], ALU.subtract, [o, tmp], [o])
        p.tt("dve", x2, x2, cb, ALU.mult, [o, cst[bi]], [o])
        p.tt("dve", x2, x2, t3, ALU.add, [o, sq], [o])
        p.act(o[:, 1536:1664], o[:, 1536:1664], AF.Identity, [o], [o], scale=32 ** -0.5)
        p.act(o[:, 2048:2304], o[:, 2048:2304], AF.Silu, [o], [o])
        p.tr(psg[0:32, 0:128], lr[:], ident[:], [lr, ident], [psg])
        p.act(lrT[:], psg[0:32, 0:128], AF.Identity, [psg], [lrT])
        p.mm(psg[:, 128:384], lrT[:], Wblk[:], True, True, [lrT, Wblk], [psg])
        p.tt("dve", gz[:], psg[:, 128:384], gbb[:], ALU.add, [psg, gbb], [gz])
        p.stt(ga[:], gz[:], -1.0, gz[:], ALU.mult, ALU.min, [gz], [ga])
        p.act(ga[:], ga[:], AF.Exp, [ga], [ga])
        p.act(ga[:], ga[:], AF.Ln, [ga], [ga], bias=1.0)
        p.ts("dve", gz[:], gz[:], 0.0, ALU.min, [gz], [gz], s2=1.0 / 16, op1=ALU.mult)
        p.stt(o[:, 2304:2560], ga[:], -1.0 / 16, gz[:], ALU.mult, ALU.add, [ga, gz], [o])
        p.dma("sp", S1[rows, :], o[:], [o], [S1])

    phaseA(0)
    for i in range(NTT):
        if i + 1 < NTT:
            phaseA(i + 1)
        phaseB(i)
    st.close()


def emit_l2(p, PS, S1, SY, ident_d, PS2):
    st = Stage(p)
    sb = st.sb
    ident = sb("ident", [128, 128])
    p.dma("sp", ident[:], ident_d[:], [ident_d], [ident])
    kT2 = sb("kT2", [128, TT], BF16)
    vA = [sb("vA%d" % g, [128, NTT, 65], BF16) for g in range(2)]
    kin = [sb("kin%d" % i, [128, 128]) for i in range(2)]
    qin = [sb("qin%d" % i, [128, 512]) for i in range(2)]
    q2 = [sb("q2_%d" % i, [128, 512], BF16) for i in range(2)]
    qrs = [sb("qr%d" % i, [128, 512]) for i in range(2)]
    psS, psO, psT = PS2, PS[4:6], PS[6:8]
    for g in range(2):
        p.op("dve", lambda e: e.memset(vA[g][:, :, 64:65], 1.0), [], [vA[g]])
        p.dma("pool", vA[g][:, :, 0:64], S1[:, 640 + g * 64:704 + g * 64].rearrange("(t p) c -> p t c", p=128), [S1], [vA[g]])
    for t in range(NTT):
        ki = kin[t % 2]
        p.dma("sp", ki[:], S1[t * 128:(t + 1) * 128, 512:640], [S1], [ki])
        pt = psT[t % 2]
        p.tr(pt[:, 0:128], ki[:], ident[:], [ki, ident], [pt])
        p.act(kT2[:, t * 128:(t + 1) * 128], pt[:, 0:128], AF.Identity, [pt], [kT2])
    pT = [sb("pT%d" % i, [128, 1024], BF16) for i in range(3)]
    oT = [sb("oT%d" % i, [65, 1024]) for i in range(2)]
    ot = [sb("ot%d" % i, [128, 512]) for i in range(2)]
    rec = sb("rec", [128, 16])
    it = 0
    for qt in range(NTT):
        kts = list(range(NTT)) if qt < 64 else [64, 65]
        qi = qin[qt % 2]
        p.dma("sp", qi[:], S1[qt * 128:(qt + 1) * 128, 0:512], [S1], [qi])
        o_t = ot[qt % 2]
        q_ = q2[qt % 2]
        pt = psT[qt % 2]
        qr = qrs[qt % 2]
        p.op("pool", lambda e: e.tensor_copy(out=qr[:].rearrange("p (h g d) -> p h g d", h=4, g=2),
                                             in_=qi[:].rearrange("p (g h d) -> p h g d", g=2, h=4)), [qi], [qr])
        for h in range(4):
            p.tr(pt[:, h * 128:(h + 1) * 128], qr[:, h * 128:(h + 1) * 128], ident[:], [qr, ident], [pt])
        p.act(q_[:], pt[:], AF.Identity, [pt], [q_])

        def pv(pend, last):
            k0, kt, ptile = pend
            for g in range(2):
                p.mm(psO[g][0:65, :], vA[g][:, kt, :], ptile[:, g * 512:(g + 1) * 512], k0 == 0, last, [vA[g], ptile], [psO[g]])

        pend = None
        for k0, kt in enumerate(kts):
            ps = psS[it % 2]
            pt_ = pT[it % 3]
            it += 1
            for g in range(2):
                rows = slice(g * 64, (g + 1) * 64)
                p.mm(ps[:, g * 512:(g + 1) * 512], kT2[rows, kt * 128:(kt + 1) * 128], q_[rows, :], True, True, [kT2, q_], [ps])
            p.act(pt_[:], ps[:], AF.Exp, [ps], [pt_])
            if pend is not None:
                pv(pend, False)
            pend = (k0, kt, pt_)
        pv(pend, True)
        o_sb = oT[qt % 2]
        for g in range(2):
            p.op("dve", lambda e: e.tensor_copy(out=o_sb[:, g * 512:(g + 1) * 512], in_=psO[g][0:65, :]), [psO[g]], [o_sb])
        for hh in range(8):
            ptt = psT[hh % 2]
            p.tr(ptt[:, 0:65], o_sb[:, hh * 128:(hh + 1) * 128], ident[0:65, 0:65], [o_sb, ident], [ptt])
            rc = rec[:, hh + (qt % 2) * 8:hh + (qt % 2) * 8 + 1]
            p.op("dve", lambda e: e.reciprocal(out=rc, in_=ptt[:, 64:65]), [ptt], [rec])
            p.ts("dve", o_t[:, hh * 64:(hh + 1) * 64], ptt[:, 0:64], rc, ALU.mult, [ptt, rec], [o_t])
        p.dma("sp", SY[qt * 128:(qt + 1) * 128, :], o_t[:], [o_t], [SY])
    st.close()


def emit_l3(p, PS, l, S1, YL, YC, wd, cd, scr, scr2):
    st = Stage(p)
    sb = st.sb
    BGa = sb("BGa", [128, SEQ])
    BGb = sb("BGb", [128, SEQ])
    H2 = sb("H2", [64, SEQ], BF16)
    Hs = sb("Hs", [128, 2, 64, 128], BF16)
    F1 = sb("F1", [64, 256], BF16)
    TWf = sb("TWf", [128, 2, 128])
    F2 = sb("F2", [128, 3, 128], BF16)
    G2 = sb("G2", [128, 2, 256], BF16)
    TWi = sb("TWi", [128, 2, 128])
    G1 = sb("G1", [128, 2, 64], BF16)
    ident = sb("ident", [128, 128])
    for nm, t in (("F1", F1), ("F2", F2), ("G2", G2), ("G1", G1)):
        p.dma("pool", t[:], cd[nm][:], [cd[nm]], [t])
    p.dma("sp", TWf[:], cd["TWf"][:], [cd["TWf"]], [TWf])
    p.dma("sp", TWi[:], cd["TWi"][:], [cd["TWi"]], [TWi])
    p.dma("sp", ident[:], cd["ident"][:], [cd["ident"]], [ident])
    w1 = sb("fw1", [33, 64])
    w2 = sb("fw2", [64, 64])
    fb = sb("fb", [64, 3])
    frb = sb("frb", [64, 2])
    p.dma("sp", w1[:], wd["filt_w1"][l], [wd["filt_w1"]], [w1])
    p.dma("sp", w2[:], wd["filt_w2"][l], [wd["filt_w2"]], [w2])
    p.dma("sp", fb[:], wd["fb"][l], [wd["fb"]], [fb])
    for j in range(2):
        p.tt("dve", frb[:, j:j + 1], fb[:, 0:1], fb[:, 1 + j:2 + j], ALU.mult, [fb], [frb])
    PA, PX, PB, PY = PS[0:2], PS[2:4], PS[4:6], PS[6:8]
    argt = sb("argt", [64, 512])
    wrp = sb("wrp", [64, 512])
    h1c = sb("h1c", [64, 512])
    zch = [sb("zch%d" % i, [33, 512]) for i in range(2)]
    h2c = sb("h2c", [64, CTX])

    def wrap_sin(dst_t, dst_ap, ps_ap, ps_t, j, n):
        a = argt[0:64, 0:n]
        p.ts("dve", a, ps_ap, fb[:, 0:1], ALU.mult, [ps_t, fb, frb], [argt], s2=frb[:, j:j + 1], op1=ALU.add)
        w_ = wrp[0:64, 0:n]
        for bound, period in ((3 * PI, 4 * PI), (PI, 2 * PI)):
            p.ts("dve", w_, a, bound, ALU.is_gt, [argt], [wrp], s2=-period, op1=ALU.mult)
            p.tt("dve", a, a, w_, ALU.add, [argt, wrp], [argt])
            p.ts("dve", w_, a, -bound, ALU.is_lt, [argt], [wrp], s2=period, op1=ALU.mult)
            p.tt("dve", a, a, w_, ALU.add, [argt, wrp], [argt])
        p.act(dst_ap, a, AF.Sin, [argt], [dst_t])

    def mlp(z_dram, n, dst_t):
        for q in range(0, n, 512):
            m = min(512, n - q)
            zt = zch[(q // 512) % 2]
            p.dma("sp", zt[:, 0:m], z_dram[:, q:q + m], [z_dram], [zt])
            p.mm(PA[0][0:64, 0:m], w1[:], zt[:, 0:m], True, True, [w1, zt], [PA[0]])
            wrap_sin(h1c, h1c[:, 0:m], PA[0][0:64, 0:m], PA[0], 0, m)
            p.mm(PA[1][0:64, 0:m], w2[:], h1c[:, 0:m], True, True, [w2, h1c], [PA[1]])
            wrap_sin(dst_t, dst_t[:, q:q + m], PA[1][0:64, 0:m], PA[1], 1, m)

    mlp(cd["zl"], SEQ, H2)
    mlp(cd["zc"], CTX, h2c)

    cw = sb("cw", [64, 3, 4])
    hbr = sb("hbr", [64, 2, 64])
    hbc = sb("hbc", [64, 2])
    w3 = sb("fw3", [64, 4, 64], BF16)
    w3f = sb("fw3f", [64, 4, 64])
    wc = sb("wc", [64, CTX])
    uc = sb("uc", [64, 3, CTX])
    xct = sb("xct", [64, 3, CTX])
    hfc = sb("hfc", [64, 4, CTX])
    zc1 = sb("zc1", [64, CTX])
    yct = sb("yct", [64, CTX])
    xin = [sb("xin%d" % i, [128, 64]) for i in range(3)]
    Af = [sb("Af%d" % i, [128, 512]) for i in range(2)]
    tm = [sb("tm%d" % i, [128, 512]) for i in range(4)]
    tms = [[sb("tms%d_%d" % (a, b), [128, 256]) for b in range(4)] for a in range(4)]
    Apr = [sb("Apr%d" % i, [128, 4, 128], BF16) for i in range(2)]
    Api = [sb("Api%d" % i, [128, 4, 128], BF16) for i in range(2)]
    Xf = [sb("Xf%d" % i, [128, 512]) for i in range(4)]
    Yr = [sb("Yr%d" % i, [128, 4, 128], BF16) for i in range(2)]
    Yi = [sb("Yi%d" % i, [128, 4, 128], BF16) for i in range(2)]
    zf = [sb("zf%d" % i, [64, 4, 128]) for i in range(2)]
    zb = [sb("zb%d" % i, [64, 4, 128], BF16) for i in range(2)]
    xgb = [sb("xgb%d" % i, [64, 4, 128]) for i in range(2)]
    cnt = {"f": 0, "e": 0, "x": 0}

    def eng():
        cnt["e"] += 1
        return "dve" if cnt["e"] % 2 else "pool"

    def cmul(src_t, sr, si, tw_t, twr, twi, dr_t, dr, di_t, di, T4, e2):
        (a1, A1), (b1, B1), (a2, A2), (b2, B2) = T4
        p.tt("dve", a1, sr, twr, ALU.mult, [src_t, tw_t], [A1])
        p.tt("dve", b1, si, twi, ALU.mult, [src_t, tw_t], [B1])
        p.tt("dve", dr, a1, b1, ALU.subtract, [A1, B1], [dr_t])
        p.tt(e2, a2, sr, twi, ALU.mult, [src_t, tw_t], [A2])
        p.tt(e2, b2, si, twr, ALU.mult, [src_t, tw_t], [B2])
        p.tt(e2, di, a2, b2, ALU.add, [A2, B2], [di_t])

    def stage12(src_t, lhs_fn, rhs_t, rhs0, rhs1, lhs2_fn, TW, dr_t, di_t, src2_t=None):
        cnt["f"] += 1
        for hf in range(2):
            ps = (PA if rhs1 is None else PB)[hf]
            for k in range(2):
                c = 2 * hf + k
                cols = slice(k * 256, (k + 1) * 256)
                if rhs1 is None:
                    p.mm(ps[:, cols], lhs_fn(c), rhs0, True, True, [src_t, rhs_t], [ps])
                else:
                    p.mm(ps[:, cols], lhs_fn(c), rhs0, True, False, [src_t, rhs_t], [ps])
                    p.mm(ps[:, cols], lhs2_fn(c), rhs1, False, True, [src2_t, rhs_t], [ps])
            af = Af[hf]
            p.act(af[:], ps[:], AF.Identity, [ps], [af])
            a4 = af[:].rearrange("p (k r j) -> p k r j", k=2, r=2)
            twr = TW[:, 0, :].unsqueeze(1).broadcast_to([128, 2, 128])
            twi = TW[:, 1, :].unsqueeze(1).broadcast_to([128, 2, 128])
            tset = tms[(cnt["f"] % 2) * 2 + hf]
            T4 = [(t_[:].rearrange("p (k j) -> p k j", k=2), t_) for t_ in tset]
            cmul(af, a4[:, :, 0, :], a4[:, :, 1, :], TW, twr, twi,
                 dr_t, dr_t[:, 2 * hf:2 * hf + 2, :], di_t, di_t[:, 2 * hf:2 * hf + 2, :],
                 T4, "pool" if hf == 1 else "dve")

    def fwd_batch(src_t, lhs_fn):
        b = cnt["f"] % 2
        ar, ai = Apr[b], Api[b]
        stage12(src_t, lhs_fn, F1, F1[:], None, None, TWf, ar, ai)
        arf = ar[:].rearrange("p c j -> p (c j)")
        aif = ai[:].rearrange("p c j -> p (c j)")
        p.mm(PX[0][:], F2[:, 0, :], arf, True, False, [F2, ar], [PX[0]])
        p.mm(PX[0][:], F2[:, 2, :], aif, False, True, [F2, ai], [PX[0]])
        p.mm(PX[1][:], F2[:, 0, :], aif, True, False, [F2, ai], [PX[1]])
        p.mm(PX[1][:], F2[:, 1, :], arf, False, True, [F2, ar], [PX[1]])

    def inv_batch(yr, yi, py):
        b = cnt["f"] % 2
        br, bi = Apr[b], Api[b]
        stage12(yr, lambda c: yr[:, c, :], G2, G2[:, 0, :], G2[:, 1, :], lambda c: yi[:, c, :], TWi, br, bi, src2_t=yi)
        p.mm(py[0:64, :], G1[:, 0, :], br[:].rearrange("p c j -> p (c j)"), True, False, [G1, br], [py])
        p.mm(py[0:64, :], G1[:, 1, :], bi[:].rearrange("p c j -> p (c j)"), False, True, [G1, bi], [py])

    def short_conv(xt, xap, ut, uap, g, n):
        p.act(uap, xap, AF.Identity, [xt, cw], [ut], scale=cw[:, g, 1:2], bias=cw[:, g, 3:4])
        p.stt(uap[:, 1:n], xap[:, 0:n - 1], cw[:, g, 0:1], uap[:, 1:n], ALU.mult, ALU.add, [xt, cw, ut], [ut])
        p.stt(uap[:, 0:n - 1], xap[:, 1:n], cw[:, g, 2:3], uap[:, 0:n - 1], ALU.mult, ALU.add, [xt, cw, ut], [ut])

    def ctx_conv(zt, zap, o, gate_ap, out_t, out_ap):
        p.ts("dve", yct[:], zap, hbc[:, o:o + 1], ALU.mult, [zt, hbc], [yct])
        for q in range(CTX):
            p.stt(yct[:, q:CTX], zap[:, 0:CTX - q], hfc[:, 2 * o, q:q + 1], yct[:, q:CTX], ALU.mult, ALU.add, [zt, hfc, yct], [yct])
        for q in range(1, CTX):
            p.stt(yct[:, 0:CTX - q], zap[:, q:CTX], hfc[:, 2 * o + 1, q:q + 1], yct[:, 0:CTX - q], ALU.mult, ALU.add, [zt, hfc, yct], [yct])
        p.tt("dve", out_ap, yct[:], gate_ap, ALU.mult, [yct, uc], [out_t])

    hfv = BGb[:].bitcast(BF16)[0:64, :].rearrange("p (s c j) -> p s c j", s=2, c=64)
    win3 = BGa[0:64, :].rearrange("p (c j) -> p c j", j=128)
    for cg in range(4):
        p.dma("sp", cw[:], wd["cw"][l, cg], [wd["cw"]], [cw])
        p.dma("sp", hbc[:], wd["hbc"][l, cg], [wd["hbc"]], [hbc])
        for o in range(2):
            p.dma("sp", hbr[:, o, :], wd["hb"][l, cg, o:o + 1, :].partition_broadcast(64)[:, 0, :], [wd["hb"]], [hbr])
        p.dma("sp", w3f[:], wd["fw3"][l, cg], [wd["fw3"]], [w3f])
        p.dma("pool", w3[:], wd["fw3"][l, cg], [wd["fw3"]], [w3])
        p.dma("sp", wc[:], cd["wc"][cg], [cd["wc"]], [wc])
        for g in range(3):
            c0 = 768 + g * 256 + cg * 64
            for t in range(NTT):
                xi = xin[cnt["x"] % 3]
                cnt["x"] += 1
                p.dma("sp", xi[:], S1[t * 128:(t + 1) * 128, c0:c0 + 64], [S1], [xi])
                pt = PA[(t // 4) % 2]
                p.tr(pt[0:64, (t % 4) * 128:(t % 4 + 1) * 128], xi[:], ident[:], [xi, ident], [pt])
                if t % 4 == 3 and t < 64:
                    p.act(BGa[0:64, (t - 3) * 128:(t + 1) * 128], pt[0:64, :], AF.Identity, [pt], [BGa])
                if t == 65:
                    p.act(xct[:, g, :], pt[0:64, 0:256], AF.Identity, [pt], [xct])
            short_conv(BGa, BGa[0:64, :], BGb, BGb[0:64, :], g, SEQ)
            p.dma("sp", scr[g], BGb[0:64, :], [BGb], [scr])
            short_conv(xct, xct[:, g, :], uc, uc[:, g, :], g, CTX)
        for blk in range(4):
            p.mm(PX[0][0:64, 0:CTX], w3f[:, blk, :], h2c[:], True, True, [w3f, h2c], [PX[0]])
            p.tt("dve", hfc[:, blk, :], PX[0][0:64, 0:CTX], wc[:], ALU.mult, [PX[0], wc], [hfc])
        ctx_conv(uc, uc[:, 0, :], 0, uc[:, 1, :], zc1, zc1[:])
        ctx_conv(zc1, zc1[:], 1, uc[:, 2, :], zc1, zc1[:])
        p.dma("sp", YC[cg], zc1[:], [zc1], [YC])
        p.dma("sp", BGa[0:64, :], cd["wl"][cg].rearrange("p c j -> p (c j)"), [cd["wl"]], [BGa])
        for o in range(2):
            for n2 in range(128):
                ps = PY[n2 % 2]
                p.mm(ps[0:64, 0:128], H2[:, n2:SEQ:128], w3[:, 2 * o:2 * o + 2, :].rearrange("p s c -> p (s c)"), True, True, [H2, w3], [ps])
                p.tt("dve", hfv[:, :, :, n2], ps[0:64, 0:128].rearrange("p (s c) -> p s c", s=2),
                     win3[:, :, n2].unsqueeze(1).broadcast_to([64, 2, 64]), ALU.mult, [ps, BGa], [BGb])
            p.op("dve", lambda e: e.memset(hfv[0:1, 1, :, 0], 0.0), [], [BGb])
            for bt in range(16):
                c0 = 4 * bt
                fwd_batch(BGb, lambda c: hfv[:, 0, c0 + c, :])
                xr, xi_ = Xf[0], Xf[1]
                p.act(xr[:], PX[0][:], AF.Identity, [PX[0]], [xr])
                p.act(xi_[:], PX[1][:], AF.Identity, [PX[1]], [xi_])
                fwd_batch(BGb, lambda c: hfv[:, 1, c0 + c, :])
                p.tt("dve", Hs[:, 0, c0:c0 + 4, :], xr[:].rearrange("p (c j) -> p c j", c=4),
                     PX[0][:].rearrange("p (c j) -> p c j", c=4), ALU.add, [xr, PX[0]], [Hs])
                p.tt("dve", Hs[:, 1, c0:c0 + 4, :], xi_[:].rearrange("p (c j) -> p c j", c=4),
                     PX[1][:].rearrange("p (c j) -> p c j", c=4), ALU.subtract, [xi_, PX[1]], [Hs])
            for bt in range(16):
                c0 = 4 * bt
                b = bt % 2
                z_, zb_, g_ = zf[b], zb[b], xgb[b]
                if o == 0:
                    p.dma("sp", z_[:], scr[0, c0:c0 + 4, :].rearrange("c (p j) -> p c j", j=128), [scr], [z_])
                else:
                    p.dma("sp", z_[:], scr2[:, c0:c0 + 4, :], [scr2], [z_])
                p.dma("sp", g_[:], scr[1 + o, c0:c0 + 4, :].rearrange("c (p j) -> p c j", j=128), [scr], [g_])
                p.op("pool", lambda e: e.tensor_copy(out=zb_[:], in_=z_[:]), [z_], [zb_])
                p.tt("pool", z_[:], z_[:], hbr[:, o, c0:c0 + 4].unsqueeze(2).broadcast_to([64, 4, 128]), ALU.mult, [z_, hbr], [z_])
                fwd_batch(zb_, lambda c: zb_[:, c, :])
                xr, xi_ = Xf[2], Xf[3]
                p.act(xr[:], PX[0][:], AF.Identity, [PX[0]], [xr])
                p.act(xi_[:], PX[1][:], AF.Identity, [PX[1]], [xi_])
                x4r = xr[:].rearrange("p (c j) -> p c j", c=4)
                x4i = xi_[:].rearrange("p (c j) -> p c j", c=4)
                ta = tm[0][:].rearrange("p (c j) -> p c j", c=4)
                tb = tm[1][:].rearrange("p (c j) -> p c j", c=4)
                tc_ = tm[2][:].rearrange("p (c j) -> p c j", c=4)
                td_ = tm[3][:].rearrange("p (c j) -> p c j", c=4)
                p.tt("dve", ta, x4r, Hs[:, 0, c0:c0 + 4, :], ALU.mult, [xr, Hs], [tm[0]])
                p.tt("dve", tb, x4i, Hs[:, 1, c0:c0 + 4, :], ALU.mult, [xi_, Hs], [tm[1]])
                p.tt("dve", Yr[b][:], ta, tb, ALU.subtract, [tm[0], tm[1]], [Yr[b]])
                p.tt("pool", tc_, x4r, Hs[:, 1, c0:c0 + 4, :], ALU.mult, [xr, Hs], [tm[2]])
                p.tt("pool", td_, x4i, Hs[:, 0, c0:c0 + 4, :], ALU.mult, [xi_, Hs], [tm[3]])
                p.tt("pool", Yi[b][:], tc_, td_, ALU.add, [tm[2], tm[3]], [Yi[b]])
                py = PY[bt % 2]
                inv_batch(Yr[b], Yi[b], py)
                p.tt("dve", z_[:], py[0:64, :].rearrange("p (c j) -> p c j", c=4), z_[:], ALU.add, [py, z_], [z_])
                p.tt("pool", z_[:], z_[:], g_[:], ALU.mult, [z_, g_], [z_])
                if o == 0:
                    p.dma("sp", scr2[:, c0:c0 + 4, :], z_[:], [z_], [scr2])
                else:
                    p.dma("sp", YL[cg, :, c0:c0 + 4, :], z_[:], [z_], [YL])
    st.close()


def emit_l4(p, PS, S1, GO, cd):
    st = Stage(p)
    sb = st.sb
    ident = sb("ident", [128, 128])
    mask = sb("mask", [128, 128])
    Jm = sb("Jm", [128, 128], BF16)
    pm = sb("pm", [128, 2])
    ones = sb("ones", [32, 1])
    p.dma("sp", ident[:], cd["ident"][:], [cd["ident"]], [ident])
    p.dma("sp", mask[:], cd["mask"][:], [cd["mask"]], [mask])
    p.dma("pool", Jm[:], cd["J"][:], [cd["J"]], [Jm])
    p.dma("sp", pm[:], cd["pm"][:], [cd["pm"]], [pm])
    p.op("dve", lambda e: e.memset(ones[:], 1.0), [], [ones])
    qd = sb("qd", [32, TG], BF16)
    kd = sb("kd", [32, TG], BF16)
    ktT = [sb("ktT%d" % i, [128, NPR, 32], BF16) for i in range(2)]
    vnat = sb("vnat", [128, NPR, 64], BF16)
    vt = sb("vt", [128, NPR, 64], BF16)
    STm = sb("STm", [128, NPR, 128], BF16)
    Sbf = sb("Sbf", [32, NCH + 1, 64], BF16)
    dc = sb("dc", [32, NCH])
    oT = sb("oT", [64, TG])
    Scur = [sb("Scur%d" % i, [32, 64]) for i in range(2)]
    seg3 = sb("seg3", [32, 3, SEGL])
    gin = [sb("gin%d" % i, [128, 3, 32]) for i in range(3)]
    Gs = [sb("Gs%d" % i, [32, SEGL]) for i in range(2)]
    At = sb("At", [32, SEGL])
    Bt = sb("Bt", [32, SEGL])
    Sc = sb("Sc", [32, 22])
    tmpc = sb("tmpc", [32, 22])
    psT, psS, psK, psO = PS[0:2], PS[2:4], PS[4:6], PS[6:8]
    ng = 0
    for hd in range(4):
        p.dma("pool", vnat[:], S1[:, 1792 + hd * 64:1856 + hd * 64].rearrange("(t p) c -> p t c", p=128), [S1], [vnat], max_dma_last_dim=4096)
        for d in range(2):
            def gtile(j):
                return (64 + j if j < 2 else j - 2) if d == 0 else 65 - j
            if d == 0:
                p.op("pool", lambda e: e.tensor_copy(out=vt[:, 0:2, :], in_=vnat[:, 64:66, :]), [vnat], [vt])
                p.op("pool", lambda e: e.tensor_copy(out=vt[:, 2:66, :], in_=vnat[:, 0:64, :]), [vnat], [vt])
            else:
                for j0 in range(0, NPR, 8):
                    n = min(8, NPR - j0)
                    ps = psK[(j0 // 8) % 2]
                    for j in range(n):
                        p.mm(ps[:, j * 64:(j + 1) * 64], Jm[:], vnat[:, gtile(j0 + j), :], True, True, [Jm, vnat], [ps])
                    p.act(vt[:, j0:j0 + n, :], ps[:, 0:n * 64].rearrange("p (a b) -> p a b", b=64), AF.Identity, [ps], [vt])
            gcol = (2304 if d == 0 else 2432) + hd * 32
            for s in range(NSEG):
                seg = slice(s * SEGL, (s + 1) * SEGL)
                for jj in range(11):
                    j = s * 11 + jj
                    gt = gtile(j)
                    gi = gin[ng % 3]
                    ps = psT[ng % 2]
                    ng += 1
                    rows = slice(gt * 128, (gt + 1) * 128)
                    p.dma("sp", gi[:, 0:2, :], S1[rows, 1536 + hd * 32:1536 + hd * 32 + 256].rearrange("p (a c) -> p a c", a=2)[:, :, 0:32], [S1], [gi])
                    p.dma("sp", gi[:, 2, :], S1[rows, gcol:gcol + 32], [S1], [gi])
                    for a in range(3):
                        p.tr(ps[0:32, a * 128:(a + 1) * 128], gi[:, a, :], ident[:], [gi, ident], [ps])
                    dst = seg3[:, :, jj * 128:(jj + 1) * 128]
                    if d == 1:
                        dst = dst[:, :, ::-1]
                    p.act(dst, ps[0:32, 0:384].rearrange("p (a j) -> p a j", a=3), AF.Identity, [ps], [seg3])
                qseg, kseg, gseg = seg3[:, 0, :], seg3[:, 1, :], seg3[:, 2, :]
                G = Gs[s % 2]
                Gp = Gs[(s + 1) % 2]
                init = 0.0 if s == 0 else Gp[:, SEGL - 1:SEGL]
                rd = [seg3, ones] + ([] if s == 0 else [Gp])
                p.op("dve", lambda e: e.tensor_tensor_scan(out=G[:], data0=ones[:, 0:1].broadcast_to([32, SEGL]), data1=gseg,
                                                           initial=init, op0=ALU.mult, op1=ALU.add), rd, [G])
                G3 = G[:].rearrange("p (c j) -> p c j", j=64)
                Ec = G3[:, :, 63]
                if s == 0:
                    p.op("dve", lambda e: e.memset(Sc[:, 0:1], 0.0), [], [Sc])
                else:
                    p.op("dve", lambda e: e.tensor_copy(out=Sc[:, 0:1], in_=Gp[:, SEGL - 1:SEGL]), [Gp], [Sc])
                p.op("dve", lambda e: e.tensor_copy(out=Sc[:, 1:22], in_=G3[:, 0:21, 63]), [G], [Sc])
                A3 = At[:].rearrange("p (c j) -> p c j", j=64)
                p.tt("dve", A3, G3, Sc[:].unsqueeze(2).broadcast_to([32, 22, 64]), ALU.subtract, [G, Sc], [At])
                p.act(Bt[:], At[:], AF.Exp, [At], [Bt])
                p.tt("dve", qd[:, seg], qseg, Bt[:], ALU.mult, [seg3, Bt], [qd])
                p.act(Bt[:], At[:], AF.Exp, [At], [Bt], scale=-1.0)
                p.tt("dve", kd[:, seg], kseg, Bt[:], ALU.mult, [seg3, Bt], [kd])
                p.tt("dve", tmpc[:], Ec, Sc[:], ALU.subtract, [G, Sc], [tmpc])
                p.act(dc[:, s * 22:(s + 1) * 22], tmpc[:], AF.Exp, [tmpc], [dc])
                p.tt("dve", A3, Ec.unsqueeze(2).broadcast_to([32, 22, 64]), G3, ALU.subtract, [G], [At])
                p.act(At[:], At[:], AF.Exp, [At], [At])
                p.tt("dve", Bt[:], kseg, At[:], ALU.mult, [seg3, At], [Bt])
                ps = psS[s % 2]
                for j in range(11):
                    p.tr(ps[:, j * 32:(j + 1) * 32], Bt[:, j * 128:(j + 1) * 128], ident[0:32, 0:32], [Bt, ident], [ps])
                for hf in range(2):
                    p.act(ktT[hf][:, s * 11:(s + 1) * 11, :], ps[:, 0:352].rearrange("p (a b) -> p a b", b=32), AF.Identity,
                          [ps, pm], [ktT[hf]], scale=pm[:, hf:hf + 1])
            for g0 in range(0, NPR, 4):
                n = min(4, NPR - g0)
                ps = psS[(g0 // 4) % 2]
                for j in range(n):
                    tok = slice((g0 + j) * 128, (g0 + j + 1) * 128)
                    p.mm(ps[:, j * 128:(j + 1) * 128], kd[:, tok], qd[:, tok], True, True, [kd, qd], [ps])
                p.tt("dve", STm[:, g0:g0 + n, :], ps[:, 0:n * 128].rearrange("p (a b) -> p a b", b=128),
                     mask[:].unsqueeze(1).broadcast_to([128, n, 128]), ALU.mult, [ps, mask], [STm])
            p.op("dve", lambda e: e.memset(Scur[0][:], 0.0), [], [Scur[0]])
            p.op("dve", lambda e: e.memset(Sbf[:, 0, :], 0.0), [], [Sbf])
            for c0 in range(0, NCH, 8):
                n = min(8, NCH - c0)
                ps = psK[(c0 // 8) % 2]
                for j in range(n):
                    pr, hf = divmod(c0 + j, 2)
                    p.mm(ps[0:32, j * 64:(j + 1) * 64], ktT[hf][:, pr, :], vt[:, pr, :], True, True, [ktT[hf], vt], [ps])
                for j in range(n):
                    c = c0 + j
                    sa, sb_ = Scur[c % 2], Scur[(c + 1) % 2]
                    p.stt(sb_[:], sa[:], dc[:, c:c + 1], ps[0:32, j * 64:(j + 1) * 64], ALU.mult, ALU.add, [sa, dc, ps], [sb_])
                    p.act(Sbf[:, c + 1, :], sb_[:], AF.Identity, [sb_], [Sbf])
            for g0 in range(0, NPR, 4):
                n = min(4, NPR - g0)
                ps = psO[(g0 // 4) % 2]
                for j in range(n):
                    pr = g0 + j
                    cs_ = slice(j * 128, (j + 1) * 128)
                    p.mm(ps[0:64, cs_], vt[:, pr, :], STm[:, pr, :], True, False, [vt, STm], [ps])
                    for hf in range(2):
                        c = 2 * pr + hf
                        tok = slice(c * 64, (c + 1) * 64)
                        p.mm(ps[0:64, j * 128 + hf * 64:j * 128 + (hf + 1) * 64], Sbf[:, c, :], qd[:, tok], False, hf == 1, [Sbf, qd], [ps])
                p.act(oT[:, g0 * 128:(g0 + n) * 128], ps[0:64, 0:n * 128], AF.Identity, [ps], [oT])
            p.dma("sp", GO[hd, d], oT[:], [oT], [GO])
    st.close()


def emit_l5a(p, PS, l, SY, YL, YC, GO, S1, Hin, H1, MOD, wd, ident_d):
    st = Stage(p)
    sb = st.sb
    W = sb("W", [128, 8, D], BF16)
    og = sb("og", [128, 8])
    g1 = [sb("g1_%d" % k, [128, D]) for k in range(2)]
    ident = sb("ident", [128, 128])
    epst = sb("eps", [128, 1])
    p.op("dve", lambda e: e.memset(epst[:], EPS), [], [epst])
    p.dma("sp", og[:], wd["out_norm_g"][l], [wd["out_norm_g"]], [og])
    p.dma("sp", ident[:], ident_d[:], [ident_d], [ident])
    for k in range(2):
        p.dma("sp", g1[k][:], MOD[l, k:k + 1, 2 * D:3 * D].partition_broadcast(128)[:, 0, :], [MOD], [g1[k]])
    for c in range(8):
        p.dma("pool", W[:, c, :], wd["w_out"][l, c * 128:(c + 1) * 128, :], [wd["w_out"]], [W], max_dma_last_dim=4096)
    NB = 2
    yt = [sb("yt%d" % i, [128, D]) for i in range(NB)]
    ht = [sb("ht%d" % i, [128, D]) for i in range(NB)]
    srt = [sb("srt%d" % i, [128, 256]) for i in range(NB)]
    hin = [sb("hin%d" % i, [64, 4, 128]) for i in range(NB)]
    gf = [sb("gf%d" % i, [64, 4, 128]) for i in range(NB)]
    gb = [sb("gb%d" % i, [64, 4, 128]) for i in range(NB)]
    ho = [sb("ho%d" % i, [128, D]) for i in range(NB)]
    yT = [sb("yT%d" % i, [128, 8, 128], BF16) for i in range(NB)]
    sq = sb("sq", [128, D])
    st_ = [sb("st%d" % i, [128, 16]) for i in range(NB)]
    pst, pso, psx = PS[0:2], PS[2:6], PS[6:8]
    def phaseA(i):
        bi = i % NB
        rows = slice(i * 128, (i + 1) * 128)
        y, h, sr, s = yt[bi], ht[bi], srt[bi], st_[bi]
        p.dma("sp", y[:, 0:512], SY[rows, :], [SY], [y])
        p.dma("sp", sr[:], S1[rows, 2048:2304], [S1], [sr])
        p.dma("sp", h[:], Hin[rows, :], [Hin], [h])
        if i < 64:
            p.dma("sp", hin[bi][:], YL[:, i, :, :].rearrange("g c j -> c g j"), [YL], [hin[bi]])
            f0, b0 = 256 + i * 128, 256 + (63 - i) * 128
        else:
            c = i - 64
            p.dma("sp", hin[bi][:], YC[:, :, c * 128:(c + 1) * 128].rearrange("g c j -> c g j"), [YC], [hin[bi]])
            f0, b0 = c * 128, (1 - c) * 128
        p.dma("sp", gf[bi][:], GO[:, 0, :, f0:f0 + 128].rearrange("h v j -> v h j"), [GO], [gf[bi]])
        p.dma("sp", gb[bi][:], GO[:, 1, :, b0:b0 + 128].rearrange("h v j -> v h j"), [GO], [gb[bi]])
        p.tt("pool", gf[bi][:], gf[bi][:], gb[bi][:, :, ::-1], ALU.add, [gf[bi], gb[bi]], [gf[bi]])
        for a in range(4):
            p.tr(psx[0][:, a * 64:(a + 1) * 64], hin[bi][:, a, :], ident[0:64, 0:64], [hin[bi], ident], [psx[0]])
            p.tr(psx[1][:, a * 64:(a + 1) * 64], gf[bi][:, a, :], ident[0:64, 0:64], [gf[bi], ident], [psx[1]])
        p.act(y[:, 512:768], psx[0][:, 0:256], AF.Identity, [psx[0]], [y])
        p.act(y[:, 768:1024], psx[1][:, 0:256], AF.Identity, [psx[1]], [y])
        p.tt("pool", sq[:], y[:], y[:], ALU.mult, [y], [sq])
        p.op("dve", lambda e: e.tensor_reduce(out=s[:], in_=sq[:].rearrange("p (h d) -> p h d", d=64), axis=AX.X, op=ALU.add), [sq], [s])
        p.rsqrt(s, s[:], s, s[:], 1.0 / 64, epst)
        y3 = y[:].rearrange("p (h d) -> p h d", d=64)
        p.tt("dve", y3, y3, s[:].unsqueeze(2).broadcast_to([128, 16, 64]), ALU.mult, [y, s], [y])
        p.tt("pool", y[:, 768:1024], y[:, 768:1024], sr[:], ALU.mult, [y, sr], [y])
        for c in range(8):
            ps = pst[c // 4]
            blk = ps[:, (c % 4) * 128:(c % 4 + 1) * 128]
            p.tr(blk, y[:, c * 128:(c + 1) * 128], ident[:], [y, ident], [ps])
            p.act(yT[bi][:, c, :], blk, AF.Identity, [ps, og], [yT[bi]], scale=og[:, c:c + 1])

    def phaseB(i):
        kind = 0 if i < 64 else 1
        bi = i % NB
        rows = slice(i * 128, (i + 1) * 128)
        h = ht[bi]
        for hf in range(2):
            ps = pso[(2 * i + hf) % 4]
            cols = slice(hf * 512, (hf + 1) * 512)
            for c in range(8):
                p.mm(ps[:], yT[bi][:, c, :], W[:, c, cols], c == 0, c == 7, [yT[bi], W], [ps])
            p.tt("dve", ho[bi][:, cols], ps[:], g1[kind][:, cols], ALU.mult, [ps, g1[kind]], [ho[bi]])
            p.tt("pool", ho[bi][:, cols], ho[bi][:, cols], h[:, cols], ALU.add, [ho[bi], h], [ho[bi]])
        p.dma("sp", H1[rows, :], ho[bi][:], [ho[bi]], [H1])

    phaseA(0)
    for i in range(NTT):
        if i + 1 < NTT:
            phaseA(i + 1)
        phaseB(i)
    st.close()


def emit_l5b(p, PS, l, H1, Hout, MOD, wd, ident_d, final):
    st = Stage(p)
    sb = st.sb
    W1 = sb("W1", [128, 8, FFN], BF16)
    W3 = sb("W3", [128, 8, FFN], BF16)
    W2 = sb("W2", [128, NF, D], BF16)
    A = sb("A", [128, 8, 2])
    ng = sb("ng", [128, 8])
    g2 = [sb("g2_%d" % k, [128, D]) for k in range(3 if final else 2)]
    ident = sb("ident", [128, 128])
    epst = sb("eps", [128, 1])
    p.op("dve", lambda e: e.memset(epst[:], EPS), [], [epst])
    p.dma("sp", ident[:], ident_d[:], [ident_d], [ident])
    p.dma("sp", ng[:], wd["norm2_g"][l], [wd["norm2_g"]], [ng])
    for k in range(2):
        p.dma("sp", g2[k][:], MOD[l, k:k + 1, 5 * D:6 * D].partition_broadcast(128)[:, 0, :], [MOD], [g2[k]])
    if final:
        p.dma("sp", g2[2][:], wd["final_norm_g"][:].partition_broadcast(128)[:, 0, :], [wd["final_norm_g"]], [g2[2]])
    for c in range(8):
        p.dma("pool", W1[:, c, :], wd["ffn_w1"][l, c * 128:(c + 1) * 128, :], [wd["ffn_w1"]], [W1], max_dma_last_dim=4096)
        p.dma("pool", W3[:, c, :], wd["ffn_w3"][l, c * 128:(c + 1) * 128, :], [wd["ffn_w3"]], [W3], max_dma_last_dim=4096)
    for f in range(NF):
        p.dma("pool", W2[:, f, :], wd["ffn_w2"][l, f * 128:(f + 1) * 128, :], [wd["ffn_w2"]], [W2], max_dma_last_dim=4096)
    modT = load_modT(p, st, PS, MOD, l, ident)
    for k in range(2):
        p.stt(A[:, :, k], modT[:, k * 48 + 32:k * 48 + 40], 1.0, ng[:], ALU.add, ALU.mult, [modT, ng], [A])
    G = 2
    hts = [sb("ht%d" % i, [128, D]) for i in range(2 * G)]
    xs = sb("xs", [128, D])
    junk = sb("junk", [128, D])
    u2T = [sb("u2T%d" % i, [128, 8, G * 128], BF16) for i in range(2)]
    hidT = sb("hidT", [128, NF, G * 128], BF16)
    sil = [sb("sil%d" % i, [128, G * 128]) for i in range(2)]
    ho = [sb("ho%d" % i, [128, D]) for i in range(2)]
    st_ = sb("st", [128, 8])
    pst, psu, psd = PS[0:2], PS[2:6], PS[6:8]
    groups = [list(range(a, a + G)) for a in range(0, 64, G)] + [[64, 65]]
    nd = 0
    for gi, tiles in enumerate(groups):
        kind = 0 if tiles[0] < 64 else 1
        N = len(tiles) * 128
        u = u2T[gi % 2]
        for j, i in enumerate(tiles):
            h = hts[(gi % 2) * G + j]
            rows = slice(i * 128, (i + 1) * 128)
            p.dma("sp", h[:], H1[rows, :], [H1], [h])
            sc = st_[:, 2 * j:2 * j + 1]
            sr_ = st_[:, 2 * j + 1:2 * j + 2]
            p.act(junk[:], h[:], AF.Square, [h], [junk, st_], accum_out=sc)
            p.rsqrt(st_, sr_, st_, sc, 1.0 / D, epst)
            p.act(xs[:], h[:], AF.Identity, [h, st_], [xs], scale=sr_)
            for c in range(8):
                ps = pst[c // 4]
                blk = ps[:, (c % 4) * 128:(c % 4 + 1) * 128]
                p.tr(blk, xs[:, c * 128:(c + 1) * 128], ident[:], [xs, ident], [ps])
                p.act(u[:, c, j * 128:(j + 1) * 128], blk, AF.Identity, [ps, A, modT], [u],
                      scale=A[:, c, kind:kind + 1], bias=modT[:, kind * 48 + 24 + c:kind * 48 + 25 + c])
        for f in range(NF):
            ps1 = psu[(2 * f) % 4]
            ps3 = psu[(2 * f + 1) % 4]
            fc = slice(f * 128, (f + 1) * 128)
            for c in range(8):
                p.mm(ps1[:, 0:N], W1[:, c, fc], u[:, c, 0:N], c == 0, c == 7, [W1, u], [ps1])
            for c in range(8):
                p.mm(ps3[:, 0:N], W3[:, c, fc], u[:, c, 0:N], c == 0, c == 7, [W3, u], [ps3])
            s_ = sil[f % 2]
            p.act(s_[:, 0:N], ps1[:, 0:N], AF.Silu, [ps1], [s_])
            p.tt("dve", hidT[:, f, 0:N], s_[:, 0:N], ps3[:, 0:N], ALU.mult, [s_, ps3], [hidT])
        for j, i in enumerate(tiles):
            h = hts[(gi % 2) * G + j]
            rows = slice(i * 128, (i + 1) * 128)
            o = ho[nd % 2]
            nd += 1
            for hf in range(2):
                ps = psd[hf]
                cols = slice(hf * 512, (hf + 1) * 512)
                for f in range(NF):
                    p.mm(ps[:], hidT[:, f, j * 128:(j + 1) * 128], W2[:, f, cols], f == 0, f == NF - 1, [hidT, W2], [ps])
                p.tt("dve", o[:, cols], ps[:], g2[kind][:, cols], ALU.mult, [ps, g2[kind]], [o])
                p.tt("pool", o[:, cols], o[:, cols], h[:, cols], ALU.add, [o, h], [o])
            if final:
                sc = st_[:, 4:5]
                sr_ = st_[:, 5:6]
                p.act(junk[:], o[:], AF.Square, [o], [junk, st_], accum_out=sc)
                p.rsqrt(st_, sr_, st_, sc, 1.0 / D, epst)
                p.stt(o[:], o[:], sr_, g2[2][:], ALU.mult, ALU.mult, [o, st_, g2[2]], [o])
            p.dma("sp", Hout[rows, :], o[:], [o], [Hout])
    st.close()


FUSED_W = ["ada_w", "ada_b", "w_in", "qkg", "gla_gate_w", "gla_gate_b", "norm1_g", "norm2_g", "out_norm_g", "w_out",
           "ffn_w1", "ffn_w3", "ffn_w2", "final_norm_g", "cw", "hbc", "hb", "fw3", "filt_w1", "filt_w2", "fb"]
FUSED_C = ["cs", "ident", "mask", "J", "pm", "zl", "zc", "wl", "wc", "F1", "TWf", "F2", "G2", "TWi", "G1"]


def fused_host_inputs(w):
    rope = rope_table()
    one = np.concatenate([np.ones((CTX, 32), np.float32), np.zeros((CTX, 32), np.float32)], axis=1)
    j = np.arange(128)
    zl, win_l = hy_features(SEQ)
    zc, win_c = hy_features(CTX)
    consts = hy_consts()
    consts.update({
        "cs": np.concatenate([rope, one], axis=0),
        "ident": np.eye(128, dtype=np.float32),
        "mask": ((j[:, None] // 64 == j[None, :] // 64) & (j[None, :] >= j[:, None])).astype(np.float32),
        "J": np.eye(128, dtype=np.float32)[::-1].copy(),
        "pm": L4_PM,
        "zl": zl, "zc": zc,
        "wl": win_l.reshape(64, 128, 4, 64).transpose(2, 0, 3, 1),
        "wc": win_c.reshape(CTX, 4, 64).transpose(1, 2, 0),
    })
    fmL = lambda a: np.stack([fm(a[l]) for l in range(DEPTH)])
    cwf = w["hy_conv_w"].reshape(DEPTH, 3, 3, 4, 64)
    cbf = w["hy_conv_b"].reshape(DEPTH, 3, 4, 64)
    cw = np.concatenate([cwf.transpose(0, 3, 4, 2, 1), cbf.transpose(0, 2, 3, 1)[..., None]], axis=-1)
    hb = w["hy_bias"].reshape(DEPTH, 2, 4, 64).transpose(0, 2, 1, 3)
    ws = {
        "ada_w": w["ada_w"], "ada_b": w["ada_b"], "w_in": w["w_in"],
        "qkg": np.concatenate([w["q_norm_g"], w["k_norm_g"]], axis=1),
        "gla_gate_w": w["gla_gate_w"], "gla_gate_b": w["gla_gate_b"].reshape(DEPTH, 256),
        "norm1_g": fmL(w["norm1_g"]), "norm2_g": fmL(w["norm2_g"]), "out_norm_g": fmL(w["out_norm_g"]),
        "w_out": w["w_out"], "ffn_w1": w["ffn_w1"], "ffn_w3": w["ffn_w3"], "ffn_w2": w["ffn_w2"],
        "final_norm_g": w["final_norm_g"].reshape(1, D),
        "cw": cw, "hbc": hb.transpose(0, 1, 3, 2), "hb": hb,
        "fw3": w["filt_w3"].reshape(DEPTH, 64, 4, 4, 64).transpose(0, 3, 1, 2, 4),
        "filt_w1": w["filt_w1"], "filt_w2": w["filt_w2"],
        "fb": np.stack([w["filt_freq"], w["filt_b1"], w["filt_b2"]], axis=-1),
    }
    shared = {k: np.ascontiguousarray(v, dtype=np.float32) for k, v in {**ws, **consts}.items()}
    in_maps = []
    for i in range(NCORES):
        b = i // 4
        m = dict(shared)
        m["H0"] = np.ascontiguousarray(np.concatenate([w["x"][b], w["ctx"][b]], axis=0))
        cvec = np.stack([w["c"][b], w["c_ctx"]], axis=0)
        m["cT"] = np.ascontiguousarray(cvec.T.reshape(8, 128, 2).transpose(1, 0, 2))
        in_maps.append(m)
    return in_maps


def build_fused(shapes, nlayers=DEPTH, final=True, dump=()):
    p = Prog()
    H0 = p.dram_in("H0", [TT, D])
    cT = p.dram_in("cT", [128, 8, 2])
    wd = {k: p.dram_in(k, shapes[k]) for k in FUSED_W}
    cd = {k: p.dram_in(k, shapes[k]) for k in FUSED_C}
    out = p.dram_out("out", [TT, D])
    MOD = scratch(p, "MOD", [DEPTH, 2, 6 * D])
    S1 = scratch(p, "S1", [TT, OUT1])
    SY = scratch(p, "SY", [TT, 512])
    YL = scratch(p, "YL", [4, 64, 64, 128])
    YC = scratch(p, "YC", [4, 64, CTX])
    GO = scratch(p, "GO", [4, 2, 64, TT])
    H1 = scratch(p, "H1", [TT, D])
    HA = scratch(p, "HA", [TT, D])
    scr = scratch(p, "scr", [3, 64, SEQ])
    scr2 = scratch(p, "scr2", [64, 64, 128])
    PS2 = [p.ps("Q%d" % i, (128, 1024)) for i in range(2)]
    PS = [Tile(PS2[i // 2].h[:, (i % 2) * 512:(i % 2 + 1) * 512]) for i in range(4)] + [p.ps("P%d" % i) for i in range(4, 8)]
    emit_ada(p, PS, cT, wd["ada_w"], wd["ada_b"], MOD)
    Hin = H0
    for l in range(nlayers):
        last = l == nlayers - 1
        emit_l1(p, PS, l, Hin, S1, MOD, wd, cd["cs"], cd["ident"])
        emit_l2(p, PS, S1, SY, cd["ident"], PS2)
        emit_l3(p, PS, l, S1, YL, YC, wd, cd, scr, scr2)
        emit_l4(p, PS, S1, GO, cd)
        emit_l5a(p, PS, l, SY, YL, YC, GO, S1, Hin, H1, MOD, wd, cd["ident"])
        emit_l5b(p, PS, l, H1, out if last else HA, MOD, wd, cd["ident"], final and last)
        Hin = HA
    for name in dump:
        src = {"S1": S1, "SY": SY, "YL": YL, "YC": YC, "GO": GO, "H1": H1, "MOD": MOD}[name]
        shp = list(src.h.shape)
        d = p.dram_out("dump_" + name, shp)
        flat = lambda ap: ap.rearrange(" ".join("abcd"[:len(shp)]) + " -> " + ("(" + " ".join("abcd"[:len(shp) - 1]) + ") " + "abcd"[len(shp) - 1] if len(shp) > 2 else "a b"))
        p.dma("sp", flat(d[:]), flat(src[:]), [src], [d])
    return p


def kernel_fused(inputs, nlayers=DEPTH, final=True, dump=()):
    w = {k: np.ascontiguousarray(np.asarray(v, dtype=np.float32)) for k, v in inputs.items()}
    in_maps = fused_host_inputs(w)
    shapes = {k: list(v.shape) for k, v in in_maps[0].items()}
    res = run(build_fused(shapes, nlayers, final, dump), in_maps)
    return res


def kernel(**inputs):
    res = kernel_fused(inputs)
    return np.ascontiguousarray(np.stack([res[0]["out"][:SEQ], res[4]["out"][:SEQ]], axis=0), dtype=np.float32)
```

```python
import math
import numpy as np
import concourse.bass as bass
import concourse.mybir as mybir
from concourse.bass_utils import run_bass_kernel_spmd

F32 = mybir.dt.float32
BF16 = mybir.dt.bfloat16
AF = mybir.ActivationFunctionType
ALU = mybir.AluOpType
AX = mybir.AxisListType

NCORES = 8
D = 1024
SEQ = 8192
CTX = 256
DEPTH = 4
IN_W = 2336
FFN = 2816
EPS = 1e-6
NSLOT = 8


class Tile:
    def __init__(self, h):
        self.h = h
        self.w = None
        self.r = {}

    def __getitem__(self, idx):
        return self.h[idx]


class Prog:
    def __init__(self, self_sync=True):
        self.nc = bass.Bass("TRN2", target_bir_lowering=False)
        nc = self.nc
        self.eng = {"pe": nc.tensor, "dve": nc.vector, "act": nc.scalar, "pool": nc.gpsimd, "sp": nc.sync}
        self.sems = {}
        self.cnt = {}
        self.known = {e: {} for e in self.eng}
        self.dslot = {"sp": 0, "act": 0, "pool": 0}
        self.self_sync = self_sync
        self._sem_cms = []
        self.n_inst = 0

    def _sem(self, sid):
        if sid not in self.sems:
            cm = self.nc.semaphore(sid)
            self._sem_cms.append(cm)
            self.sems[sid] = cm.__enter__()
            self.cnt[sid] = 0
        return self.sems[sid]

    def sb(self, name, shape, dtype=F32):
        return Tile(self.nc.alloc_sbuf_tensor("sb_" + name, list(shape), dtype))

    def ps(self, name, shape=(128, 512), dtype=F32):
        return Tile(self.nc.alloc_psum_tensor("ps_" + name, list(shape), dtype))

    def dram_in(self, name, shape, dtype=F32):
        return Tile(self.nc.dram_tensor(name, list(shape), dtype, kind="ExternalInput").ap())

    def dram_out(self, name, shape, dtype=F32):
        return Tile(self.nc.dram_tensor(name, list(shape), dtype, kind="ExternalOutput").ap())

    def _deps(self, eng, reads, writes):
        deps = {}

        def add(sid, v):
            if deps.get(sid, 0) < v:
                deps[sid] = v

        for t in reads:
            if t.w:
                add(*t.w)
        for t in writes:
            if t.w:
                add(*t.w)
            for sid, v in t.r.items():
                add(sid, v)
        if eng == "pe" or not self.self_sync:
            deps.pop("c_" + eng, None)
        return deps

    def _wait(self, eng, deps):
        e = self.eng[eng]
        kn = self.known[eng]
        for sid, val in deps.items():
            if kn.get(sid, 0) >= val:
                continue
            e.wait_ge(self.sems[sid], val)
            kn[sid] = val

    def _mark(self, tok, reads, writes):
        sid, v = tok
        for t in reads:
            if t.r.get(sid, 0) < v:
                t.r[sid] = v
        for t in writes:
            t.w = tok
            t.r = {}

    def op(self, eng, fn, reads=(), writes=()):
        self._wait(eng, self._deps(eng, reads, writes))
        inst = fn(self.eng[eng])
        sid = "c_" + eng
        sem = self._sem(sid)
        self.cnt[sid] += 1
        inst.then_inc(sem, 1)
        self._mark((sid, self.cnt[sid]), reads, writes)
        self.n_inst += 1
        return inst

    def dma(self, q, out, in_, reads=(), writes=(), **kw):
        deps = self._deps(q, reads, writes)
        slot = self.dslot[q]
        self.dslot[q] = (slot + 1) % NSLOT
        sid = "d_%s_%d" % (q, slot)
        sem = self._sem(sid)
        if self.cnt[sid] > 0 and deps.get(sid, 0) < self.cnt[sid]:
            deps[sid] = self.cnt[sid]
        self._wait(q, deps)
        inst = self.eng[q].dma_start(out=out, in_=in_, **kw)
        self.cnt[sid] += 16
        inst.then_inc(sem, 16)
        self._mark((sid, self.cnt[sid]), reads, writes)
        self.n_inst += 1
        return inst

    def finish(self):
        deps = {sid: c for sid, c in self.cnt.items() if sid.startswith("d_") and c > 0}
        self._wait("sp", deps)
        return self.nc

    def tt(self, eng, out, in0, in1, op, reads, writes):
        return self.op(eng, lambda e: e.tensor_tensor(out=out, in0=in0, in1=in1, op=op), reads, writes)

    def ts(self, eng, out, in0, s1, op0, reads, writes, s2=None, op1=None):
        if op1 is None:
            return self.op(eng, lambda e: e.tensor_scalar(out=out, in0=in0, scalar1=s1, scalar2=None, op0=op0), reads, writes)
        return self.op(eng, lambda e: e.tensor_scalar(out=out, in0=in0, scalar1=s1, scalar2=s2, op0=op0, op1=op1), reads, writes)

    def stt(self, out, in0, scalar, in1, op0, op1, reads, writes):
        return self.op("dve", lambda e: e.scalar_tensor_tensor(out=out, in0=in0, scalar=scalar, in1=in1, op0=op0, op1=op1), reads, writes)

    def act(self, out, in_, func, reads, writes, bias=None, scale=None, accum_out=None):
        kw = {}
        if bias is not None:
            kw["bias"] = bias
        if scale is not None:
            kw["scale"] = scale
        if accum_out is not None:
            kw["accum_out"] = accum_out
        return self.op("act", lambda e: e.activation(out=out, in_=in_, func=func, **kw), reads, writes)

    def mm(self, out, lhsT, rhs, start, stop, reads, writes):
        return self.op("pe", lambda e: e.matmul(out, lhsT, rhs, start=start, stop=stop), reads, writes)

    def tr(self, out, in_, ident, reads, writes):
        return self.op("pe", lambda e: e.transpose(out, in_, ident), reads, writes)

    def rsqrt(self, out_t, out_ap, in_t, in_ap, scale, eps_t):
        self.act(out_ap, in_ap, AF.Sqrt, [in_t, eps_t], [out_t], bias=eps_t[0:in_ap.shape[0], 0:1], scale=scale)
        self.op("dve", lambda e: e.reciprocal(out=out_ap, in_=out_ap), [out_t], [out_t])


def run(prog, in_maps):
    nc = prog.finish()
    res = run_bass_kernel_spmd(nc, in_maps, core_ids=list(range(len(in_maps))))
    return res.results


ADA_COLS = DEPTH * 6 * D // NCORES


def build_ada():
    p = Prog()
    cT = p.dram_in("cT", [128, 8, 3])
    w = p.dram_in("w", [D, ADA_COLS])
    b = p.dram_in("b", [1, ADA_COLS])
    out = p.dram_out("mod", [3, ADA_COLS])
    W = p.sb("W", [128, 8, ADA_COLS])
    cs = p.sb("cs", [128, 8, 3])
    bb = p.sb("bb", [3, ADA_COLS])
    res = p.sb("res", [3, ADA_COLS])
    p.dma("sp", cs[:], cT[:], [cT], [cs])
    p.dma("sp", bb[:], b[:].partition_broadcast(3)[:, 0, :], [b], [bb])
    for c in range(8):
        p.dma("sp" if c % 2 == 0 else "pool", W[:, c, :], w[c * 128:(c + 1) * 128, :], [w], [W])
    p.act(cs[:], cs[:], AF.Silu, [cs], [cs])
    pss = [p.ps("ps%d" % i) for i in range(2)]
    for j in range(ADA_COLS // 512):
        ps = pss[j % 2]
        for c in range(8):
            p.mm(ps[0:3, :], cs[:, c, :], W[:, c, j * 512:(j + 1) * 512], c == 0, c == 7, [cs, W], [ps])
        p.tt("dve", res[:, j * 512:(j + 1) * 512], ps[0:3, :], bb[:, j * 512:(j + 1) * 512], ALU.add, [ps, bb], [res])
    p.dma("sp", out[:], res[:], [res], [out])
    return p


def run_ada(c, c_ctx, ada_w, ada_b):
    cvec = np.concatenate([c, c_ctx[None, :]], axis=0)
    cT = np.ascontiguousarray(cvec.T.reshape(8, 128, 3).transpose(1, 0, 2))
    wall = np.ascontiguousarray(ada_w.transpose(1, 0, 2).reshape(D, DEPTH * 6 * D))
    ball = ada_b.reshape(1, DEPTH * 6 * D)
    in_maps = []
    for i in range(NCORES):
        sl = slice(i * ADA_COLS, (i + 1) * ADA_COLS)
        in_maps.append({"cT": cT, "w": np.ascontiguousarray(wall[:, sl]), "b": np.ascontiguousarray(ball[:, sl])})
    res = run(build_ada(), in_maps)
    mod = np.concatenate([r["mod"] for r in res], axis=1)
    return mod.reshape(3, DEPTH, 6, D)


NT1 = 17
OUT1 = 2560
GROUPS1 = [(0, 512), (512, 1024), (1024, 1536), (1536, 2048), (2048, 2336)]


def build_l1():
    p = Prog()
    ntok = NT1 * 128
    x_tm = p.dram_in("x_tm", [ntok, D])
    x_fm = p.dram_in("x_fm", [128, 8, ntok])
    w_in = p.dram_in("w_in", [D, IN_W])
    vecs = p.dram_in("vecs", [128, 8, 5])
    qkg = p.dram_in("qkg", [1, 128])
    cs_d = p.dram_in("cs", [ntok, 64])
    gw = p.dram_in("gw", [2, 16, 128])
    gb = p.dram_in("gb", [1, 256])
    ident_d = p.dram_in("ident", [128, 128])
    out = p.dram_out("out", [ntok, OUT1])

    W = p.sb("W", [128, 8, IN_W], BF16)
    V = p.sb("V", [128, 8, 5])
    A = p.sb("A", [128, 8, 2])
    Brep = p.sb("Brep", [128, 8, 128], BF16)
    bW = [p.sb("bW%d" % k, [128, IN_W]) for k in range(2)]
    gains = p.sb("gains", [128, 10, 64])
    Wblk = p.sb("Wblk", [32, 256])
    gbb = p.sb("gbb", [128, 256])
    ident = p.sb("ident", [128, 128])
    epst = p.sb("eps", [128, 1])
    pss = [p.ps("ps%d" % i) for i in range(6)]
    pst = p.ps("pst")
    psg = p.ps("psg")

    p.op("dve", lambda e: e.memset(epst[:], EPS), [], [epst])
    p.op("dve", lambda e: e.memset(Wblk[:], 0.0), [], [Wblk])
    p.dma("sp", V[:], vecs[:], [vecs], [V])
    p.dma("sp", ident[:], ident_d[:], [ident_d], [ident])
    p.dma("sp", gbb[:], gb[:].partition_broadcast(128)[:, 0, :], [gb], [gbb])
    p.dma("sp", Wblk[0:16, 0:128], gw[0], [gw], [Wblk])
    p.dma("sp", Wblk[16:32, 128:256], gw[1], [gw], [Wblk])
    for h in range(10):
        src = qkg[:, 0:64] if h < 8 else qkg[:, 64:128]
        p.dma("sp", gains[:, h, :], src.partition_broadcast(128)[:, 0, :], [qkg], [gains])
    p.ts("dve", gains[:, 0:8, :], gains[:, 0:8, :], 0.125, ALU.mult, [gains], [gains])
    for c in range(8):
        p.dma("pool", W[:, c, :], w_in[c * 128:(c + 1) * 128, :], [w_in], [W], max_dma_last_dim=4096)
    for k in range(2):
        p.stt(A[:, :, k], V[:, :, 1 + 2 * k], 1.0, V[:, :, 0], ALU.add, ALU.mult, [V], [A])
        p.op("dve", lambda e: e.tensor_copy(out=Brep[:], in_=V[:, :, 2 + 2 * k].unsqueeze(2).broadcast_to([128, 8, 128])), [V], [Brep])
        for gi, (a, b) in enumerate(GROUPS1):
            ps = pss[gi]
            for c in range(8):
                p.mm(ps[:, 0:b - a], Brep[:, c, :], W[:, c, a:b], c == 0, c == 7, [Brep, W], [ps])
            p.act(bW[k][:, a:b], ps[:, 0:b - a], AF.Identity, [ps], [bW[k]])

    NB = 2
    xt = [p.sb("xt%d" % i, [128, D]) for i in range(NB)]
    xf = [p.sb("xf%d" % i, [128, 8, 128]) for i in range(NB)]
    xa = [p.sb("xa%d" % i, [128, 8, 128], BF16) for i in range(NB)]
    cst = [p.sb("cst%d" % i, [128, 64]) for i in range(NB)]
    O = [p.sb("O%d" % i, [128, OUT1]) for i in range(NB)]
    junk = p.sb("junk", [128, D])
    sq = p.sb("sq", [128, 640])
    tmp = p.sb("tmp", [128, 10, 32])
    st = [p.sb("st%d" % i, [128, 16]) for i in range(NB)]
    lr = p.sb("lr", [128, 32])
    lrT = p.sb("lrT", [32, 128])
    gz = p.sb("gz", [128, 256])
    ga = p.sb("ga", [128, 256])
    pi = 0
    for i in range(NT1):
        kind = 0 if i < 16 else 1
        bi = i % NB
        rows = slice(i * 128, (i + 1) * 128)
        p.dma("sp", xt[bi][:], x_tm[rows, :], [x_tm], [xt[bi]])
        p.dma("sp", xf[bi][:], x_fm[:, :, rows], [x_fm], [xf[bi]])
        p.dma("sp", cst[bi][:], cs_d[rows, :], [cs_d], [cst[bi]])
        s = st[bi]
        p.act(junk[:], xt[bi][:], AF.Square, [xt[bi]], [junk, s], accum_out=s[:, 0:1])
        p.rsqrt(s, s[:, 1:2], s, s[:, 0:1], 1.0 / D, epst)
        p.tt("dve", xa[bi][:], xf[bi][:], A[:, :, kind].unsqueeze(2).broadcast_to([128, 8, 128]), ALU.mult, [xf[bi], A], [xa[bi]])
        o = O[bi]
        for gi, (a, b) in enumerate(GROUPS1):
            ps = pss[pi % 6]
            pi += 1
            for c in range(8):
                p.mm(ps[:, 0:b - a], xa[bi][:, c, :], W[:, c, a:b], c == 0, c == 7, [xa[bi], W], [ps])
            if gi < 4:
                p.stt(o[:, a:b], ps[:, 0:b - a], s[:, 1:2], bW[kind][:, a:b], ALU.mult, ALU.add, [ps, s, bW[kind]], [o])
            else:
                p.stt(o[:, 2048:2304], ps[:, 0:256], s[:, 1:2], bW[kind][:, 2048:2304], ALU.mult, ALU.add, [ps, s, bW[kind]], [o])
                p.stt(lr[:], ps[:, 256:288], s[:, 1:2], bW[kind][:, 2304:2336], ALU.mult, ALU.add, [ps, s, bW[kind]], [lr])
        qk = o[:, 0:640]
        p.tt("pool", sq[:], qk, qk, ALU.mult, [o], [sq])
        p.op("dve", lambda e: e.tensor_reduce(out=s[:, 2:12], in_=sq[:].rearrange("p (h d) -> p h d", d=64), axis=AX.X, op=ALU.add), [sq], [s])
        p.rsqrt(s, s[:, 2:12], s, s[:, 2:12], 1.0 / 64, epst)
        qk3 = qk.rearrange("p (h d) -> p h d", d=64)
        p.tt("dve", qk3, qk3, s[:, 2:12].unsqueeze(2).broadcast_to([128, 10, 64]), ALU.mult, [o, s], [o])
        p.tt("pool", qk3, qk3, gains[:], ALU.mult, [o, gains], [o])
        x1 = qk3[:, :, 0:32]
        x2 = qk3[:, :, 32:64]
        cb = cst[bi][:, 0:32].unsqueeze(1).broadcast_to([128, 10, 32])
        sb_ = cst[bi][:, 32:64].unsqueeze(1).broadcast_to([128, 10, 32])
        t3 = sq[:, 0:320].rearrange("p (h d) -> p h d", d=32)
        t4 = sq[:, 320:640].rearrange("p (h d) -> p h d", d=32)
        p.tt("dve", tmp[:], x2, sb_, ALU.mult, [o, cst[bi]], [tmp])
        p.tt("pool", t3, x1, sb_, ALU.mult, [o, cst[bi]], [sq])
        p.tt("dve", x1, x1, cb, ALU.mult, [o, cst[bi]], [o])
        p.tt("dve", x1, x1, tmp[:], ALU.subtract, [o, tmp], [o])
        p.tt("dve", x2, x2, cb, ALU.mult, [o, cst[bi]], [o])
        p.tt("dve", x2, x2, t3, ALU.add, [o, sq], [o])
        p.act(o[:, 1536:1664], o[:, 1536:1664], AF.Identity, [o], [o], scale=32 ** -0.5)
        p.act(o[:, 2048:2304], o[:, 2048:2304], AF.Silu, [o], [o])
        p.tr(pst[0:32, 0:128], lr[:], ident[:], [lr, ident], [pst])
        p.act(lrT[:], pst[0:32, 0:128], AF.Identity, [pst], [lrT])
        p.mm(psg[:, 0:256], lrT[:], Wblk[:], True, True, [lrT, Wblk], [psg])
        p.tt("dve", gz[:], psg[:, 0:256], gbb[:], ALU.add, [psg, gbb], [gz])
        p.stt(ga[:], gz[:], -1.0, gz[:], ALU.mult, ALU.min, [gz], [ga])
        p.act(ga[:], ga[:], AF.Exp, [ga], [ga])
        p.act(ga[:], ga[:], AF.Ln, [ga], [ga], bias=1.0)
        p.ts("dve", gz[:], gz[:], 0.0, ALU.min, [gz], [gz], s2=1.0 / 16, op1=ALU.mult)
        p.stt(o[:, 2304:2560], ga[:], -1.0 / 16, gz[:], ALU.mult, ALU.add, [ga, gz], [o])
        p.dma("sp", out[rows, :], o[:], [o], [out])
    return p


def fm(v):
    return np.ascontiguousarray(v.reshape(-1, 128).T)


def tok_split(lat, ctx):
    F = lat.shape[-1]
    outs = []
    for i in range(NCORES):
        b, j = divmod(i, 4)
        a = np.zeros((NT1 * 128, F), lat.dtype)
        a[:2048] = lat[b, j * 2048:(j + 1) * 2048]
        if j < 2:
            a[2048:] = ctx[b, j * 128:(j + 1) * 128]
        outs.append(a)
    return outs


def tok_merge(per_core):
    F = per_core[0].shape[-1]
    lat = np.zeros((2, SEQ, F), per_core[0].dtype)
    ctx = np.zeros((2, CTX, F), per_core[0].dtype)
    for i in range(NCORES):
        b, j = divmod(i, 4)
        lat[b, j * 2048:(j + 1) * 2048] = per_core[i][:2048]
        if j < 2:
            ctx[b, j * 128:(j + 1) * 128] = per_core[i][2048:]
    return lat, ctx


def rope_table():
    t = np.arange(SEQ)
    row = (t // 64).astype(np.float32)
    col = (t % 64).astype(np.float32)
    inv = np.power(np.float32(10000.0), -np.arange(16, dtype=np.float32) / np.float32(16)).astype(np.float32)
    ang = np.concatenate([row[:, None] * inv, col[:, None] * inv], axis=-1).astype(np.float32)
    return np.concatenate([np.cos(ang), np.sin(ang)], axis=-1).astype(np.float32)


_PROGS = {}


def get_prog(name, builder):
    return builder()


def run_l1(h_lat, h_ctx, mod, layer, w):
    xs = tok_split(h_lat, h_ctx)
    rope = rope_table()
    one = np.concatenate([np.ones((128, 32), np.float32), np.zeros((128, 32), np.float32)], axis=1)
    ident = np.eye(128, dtype=np.float32)
    in_maps = []
    for i in range(NCORES):
        b, j = divmod(i, 4)
        x = xs[i]
        vecs = np.stack([fm(w["norm1_g"][layer]), fm(mod[b, layer, 1]), fm(mod[b, layer, 0]),
                         fm(mod[2, layer, 1]), fm(mod[2, layer, 0])], axis=-1)
        cs = np.concatenate([rope[j * 2048:(j + 1) * 2048], one], axis=0)
        in_maps.append({
            "x_tm": x,
            "x_fm": np.ascontiguousarray(x.T.reshape(8, 128, NT1 * 128).transpose(1, 0, 2)),
            "w_in": w["w_in"][layer],
            "vecs": np.ascontiguousarray(vecs),
            "qkg": np.concatenate([w["q_norm_g"][layer], w["k_norm_g"][layer]])[None, :],
            "cs": cs,
            "gw": w["gla_gate_w"][layer],
            "gb": w["gla_gate_b"][layer].reshape(1, 256),
            "ident": ident,
        })
    res = run(build_l1(), in_maps)
    return tok_merge([r["out"] for r in res])


NQT = 33
NKT = 66


def build_l2():
    p = Prog()
    qT_d = p.dram_in("qT", [64, NQT, 512])
    kT_d = p.dram_in("kT", [64, NKT * 128])
    vA_d = p.dram_in("vA", [128, NKT, 65])
    ident_d = p.dram_in("ident", [128, 128])
    out = p.dram_out("out", [NQT * 128, 256])

    qT = p.sb("qT", [64, NQT, 512], BF16)
    kT = p.sb("kT", [64, NKT * 128], BF16)
    vA = p.sb("vA", [128, NKT, 65], BF16)
    ident = p.sb("ident", [128, 128])
    p.dma("sp", ident[:], ident_d[:], [ident_d], [ident])
    p.dma("pool", kT[:, 0:4096], kT_d[:, 0:4096], [kT_d], [kT], max_dma_last_dim=4096)
    p.dma("pool", kT[:, 4096:], kT_d[:, 4096:], [kT_d], [kT], max_dma_last_dim=4096)
    p.dma("pool", vA[:], vA_d[:], [vA_d], [vA], max_dma_last_dim=4096)
    for a in range(0, NQT, 3):
        p.dma("pool", qT[:, a:a + 3, :], qT_d[:, a:a + 3, :], [qT_d], [qT], max_dma_last_dim=4096)

    psS = [p.ps("psS%d" % i) for i in range(3)]
    psO = [p.ps("psO%d" % i) for i in range(2)]
    psT = [p.ps("psT%d" % i) for i in range(2)]
    pT = [p.sb("pT%d" % i, [128, 512], BF16) for i in range(3)]
    oT = [p.sb("oT%d" % i, [65, 512]) for i in range(2)]
    ot = [p.sb("ot%d" % i, [128, 256]) for i in range(2)]
    rec = p.sb("rec", [128, 8])
    it = 0
    for qt in range(NQT):
        kts = list(range(NKT)) if qt < 32 else [64, 65]
        po = psO[qt % 2]
        for j, kt in enumerate(kts):
            ps = psS[it % 3]
            pt = pT[it % 3]
            it += 1
            p.mm(ps[:], kT[:, kt * 128:(kt + 1) * 128], qT[:, qt, :], True, True, [kT, qT], [ps])
            p.act(pt[:], ps[:], AF.Exp, [ps], [pt])
            p.mm(po[0:65, :], vA[:, kt, :], pt[:], j == 0, j == len(kts) - 1, [vA, pt], [po])
        o_sb = oT[qt % 2]
        p.op("dve", lambda e: e.tensor_copy(out=o_sb[:], in_=po[0:65, :]), [po], [o_sb])
        o_t = ot[qt % 2]
        for h in range(4):
            pst = psT[h % 2]
            p.tr(pst[:, 0:65], o_sb[:, h * 128:(h + 1) * 128], ident[0:65, 0:65], [o_sb, ident], [pst])
            rc = rec[:, (qt % 2) * 4 + h:(qt % 2) * 4 + h + 1]
            p.op("dve", lambda e: e.reciprocal(out=rc, in_=pst[:, 64:65]), [pst], [rec])
            p.ts("dve", o_t[:, h * 64:(h + 1) * 64], pst[:, 0:64], rc, ALU.mult, [pst, rec], [o_t])
        p.dma("sp", out[qt * 128:(qt + 1) * 128, :], o_t[:], [o_t], [out])
    return p


def run_l2(lat, ctx):
    ident = np.eye(128, dtype=np.float32)
    in_maps = []
    for i in range(NCORES):
        b, g, qh = i // 4, (i // 2) % 2, i % 2
        ql = lat[b, qh * 4096:(qh + 1) * 4096, 0:512].reshape(32, 128, 8, 64)[:, :, 4 * g:4 * g + 4, :]
        qc = ctx[b, qh * 128:(qh + 1) * 128, 0:512].reshape(1, 128, 8, 64)[:, :, 4 * g:4 * g + 4, :]
        q = np.concatenate([ql, qc], axis=0)
        qT = np.ascontiguousarray(q.transpose(3, 0, 2, 1)).reshape(64, NQT, 512)
        k_all = np.concatenate([lat[b, :, 512:640], ctx[b, :, 512:640]], axis=0)[:, g * 64:(g + 1) * 64]
        kT = np.ascontiguousarray(k_all.T)
        v_all = np.concatenate([lat[b, :, 640:768], ctx[b, :, 640:768]], axis=0)[:, g * 64:(g + 1) * 64]
        vA = np.ones((128, NKT, 65), np.float32)
        vA[:, :, 0:64] = v_all.reshape(NKT, 128, 64).transpose(1, 0, 2)
        in_maps.append({"qT": qT, "kT": kT, "vA": vA, "ident": ident})
    res = run(build_l2(), in_maps)
    a_lat = np.zeros((2, SEQ, 512), np.float32)
    a_ctx = np.zeros((2, CTX, 512), np.float32)
    for i in range(NCORES):
        b, g, qh = i // 4, (i // 2) % 2, i % 2
        o = res[i]["out"]
        a_lat[b, qh * 4096:(qh + 1) * 4096, g * 256:(g + 1) * 256] = o[:4096]
        a_ctx[b, qh * 128:(qh + 1) * 128, g * 256:(g + 1) * 256] = o[4096:]
    return a_lat, a_ctx


def build_l5a():
    p = Prog()
    ntok = NT1 * 128
    y_d = p.dram_in("y", [ntok, D])
    sr_d = p.dram_in("sr", [ntok, 256])
    yb_d = p.dram_in("yb", [ntok, 256])
    h_d = p.dram_in("h", [ntok, D])
    w_d = p.dram_in("w_out", [D, D])
    og_d = p.dram_in("og", [128, 8])
    g1_d = p.dram_in("g1", [2, D])
    ident_d = p.dram_in("ident", [128, 128])
    out = p.dram_out("out", [ntok, D])

    W = p.sb("W", [128, 8, D], BF16)
    og = p.sb("og", [128, 8])
    g1 = [p.sb("g1_%d" % k, [128, D]) for k in range(2)]
    ident = p.sb("ident", [128, 128])
    epst = p.sb("eps", [128, 1])
    p.op("dve", lambda e: e.memset(epst[:], EPS), [], [epst])
    p.dma("sp", og[:], og_d[:], [og_d], [og])
    p.dma("sp", ident[:], ident_d[:], [ident_d], [ident])
    for k in range(2):
        p.dma("sp", g1[k][:], g1_d[k:k + 1, :].partition_broadcast(128)[:, 0, :], [g1_d], [g1[k]])
    for c in range(8):
        p.dma("pool", W[:, c, :], w_d[c * 128:(c + 1) * 128, :], [w_d], [W], max_dma_last_dim=4096)
    NB = 2
    yt = [p.sb("yt%d" % i, [128, D]) for i in range(NB)]
    ht = [p.sb("ht%d" % i, [128, D]) for i in range(NB)]
    srt = [p.sb("srt%d" % i, [128, 256]) for i in range(NB)]
    ybt = [p.sb("ybt%d" % i, [128, 256]) for i in range(NB)]
    ho = [p.sb("ho%d" % i, [128, D]) for i in range(NB)]
    yT = [p.sb("yT%d" % i, [128, 8, 128], BF16) for i in range(NB)]
    sq = p.sb("sq", [128, D])
    st = [p.sb("st%d" % i, [128, 16]) for i in range(NB)]
    pst = [p.ps("pst%d" % i) for i in range(2)]
    pso = [p.ps("pso%d" % i) for i in range(4)]
    for i in range(NT1):
        kind = 0 if i < 16 else 1
        bi = i % NB
        rows = slice(i * 128, (i + 1) * 128)
        y, h, sr, s = yt[bi], ht[bi], srt[bi], st[bi]
        p.dma("sp", y[:], y_d[rows, :], [y_d], [y])
        p.dma("sp", sr[:], sr_d[rows, :], [sr_d], [sr])
        p.dma("sp", h[:], h_d[rows, :], [h_d], [h])
        p.dma("sp", ybt[bi][:], yb_d[rows, :], [yb_d], [ybt[bi]])
        p.tt("pool", y[:, 768:1024], y[:, 768:1024], ybt[bi][:], ALU.add, [y, ybt[bi]], [y])
        p.tt("pool", sq[:], y[:], y[:], ALU.mult, [y], [sq])
        p.op("dve", lambda e: e.tensor_reduce(out=s[:], in_=sq[:].rearrange("p (h d) -> p h d", d=64), axis=AX.X, op=ALU.add), [sq], [s])
        p.rsqrt(s, s[:], s, s[:], 1.0 / 64, epst)
        y3 = y[:].rearrange("p (h d) -> p h d", d=64)
        p.tt("dve", y3, y3, s[:].unsqueeze(2).broadcast_to([128, 16, 64]), ALU.mult, [y, s], [y])
        p.tt("pool", y[:, 768:1024], y[:, 768:1024], sr[:], ALU.mult, [y, sr], [y])
        for c in range(8):
            ps = pst[c // 4]
            blk = ps[:, (c % 4) * 128:(c % 4 + 1) * 128]
            p.tr(blk, y[:, c * 128:(c + 1) * 128], ident[:], [y, ident], [ps])
            p.act(yT[bi][:, c, :], blk, AF.Identity, [ps, og], [yT[bi]], scale=og[:, c:c + 1])
        for hf in range(2):
            ps = pso[(2 * i + hf) % 4]
            cols = slice(hf * 512, (hf + 1) * 512)
            for c in range(8):
                p.mm(ps[:], yT[bi][:, c, :], W[:, c, cols], c == 0, c == 7, [yT[bi], W], [ps])
            p.tt("dve", ho[bi][:, cols], ps[:], g1[kind][:, cols], ALU.mult, [ps, g1[kind]], [ho[bi]])
            p.tt("pool", ho[bi][:, cols], ho[bi][:, cols], h[:, cols], ALU.add, [ho[bi], h], [ho[bi]])
        p.dma("sp", out[rows, :], ho[bi][:], [ho[bi]], [out])
    return p


def run_l5a(y_lat, y_ctx, yb_lat, yb_ctx, sr_lat, sr_ctx, h_lat, h_ctx, mod, layer, w):
    ys = tok_split(y_lat, y_ctx)
    ybs = tok_split(yb_lat, yb_ctx)
    srs = tok_split(sr_lat, sr_ctx)
    hs = tok_split(h_lat, h_ctx)
    ident = np.eye(128, dtype=np.float32)
    in_maps = []
    for i in range(NCORES):
        b = i // 4
        in_maps.append({"y": ys[i], "yb": ybs[i], "sr": srs[i], "h": hs[i], "w_out": w["w_out"][layer],
                        "og": fm(w["out_norm_g"][layer]),
                        "g1": np.ascontiguousarray(np.stack([mod[b, layer, 2], mod[2, layer, 2]])),
                        "ident": ident})
    res = run(build_l5a(), in_maps)
    return tok_merge([r["out"] for r in res])


NF = FFN // 128


def build_l5b(final):
    p = Prog()
    ntok = NT1 * 128
    h_d = p.dram_in("h", [ntok, D])
    w1_d = p.dram_in("w1", [D, FFN])
    w3_d = p.dram_in("w3", [D, FFN])
    w2_d = p.dram_in("w2", [FFN, D])
    vec_d = p.dram_in("vecs", [128, 8, 5])
    g2_d = p.dram_in("g2", [3, D])
    ident_d = p.dram_in("ident", [128, 128])
    out = p.dram_out("out", [ntok, D])

    W1 = p.sb("W1", [128, 8, FFN], BF16)
    W3 = p.sb("W3", [128, 8, FFN], BF16)
    W2 = p.sb("W2", [128, NF, D], BF16)
    V = p.sb("V", [128, 8, 5])
    A = p.sb("A", [128, 8, 2])
    g2 = [p.sb("g2_%d" % k, [128, D]) for k in range(3 if final else 2)]
    ident = p.sb("ident", [128, 128])
    epst = p.sb("eps", [128, 1])
    p.op("dve", lambda e: e.memset(epst[:], EPS), [], [epst])
    p.dma("sp", V[:], vec_d[:], [vec_d], [V])
    p.dma("sp", ident[:], ident_d[:], [ident_d], [ident])
    for k in range(len(g2)):
        p.dma("sp", g2[k][:], g2_d[k:k + 1, :].partition_broadcast(128)[:, 0, :], [g2_d], [g2[k]])
    for c in range(8):
        p.dma("pool", W1[:, c, :], w1_d[c * 128:(c + 1) * 128, :], [w1_d], [W1], max_dma_last_dim=4096)
        p.dma("pool", W3[:, c, :], w3_d[c * 128:(c + 1) * 128, :], [w3_d], [W3], max_dma_last_dim=4096)
    for f in range(NF):
        p.dma("pool", W2[:, f, :], w2_d[f * 128:(f + 1) * 128, :], [w2_d], [W2], max_dma_last_dim=4096)
    for k in range(2):
        p.stt(A[:, :, k], V[:, :, 1 + 2 * k], 1.0, V[:, :, 0], ALU.add, ALU.mult, [V], [A])

    G = 2
    hts = [p.sb("ht%d" % i, [128, D]) for i in range(2 * G)]
    xs = p.sb("xs", [128, D])
    junk = p.sb("junk", [128, D])
    u2T = [p.sb("u2T%d" % i, [128, 8, G * 128], BF16) for i in range(2)]
    hidT = p.sb("hidT", [128, NF, G * 128], BF16)
    sil = [p.sb("sil%d" % i, [128, G * 128]) for i in range(2)]
    ho = [p.sb("ho%d" % i, [128, D]) for i in range(2)]
    st = p.sb("st", [128, 8])
    pst = [p.ps("pst%d" % i) for i in range(2)]
    psu = [p.ps("psu%d" % i) for i in range(4)]
    psd = [p.ps("psd%d" % i) for i in range(2)]
    groups = [list(range(a, min(a + G, 16))) for a in range(0, 16, G)] + [[16]]
    nd = 0
    for gi, tiles in enumerate(groups):
        kind = 0 if tiles[0] < 16 else 1
        N = len(tiles) * 128
        u = u2T[gi % 2]
        for j, i in enumerate(tiles):
            h = hts[(gi % 2) * G + j]
            rows = slice(i * 128, (i + 1) * 128)
            p.dma("sp", h[:], h_d[rows, :], [h_d], [h])
            sc = st[:, 2 * j:2 * j + 1]
            sr_ = st[:, 2 * j + 1:2 * j + 2]
            p.act(junk[:], h[:], AF.Square, [h], [junk, st], accum_out=sc)
            p.rsqrt(st, sr_, st, sc, 1.0 / D, epst)
            p.act(xs[:], h[:], AF.Identity, [h, st], [xs], scale=sr_)
            for c in range(8):
                ps = pst[c // 4]
                blk = ps[:, (c % 4) * 128:(c % 4 + 1) * 128]
                p.tr(blk, xs[:, c * 128:(c + 1) * 128], ident[:], [xs, ident], [ps])
                p.act(u[:, c, j * 128:(j + 1) * 128], blk, AF.Identity, [ps, A, V], [u],
                      scale=A[:, c, kind:kind + 1], bias=V[:, c, 2 + 2 * kind:3 + 2 * kind])
        for f in range(NF):
            ps1 = psu[(2 * f) % 4]
            ps3 = psu[(2 * f + 1) % 4]
            fc = slice(f * 128, (f + 1) * 128)
            for c in range(8):
                p.mm(ps1[:, 0:N], W1[:, c, fc], u[:, c, 0:N], c == 0, c == 7, [W1, u], [ps1])
            for c in range(8):
                p.mm(ps3[:, 0:N], W3[:, c, fc], u[:, c, 0:N], c == 0, c == 7, [W3, u], [ps3])
            s_ = sil[f % 2]
            p.act(s_[:, 0:N], ps1[:, 0:N], AF.Silu, [ps1], [s_])
            p.tt("dve", hidT[:, f, 0:N], s_[:, 0:N], ps3[:, 0:N], ALU.mult, [s_, ps3], [hidT])
        for j, i in enumerate(tiles):
            h = hts[(gi % 2) * G + j]
            rows = slice(i * 128, (i + 1) * 128)
            o = ho[nd % 2]
            nd += 1
            for hf in range(2):
                ps = psd[hf]
                cols = slice(hf * 512, (hf + 1) * 512)
                for f in range(NF):
                    p.mm(ps[:], hidT[:, f, j * 128:(j + 1) * 128], W2[:, f, cols], f == 0, f == NF - 1, [hidT, W2], [ps])
                p.tt("dve", o[:, cols], ps[:], g2[kind][:, cols], ALU.mult, [ps, g2[kind]], [o])
                p.tt("pool", o[:, cols], o[:, cols], h[:, cols], ALU.add, [o, h], [o])
            if final:
                sc = st[:, 4:5]
                sr_ = st[:, 5:6]
                p.act(junk[:], o[:], AF.Square, [o], [junk, st], accum_out=sc)
                p.rsqrt(st, sr_, st, sc, 1.0 / D, epst)
                p.stt(o[:], o[:], sr_, g2[2][:], ALU.mult, ALU.mult, [o, st, g2[2]], [o])
            p.dma("sp", out[rows, :], o[:], [o], [out])
    return p


def run_l5b(h_lat, h_ctx, mod, layer, w, final):
    hs = tok_split(h_lat, h_ctx)
    ident = np.eye(128, dtype=np.float32)
    in_maps = []
    for i in range(NCORES):
        b = i // 4
        vecs = np.stack([fm(w["norm2_g"][layer]), fm(mod[b, layer, 4]), fm(mod[b, layer, 3]),
                         fm(mod[2, layer, 4]), fm(mod[2, layer, 3])], axis=-1)
        in_maps.append({"h": hs[i], "w1": w["ffn_w1"][layer], "w3": w["ffn_w3"][layer], "w2": w["ffn_w2"][layer],
                        "vecs": np.ascontiguousarray(vecs),
                        "g2": np.ascontiguousarray(np.stack([mod[b, layer, 5], mod[2, layer, 5], w["final_norm_g"]])),
                        "ident": ident})
    res = run(build_l5b(final), in_maps)
    return tok_merge([r["out"] for r in res])


TG = SEQ + CTX
NCH = TG // 64
NPR = TG // 128
SEGL = 1408
NSEG = TG // SEGL


def build_l4(stage=4):
    p = Prog()
    qkg_d = p.dram_in("qkg", [2, 3, 32, TG])
    v_d = p.dram_in("v", [2, 128, NPR, 64])
    mask_d = p.dram_in("mask", [128, 128])
    ident_d = p.dram_in("ident", [128, 128])
    out = p.dram_out("out", [2, 64, TG])

    ident = p.sb("ident", [128, 128])
    mask = p.sb("mask", [128, 128])
    ones = p.sb("ones", [32, 1])
    p.dma("sp", ident[:], ident_d[:], [ident_d], [ident])
    p.dma("sp", mask[:], mask_d[:], [mask_d], [mask])
    p.op("dve", lambda e: e.memset(ones[:], 1.0), [], [ones])

    qd = p.sb("qd", [32, TG], BF16)
    kd = p.sb("kd", [32, TG], BF16)
    ktT = [p.sb("ktT%d" % i, [128, NPR, 32], BF16) for i in range(2)]
    pm_d = p.dram_in("pm", [128, 2])
    pm = p.sb("pm", [128, 2])
    p.dma("sp", pm[:], pm_d[:], [pm_d], [pm])
    vt = p.sb("vt", [128, NPR, 64], BF16)
    STm = p.sb("STm", [128, NPR, 128], BF16)
    Sbf = p.sb("Sbf", [32, NCH + 1, 64], BF16)
    dc = p.sb("dc", [32, NCH])
    oT = p.sb("oT", [64, TG])
    Scur = [p.sb("Scur%d" % i, [32, 64]) for i in range(2)]
    gseg = p.sb("gseg", [32, SEGL])
    qseg = p.sb("qseg", [32, SEGL])
    kseg = p.sb("kseg", [32, SEGL])
    Gs = [p.sb("Gs%d" % i, [32, SEGL]) for i in range(2)]
    At = p.sb("At", [32, SEGL])
    Bt = p.sb("Bt", [32, SEGL])
    Sc = p.sb("Sc", [32, 22])
    tmpc = p.sb("tmpc", [32, 22])
    psT = [p.ps("psT%d" % i) for i in range(2)]
    psS = [p.ps("psS%d" % i) for i in range(2)]
    psK = [p.ps("psK%d" % i) for i in range(2)]
    psO = [p.ps("psO%d" % i) for i in range(2)]

    for d in range(2):
        p.dma("pool", vt[:], v_d[d], [v_d], [vt], max_dma_last_dim=4096)
        for s in range(NSEG):
            seg = slice(s * SEGL, (s + 1) * SEGL)
            G = Gs[s % 2]
            Gp = Gs[(s + 1) % 2]
            p.dma("sp", qseg[:], qkg_d[d, 0, :, seg], [qkg_d], [qseg])
            p.dma("sp", kseg[:], qkg_d[d, 1, :, seg], [qkg_d], [kseg])
            p.dma("sp", gseg[:], qkg_d[d, 2, :, seg], [qkg_d], [gseg])
            init = 0.0 if s == 0 else Gp[:, SEGL - 1:SEGL]
            rd = [gseg, ones] + ([] if s == 0 else [Gp])
            p.op("dve", lambda e: e.tensor_tensor_scan(out=G[:], data0=ones[:, 0:1].broadcast_to([32, SEGL]), data1=gseg[:],
                                                       initial=init, op0=ALU.mult, op1=ALU.add), rd, [G])
            G3 = G[:].rearrange("p (c j) -> p c j", j=64)
            Ec = G3[:, :, 63]
            if s == 0:
                p.op("dve", lambda e: e.memset(Sc[:, 0:1], 0.0), [], [Sc])
            else:
                p.op("dve", lambda e: e.tensor_copy(out=Sc[:, 0:1], in_=Gp[:, SEGL - 1:SEGL]), [Gp], [Sc])
            p.op("dve", lambda e: e.tensor_copy(out=Sc[:, 1:22], in_=G3[:, 0:21, 63]), [G], [Sc])
            A3 = At[:].rearrange("p (c j) -> p c j", j=64)
            p.tt("dve", A3, G3, Sc[:].unsqueeze(2).broadcast_to([32, 22, 64]), ALU.subtract, [G, Sc], [At])
            p.act(Bt[:], At[:], AF.Exp, [At], [Bt])
            p.tt("dve", qd[:, seg], qseg[:], Bt[:], ALU.mult, [qseg, Bt], [qd])
            p.act(Bt[:], At[:], AF.Exp, [At], [Bt], scale=-1.0)
            p.tt("dve", kd[:, seg], kseg[:], Bt[:], ALU.mult, [kseg, Bt], [kd])
            p.tt("dve", tmpc[:], Ec, Sc[:], ALU.subtract, [G, Sc], [tmpc])
            p.act(dc[:, s * 22:(s + 1) * 22], tmpc[:], AF.Exp, [tmpc], [dc])
            p.tt("dve", A3, Ec.unsqueeze(2).broadcast_to([32, 22, 64]), G3, ALU.subtract, [G], [At])
            p.act(At[:], At[:], AF.Exp, [At], [At])
            p.tt("dve", Bt[:], kseg[:], At[:], ALU.mult, [kseg, At], [Bt])
            ps = psT[s % 2]
            for j in range(11):
                p.tr(ps[:, j * 32:(j + 1) * 32], Bt[:, j * 128:(j + 1) * 128], ident[0:32, 0:32], [Bt, ident], [ps])
            for hf in range(2):
                p.act(ktT[hf][:, s * 11:(s + 1) * 11, :], ps[:, 0:352].rearrange("p (a b) -> p a b", b=32), AF.Identity,
                      [ps, pm], [ktT[hf]], scale=pm[:, hf:hf + 1])
        if stage < 4:
            p.op("dve", lambda e: e.memset(oT[:], 0.0), [], [oT])
        for g0 in (range(0, NPR, 4) if stage >= 2 else []):
            n = min(4, NPR - g0)
            ps = psS[(g0 // 4) % 2]
            for j in range(n):
                pr = g0 + j
                tok = slice(pr * 128, (pr + 1) * 128)
                p.mm(ps[:, j * 128:(j + 1) * 128], kd[:, tok], qd[:, tok], True, True, [kd, qd], [ps])
            p.tt("dve", STm[:, g0:g0 + n, :], ps[:, 0:n * 128].rearrange("p (a b) -> p a b", b=128),
                 mask[:].unsqueeze(1).broadcast_to([128, n, 128]), ALU.mult, [ps, mask], [STm])
        p.op("dve", lambda e: e.memset(Scur[0][:], 0.0), [], [Scur[0]])
        p.op("dve", lambda e: e.memset(Sbf[:, 0, :], 0.0), [], [Sbf])
        for c0 in (range(0, NCH, 8) if stage >= 3 else []):
            n = min(8, NCH - c0)
            ps = psK[(c0 // 8) % 2]
            for j in range(n):
                c = c0 + j
                pr, hf = divmod(c, 2)
                p.mm(ps[0:32, j * 64:(j + 1) * 64], ktT[hf][:, pr, :], vt[:, pr, :], True, True, [ktT[hf], vt], [ps])
            for j in range(n):
                c = c0 + j
                sa, sb_ = Scur[c % 2], Scur[(c + 1) % 2]
                p.stt(sb_[:], sa[:], dc[:, c:c + 1], ps[0:32, j * 64:(j + 1) * 64], ALU.mult, ALU.add, [sa, dc, ps], [sb_])
                p.act(Sbf[:, c + 1, :], sb_[:], AF.Identity, [sb_], [Sbf])
        for g0 in (range(0, NPR, 4) if stage >= 4 else []):
            n = min(4, NPR - g0)
            ps = psO[(g0 // 4) % 2]
            for j in range(n):
                pr = g0 + j
                cs_ = slice(j * 128, (j + 1) * 128)
                p.mm(ps[0:64, cs_], vt[:, pr, :], STm[:, pr, :], True, False, [vt, STm], [ps])
                for hf in range(2):
                    c = 2 * pr + hf
                    tok = slice(c * 64, (c + 1) * 64)
                    p.mm(ps[0:64, j * 128 + hf * 64:j * 128 + (hf + 1) * 64], Sbf[:, c, :], qd[:, tok], False, hf == 1, [Sbf, qd], [ps])
            p.act(oT[:, g0 * 128:(g0 + n) * 128], ps[0:64, 0:n * 128], AF.Identity, [ps], [oT])
        p.dma("sp", out[d], oT[:], [oT], [out])
    return p


L4_PM = np.stack([(np.arange(128) < 64), (np.arange(128) >= 64)], axis=1).astype(np.float32)


def run_l4(lat, ctx):
    ident = np.eye(128, dtype=np.float32)
    j = np.arange(128)
    mask = ((j[:, None] // 64 == j[None, :] // 64) & (j[None, :] >= j[:, None])).astype(np.float32)
    in_maps = []
    for i in range(NCORES):
        b, hd = divmod(i, 4)
        qkg = np.zeros((2, 3, 32, TG), np.float32)
        v = np.zeros((2, 128, NPR, 64), np.float32)
        for d in range(2):
            def seq(c0, w):
                a, l = ctx[b, :, c0:c0 + w], lat[b, :, c0:c0 + w]
                if d == 1:
                    a, l = a[::-1], l[::-1]
                return np.concatenate([a, l], axis=0)
            qkg[d, 0] = seq(1536 + hd * 32, 32).T
            qkg[d, 1] = seq(1664 + hd * 32, 32).T
            qkg[d, 2] = seq((2304 if d == 0 else 2432) + hd * 32, 32).T
            v[d] = seq(1792 + hd * 64, 64).reshape(NPR, 128, 64).transpose(1, 0, 2)
        in_maps.append({"qkg": qkg, "v": v, "mask": mask, "ident": ident, "pm": L4_PM})
    res = run(build_l4(), in_maps)
    f_lat = np.zeros((2, SEQ, 256), np.float32)
    b_lat = np.zeros((2, SEQ, 256), np.float32)
    f_ctx = np.zeros((2, CTX, 256), np.float32)
    b_ctx = np.zeros((2, CTX, 256), np.float32)
    for i in range(NCORES):
        b, hd = divmod(i, 4)
        o = res[i]["out"]
        cols = slice(hd * 64, (hd + 1) * 64)
        f_ctx[b, :, cols] = o[0].T[:CTX]
        f_lat[b, :, cols] = o[0].T[CTX:]
        b_ctx[b, :, cols] = o[1].T[:CTX][::-1]
        b_lat[b, :, cols] = o[1].T[CTX:][::-1]
    return f_lat, b_lat, f_ctx, b_ctx


NFFT = 16384
PI = math.pi


def hy_consts():
    n1 = np.arange(64)[:, None]
    k = np.arange(128)[None, :]
    th = 2 * np.pi * n1 * k / 128
    F1 = np.concatenate([np.cos(th), -np.sin(th)], axis=1)
    n2 = np.arange(128)[:, None]
    tw = 2 * np.pi * n2 * k / NFFT
    TWf = np.stack([np.cos(tw), -np.sin(tw)], axis=1)
    th2 = 2 * np.pi * n2 * k / 128
    F2 = np.stack([np.cos(th2), -np.sin(th2), np.sin(th2)], axis=1)
    G2 = np.stack([np.concatenate([np.cos(th2), np.sin(th2)], axis=1),
                   np.concatenate([-np.sin(th2), np.cos(th2)], axis=1)], axis=1)
    TWi = np.stack([np.cos(tw.T), np.sin(tw.T)], axis=1)
    k1 = np.arange(128)[:, None]
    n1r = np.arange(64)[None, :]
    th1 = 2 * np.pi * k1 * n1r / 128
    G1 = np.stack([np.cos(th1), -np.sin(th1)], axis=1) / NFFT
    f32 = lambda a: np.ascontiguousarray(a, dtype=np.float32)
    return {"F1": f32(F1), "TWf": f32(TWf), "F2": f32(F2), "G2": f32(G2), "TWi": f32(TWi), "G1": f32(G1)}


def hy_features(n):
    t = np.linspace(0.0, 1.0, n, dtype=np.float32)[:, None]
    omega = (np.float32(2.0 * math.pi / n) * np.arange(n, dtype=np.float32)).astype(np.float32)
    bands = np.linspace(1e-4, 15, 16, dtype=np.float32)
    phase = (omega[:, None] * bands[None, :]).astype(np.float32)
    z = np.concatenate([t, np.cos(phase), -np.sin(phase)], axis=-1).astype(np.float32)
    mn, mx = math.log(1e-2) / 1.5, math.log(1e-2) / 0.3
    deltas = np.abs(np.linspace(mn, mx, 256, dtype=np.float32))
    window = np.exp(-t * deltas[None, :]).astype(np.float32)
    return np.ascontiguousarray(z.T), window


def build_l3(debug=False):
    p = Prog()
    x_d = p.dram_in("x", [3, 64, SEQ])
    xc_d = p.dram_in("xc", [3, 64, CTX])
    cw_d = p.dram_in("cw", [64, 3, 4])
    hb_d = p.dram_in("hb", [2, 64])
    hbc_d = p.dram_in("hbc", [64, 2])
    w1_d = p.dram_in("fw1", [33, 64])
    w2_d = p.dram_in("fw2", [64, 64])
    w3_d = p.dram_in("fw3", [64, 4, 64])
    fb_d = p.dram_in("fb", [64, 3])
    zl_d = p.dram_in("zl", [33, SEQ])
    zc_d = p.dram_in("zc", [33, CTX])
    wl_d = p.dram_in("wl", [64, 64, 128])
    wc_d = p.dram_in("wc", [64, CTX])
    cd = {k: p.dram_in(k, list(v.shape)) for k, v in hy_consts().items()}
    scr = Tile(p.nc.dram_tensor("scr", [3, 64, SEQ], F32, kind="Internal").ap())
    yl_d = p.dram_out("yl", [64, 64, 128])
    yc_d = p.dram_out("yc", [64, CTX])

    BG = [p.sb("BG%d" % i, [128, SEQ]) for i in range(3)]
    Hs = p.sb("Hs", [128, 2, 64, 128], BF16)
    F1 = p.sb("F1", [64, 256], BF16)
    TWf = p.sb("TWf", [128, 2, 128])
    F2 = p.sb("F2", [128, 3, 128], BF16)
    G2 = p.sb("G2", [128, 2, 256], BF16)
    TWi = p.sb("TWi", [128, 2, 128])
    G1 = p.sb("G1", [128, 2, 64], BF16)
    for nm, t in (("F1", F1), ("F2", F2), ("G2", G2), ("G1", G1)):
        p.dma("pool", t[:], cd[nm][:], [cd[nm]], [t])
    p.dma("sp", TWf[:], cd["TWf"][:], [cd["TWf"]], [TWf])
    p.dma("sp", TWi[:], cd["TWi"][:], [cd["TWi"]], [TWi])
    cw = p.sb("cw", [64, 3, 4])
    hbr = p.sb("hbr", [64, 2, 64])
    hbc = p.sb("hbc", [64, 2])
    w1 = p.sb("fw1", [33, 64])
    w2 = p.sb("fw2", [64, 64])
    w3 = p.sb("fw3", [64, 4, 64], BF16)
    w3f = p.sb("fw3f", [64, 4, 64])
    fb = p.sb("fb", [64, 3])
    frb = p.sb("frb", [64, 2])
    wc = p.sb("wc", [64, CTX])
    p.dma("sp", cw[:], cw_d[:], [cw_d], [cw])
    p.dma("sp", hbc[:], hbc_d[:], [hbc_d], [hbc])
    for o in range(2):
        p.dma("sp", hbr[:, o, :], hb_d[o:o + 1, :].partition_broadcast(64)[:, 0, :], [hb_d], [hbr])
    p.dma("sp", w1[:], w1_d[:], [w1_d], [w1])
    p.dma("sp", w2[:], w2_d[:], [w2_d], [w2])
    p.dma("sp", w3f[:], w3_d[:], [w3_d], [w3f])
    p.dma("pool", w3[:], w3_d[:], [w3_d], [w3])
    p.dma("sp", fb[:], fb_d[:], [fb_d], [fb])
    p.dma("sp", wc[:], wc_d[:], [wc_d], [wc])
    for j in range(2):
        p.tt("dve", frb[:, j:j + 1], fb[:, 0:1], fb[:, 1 + j:2 + j], ALU.mult, [fb], [frb])
    PS = [p.ps("P%d" % i) for i in range(8)]
    PA, PX, PB, PY = PS[0:2], PS[2:4], PS[4:6], PS[6:8]

    def short_conv(xt, xap, ut, uap, g, n):
        p.act(uap, xap, AF.Identity, [xt, cw], [ut], scale=cw[:, g, 1:2], bias=cw[:, g, 3:4])
        p.stt(uap[:, 1:n], xap[:, 0:n - 1], cw[:, g, 0:1], uap[:, 1:n], ALU.mult, ALU.add, [xt, cw, ut], [ut])
        p.stt(uap[:, 0:n - 1], xap[:, 1:n], cw[:, g, 2:3], uap[:, 0:n - 1], ALU.mult, ALU.add, [xt, cw, ut], [ut])

    for g in range(3):
        p.dma("sp", BG[0][0:64, :], x_d[g], [x_d], [BG[0]])
        short_conv(BG[0], BG[0][0:64, :], BG[1], BG[1][0:64, :], g, SEQ)
        p.dma("sp", scr[g], BG[1][0:64, :], [BG[1]], [scr])
    uc = p.sb("uc", [64, 3, CTX])
    xct = p.sb("xct", [64, 3, CTX])
    p.dma("sp", xct[:], xc_d[:].rearrange("g c t -> c g t"), [xc_d], [xct])
    for g in range(3):
        short_conv(xct, xct[:, g, :], uc, uc[:, g, :], g, CTX)

    def wrap_sin(dst_t, dst_ap, ps_ap, ps_t, j, n, arg_t):
        a = arg_t[0:64, 0:n]
        p.ts("dve", a, ps_ap, fb[:, 0:1], ALU.mult, [ps_t, fb, frb], [arg_t], s2=frb[:, j:j + 1], op1=ALU.add)
        w_ = wrp[0:64, 0:n]
        for bound, period in ((3 * PI, 4 * PI), (PI, 2 * PI)):
            p.ts("dve", w_, a, bound, ALU.is_gt, [arg_t], [wrp], s2=-period, op1=ALU.mult)
            p.tt("dve", a, a, w_, ALU.add, [arg_t, wrp], [arg_t])
            p.ts("dve", w_, a, -bound, ALU.is_lt, [arg_t], [wrp], s2=period, op1=ALU.mult)
            p.tt("dve", a, a, w_, ALU.add, [arg_t, wrp], [arg_t])
        p.act(dst_ap, a, AF.Sin, [arg_t], [dst_t])

    argt = p.sb("argt", [64, 512])
    wrp = p.sb("wrp", [64, 512])
    h1c = p.sb("h1c", [64, 512])
    zch = [p.sb("zch%d" % i, [33, 512]) for i in range(2)]

    def mlp(z_dram, n, dst_t, dst_fn):
        for q in range(0, n, 512):
            m = min(512, n - q)
            zt = zch[(q // 512) % 2]
            p.dma("sp", zt[:, 0:m], z_dram[:, q:q + m], [z_dram], [zt])
            p.mm(PA[0][0:64, 0:m], w1[:], zt[:, 0:m], True, True, [w1, zt], [PA[0]])
            wrap_sin(h1c, h1c[:, 0:m], PA[0][0:64, 0:m], PA[0], 0, m, argt)
            p.mm(PA[1][0:64, 0:m], w2[:], h1c[:, 0:m], True, True, [w2, h1c], [PA[1]])
            wrap_sin(dst_t, dst_fn(q, m), PA[1][0:64, 0:m], PA[1], 1, m, argt)

    h2 = BG[1][:].bitcast(BF16)
    mlp(zl_d, SEQ, BG[1], lambda q, m: h2[0:64, q:q + m])
    h2c = p.sb("h2c", [64, CTX])
    mlp(zc_d, CTX, h2c, lambda q, m: h2c[:, q:q + m])

    hfc = p.sb("hfc", [64, 4, CTX])
    for blk in range(4):
        p.mm(PX[0][0:64, 0:CTX], w3f[:, blk, :], h2c[:], True, True, [w3f, h2c], [PX[0]])
        p.tt("dve", hfc[:, blk, :], PX[0][0:64, 0:CTX], wc[:], ALU.mult, [PX[0], wc], [hfc])
    zc1 = p.sb("zc1", [64, CTX])
    yct = p.sb("yct", [64, CTX])

    def ctx_conv(zt, zap, o, gate_ap, out_t, out_ap):
        p.ts("dve", yct[:], zap, hbc[:, o:o + 1], ALU.mult, [zt, hbc], [yct])
        for q in range(CTX):
            p.stt(yct[:, q:CTX], zap[:, 0:CTX - q], hfc[:, 2 * o, q:q + 1], yct[:, q:CTX], ALU.mult, ALU.add, [zt, hfc, yct], [yct])
        for q in range(1, CTX):
            p.stt(yct[:, 0:CTX - q], zap[:, q:CTX], hfc[:, 2 * o + 1, q:q + 1], yct[:, 0:CTX - q], ALU.mult, ALU.add, [zt, hfc, yct], [yct])
        p.tt("dve", out_ap, yct[:], gate_ap, ALU.mult, [yct, uc], [out_t])

    ctx_conv(uc, uc[:, 0, :], 0, uc[:, 1, :], zc1, zc1[:])
    ctx_conv(zc1, zc1[:], 1, uc[:, 2, :], zc1, zc1[:])
    p.dma("sp", yc_d[:], zc1[:], [zc1], [yc_d])

    Af = [p.sb("Af%d" % i, [128, 512]) for i in range(2)]
    tm = [p.sb("tm%d" % i, [128, 512]) for i in range(4)]
    Apr = [p.sb("Apr%d" % i, [128, 4, 128], BF16) for i in range(2)]
    Api = [p.sb("Api%d" % i, [128, 4, 128], BF16) for i in range(2)]
    Xf = [p.sb("Xf%d" % i, [128, 512]) for i in range(4)]
    Yr = [p.sb("Yr%d" % i, [128, 4, 128], BF16) for i in range(2)]
    Yi = [p.sb("Yi%d" % i, [128, 4, 128], BF16) for i in range(2)]
    cnt = {"f": 0, "e": 0}

    def eng():
        cnt["e"] += 1
        return "dve" if cnt["e"] % 2 else "pool"

    def cmul(src_t, sr, si, tw_t, twr, twi, dr_t, dr, di_t, di, t_a, t_b):
        e1, e2 = eng(), eng()
        p.tt(e1, t_a[0], sr, twr, ALU.mult, [src_t, tw_t], [t_a[1]])
        p.tt(e2, t_b[0], si, twi, ALU.mult, [src_t, tw_t], [t_b[1]])
        p.tt(e1, dr, t_a[0], t_b[0], ALU.subtract, [t_a[1], t_b[1]], [dr_t])
        p.tt(e1, t_a[0], sr, twi, ALU.mult, [src_t, tw_t], [t_a[1]])
        p.tt(e2, t_b[0], si, twr, ALU.mult, [src_t, tw_t], [t_b[1]])
        p.tt(e2, di, t_a[0], t_b[0], ALU.add, [t_a[1], t_b[1]], [di_t])

    def stage12(src_t, lhs_fn, rhs_t, rhs0, rhs1, lhs2_fn, TW, dr_t, di_t, src2_t=None):
        b = cnt["f"] % 2
        cnt["f"] += 1
        for hf in range(2):
            ps = (PA if rhs1 is None else PB)[hf]
            for k in range(2):
                c = 2 * hf + k
                cols = slice(k * 256, (k + 1) * 256)
                if rhs1 is None:
                    p.mm(ps[:, cols], lhs_fn(c), rhs0, True, True, [src_t, rhs_t], [ps])
                else:
                    p.mm(ps[:, cols], lhs_fn(c), rhs0, True, False, [src_t, rhs_t], [ps])
                    p.mm(ps[:, cols], lhs2_fn(c), rhs1, False, True, [src2_t, rhs_t], [ps])
            af = Af[hf]
            p.act(af[:], ps[:], AF.Identity, [ps], [af])
            a4 = af[:].rearrange("p (k r j) -> p k r j", k=2, r=2)
            twr = TW[:, 0, :].unsqueeze(1).broadcast_to([128, 2, 128])
            twi = TW[:, 1, :].unsqueeze(1).broadcast_to([128, 2, 128])
            ta = tm[2 * hf][:, 0:256].rearrange("p (k j) -> p k j", k=2)
            tb = tm[2 * hf + 1][:, 0:256].rearrange("p (k j) -> p k j", k=2)
            cmul(af, a4[:, :, 0, :], a4[:, :, 1, :], TW, twr, twi,
                 dr_t, dr_t[:, 2 * hf:2 * hf + 2, :], di_t, di_t[:, 2 * hf:2 * hf + 2, :],
                 (ta, tm[2 * hf]), (tb, tm[2 * hf + 1]))

    def fwd_batch(src_t, lhs_fn):
        b = cnt["f"] % 2
        ar, ai = Apr[b], Api[b]
        stage12(src_t, lhs_fn, F1, F1[:], None, None, TWf, ar, ai)
        arf = ar[:].rearrange("p c j -> p (c j)")
        aif = ai[:].rearrange("p c j -> p (c j)")
        p.mm(PX[0][:], F2[:, 0, :], arf, True, False, [F2, ar], [PX[0]])
        p.mm(PX[0][:], F2[:, 2, :], aif, False, True, [F2, ai], [PX[0]])
        p.mm(PX[1][:], F2[:, 0, :], aif, True, False, [F2, ai], [PX[1]])
        p.mm(PX[1][:], F2[:, 1, :], arf, False, True, [F2, ar], [PX[1]])

    def inv_batch(yr, yi, py):
        b = cnt["f"] % 2
        br, bi = Apr[b], Api[b]
        stage12(yr, lambda c: yr[:, c, :], G2, G2[:, 0, :], G2[:, 1, :], lambda c: yi[:, c, :], TWi, br, bi, src2_t=yi)
        p.mm(py[0:64, :], G1[:, 0, :], br[:].rearrange("p c j -> p (c j)"), True, False, [G1, br], [py])
        p.mm(py[0:64, :], G1[:, 1, :], bi[:].rearrange("p c j -> p (c j)"), False, True, [G1, bi], [py])

    win = BG[0]
    hf_t = BG[2]
    hfv = BG[2][:].bitcast(BF16)[0:64, :].rearrange("p (s c j) -> p s c j", s=2, c=64)
    p.dma("sp", win[0:64, :], wl_d[:].rearrange("p c j -> p (c j)"), [wl_d], [win])
    win3 = win[0:64, :].rearrange("p (c j) -> p c j", j=128)
    scr2 = Tile(p.nc.dram_tensor("scr2", [64, 64, 128], F32, kind="Internal").ap())
    zf = [p.sb("zf%d" % i, [64, 4, 128]) for i in range(2)]
    zb = [p.sb("zb%d" % i, [64, 4, 128], BF16) for i in range(2)]
    xgb = [p.sb("xgb%d" % i, [64, 4, 128]) for i in range(2)]
    for o in range(2):
        for n2 in range(128):
            ps = PY[n2 % 2]
            p.mm(ps[0:64, 0:128], h2[0:64, n2:SEQ:128], w3[:, 2 * o:2 * o + 2, :].rearrange("p s c -> p (s c)"), True, True, [BG[1], w3], [ps])
            p.tt("dve", hfv[:, :, :, n2], ps[0:64, 0:128].rearrange("p (s c) -> p s c", s=2),
                 win3[:, :, n2].unsqueeze(1).broadcast_to([64, 2, 64]), ALU.mult, [ps, win], [hf_t])
        p.op("dve", lambda e: e.memset(hfv[0:1, 1, :, 0], 0.0), [], [hf_t])
        for bt in range(16):
            c0 = 4 * bt
            fwd_batch(hf_t, lambda c: hfv[:, 0, c0 + c, :])
            xr, xi = Xf[0], Xf[1]
            p.act(xr[:], PX[0][:], AF.Identity, [PX[0]], [xr])
            p.act(xi[:], PX[1][:], AF.Identity, [PX[1]], [xi])
            fwd_batch(hf_t, lambda c: hfv[:, 1, c0 + c, :])
            p.tt("dve", Hs[:, 0, c0:c0 + 4, :], xr[:].rearrange("p (c j) -> p c j", c=4),
                 PX[0][:].rearrange("p (c j) -> p c j", c=4), ALU.add, [xr, PX[0]], [Hs])
            p.tt("dve", Hs[:, 1, c0:c0 + 4, :], xi[:].rearrange("p (c j) -> p c j", c=4),
                 PX[1][:].rearrange("p (c j) -> p c j", c=4), ALU.subtract, [xi, PX[1]], [Hs])
        if debug:
            dh = p.dram_out("dbg_hf", [64, 2, 64, 128])
            dH = p.dram_out("dbg_H", [128, 2, 64, 128])
            dh2 = p.dram_out("dbg_h2", [64, SEQ])
            p.dma("pool", dh[:], hfv, [hf_t], [dh])
            p.dma("pool", dH[:], Hs[:], [Hs], [dH])
            p.dma("pool", dh2[:], h2[0:64, 0:SEQ], [BG[1]], [dh2], max_dma_last_dim=2048)
            return p
        for bt in range(16):
            c0 = 4 * bt
            b = bt % 2
            z_, zb_, g_ = zf[b], zb[b], xgb[b]
            if o == 0:
                p.dma("sp", z_[:], scr[0, c0:c0 + 4, :].rearrange("c (p j) -> p c j", j=128), [scr], [z_])
            else:
                p.dma("sp", z_[:], scr2[:, c0:c0 + 4, :], [scr2], [z_])
            p.dma("sp", g_[:], scr[1 + o, c0:c0 + 4, :].rearrange("c (p j) -> p c j", j=128), [scr], [g_])
            p.op("pool", lambda e: e.tensor_copy(out=zb_[:], in_=z_[:]), [z_], [zb_])
            p.tt("pool", z_[:], z_[:], hbr[:, o, c0:c0 + 4].unsqueeze(2).broadcast_to([64, 4, 128]), ALU.mult, [z_, hbr], [z_])
            fwd_batch(zb_, lambda c: zb_[:, c, :])
            xr, xi = Xf[2], Xf[3]
            p.act(xr[:], PX[0][:], AF.Identity, [PX[0]], [xr])
            p.act(xi[:], PX[1][:], AF.Identity, [PX[1]], [xi])
            x4r = xr[:].rearrange("p (c j) -> p c j", c=4)
            x4i = xi[:].rearrange("p (c j) -> p c j", c=4)
            ta = tm[0][:].rearrange("p (c j) -> p c j", c=4)
            tb = tm[1][:].rearrange("p (c j) -> p c j", c=4)
            e1, e2 = eng(), eng()
            p.tt(e1, ta, x4r, Hs[:, 0, c0:c0 + 4, :], ALU.mult, [xr, Hs], [tm[0]])
            p.tt(e2, tb, x4i, Hs[:, 1, c0:c0 + 4, :], ALU.mult, [xi, Hs], [tm[1]])
            p.tt(e1, Yr[b][:], ta, tb, ALU.subtract, [tm[0], tm[1]], [Yr[b]])
            p.tt(e1, ta, x4r, Hs[:, 1, c0:c0 + 4, :], ALU.mult, [xr, Hs], [tm[0]])
            p.tt(e2, tb, x4i, Hs[:, 0, c0:c0 + 4, :], ALU.mult, [xi, Hs], [tm[1]])
            p.tt(e2, Yi[b][:], ta, tb, ALU.add, [tm[0], tm[1]], [Yi[b]])
            py = PY[bt % 2]
            inv_batch(Yr[b], Yi[b], py)
            p.tt("dve", z_[:], py[0:64, :].rearrange("p (c j) -> p c j", c=4), z_[:], ALU.add, [py, z_], [z_])
            p.tt("pool", z_[:], z_[:], g_[:], ALU.mult, [z_, g_], [z_])
            if o == 0:
                p.dma("sp", scr2[:, c0:c0 + 4, :], z_[:], [z_], [scr2])
            else:
                p.dma("sp", yl_d[:, c0:c0 + 4, :], z_[:], [z_], [yl_d])
    return p


def run_l3(lat, ctx, layer, w):
    consts = hy_consts()
    zl, win_l = hy_features(SEQ)
    zc, win_c = hy_features(CTX)
    in_maps = []
    for i in range(NCORES):
        b, cg = divmod(i, 4)
        ch = slice(cg * 64, (cg + 1) * 64)
        cols = [768 + g * 256 + cg * 64 for g in range(3)]
        x = np.stack([lat[b, :, c:c + 64].T for c in cols])
        xc = np.stack([ctx[b, :, c:c + 64].T for c in cols])
        cwf = w["hy_conv_w"][layer].reshape(3, 3, 256)[:, :, ch]
        cbf = w["hy_conv_b"][layer].reshape(3, 256)[:, ch]
        cw = np.concatenate([cwf.transpose(2, 1, 0), cbf.T[:, :, None]], axis=2)
        hb = w["hy_bias"][layer][:, ch]
        w3 = w["filt_w3"][layer].reshape(64, 4, 256)[:, :, ch]
        fb = np.stack([w["filt_freq"][layer], w["filt_b1"][layer], w["filt_b2"][layer]], axis=1)
        wl = win_l[:, ch].reshape(64, 128, 64).transpose(0, 2, 1)
        m = {"x": x, "xc": xc, "cw": cw, "hb": hb, "hbc": hb.T, "fw1": w["filt_w1"][layer], "fw2": w["filt_w2"][layer],
             "fw3": w3, "fb": fb, "zl": zl, "zc": zc, "wl": wl, "wc": win_c[:, ch].T}
        m.update(consts)
        in_maps.append({k: np.ascontiguousarray(v, dtype=np.float32) for k, v in m.items()})
    res = run(build_l3(), in_maps)
    y_lat = np.zeros((2, SEQ, 256), np.float32)
    y_ctx = np.zeros((2, CTX, 256), np.float32)
    for i in range(NCORES):
        b, cg = divmod(i, 4)
        ch = slice(cg * 64, (cg + 1) * 64)
        y_lat[b, :, ch] = res[i]["yl"].transpose(0, 2, 1).reshape(SEQ, 64)
        y_ctx[b, :, ch] = res[i]["yc"].T
    return y_lat, y_ctx


def kernel(**inputs):
    w = {k: np.ascontiguousarray(np.asarray(v, dtype=np.float32)) for k, v in inputs.items()}
    mod = run_ada(w["c"], w["c_ctx"], w["ada_w"], w["ada_b"])
    h_lat, h_ctx = w["x"], w["ctx"]
    for layer in range(DEPTH):
        l1_lat, l1_ctx = run_l1(h_lat, h_ctx, mod, layer, w)
        a_lat, a_ctx = run_l2(l1_lat, l1_ctx)
        hy_lat, hy_ctx = run_l3(l1_lat, l1_ctx, layer, w)
        f_lat, b_lat, f_ctx, b_ctx = run_l4(l1_lat, l1_ctx)
        y_lat = np.concatenate([a_lat, hy_lat, f_lat], axis=-1)
        y_ctx = np.concatenate([a_ctx, hy_ctx, f_ctx], axis=-1)
        h_lat, h_ctx = run_l5a(y_lat, y_ctx, b_lat, b_ctx, l1_lat[:, :, 2048:2304], l1_ctx[:, :, 2048:2304],
                               h_lat, h_ctx, mod, layer, w)
        h_lat, h_ctx = run_l5b(h_lat, h_ctx, mod, layer, w, layer == DEPTH - 1)
    return np.ascontiguousarray(h_lat, dtype=np.float32)


TT = SEQ + CTX
NTT = TT // 128


class Stage:
    def __init__(self, p):
        self.p = p
        self.cms = []

    def sb(self, name, shape, dtype=F32):
        p = self.p
        p.uid = getattr(p, "uid", 0) + 1
        cm = p.nc.sbuf_tensor("%s_%d" % (name, p.uid), list(shape), dtype)
        h = cm.__enter__()
        self.cms.append(cm)
        return Tile(h)

    def close(self):
        p = self.p
        tot = {sid: c for sid, c in p.cnt.items() if c > 0}
        for e in p.eng:
            p._wait(e, dict(tot))
        for cm in reversed(self.cms):
            cm.__exit__(None, None, None)
        self.cms = []


def scratch(p, name, shape):
    return Tile(p.nc.dram_tensor(name, list(shape), F32, kind="Internal").ap())


def emit_ada(p, PS, cT_d, adaw_d, adab_d, MOD):
    st = Stage(p)
    cs = st.sb("cs", [128, 8, 2])
    Wc = [st.sb("Wc%d" % i, [128, 8, 512]) for i in range(3)]
    bb = st.sb("bb", [2, 6 * D])
    res = st.sb("res", [2, 6 * D])
    p.dma("sp", cs[:], cT_d[:], [cT_d], [cs])
    p.act(cs[:], cs[:], AF.Silu, [cs], [cs])
    k = 0
    for l in range(DEPTH):
        p.dma("sp", bb[:], adab_d[l:l + 1, :].partition_broadcast(2)[:, 0, :], [adab_d], [bb])
        for j in range(6 * D // 512):
            W = Wc[k % 3]
            q = "sp" if k % 2 == 0 else "act"
            p.dma(q, W[:], adaw_d[l, :, j * 512:(j + 1) * 512].rearrange("(c p) n -> p c n", p=128), [adaw_d], [W])
            ps = PS[k % 2]
            k += 1
            for c in range(8):
                p.mm(ps[0:2, :], cs[:, c, :], W[:, c, :], c == 0, c == 7, [cs, W], [ps])
            p.tt("dve", res[:, j * 512:(j + 1) * 512], ps[0:2, :], bb[:, j * 512:(j + 1) * 512], ALU.add, [ps, bb], [res])
        p.dma("sp", MOD[l], res[:], [res], [MOD])
    st.close()


def load_modT(p, st, PS, MOD, l, ident):
    raw = st.sb("modraw", [96, 128])
    modT = st.sb("modT", [128, 96])
    p.dma("sp", raw[:], MOD[l].rearrange("r (s p) -> (r s) p", p=128), [MOD], [raw])
    p.tr(PS[7][:, 0:96], raw[:], ident[0:96, 0:96], [raw, ident], [PS[7]])
    p.act(modT[:], PS[7][:, 0:96], AF.Identity, [PS[7]], [modT])
    return modT


def emit_l1(p, PS, l, Hin, S1, MOD, wd, cs_d, ident_d):
    st = Stage(p)
    sb = st.sb
    W = sb("W", [128, 8, IN_W], BF16)
    A = sb("A", [128, 8, 2])
    Brep = sb("Brep", [128, 8, 128], BF16)
    bW = [sb("bW%d" % k, [128, IN_W]) for k in range(2)]
    gains = sb("gains", [128, 10, 64])
    Wblk = sb("Wblk", [32, 256])
    gbb = sb("gbb", [128, 256])
    ident = sb("ident", [128, 128])
    epst = sb("eps", [128, 1])
    ng = sb("ng", [128, 8])
    pss, pst, psg = PS[0:5], PS[5:7], PS[7]
    p.op("dve", lambda e: e.memset(epst[:], EPS), [], [epst])
    p.op("dve", lambda e: e.memset(Wblk[:], 0.0), [], [Wblk])
    p.dma("sp", ident[:], ident_d[:], [ident_d], [ident])
    p.dma("sp", ng[:], wd["norm1_g"][l], [wd["norm1_g"]], [ng])
    p.dma("sp", gbb[:], wd["gla_gate_b"][l:l + 1, :].partition_broadcast(128)[:, 0, :], [wd["gla_gate_b"]], [gbb])
    p.dma("sp", Wblk[0:16, 0:128], wd["gla_gate_w"][l, 0], [wd["gla_gate_w"]], [Wblk])
    p.dma("sp", Wblk[16:32, 128:256], wd["gla_gate_w"][l, 1], [wd["gla_gate_w"]], [Wblk])
    for h in range(10):
        src = wd["qkg"][l:l + 1, 0:64] if h < 8 else wd["qkg"][l:l + 1, 64:128]
        p.dma("sp", gains[:, h, :], src.partition_broadcast(128)[:, 0, :], [wd["qkg"]], [gains])
    p.ts("dve", gains[:, 0:8, :], gains[:, 0:8, :], 0.125, ALU.mult, [gains], [gains])
    for c in range(8):
        p.dma("pool", W[:, c, :], wd["w_in"][l, c * 128:(c + 1) * 128, :], [wd["w_in"]], [W], max_dma_last_dim=4096)
    modT = load_modT(p, st, PS, MOD, l, ident)
    for k in range(2):
        sc = modT[:, k * 48 + 8:k * 48 + 16]
        sh = modT[:, k * 48 + 0:k * 48 + 8]
        p.stt(A[:, :, k], sc, 1.0, ng[:], ALU.add, ALU.mult, [modT, ng], [A])
        p.op("dve", lambda e: e.tensor_copy(out=Brep[:], in_=sh.unsqueeze(2).broadcast_to([128, 8, 128])), [modT], [Brep])
        for gi, (a, b) in enumerate(GROUPS1):
            ps = pss[gi]
            for c in range(8):
                p.mm(ps[:, 0:b - a], Brep[:, c, :], W[:, c, a:b], c == 0, c == 7, [Brep, W], [ps])
            p.act(bW[k][:, a:b], ps[:, 0:b - a], AF.Identity, [ps], [bW[k]])
    NB = 2
    xt = [sb("xt%d" % i, [128, D]) for i in range(NB)]
    xa = [sb("xa%d" % i, [128, 8, 128], BF16) for i in range(NB)]
    cst = [sb("cst%d" % i, [128, 64]) for i in range(NB)]
    O = [sb("O%d" % i, [128, OUT1]) for i in range(NB)]
    junk = sb("junk", [128, D])
    sq = sb("sq", [128, 640])
    tmp = sb("tmp", [128, 10, 32])
    stt_ = [sb("st%d" % i, [128, 16]) for i in range(NB)]
    lrs = [sb("lr%d" % i, [128, 32]) for i in range(NB)]
    lrT = sb("lrT", [32, 128])
    gz = sb("gz", [128, 256])
    ga = sb("ga", [128, 256])
    pic = [0]

    def phaseA(i):
        kind = 0 if i < 64 else 1
        bi = i % NB
        rows = slice(i * 128, (i + 1) * 128)
        p.dma("sp", xt[bi][:], Hin[rows, :], [Hin], [xt[bi]])
        p.dma("sp", cst[bi][:], cs_d[rows, :], [cs_d], [cst[bi]])
        s = stt_[bi]
        p.act(junk[:], xt[bi][:], AF.Square, [xt[bi]], [junk, s], accum_out=s[:, 0:1])
        p.rsqrt(s, s[:, 1:2], s, s[:, 0:1], 1.0 / D, epst)
        for c in range(8):
            pt = pst[c // 4]
            blk = pt[:, (c % 4) * 128:(c % 4 + 1) * 128]
            p.tr(blk, xt[bi][:, c * 128:(c + 1) * 128], ident[:], [xt[bi], ident], [pt])
            p.act(xa[bi][:, c, :], blk, AF.Identity, [pt, A], [xa[bi]], scale=A[:, c, kind:kind + 1])
        o = O[bi]
        for gi, (a, b) in enumerate(GROUPS1):
            ps = pss[pic[0] % 5]
            pic[0] += 1
            for c in range(8):
                p.mm(ps[:, 0:b - a], xa[bi][:, c, :], W[:, c, a:b], c == 0, c == 7, [xa[bi], W], [ps])
            if gi < 4:
                p.stt(o[:, a:b], ps[:, 0:b - a], s[:, 1:2], bW[kind][:, a:b], ALU.mult, ALU.add, [ps, s, bW[kind]], [o])
            else:
                p.stt(o[:, 2048:2304], ps[:, 0:256], s[:, 1:2], bW[kind][:, 2048:2304], ALU.mult, ALU.add, [ps, s, bW[kind]], [o])
                p.stt(lrs[bi][:], ps[:, 256:288], s[:, 1:2], bW[kind][:, 2304:2336], ALU.mult, ALU.add, [ps, s, bW[kind]], [lrs[bi]])

    def phaseB(i):
        bi = i % NB
        rows = slice(i * 128, (i + 1) * 128)
        s = stt_[bi]
        o = O[bi]
        lr = lrs[bi]
        qk = o[:, 0:640]
        p.tt("pool", sq[:], qk, qk, ALU.mult, [o], [sq])
        p.op("dve", lambda e: e.tensor_reduce(out=s[:, 2:12], in_=sq[:].rearrange("p (h d) -> p h d", d=64), axis=AX.X, op=ALU.add), [sq], [s])
        p.rsqrt(s, s[:, 2:12], s, s[:, 2:12], 1.0 / 64, epst)
        qk3 = qk.rearrange("p (h d) -> p h d", d=64)
        p.tt("dve", qk3, qk3, s[:, 2:12].unsqueeze(2).broadcast_to([128, 10, 64]), ALU.mult, [o, s], [o])
        p.tt("pool", qk3, qk3, gains[:], ALU.mult, [o, gains], [o])
        x1 = qk3[:, :, 0:32]
        x2 = qk3[:, :, 32:64]
        cb = cst[bi][:, 0:32].unsqueeze(1).broadcast_to([128, 10, 32])
        sb_ = cst[bi][:, 32:64].unsqueeze(1).broadcast_to([128, 10, 32])
        t3 = sq[:, 0:320].rearrange("p (h d) -> p h d", d=32)
        p.tt("dve", tmp[:], x2, sb_, ALU.mult, [o, cst[bi]], [tmp])
        p.tt("pool", t3, x1, sb_, ALU.mult, [o, cst[bi]], [sq])
        p.tt("dve", x1, x1, cb, ALU.mult, [o, cst[bi]], [o])
        p.tt("dve", x1, x1, tmp[:], ALU.subtract, [o, tmp], [o])
        p.tt("dve", x2, x2, cb, ALU.mult, [o, cst[bi]], [o])
        p.tt("dve", x2, x2, t3, ALU.add, [o, sq], [o])
        p.act(o[:, 1536:1664], o[:, 1536:1664], AF.Identity, [o], [o], scale=32 ** -0.5)
        p.act(o[:, 2048:2304], o[:, 2048:2304], AF.Silu, [o], [o])
        p.tr(psg[0:32, 0:128], lr[:], ident[:], [lr, ident], [psg])
        p.act(lrT[:], psg[0:32, 0:128], AF.Identity, [psg], [lrT])
        p.mm(psg[:, 128:384], lrT[:], Wblk[:], True, True, [lrT, Wblk], [psg])
        p.tt("dve", gz[:], psg[:, 128:384], gbb[:], ALU.add, [psg, gbb], [gz])
        p.stt(ga[:], gz[:], -1.0, gz[:], ALU.mult, ALU.min, [gz], [ga])
        p.act(ga[:], ga[:], AF.Exp, [ga], [ga])
        p.act(ga[:], ga[:], AF.Ln, [ga], [ga], bias=1.0)
        p.ts("dve", gz[:], gz[:], 0.0, ALU.min, [gz], [gz], s2=1.0 / 16, op1=ALU.mult)
        p.stt(o[:, 2304:2560], ga[:], -1.0 / 16, gz[:], ALU.mult, ALU.add, [ga, gz], [o])
        p.dma("sp", S1[rows, :], o[:], [o], [S1])

    phaseA(0)
    for i in range(NTT):
        if i + 1 < NTT:
            phaseA(i + 1)
        phaseB(i)
    st.close()


def emit_l2(p, PS, S1, SY, ident_d, PS2):
    st = Stage(p)
    sb = st.sb
    ident = sb("ident", [128, 128])
    p.dma("sp", ident[:], ident_d[:], [ident_d], [ident])
    kT2 = sb("kT2", [128, TT], BF16)
    vA = [sb("vA%d" % g, [128, NTT, 65], BF16) for g in range(2)]
    kin = [sb("kin%d" % i, [128, 128]) for i in range(2)]
    qin = [sb("qin%d" % i, [128, 512]) for i in range(2)]
    q2 = [sb("q2_%d" % i, [128, 512], BF16) for i in range(2)]
    qrs = [sb("qr%d" % i, [128, 512]) for i in range(2)]
    psS = PS2[0:3]
    psO = PS2[3]
    for g in range(2):
        p.op("dve", lambda e: e.memset(vA[g][:, :, 64:65], 1.0), [], [vA[g]])
        p.dma("pool", vA[g][:, :, 0:64], S1[:, 640 + g * 64:704 + g * 64].rearrange("(t p) c -> p t c", p=128), [S1], [vA[g]])
    for t in range(NTT):
        ki = kin[t % 2]
        p.dma("sp", ki[:], S1[t * 128:(t + 1) * 128, 512:640], [S1], [ki])
        pt = psS[t % 3]
        p.tr(pt[:, 0:128], ki[:], ident[:], [ki, ident], [pt])
        p.act(kT2[:, t * 128:(t + 1) * 128], pt[:, 0:128], AF.Identity, [pt], [kT2])
    LA = 2
    pT = [sb("pT%d" % i, [128, 1024], BF16) for i in range(LA + 2)]
    oT = [sb("oT%d" % i, [65, 1024]) for i in range(2)]
    ot = [sb("ot%d" % i, [128, 512]) for i in range(2)]
    rec = sb("rec", [128, 16])
    it = 0
    for qt in range(NTT):
        kts = list(range(NTT)) if qt < 64 else [64, 65]
        qi = qin[qt % 2]
        p.dma("sp", qi[:], S1[qt * 128:(qt + 1) * 128, 0:512], [S1], [qi])
        o_t = ot[qt % 2]
        q_ = q2[qt % 2]
        pt = psS[it % 3]
        qr = qrs[qt % 2]
        p.op("pool", lambda e: e.tensor_copy(out=qr[:].rearrange("p (h g d) -> p h g d", h=4, g=2),
                                             in_=qi[:].rearrange("p (g h d) -> p h g d", g=2, h=4)), [qi], [qr])
        for h in range(4):
            p.tr(pt[:, h * 128:(h + 1) * 128], qr[:, h * 128:(h + 1) * 128], ident[:], [qr, ident], [pt])
        p.act(q_[:], pt[:, 0:512], AF.Identity, [pt], [q_])

        def pv(pend, last):
            k0, kt, ptile = pend
            for g in range(2):
                p.mm(psO[0:65, g * 512:(g + 1) * 512], vA[g][:, kt, :], ptile[:, g * 512:(g + 1) * 512], k0 == 0, last,
                     [vA[g], ptile], [psO])

        pend = []
        for k0, kt in enumerate(kts):
            ps = psS[it % 3]
            pt_ = pT[it % (LA + 2)]
            it += 1
            for g in range(2):
                rows = slice(g * 64, (g + 1) * 64)
                p.mm(ps[:, g * 512:(g + 1) * 512], kT2[rows, kt * 128:(kt + 1) * 128], q_[rows, :], True, True, [kT2, q_], [ps])
            p.act(pt_[:], ps[:], AF.Exp, [ps], [pt_])
            pend.append((k0, kt, pt_))
            if len(pend) > LA:
                pv(pend.pop(0), False)
        while pend:
            x_ = pend.pop(0)
            pv(x_, len(pend) == 0)
        o_sb = oT[qt % 2]
        p.op("dve", lambda e: e.tensor_copy(out=o_sb[:], in_=psO[0:65, :]), [psO], [o_sb])
        for hh in range(8):
            pb = psS[(it + 1 + hh % 2) % 3]
            ptt = pb[:, (hh // 2 % 2) * 512:(hh // 2 % 2) * 512 + 65]
            p.tr(ptt, o_sb[:, hh * 128:(hh + 1) * 128], ident[0:65, 0:65], [o_sb, ident], [pb])
            rc = rec[:, hh + (qt % 2) * 8:hh + (qt % 2) * 8 + 1]
            p.op("dve", lambda e: e.reciprocal(out=rc, in_=ptt[:, 64:65]), [pb], [rec])
            p.ts("dve", o_t[:, hh * 64:(hh + 1) * 64], ptt[:, 0:64], rc, ALU.mult, [pb, rec], [o_t])
        p.dma("sp", SY[qt * 128:(qt + 1) * 128, :], o_t[:], [o_t], [SY])
    st.close()


def emit_l3(p, PS, l, S1, YL, YC, wd, cd, scr, scr2):
    st = Stage(p)
    sb = st.sb
    BGa = sb("BGa", [128, SEQ])
    BGb = sb("BGb", [128, SEQ])
    H2 = sb("H2", [64, SEQ], BF16)
    Hs = sb("Hs", [128, 2, 64, 128], BF16)
    F1 = sb("F1", [64, 256], BF16)
    TWf = sb("TWf", [128, 2, 128])
    F2 = sb("F2", [128, 3, 128], BF16)
    G2 = sb("G2", [128, 2, 256], BF16)
    TWi = sb("TWi", [128, 2, 128])
    G1 = sb("G1", [128, 2, 64], BF16)
    ident = sb("ident", [128, 128])
    for nm, t in (("F1", F1), ("F2", F2), ("G2", G2), ("G1", G1)):
        p.dma("pool", t[:], cd[nm][:], [cd[nm]], [t])
    p.dma("sp", TWf[:], cd["TWf"][:], [cd["TWf"]], [TWf])
    p.dma("sp", TWi[:], cd["TWi"][:], [cd["TWi"]], [TWi])
    p.dma("sp", ident[:], cd["ident"][:], [cd["ident"]], [ident])
    w1 = sb("fw1", [33, 64])
    w2 = sb("fw2", [64, 64])
    fb = sb("fb", [64, 3])
    frb = sb("frb", [64, 2])
    p.dma("sp", w1[:], wd["filt_w1"][l], [wd["filt_w1"]], [w1])
    p.dma("sp", w2[:], wd["filt_w2"][l], [wd["filt_w2"]], [w2])
    p.dma("sp", fb[:], wd["fb"][l], [wd["fb"]], [fb])
    for j in range(2):
        p.tt("dve", frb[:, j:j + 1], fb[:, 0:1], fb[:, 1 + j:2 + j], ALU.mult, [fb], [frb])
    PA, PX, PB, PY = PS[0:2], PS[2:4], PS[4:6], PS[6:8]
    argt = sb("argt", [64, 512])
    wrp = sb("wrp", [64, 512])
    h1c = sb("h1c", [64, 512])
    zch = [sb("zch%d" % i, [33, 512]) for i in range(2)]
    h2c = sb("h2c", [64, CTX])

    def wrap_sin(dst_t, dst_ap, ps_ap, ps_t, j, n):
        a = argt[0:64, 0:n]
        p.ts("dve", a, ps_ap, fb[:, 0:1], ALU.mult, [ps_t, fb, frb], [argt], s2=frb[:, j:j + 1], op1=ALU.add)
        w_ = wrp[0:64, 0:n]
        for bound, period in ((3 * PI, 4 * PI), (PI, 2 * PI)):
            p.ts("dve", w_, a, bound, ALU.is_gt, [argt], [wrp], s2=-period, op1=ALU.mult)
            p.tt("dve", a, a, w_, ALU.add, [argt, wrp], [argt])
            p.ts("dve", w_, a, -bound, ALU.is_lt, [argt], [wrp], s2=period, op1=ALU.mult)
            p.tt("dve", a, a, w_, ALU.add, [argt, wrp], [argt])
        p.act(dst_ap, a, AF.Sin, [argt], [dst_t])

    def mlp(z_dram, n, dst_t):
        for q in range(0, n, 512):
            m = min(512, n - q)
            zt = zch[(q // 512) % 2]
            p.dma("sp", zt[:, 0:m], z_dram[:, q:q + m], [z_dram], [zt])
            p.mm(PA[0][0:64, 0:m], w1[:], zt[:, 0:m], True, True, [w1, zt], [PA[0]])
            wrap_sin(h1c, h1c[:, 0:m], PA[0][0:64, 0:m], PA[0], 0, m)
            p.mm(PA[1][0:64, 0:m], w2[:], h1c[:, 0:m], True, True, [w2, h1c], [PA[1]])
            wrap_sin(dst_t, dst_t[:, q:q + m], PA[1][0:64, 0:m], PA[1], 1, m)

    mlp(cd["zl"], SEQ, H2)
    mlp(cd["zc"], CTX, h2c)

    cw = sb("cw", [64, 3, 4])
    hbr = sb("hbr", [64, 2, 64])
    hbc = sb("hbc", [64, 2])
    w3 = sb("fw3", [64, 4, 64], BF16)
    w3f = sb("fw3f", [64, 4, 64])
    wc = sb("wc", [64, CTX])
    uc = sb("uc", [64, 3, CTX])
    xct = sb("xct", [64, 3, CTX])
    hfc = sb("hfc", [64, 4, CTX])
    zc1 = sb("zc1", [64, CTX])
    yct = sb("yct", [64, CTX])
    xin = [sb("xin%d" % i, [128, 64]) for i in range(3)]
    Af = [sb("Af%d" % i, [128, 512]) for i in range(2)]
    tm = [sb("tm%d" % i, [128, 512]) for i in range(4)]
    tms = [[sb("tms%d_%d" % (a, b), [128, 256]) for b in range(4)] for a in range(4)]
    Apr = [sb("Apr%d" % i, [128, 4, 128], BF16) for i in range(2)]
    Api = [sb("Api%d" % i, [128, 4, 128], BF16) for i in range(2)]
    Xf = [sb("Xf%d" % i, [128, 512]) for i in range(4)]
    Yr = [sb("Yr%d" % i, [128, 4, 128], BF16) for i in range(2)]
    Yi = [sb("Yi%d" % i, [128, 4, 128], BF16) for i in range(2)]
    zf = [sb("zf%d" % i, [64, 4, 128]) for i in range(2)]
    zb = [sb("zb%d" % i, [64, 4, 128], BF16) for i in range(2)]
    xgb = [sb("xgb%d" % i, [64, 4, 128]) for i in range(2)]
    cnt = {"f": 0, "e": 0, "x": 0}

    def eng():
        cnt["e"] += 1
        return "dve" if cnt["e"] % 2 else "pool"

    def cmul(src_t, sr, si, tw_t, twr, twi, dr_t, dr, di_t, di, T4, e2):
        (a1, A1), (b1, B1), (a2, A2), (b2, B2) = T4
        p.tt("dve", a1, sr, twr, ALU.mult, [src_t, tw_t], [A1])
        p.tt("dve", b1, si, twi, ALU.mult, [src_t, tw_t], [B1])
        p.tt("dve", dr, a1, b1, ALU.subtract, [A1, B1], [dr_t])
        p.tt(e2, a2, sr, twi, ALU.mult, [src_t, tw_t], [A2])
        p.tt(e2, b2, si, twr, ALU.mult, [src_t, tw_t], [B2])
        p.tt(e2, di, a2, b2, ALU.add, [A2, B2], [di_t])

    def stage12(src_t, lhs_fn, rhs_t, rhs0, rhs1, lhs2_fn, TW, dr_t, di_t, src2_t=None):
        cnt["f"] += 1
        for hf in range(2):
            ps = (PA if rhs1 is None else PB)[hf]
            for k in range(2):
                c = 2 * hf + k
                cols = slice(k * 256, (k + 1) * 256)
                if rhs1 is None:
                    p.mm(ps[:, cols], lhs_fn(c), rhs0, True, True, [src_t, rhs_t], [ps])
                else:
                    p.mm(ps[:, cols], lhs_fn(c), rhs0, True, False, [src_t, rhs_t], [ps])
                    p.mm(ps[:, cols], lhs2_fn(c), rhs1, False, True, [src2_t, rhs_t], [ps])
            af = Af[hf]
            p.act(af[:], ps[:], AF.Identity, [ps], [af])
            a4 = af[:].rearrange("p (k r j) -> p k r j", k=2, r=2)
            twr = TW[:, 0, :].unsqueeze(1).broadcast_to([128, 2, 128])
            twi = TW[:, 1, :].unsqueeze(1).broadcast_to([128, 2, 128])
            tset = tms[(cnt["f"] % 2) * 2 + hf]
            T4 = [(t_[:].rearrange("p (k j) -> p k j", k=2), t_) for t_ in tset]
            cmul(af, a4[:, :, 0, :], a4[:, :, 1, :], TW, twr, twi,
                 dr_t, dr_t[:, 2 * hf:2 * hf + 2, :], di_t, di_t[:, 2 * hf:2 * hf + 2, :],
                 T4, "pool" if hf == 1 else "dve")

    def fwd_batch(src_t, lhs_fn):
        b = cnt["f"] % 2
        ar, ai = Apr[b], Api[b]
        stage12(src_t, lhs_fn, F1, F1[:], None, None, TWf, ar, ai)
        arf = ar[:].rearrange("p c j -> p (c j)")
        aif = ai[:].rearrange("p c j -> p (c j)")
        p.mm(PX[0][:], F2[:, 0, :], arf, True, False, [F2, ar], [PX[0]])
        p.mm(PX[0][:], F2[:, 2, :], aif, False, True, [F2, ai], [PX[0]])
        p.mm(PX[1][:], F2[:, 0, :], aif, True, False, [F2, ai], [PX[1]])
        p.mm(PX[1][:], F2[:, 1, :], arf, False, True, [F2, ar], [PX[1]])

    def inv_batch(yr, yi, py):
        b = cnt["f"] % 2
        br, bi = Apr[b], Api[b]
        stage12(yr, lambda c: yr[:, c, :], G2, G2[:, 0, :], G2[:, 1, :], lambda c: yi[:, c, :], TWi, br, bi, src2_t=yi)
        p.mm(py[0:64, :], G1[:, 0, :], br[:].rearrange("p c j -> p (c j)"), True, False, [G1, br], [py])
        p.mm(py[0:64, :], G1[:, 1, :], bi[:].rearrange("p c j -> p (c j)"), False, True, [G1, bi], [py])

    def short_conv(xt, xap, ut, uap, g, n):
        p.act(uap, xap, AF.Identity, [xt, cw], [ut], scale=cw[:, g, 1:2], bias=cw[:, g, 3:4])
        p.stt(uap[:, 1:n], xap[:, 0:n - 1], cw[:, g, 0:1], uap[:, 1:n], ALU.mult, ALU.add, [xt, cw, ut], [ut])
        p.stt(uap[:, 0:n - 1], xap[:, 1:n], cw[:, g, 2:3], uap[:, 0:n - 1], ALU.mult, ALU.add, [xt, cw, ut], [ut])

    def ctx_conv(zt, zap, o, gate_ap, out_t, out_ap):
        p.ts("dve", yct[:], zap, hbc[:, o:o + 1], ALU.mult, [zt, hbc], [yct])
        for q in range(CTX):
            p.stt(yct[:, q:CTX], zap[:, 0:CTX - q], hfc[:, 2 * o, q:q + 1], yct[:, q:CTX], ALU.mult, ALU.add, [zt, hfc, yct], [yct])
        for q in range(1, CTX):
            p.stt(yct[:, 0:CTX - q], zap[:, q:CTX], hfc[:, 2 * o + 1, q:q + 1], yct[:, 0:CTX - q], ALU.mult, ALU.add, [zt, hfc, yct], [yct])
        p.tt("dve", out_ap, yct[:], gate_ap, ALU.mult, [yct, uc], [out_t])

    hfv = BGb[:].bitcast(BF16)[0:64, :].rearrange("p (s c j) -> p s c j", s=2, c=64)
    win3 = BGa[0:64, :].rearrange("p (c j) -> p c j", j=128)
    for cg in range(4):
        p.dma("sp", cw[:], wd["cw"][l, cg], [wd["cw"]], [cw])
        p.dma("sp", hbc[:], wd["hbc"][l, cg], [wd["hbc"]], [hbc])
        for o in range(2):
            p.dma("sp", hbr[:, o, :], wd["hb"][l, cg, o:o + 1, :].partition_broadcast(64)[:, 0, :], [wd["hb"]], [hbr])
        p.dma("sp", w3f[:], wd["fw3"][l, cg], [wd["fw3"]], [w3f])
        p.dma("pool", w3[:], wd["fw3"][l, cg], [wd["fw3"]], [w3])
        p.dma("sp", wc[:], cd["wc"][cg], [cd["wc"]], [wc])
        for g in range(3):
            c0 = 768 + g * 256 + cg * 64
            for t in range(NTT):
                xi = xin[cnt["x"] % 3]
                cnt["x"] += 1
                p.dma("sp", xi[:], S1[t * 128:(t + 1) * 128, c0:c0 + 64], [S1], [xi])
                pt = PA[(t // 4) % 2]
                p.tr(pt[0:64, (t % 4) * 128:(t % 4 + 1) * 128], xi[:], ident[:], [xi, ident], [pt])
                if t % 4 == 3 and t < 64:
                    p.act(BGa[0:64, (t - 3) * 128:(t + 1) * 128], pt[0:64, :], AF.Identity, [pt], [BGa])
                if t == 65:
                    p.act(xct[:, g, :], pt[0:64, 0:256], AF.Identity, [pt], [xct])
            short_conv(BGa, BGa[0:64, :], BGb, BGb[0:64, :], g, SEQ)
            p.dma("sp", scr[g], BGb[0:64, :], [BGb], [scr])
            short_conv(xct, xct[:, g, :], uc, uc[:, g, :], g, CTX)
        for blk in range(4):
            p.mm(PX[0][0:64, 0:CTX], w3f[:, blk, :], h2c[:], True, True, [w3f, h2c], [PX[0]])
            p.tt("dve", hfc[:, blk, :], PX[0][0:64, 0:CTX], wc[:], ALU.mult, [PX[0], wc], [hfc])
        ctx_conv(uc, uc[:, 0, :], 0, uc[:, 1, :], zc1, zc1[:])
        ctx_conv(zc1, zc1[:], 1, uc[:, 2, :], zc1, zc1[:])
        p.dma("sp", YC[cg], zc1[:], [zc1], [YC])
        p.dma("sp", BGa[0:64, :], cd["wl"][cg].rearrange("p c j -> p (c j)"), [cd["wl"]], [BGa])
        for o in range(2):
            for n2 in range(128):
                ps = PY[n2 % 2]
                p.mm(ps[0:64, 0:128], H2[:, n2:SEQ:128], w3[:, 2 * o:2 * o + 2, :].rearrange("p s c -> p (s c)"), True, True, [H2, w3], [ps])
                p.tt("dve", hfv[:, :, :, n2], ps[0:64, 0:128].rearrange("p (s c) -> p s c", s=2),
                     win3[:, :, n2].unsqueeze(1).broadcast_to([64, 2, 64]), ALU.mult, [ps, BGa], [BGb])
            p.op("dve", lambda e: e.memset(hfv[0:1, 1, :, 0], 0.0), [], [BGb])
            for bt in range(16):
                c0 = 4 * bt
                fwd_batch(BGb, lambda c: hfv[:, 0, c0 + c, :])
                xr, xi_ = Xf[0], Xf[1]
                p.act(xr[:], PX[0][:], AF.Identity, [PX[0]], [xr])
                p.act(xi_[:], PX[1][:], AF.Identity, [PX[1]], [xi_])
                fwd_batch(BGb, lambda c: hfv[:, 1, c0 + c, :])
                p.tt("dve", Hs[:, 0, c0:c0 + 4, :], xr[:].rearrange("p (c j) -> p c j", c=4),
                     PX[0][:].rearrange("p (c j) -> p c j", c=4), ALU.add, [xr, PX[0]], [Hs])
                p.tt("dve", Hs[:, 1, c0:c0 + 4, :], xi_[:].rearrange("p (c j) -> p c j", c=4),
                     PX[1][:].rearrange("p (c j) -> p c j", c=4), ALU.subtract, [xi_, PX[1]], [Hs])
            for bt in range(16):
                c0 = 4 * bt
                b = bt % 2
                z_, zb_, g_ = zf[b], zb[b], xgb[b]
                if o == 0:
                    p.dma("sp", z_[:], scr[0, c0:c0 + 4, :].rearrange("c (p j) -> p c j", j=128), [scr], [z_])
                else:
                    p.dma("sp", z_[:], scr2[:, c0:c0 + 4, :], [scr2], [z_])
                p.dma("sp", g_[:], scr[1 + o, c0:c0 + 4, :].rearrange("c (p j) -> p c j", j=128), [scr], [g_])
                p.op("pool", lambda e: e.tensor_copy(out=zb_[:], in_=z_[:]), [z_], [zb_])
                p.tt("pool", z_[:], z_[:], hbr[:, o, c0:c0 + 4].unsqueeze(2).broadcast_to([64, 4, 128]), ALU.mult, [z_, hbr], [z_])
                fwd_batch(zb_, lambda c: zb_[:, c, :])
                xr, xi_ = Xf[2], Xf[3]
                p.act(xr[:], PX[0][:], AF.Identity, [PX[0]], [xr])
                p.act(xi_[:], PX[1][:], AF.Identity, [PX[1]], [xi_])
                x4r = xr[:].rearrange("p (c j) -> p c j", c=4)
                x4i = xi_[:].rearrange("p (c j) -> p c j", c=4)
                ta = tm[0][:].rearrange("p (c j) -> p c j", c=4)
                tb = tm[1][:].rearrange("p (c j) -> p c j", c=4)
                tc_ = tm[2][:].rearrange("p (c j) -> p c j", c=4)
                td_ = tm[3][:].rearrange("p (c j) -> p c j", c=4)
                p.tt("dve", ta, x4r, Hs[:, 0, c0:c0 + 4, :], ALU.mult, [xr, Hs], [tm[0]])
                p.tt("dve", tb, x4i, Hs[:, 1, c0:c0 + 4, :], ALU.mult, [xi_, Hs], [tm[1]])
                p.tt("dve", Yr[b][:], ta, tb, ALU.subtract, [tm[0], tm[1]], [Yr[b]])
                p.tt("pool", tc_, x4r, Hs[:, 1, c0:c0 + 4, :], ALU.mult, [xr, Hs], [tm[2]])
                p.tt("pool", td_, x4i, Hs[:, 0, c0:c0 + 4, :], ALU.mult, [xi_, Hs], [tm[3]])
                p.tt("pool", Yi[b][:], tc_, td_, ALU.add, [tm[2], tm[3]], [Yi[b]])
                py = PY[bt % 2]
                inv_batch(Yr[b], Yi[b], py)
                p.tt("dve", z_[:], py[0:64, :].rearrange("p (c j) -> p c j", c=4), z_[:], ALU.add, [py, z_], [z_])
                p.tt("pool", z_[:], z_[:], g_[:], ALU.mult, [z_, g_], [z_])
                if o == 0:
                    p.dma("sp", scr2[:, c0:c0 + 4, :], z_[:], [z_], [scr2])
                else:
                    p.dma("sp", YL[cg, :, c0:c0 + 4, :], z_[:], [z_], [YL])
    st.close()


def emit_l4(p, PS, S1, GO, cd):
    st = Stage(p)
    sb = st.sb
    ident = sb("ident", [128, 128])
    mask = sb("mask", [128, 128])
    Jm = sb("Jm", [128, 128], BF16)
    pm = sb("pm", [128, 2])
    ones = sb("ones", [32, 1])
    p.dma("sp", ident[:], cd["ident"][:], [cd["ident"]], [ident])
    p.dma("sp", mask[:], cd["mask"][:], [cd["mask"]], [mask])
    p.dma("pool", Jm[:], cd["J"][:], [cd["J"]], [Jm])
    p.dma("sp", pm[:], cd["pm"][:], [cd["pm"]], [pm])
    p.op("dve", lambda e: e.memset(ones[:], 1.0), [], [ones])
    qd = sb("qd", [32, TG], BF16)
    kd = sb("kd", [32, TG], BF16)
    ktT = [sb("ktT%d" % i, [128, NPR, 32], BF16) for i in range(2)]
    vnat = sb("vnat", [128, NPR, 64], BF16)
    vt = sb("vt", [128, NPR, 64], BF16)
    STm = sb("STm", [128, NPR, 128], BF16)
    Sbf = sb("Sbf", [32, NCH + 1, 64], BF16)
    dc = sb("dc", [32, NCH])
    oT = sb("oT", [64, TG])
    Scur = [sb("Scur%d" % i, [32, 64]) for i in range(2)]
    seg3 = sb("seg3", [32, 3, SEGL])
    gin = [sb("gin%d" % i, [128, 3, 32]) for i in range(3)]
    Gs = [sb("Gs%d" % i, [32, SEGL]) for i in range(2)]
    At = sb("At", [32, SEGL])
    Bt = sb("Bt", [32, SEGL])
    Sc = sb("Sc", [32, 22])
    tmpc = sb("tmpc", [32, 22])
    psT, psS, psK, psO = PS[0:2], PS[2:4], PS[4:6], PS[6:8]
    ng = 0
    for hd in range(4):
        p.dma("pool", vnat[:], S1[:, 1792 + hd * 64:1856 + hd * 64].rearrange("(t p) c -> p t c", p=128), [S1], [vnat], max_dma_last_dim=4096)
        for d in range(2):
            def gtile(j):
                return (64 + j if j < 2 else j - 2) if d == 0 else 65 - j
            if d == 0:
                p.op("pool", lambda e: e.tensor_copy(out=vt[:, 0:2, :], in_=vnat[:, 64:66, :]), [vnat], [vt])
                p.op("pool", lambda e: e.tensor_copy(out=vt[:, 2:66, :], in_=vnat[:, 0:64, :]), [vnat], [vt])
            else:
                for j0 in range(0, NPR, 8):
                    n = min(8, NPR - j0)
                    ps = psK[(j0 // 8) % 2]
                    for j in range(n):
                        p.mm(ps[:, j * 64:(j + 1) * 64], Jm[:], vnat[:, gtile(j0 + j), :], True, True, [Jm, vnat], [ps])
                    p.act(vt[:, j0:j0 + n, :], ps[:, 0:n * 64].rearrange("p (a b) -> p a b", b=64), AF.Identity, [ps], [vt])
            gcol = (2304 if d == 0 else 2432) + hd * 32
            for s in range(NSEG):
                seg = slice(s * SEGL, (s + 1) * SEGL)
                for jj in range(11):
                    j = s * 11 + jj
                    gt = gtile(j)
                    gi = gin[ng % 3]
                    ps = psT[ng % 2]
                    ng += 1
                    rows = slice(gt * 128, (gt + 1) * 128)
                    p.dma("sp", gi[:, 0:2, :], S1[rows, 1536 + hd * 32:1536 + hd * 32 + 256].rearrange("p (a c) -> p a c", a=2)[:, :, 0:32], [S1], [gi])
                    p.dma("sp", gi[:, 2, :], S1[rows, gcol:gcol + 32], [S1], [gi])
                    for a in range(3):
                        p.tr(ps[0:32, a * 128:(a + 1) * 128], gi[:, a, :], ident[:], [gi, ident], [ps])
                    dst = seg3[:, :, jj * 128:(jj + 1) * 128]
                    if d == 1:
                        dst = dst[:, :, ::-1]
                    p.act(dst, ps[0:32, 0:384].rearrange("p (a j) -> p a j", a=3), AF.Identity, [ps], [seg3])
                qseg, kseg, gseg = seg3[:, 0, :], seg3[:, 1, :], seg3[:, 2, :]
                G = Gs[s % 2]
                Gp = Gs[(s + 1) % 2]
                init = 0.0 if s == 0 else Gp[:, SEGL - 1:SEGL]
                rd = [seg3, ones] + ([] if s == 0 else [Gp])
                p.op("dve", lambda e: e.tensor_tensor_scan(out=G[:], data0=ones[:, 0:1].broadcast_to([32, SEGL]), data1=gseg,
                                                           initial=init, op0=ALU.mult, op1=ALU.add), rd, [G])
                G3 = G[:].rearrange("p (c j) -> p c j", j=64)
                Ec = G3[:, :, 63]
                if s == 0:
                    p.op("dve", lambda e: e.memset(Sc[:, 0:1], 0.0), [], [Sc])
                else:
                    p.op("dve", lambda e: e.tensor_copy(out=Sc[:, 0:1], in_=Gp[:, SEGL - 1:SEGL]), [Gp], [Sc])
                p.op("dve", lambda e: e.tensor_copy(out=Sc[:, 1:22], in_=G3[:, 0:21, 63]), [G], [Sc])
                A3 = At[:].rearrange("p (c j) -> p c j", j=64)
                p.tt("dve", A3, G3, Sc[:].unsqueeze(2).broadcast_to([32, 22, 64]), ALU.subtract, [G, Sc], [At])
                p.act(Bt[:], At[:], AF.Exp, [At], [Bt])
                p.tt("dve", qd[:, seg], qseg, Bt[:], ALU.mult, [seg3, Bt], [qd])
                p.act(Bt[:], At[:], AF.Exp, [At], [Bt], scale=-1.0)
                p.tt("dve", kd[:, seg], kseg, Bt[:], ALU.mult, [seg3, Bt], [kd])
                p.tt("dve", tmpc[:], Ec, Sc[:], ALU.subtract, [G, Sc], [tmpc])
                p.act(dc[:, s * 22:(s + 1) * 22], tmpc[:], AF.Exp, [tmpc], [dc])
                p.tt("dve", A3, Ec.unsqueeze(2).broadcast_to([32, 22, 64]), G3, ALU.subtract, [G], [At])
                p.act(At[:], At[:], AF.Exp, [At], [At])
                p.tt("dve", Bt[:], kseg, At[:], ALU.mult, [seg3, At], [Bt])
                ps = psS[s % 2]
                for j in range(11):
                    p.tr(ps[:, j * 32:(j + 1) * 32], Bt[:, j * 128:(j + 1) * 128], ident[0:32, 0:32], [Bt, ident], [ps])
                for hf in range(2):
                    p.act(ktT[hf][:, s * 11:(s + 1) * 11, :], ps[:, 0:352].rearrange("p (a b) -> p a b", b=32), AF.Identity,
                          [ps, pm], [ktT[hf]], scale=pm[:, hf:hf + 1])
            for g0 in range(0, NPR, 4):
                n = min(4, NPR - g0)
                ps = psS[(g0 // 4) % 2]
                for j in range(n):
                    tok = slice((g0 + j) * 128, (g0 + j + 1) * 128)
                    p.mm(ps[:, j * 128:(j + 1) * 128], kd[:, tok], qd[:, tok], True, True, [kd, qd], [ps])
                p.tt("dve", STm[:, g0:g0 + n, :], ps[:, 0:n * 128].rearrange("p (a b) -> p a b", b=128),
                     mask[:].unsqueeze(1).broadcast_to([128, n, 128]), ALU.mult, [ps, mask], [STm])
            p.op("dve", lambda e: e.memset(Scur[0][:], 0.0), [], [Scur[0]])
            p.op("dve", lambda e: e.memset(Sbf[:, 0, :], 0.0), [], [Sbf])
            for c0 in range(0, NCH, 8):
                n = min(8, NCH - c0)
                ps = psK[(c0 // 8) % 2]
                for j in range(n):
                    pr, hf = divmod(c0 + j, 2)
                    p.mm(ps[0:32, j * 64:(j + 1) * 64], ktT[hf][:, pr, :], vt[:, pr, :], True, True, [ktT[hf], vt], [ps])
                for j in range(n):
                    c = c0 + j
                    sa, sb_ = Scur[c % 2], Scur[(c + 1) % 2]
                    p.stt(sb_[:], sa[:], dc[:, c:c + 1], ps[0:32, j * 64:(j + 1) * 64], ALU.mult, ALU.add, [sa, dc, ps], [sb_])
                    p.act(Sbf[:, c + 1, :], sb_[:], AF.Identity, [sb_], [Sbf])
            for g0 in range(0, NPR, 4):
                n = min(4, NPR - g0)
                ps = psO[(g0 // 4) % 2]
                for j in range(n):
                    pr = g0 + j
                    cs_ = slice(j * 128, (j + 1) * 128)
                    p.mm(ps[0:64, cs_], vt[:, pr, :], STm[:, pr, :], True, False, [vt, STm], [ps])
                    for hf in range(2):
                        c = 2 * pr + hf
                        tok = slice(c * 64, (c + 1) * 64)
                        p.mm(ps[0:64, j * 128 + hf * 64:j * 128 + (hf + 1) * 64], Sbf[:, c, :], qd[:, tok], False, hf == 1, [Sbf, qd], [ps])
                p.act(oT[:, g0 * 128:(g0 + n) * 128], ps[0:64, 0:n * 128], AF.Identity, [ps], [oT])
            p.dma("sp", GO[hd, d], oT[:], [oT], [GO])
    st.close()


def emit_l5a(p, PS, l, SY, YL, YC, GO, S1, Hin, H1, MOD, wd, ident_d):
    st = Stage(p)
    sb = st.sb
    W = sb("W", [128, 8, D], BF16)
    og = sb("og", [128, 8])
    g1 = [sb("g1_%d" % k, [128, D]) for k in range(2)]
    ident = sb("ident", [128, 128])
    epst = sb("eps", [128, 1])
    p.op("dve", lambda e: e.memset(epst[:], EPS), [], [epst])
    p.dma("sp", og[:], wd["out_norm_g"][l], [wd["out_norm_g"]], [og])
    p.dma("sp", ident[:], ident_d[:], [ident_d], [ident])
    for k in range(2):
        p.dma("sp", g1[k][:], MOD[l, k:k + 1, 2 * D:3 * D].partition_broadcast(128)[:, 0, :], [MOD], [g1[k]])
    for c in range(8):
        p.dma("pool", W[:, c, :], wd["w_out"][l, c * 128:(c + 1) * 128, :], [wd["w_out"]], [W], max_dma_last_dim=4096)
    NB = 2
    yt = [sb("yt%d" % i, [128, D]) for i in range(NB)]
    ht = [sb("ht%d" % i, [128, D]) for i in range(NB)]
    srt = [sb("srt%d" % i, [128, 256]) for i in range(NB)]
    hin = [sb("hin%d" % i, [64, 4, 128]) for i in range(NB)]
    gf = [sb("gf%d" % i, [64, 4, 128]) for i in range(NB)]
    gb = [sb("gb%d" % i, [64, 4, 128]) for i in range(NB)]
    ho = [sb("ho%d" % i, [128, D]) for i in range(NB)]
    yT = [sb("yT%d" % i, [128, 8, 128], BF16) for i in range(NB)]
    sq = sb("sq", [128, D])
    st_ = [sb("st%d" % i, [128, 16]) for i in range(NB)]
    pst, pso, psx = PS[0:2], PS[2:6], PS[6:8]
    def phaseA(i):
        bi = i % NB
        rows = slice(i * 128, (i + 1) * 128)
        y, h, sr, s = yt[bi], ht[bi], srt[bi], st_[bi]
        p.dma("sp", y[:, 0:512], SY[rows, :], [SY], [y])
        p.dma("sp", sr[:], S1[rows, 2048:2304], [S1], [sr])
        p.dma("sp", h[:], Hin[rows, :], [Hin], [h])
        if i < 64:
            p.dma("sp", hin[bi][:], YL[:, i, :, :].rearrange("g c j -> c g j"), [YL], [hin[bi]])
            f0, b0 = 256 + i * 128, 256 + (63 - i) * 128
        else:
            c = i - 64
            p.dma("sp", hin[bi][:], YC[:, :, c * 128:(c + 1) * 128].rearrange("g c j -> c g j"), [YC], [hin[bi]])
            f0, b0 = c * 128, (1 - c) * 128
        p.dma("sp", gf[bi][:], GO[:, 0, :, f0:f0 + 128].rearrange("h v j -> v h j"), [GO], [gf[bi]])
        p.dma("sp", gb[bi][:], GO[:, 1, :, b0:b0 + 128].rearrange("h v j -> v h j"), [GO], [gb[bi]])
        p.tt("pool", gf[bi][:], gf[bi][:], gb[bi][:, :, ::-1], ALU.add, [gf[bi], gb[bi]], [gf[bi]])
        for a in range(4):
            p.tr(psx[0][:, a * 64:(a + 1) * 64], hin[bi][:, a, :], ident[0:64, 0:64], [hin[bi], ident], [psx[0]])
            p.tr(psx[1][:, a * 64:(a + 1) * 64], gf[bi][:, a, :], ident[0:64, 0:64], [gf[bi], ident], [psx[1]])
        p.act(y[:, 512:768], psx[0][:, 0:256], AF.Identity, [psx[0]], [y])
        p.act(y[:, 768:1024], psx[1][:, 0:256], AF.Identity, [psx[1]], [y])
        p.tt("pool", sq[:], y[:], y[:], ALU.mult, [y], [sq])
        p.op("dve", lambda e: e.tensor_reduce(out=s[:], in_=sq[:].rearrange("p (h d) -> p h d", d=64), axis=AX.X, op=ALU.add), [sq], [s])
        p.rsqrt(s, s[:], s, s[:], 1.0 / 64, epst)
        y3 = y[:].rearrange("p (h d) -> p h d", d=64)
        p.tt("dve", y3, y3, s[:].unsqueeze(2).broadcast_to([128, 16, 64]), ALU.mult, [y, s], [y])
        p.tt("pool", y[:, 768:1024], y[:, 768:1024], sr[:], ALU.mult, [y, sr], [y])
        for c in range(8):
            ps = pst[c // 4]
            blk = ps[:, (c % 4) * 128:(c % 4 + 1) * 128]
            p.tr(blk, y[:, c * 128:(c + 1) * 128], ident[:], [y, ident], [ps])
            p.act(yT[bi][:, c, :], blk, AF.Identity, [ps, og], [yT[bi]], scale=og[:, c:c + 1])

    def phaseB(i):
        kind = 0 if i < 64 else 1
        bi = i % NB
        rows = slice(i * 128, (i + 1) * 128)
        h = ht[bi]
        for hf in range(2):
            ps = pso[(2 * i + hf) % 4]
            cols = slice(hf * 512, (hf + 1) * 512)
            for c in range(8):
                p.mm(ps[:], yT[bi][:, c, :], W[:, c, cols], c == 0, c == 7, [yT[bi], W], [ps])
            p.tt("dve", ho[bi][:, cols], ps[:], g1[kind][:, cols], ALU.mult, [ps, g1[kind]], [ho[bi]])
            p.tt("pool", ho[bi][:, cols], ho[bi][:, cols], h[:, cols], ALU.add, [ho[bi], h], [ho[bi]])
        p.dma("sp", H1[rows, :], ho[bi][:], [ho[bi]], [H1])

    phaseA(0)
    for i in range(NTT):
        if i + 1 < NTT:
            phaseA(i + 1)
        phaseB(i)
    st.close()


def emit_l5b(p, PS, l, H1, Hout, MOD, wd, ident_d, final):
    st = Stage(p)
    sb = st.sb
    W1 = sb("W1", [128, 8, FFN], BF16)
    W3 = sb("W3", [128, 8, FFN], BF16)
    W2 = sb("W2", [128, NF, D], BF16)
    A = sb("A", [128, 8, 2])
    ng = sb("ng", [128, 8])
    g2 = [sb("g2_%d" % k, [128, D]) for k in range(3 if final else 2)]
    ident = sb("ident", [128, 128])
    epst = sb("eps", [128, 1])
    p.op("dve", lambda e: e.memset(epst[:], EPS), [], [epst])
    p.dma("sp", ident[:], ident_d[:], [ident_d], [ident])
    p.dma("sp", ng[:], wd["norm2_g"][l], [wd["norm2_g"]], [ng])
    for k in range(2):
        p.dma("sp", g2[k][:], MOD[l, k:k + 1, 5 * D:6 * D].partition_broadcast(128)[:, 0, :], [MOD], [g2[k]])
    if final:
        p.dma("sp", g2[2][:], wd["final_norm_g"][:].partition_broadcast(128)[:, 0, :], [wd["final_norm_g"]], [g2[2]])
    for c in range(8):
        p.dma("pool", W1[:, c, :], wd["ffn_w1"][l, c * 128:(c + 1) * 128, :], [wd["ffn_w1"]], [W1], max_dma_last_dim=4096)
        p.dma("pool", W3[:, c, :], wd["ffn_w3"][l, c * 128:(c + 1) * 128, :], [wd["ffn_w3"]], [W3], max_dma_last_dim=4096)
    for f in range(NF):
        p.dma("pool", W2[:, f, :], wd["ffn_w2"][l, f * 128:(f + 1) * 128, :], [wd["ffn_w2"]], [W2], max_dma_last_dim=4096)
    modT = load_modT(p, st, PS, MOD, l, ident)
    for k in range(2):
        p.stt(A[:, :, k], modT[:, k * 48 + 32:k * 48 + 40], 1.0, ng[:], ALU.add, ALU.mult, [modT, ng], [A])
    G = 2
    hts = [sb("ht%d" % i, [128, D]) for i in range(2 * G)]
    xs = sb("xs", [128, D])
    junk = sb("junk", [128, D])
    u2T = [sb("u2T%d" % i, [128, 8, G * 128], BF16) for i in range(2)]
    hidT = sb("hidT", [128, NF, G * 128], BF16)
    sil = [sb("sil%d" % i, [128, G * 128]) for i in range(2)]
    ho = [sb("ho%d" % i, [128, D]) for i in range(2)]
    st_ = sb("st", [128, 8])
    pst, psu, psd = PS[0:2], PS[2:6], PS[6:8]
    groups = [list(range(a, a + G)) for a in range(0, 64, G)] + [[64, 65]]
    nd = 0
    for gi, tiles in enumerate(groups):
        kind = 0 if tiles[0] < 64 else 1
        N = len(tiles) * 128
        u = u2T[gi % 2]
        for j, i in enumerate(tiles):
            h = hts[(gi % 2) * G + j]
            rows = slice(i * 128, (i + 1) * 128)
            p.dma("sp", h[:], H1[rows, :], [H1], [h])
            sc = st_[:, 2 * j:2 * j + 1]
            sr_ = st_[:, 2 * j + 1:2 * j + 2]
            p.act(junk[:], h[:], AF.Square, [h], [junk, st_], accum_out=sc)
            p.rsqrt(st_, sr_, st_, sc, 1.0 / D, epst)
            p.act(xs[:], h[:], AF.Identity, [h, st_], [xs], scale=sr_)
            for c in range(8):
                ps = pst[c // 4]
                blk = ps[:, (c % 4) * 128:(c % 4 + 1) * 128]
                p.tr(blk, xs[:, c * 128:(c + 1) * 128], ident[:], [xs, ident], [ps])
                p.act(u[:, c, j * 128:(j + 1) * 128], blk, AF.Identity, [ps, A, modT], [u],
                      scale=A[:, c, kind:kind + 1], bias=modT[:, kind * 48 + 24 + c:kind * 48 + 25 + c])
        for f in range(NF):
            ps1 = psu[(2 * f) % 4]
            ps3 = psu[(2 * f + 1) % 4]
            fc = slice(f * 128, (f + 1) * 128)
            for c in range(8):
                p.mm(ps1[:, 0:N], W1[:, c, fc], u[:, c, 0:N], c == 0, c == 7, [W1, u], [ps1])
            for c in range(8):
                p.mm(ps3[:, 0:N], W3[:, c, fc], u[:, c, 0:N], c == 0, c == 7, [W3, u], [ps3])
            s_ = sil[f % 2]
            p.act(s_[:, 0:N], ps1[:, 0:N], AF.Silu, [ps1], [s_])
            p.tt("dve", hidT[:, f, 0:N], s_[:, 0:N], ps3[:, 0:N], ALU.mult, [s_, ps3], [hidT])
        for j, i in enumerate(tiles):
            h = hts[(gi % 2) * G + j]
            rows = slice(i * 128, (i + 1) * 128)
            o = ho[nd % 2]
            nd += 1
            for hf in range(2):
                ps = psd[hf]
                cols = slice(hf * 512, (hf + 1) * 512)
                for f in range(NF):
                    p.mm(ps[:], hidT[:, f, j * 128:(j + 1) * 128], W2[:, f, cols], f == 0, f == NF - 1, [hidT, W2], [ps])
                p.tt("dve", o[:, cols], ps[:], g2[kind][:, cols], ALU.mult, [ps, g2[kind]], [o])
                p.tt("pool", o[:, cols], o[:, cols], h[:, cols], ALU.add, [o, h], [o])
            if final:
                sc = st_[:, 4:5]
                sr_ = st_[:, 5:6]
                p.act(junk[:], o[:], AF.Square, [o], [junk, st_], accum_out=sc)
                p.rsqrt(st_, sr_, st_, sc, 1.0 / D, epst)
                p.stt(o[:], o[:], sr_, g2[2][:], ALU.mult, ALU.mult, [o, st_, g2[2]], [o])
            p.dma("sp", Hout[rows, :], o[:], [o], [Hout])
    st.close()


FUSED_W = ["ada_w", "ada_b", "w_in", "qkg", "gla_gate_w", "gla_gate_b", "norm1_g", "norm2_g", "out_norm_g", "w_out",
           "ffn_w1", "ffn_w3", "ffn_w2", "final_norm_g", "cw", "hbc", "hb", "fw3", "filt_w1", "filt_w2", "fb"]
FUSED_C = ["cs", "ident", "mask", "J", "pm", "zl", "zc", "wl", "wc", "F1", "TWf", "F2", "G2", "TWi", "G1"]


def fused_host_inputs(w):
    rope = rope_table()
    one = np.concatenate([np.ones((CTX, 32), np.float32), np.zeros((CTX, 32), np.float32)], axis=1)
    j = np.arange(128)
    zl, win_l = hy_features(SEQ)
    zc, win_c = hy_features(CTX)
    consts = hy_consts()
    consts.update({
        "cs": np.concatenate([rope, one], axis=0),
        "ident": np.eye(128, dtype=np.float32),
        "mask": ((j[:, None] // 64 == j[None, :] // 64) & (j[None, :] >= j[:, None])).astype(np.float32),
        "J": np.eye(128, dtype=np.float32)[::-1].copy(),
        "pm": L4_PM,
        "zl": zl, "zc": zc,
        "wl": win_l.reshape(64, 128, 4, 64).transpose(2, 0, 3, 1),
        "wc": win_c.reshape(CTX, 4, 64).transpose(1, 2, 0),
    })
    fmL = lambda a: np.stack([fm(a[l]) for l in range(DEPTH)])
    cwf = w["hy_conv_w"].reshape(DEPTH, 3, 3, 4, 64)
    cbf = w["hy_conv_b"].reshape(DEPTH, 3, 4, 64)
    cw = np.concatenate([cwf.transpose(0, 3, 4, 2, 1), cbf.transpose(0, 2, 3, 1)[..., None]], axis=-1)
    hb = w["hy_bias"].reshape(DEPTH, 2, 4, 64).transpose(0, 2, 1, 3)
    ws = {
        "ada_w": w["ada_w"], "ada_b": w["ada_b"], "w_in": w["w_in"],
        "qkg": np.concatenate([w["q_norm_g"], w["k_norm_g"]], axis=1),
        "gla_gate_w": w["gla_gate_w"], "gla_gate_b": w["gla_gate_b"].reshape(DEPTH, 256),
        "norm1_g": fmL(w["norm1_g"]), "norm2_g": fmL(w["norm2_g"]), "out_norm_g": fmL(w["out_norm_g"]),
        "w_out": w["w_out"], "ffn_w1": w["ffn_w1"], "ffn_w3": w["ffn_w3"], "ffn_w2": w["ffn_w2"],
        "final_norm_g": w["final_norm_g"].reshape(1, D),
        "cw": cw, "hbc": hb.transpose(0, 1, 3, 2), "hb": hb,
        "fw3": w["filt_w3"].reshape(DEPTH, 64, 4, 4, 64).transpose(0, 3, 1, 2, 4),
        "filt_w1": w["filt_w1"], "filt_w2": w["filt_w2"],
        "fb": np.stack([w["filt_freq"], w["filt_b1"], w["filt_b2"]], axis=-1),
    }
    shared = {k: np.ascontiguousarray(v, dtype=np.float32) for k, v in {**ws, **consts}.items()}
    in_maps = []
    for i in range(NCORES):
        b = i // 4
        m = dict(shared)
        m["H0"] = np.ascontiguousarray(np.concatenate([w["x"][b], w["ctx"][b]], axis=0))
        cvec = np.stack([w["c"][b], w["c_ctx"]], axis=0)
        m["cT"] = np.ascontiguousarray(cvec.T.reshape(8, 128, 2).transpose(1, 0, 2))
        in_maps.append(m)
    return in_maps


def build_fused(shapes, nlayers=DEPTH, final=True, dump=()):
    p = Prog()
    H0 = p.dram_in("H0", [TT, D])
    cT = p.dram_in("cT", [128, 8, 2])
    wd = {k: p.dram_in(k, shapes[k]) for k in FUSED_W}
    cd = {k: p.dram_in(k, shapes[k]) for k in FUSED_C}
    out = p.dram_out("out", [TT, D])
    MOD = scratch(p, "MOD", [DEPTH, 2, 6 * D])
    S1 = scratch(p, "S1", [TT, OUT1])
    SY = scratch(p, "SY", [TT, 512])
    YL = scratch(p, "YL", [4, 64, 64, 128])
    YC = scratch(p, "YC", [4, 64, CTX])
    GO = scratch(p, "GO", [4, 2, 64, TT])
    H1 = scratch(p, "H1", [TT, D])
    HA = scratch(p, "HA", [TT, D])
    scr = scratch(p, "scr", [3, 64, SEQ])
    scr2 = scratch(p, "scr2", [64, 64, 128])
    PS2 = [p.ps("Q%d" % i, (128, 1024)) for i in range(4)]
    PS = [Tile(PS2[i // 2].h[:, (i % 2) * 512:(i % 2 + 1) * 512]) for i in range(8)]
    emit_ada(p, PS, cT, wd["ada_w"], wd["ada_b"], MOD)
    Hin = H0
    for l in range(nlayers):
        last = l == nlayers - 1
        emit_l1(p, PS, l, Hin, S1, MOD, wd, cd["cs"], cd["ident"])
        emit_l2(p, PS, S1, SY, cd["ident"], PS2)
        emit_l3(p, PS, l, S1, YL, YC, wd, cd, scr, scr2)
        emit_l4(p, PS, S1, GO, cd)
        emit_l5a(p, PS, l, SY, YL, YC, GO, S1, Hin, H1, MOD, wd, cd["ident"])
        emit_l5b(p, PS, l, H1, out if last else HA, MOD, wd, cd["ident"], final and last)
        Hin = HA
    for name in dump:
        src = {"S1": S1, "SY": SY, "YL": YL, "YC": YC, "GO": GO, "H1": H1, "MOD": MOD}[name]
        shp = list(src.h.shape)
        d = p.dram_out("dump_" + name, shp)
        flat = lambda ap: ap.rearrange(" ".join("abcd"[:len(shp)]) + " -> " + ("(" + " ".join("abcd"[:len(shp) - 1]) + ") " + "abcd"[len(shp) - 1] if len(shp) > 2 else "a b"))
        p.dma("sp", flat(d[:]), flat(src[:]), [src], [d])
    return p


def kernel_fused(inputs, nlayers=DEPTH, final=True, dump=()):
    w = {k: np.ascontiguousarray(np.asarray(v, dtype=np.float32)) for k, v in inputs.items()}
    in_maps = fused_host_inputs(w)
    shapes = {k: list(v.shape) for k, v in in_maps[0].items()}
    res = run(build_fused(shapes, nlayers, final, dump), in_maps)
    return res


def kernel(**inputs):
    res = kernel_fused(inputs)
    return np.ascontiguousarray(np.stack([res[0]["out"][:SEQ], res[4]["out"][:SEQ]], axis=0), dtype=np.float32)
```

```python
import math
import numpy as np
import concourse.bass as bass
import concourse.mybir as mybir
from concourse.bass_utils import run_bass_kernel_spmd

F32 = mybir.dt.float32
BF16 = mybir.dt.bfloat16
AF = mybir.ActivationFunctionType
ALU = mybir.AluOpType
AX = mybir.AxisListType

NCORES = 8
D = 1024
SEQ = 8192
CTX = 256
DEPTH = 4
IN_W = 2336
FFN = 2816
EPS = 1e-6
NSLOT = 8


class Tile:
    def __init__(self, h):
        self.h = h
        self.w = None
        self.r = {}

    def __getitem__(self, idx):
        return self.h[idx]


class Prog:
    def __init__(self, self_sync=True):
        self.nc = bass.Bass("TRN2", target_bir_lowering=False)
        nc = self.nc
        self.eng = {"pe": nc.tensor, "dve": nc.vector, "act": nc.scalar, "pool": nc.gpsimd, "sp": nc.sync}
        self.sems = {}
        self.cnt = {}
        self.known = {e: {} for e in self.eng}
        self.dslot = {"sp": 0, "act": 0, "pool": 0}
        self.self_sync = self_sync
        self._sem_cms = []
        self.n_inst = 0

    def _sem(self, sid):
        if sid not in self.sems:
            cm = self.nc.semaphore(sid)
            self._sem_cms.append(cm)
            self.sems[sid] = cm.__enter__()
            self.cnt[sid] = 0
        return self.sems[sid]

    def sb(self, name, shape, dtype=F32):
        return Tile(self.nc.alloc_sbuf_tensor("sb_" + name, list(shape), dtype))

    def ps(self, name, shape=(128, 512), dtype=F32):
        return Tile(self.nc.alloc_psum_tensor("ps_" + name, list(shape), dtype))

    def dram_in(self, name, shape, dtype=F32):
        return Tile(self.nc.dram_tensor(name, list(shape), dtype, kind="ExternalInput").ap())

    def dram_out(self, name, shape, dtype=F32):
        return Tile(self.nc.dram_tensor(name, list(shape), dtype, kind="ExternalOutput").ap())

    def _deps(self, eng, reads, writes):
        deps = {}

        def add(sid, v):
            if deps.get(sid, 0) < v:
                deps[sid] = v

        for t in reads:
            if t.w:
                add(*t.w)
        for t in writes:
            if t.w:
                add(*t.w)
            for sid, v in t.r.items():
                add(sid, v)
        if eng == "pe" or not self.self_sync:
            deps.pop("c_" + eng, None)
        return deps

    def _wait(self, eng, deps):
        e = self.eng[eng]
        kn = self.known[eng]
        for sid, val in deps.items():
            if kn.get(sid, 0) >= val:
                continue
            e.wait_ge(self.sems[sid], val)
            kn[sid] = val

    def _mark(self, tok, reads, writes):
        sid, v = tok
        for t in reads:
            if t.r.get(sid, 0) < v:
                t.r[sid] = v
        for t in writes:
            t.w = tok
            t.r = {}

    def op(self, eng, fn, reads=(), writes=()):
        self._wait(eng, self._deps(eng, reads, writes))
        inst = fn(self.eng[eng])
        sid = "c_" + eng
        sem = self._sem(sid)
        self.cnt[sid] += 1
        inst.then_inc(sem, 1)
        self._mark((sid, self.cnt[sid]), reads, writes)
        self.n_inst += 1
        return inst

    def dma(self, q, out, in_, reads=(), writes=(), **kw):
        deps = self._deps(q, reads, writes)
        slot = self.dslot[q]
        self.dslot[q] = (slot + 1) % NSLOT
        sid = "d_%s_%d" % (q, slot)
        sem = self._sem(sid)
        if self.cnt[sid] > 0 and deps.get(sid, 0) < self.cnt[sid]:
            deps[sid] = self.cnt[sid]
        self._wait(q, deps)
        inst = self.eng[q].dma_start(out=out, in_=in_, **kw)
        self.cnt[sid] += 16
        inst.then_inc(sem, 16)
        self._mark((sid, self.cnt[sid]), reads, writes)
        self.n_inst += 1
        return inst

    def finish(self):
        deps = {sid: c for sid, c in self.cnt.items() if sid.startswith("d_") and c > 0}
        self._wait("sp", deps)
        return self.nc

    def tt(self, eng, out, in0, in1, op, reads, writes):
        return self.op(eng, lambda e: e.tensor_tensor(out=out, in0=in0, in1=in1, op=op), reads, writes)

    def ts(self, eng, out, in0, s1, op0, reads, writes, s2=None, op1=None):
        if op1 is None:
            return self.op(eng, lambda e: e.tensor_scalar(out=out, in0=in0, scalar1=s1, scalar2=None, op0=op0), reads, writes)
        return self.op(eng, lambda e: e.tensor_scalar(out=out, in0=in0, scalar1=s1, scalar2=s2, op0=op0, op1=op1), reads, writes)

    def stt(self, out, in0, scalar, in1, op0, op1, reads, writes):
        return self.op("dve", lambda e: e.scalar_tensor_tensor(out=out, in0=in0, scalar=scalar, in1=in1, op0=op0, op1=op1), reads, writes)

    def act(self, out, in_, func, reads, writes, bias=None, scale=None, accum_out=None):
        kw = {}
        if bias is not None:
            kw["bias"] = bias
        if scale is not None:
            kw["scale"] = scale
        if accum_out is not None:
            kw["accum_out"] = accum_out
        return self.op("act", lambda e: e.activation(out=out, in_=in_, func=func, **kw), reads, writes)

    def mm(self, out, lhsT, rhs, start, stop, reads, writes):
        return self.op("pe", lambda e: e.matmul(out, lhsT, rhs, start=start, stop=stop), reads, writes)

    def tr(self, out, in_, ident, reads, writes):
        return self.op("pe", lambda e: e.transpose(out, in_, ident), reads, writes)

    def rsqrt(self, out_t, out_ap, in_t, in_ap, scale, eps_t):
        self.act(out_ap, in_ap, AF.Sqrt, [in_t, eps_t], [out_t], bias=eps_t[0:in_ap.shape[0], 0:1], scale=scale)
        self.op("dve", lambda e: e.reciprocal(out=out_ap, in_=out_ap), [out_t], [out_t])


def run(prog, in_maps):
    nc = prog.finish()
    res = run_bass_kernel_spmd(nc, in_maps, core_ids=list(range(len(in_maps))))
    return res.results


ADA_COLS = DEPTH * 6 * D // NCORES


def build_ada():
    p = Prog()
    cT = p.dram_in("cT", [128, 8, 3])
    w = p.dram_in("w", [D, ADA_COLS])
    b = p.dram_in("b", [1, ADA_COLS])
    out = p.dram_out("mod", [3, ADA_COLS])
    W = p.sb("W", [128, 8, ADA_COLS])
    cs = p.sb("cs", [128, 8, 3])
    bb = p.sb("bb", [3, ADA_COLS])
    res = p.sb("res", [3, ADA_COLS])
    p.dma("sp", cs[:], cT[:], [cT], [cs])
    p.dma("sp", bb[:], b[:].partition_broadcast(3)[:, 0, :], [b], [bb])
    for c in range(8):
        p.dma("sp" if c % 2 == 0 else "pool", W[:, c, :], w[c * 128:(c + 1) * 128, :], [w], [W])
    p.act(cs[:], cs[:], AF.Silu, [cs], [cs])
    pss = [p.ps("ps%d" % i) for i in range(2)]
    for j in range(ADA_COLS // 512):
        ps = pss[j % 2]
        for c in range(8):
            p.mm(ps[0:3, :], cs[:, c, :], W[:, c, j * 512:(j + 1) * 512], c == 0, c == 7, [cs, W], [ps])
        p.tt("dve", res[:, j * 512:(j + 1) * 512], ps[0:3, :], bb[:, j * 512:(j + 1) * 512], ALU.add, [ps, bb], [res])
    p.dma("sp", out[:], res[:], [res], [out])
    return p


def run_ada(c, c_ctx, ada_w, ada_b):
    cvec = np.concatenate([c, c_ctx[None, :]], axis=0)
    cT = np.ascontiguousarray(cvec.T.reshape(8, 128, 3).transpose(1, 0, 2))
    wall = np.ascontiguousarray(ada_w.transpose(1, 0, 2).reshape(D, DEPTH * 6 * D))
    ball = ada_b.reshape(1, DEPTH * 6 * D)
    in_maps = []
    for i in range(NCORES):
        sl = slice(i * ADA_COLS, (i + 1) * ADA_COLS)
        in_maps.append({"cT": cT, "w": np.ascontiguousarray(wall[:, sl]), "b": np.ascontiguousarray(ball[:, sl])})
    res = run(build_ada(), in_maps)
    mod = np.concatenate([r["mod"] for r in res], axis=1)
    return mod.reshape(3, DEPTH, 6, D)


NT1 = 17
OUT1 = 2560
GROUPS1 = [(0, 512), (512, 1024), (1024, 1536), (1536, 2048), (2048, 2336)]


def build_l1():
    p = Prog()
    ntok = NT1 * 128
    x_tm = p.dram_in("x_tm", [ntok, D])
    x_fm = p.dram_in("x_fm", [128, 8, ntok])
    w_in = p.dram_in("w_in", [D, IN_W])
    vecs = p.dram_in("vecs", [128, 8, 5])
    qkg = p.dram_in("qkg", [1, 128])
    cs_d = p.dram_in("cs", [ntok, 64])
    gw = p.dram_in("gw", [2, 16, 128])
    gb = p.dram_in("gb", [1, 256])
    ident_d = p.dram_in("ident", [128, 128])
    out = p.dram_out("out", [ntok, OUT1])

    W = p.sb("W", [128, 8, IN_W], BF16)
    V = p.sb("V", [128, 8, 5])
    A = p.sb("A", [128, 8, 2])
    Brep = p.sb("Brep", [128, 8, 128], BF16)
    bW = [p.sb("bW%d" % k, [128, IN_W]) for k in range(2)]
    gains = p.sb("gains", [128, 10, 64])
    Wblk = p.sb("Wblk", [32, 256])
    gbb = p.sb("gbb", [128, 256])
    ident = p.sb("ident", [128, 128])
    epst = p.sb("eps", [128, 1])
    pss = [p.ps("ps%d" % i) for i in range(6)]
    pst = p.ps("pst")
    psg = p.ps("psg")

    p.op("dve", lambda e: e.memset(epst[:], EPS), [], [epst])
    p.op("dve", lambda e: e.memset(Wblk[:], 0.0), [], [Wblk])
    p.dma("sp", V[:], vecs[:], [vecs], [V])
    p.dma("sp", ident[:], ident_d[:], [ident_d], [ident])
    p.dma("sp", gbb[:], gb[:].partition_broadcast(128)[:, 0, :], [gb], [gbb])
    p.dma("sp", Wblk[0:16, 0:128], gw[0], [gw], [Wblk])
    p.dma("sp", Wblk[16:32, 128:256], gw[1], [gw], [Wblk])
    for h in range(10):
        src = qkg[:, 0:64] if h < 8 else qkg[:, 64:128]
        p.dma("sp", gains[:, h, :], src.partition_broadcast(128)[:, 0, :], [qkg], [gains])
    p.ts("dve", gains[:, 0:8, :], gains[:, 0:8, :], 0.125, ALU.mult, [gains], [gains])
    for c in range(8):
        p.dma("pool", W[:, c, :], w_in[c * 128:(c + 1) * 128, :], [w_in], [W], max_dma_last_dim=4096)
    for k in range(2):
        p.stt(A[:, :, k], V[:, :, 1 + 2 * k], 1.0, V[:, :, 0], ALU.add, ALU.mult, [V], [A])
        p.op("dve", lambda e: e.tensor_copy(out=Brep[:], in_=V[:, :, 2 + 2 * k].unsqueeze(2).broadcast_to([128, 8, 128])), [V], [Brep])
        for gi, (a, b) in enumerate(GROUPS1):
            ps = pss[gi]
            for c in range(8):
                p.mm(ps[:, 0:b - a], Brep[:, c, :], W[:, c, a:b], c == 0, c == 7, [Brep, W], [ps])
            p.act(bW[k][:, a:b], ps[:, 0:b - a], AF.Identity, [ps], [bW[k]])

    NB = 2
    xt = [p.sb("xt%d" % i, [128, D]) for i in range(NB)]
    xf = [p.sb("xf%d" % i, [128, 8, 128]) for i in range(NB)]
    xa = [p.sb("xa%d" % i, [128, 8, 128], BF16) for i in range(NB)]
    cst = [p.sb("cst%d" % i, [128, 64]) for i in range(NB)]
    O = [p.sb("O%d" % i, [128, OUT1]) for i in range(NB)]
    junk = p.sb("junk", [128, D])
    sq = p.sb("sq", [128, 640])
    tmp = p.sb("tmp", [128, 10, 32])
    st = [p.sb("st%d" % i, [128, 16]) for i in range(NB)]
    lr = p.sb("lr", [128, 32])
    lrT = p.sb("lrT", [32, 128])
    gz = p.sb("gz", [128, 256])
    ga = p.sb("ga", [128, 256])
    pi = 0
    for i in range(NT1):
        kind = 0 if i < 16 else 1
        bi = i % NB
        rows = slice(i * 128, (i + 1) * 128)
        p.dma("sp", xt[bi][:], x_tm[rows, :], [x_tm], [xt[bi]])
        p.dma("sp", xf[bi][:], x_fm[:, :, rows], [x_fm], [xf[bi]])
        p.dma("sp", cst[bi][:], cs_d[rows, :], [cs_d], [cst[bi]])
        s = st[bi]
        p.act(junk[:], xt[bi][:], AF.Square, [xt[bi]], [junk, s], accum_out=s[:, 0:1])
        p.rsqrt(s, s[:, 1:2], s, s[:, 0:1], 1.0 / D, epst)
        p.tt("dve", xa[bi][:], xf[bi][:], A[:, :, kind].unsqueeze(2).broadcast_to([128, 8, 128]), ALU.mult, [xf[bi], A], [xa[bi]])
        o = O[bi]
        for gi, (a, b) in enumerate(GROUPS1):
            ps = pss[pi % 6]
            pi += 1
            for c in range(8):
                p.mm(ps[:, 0:b - a], xa[bi][:, c, :], W[:, c, a:b], c == 0, c == 7, [xa[bi], W], [ps])
            if gi < 4:
                p.stt(o[:, a:b], ps[:, 0:b - a], s[:, 1:2], bW[kind][:, a:b], ALU.mult, ALU.add, [ps, s, bW[kind]], [o])
            else:
                p.stt(o[:, 2048:2304], ps[:, 0:256], s[:, 1:2], bW[kind][:, 2048:2304], ALU.mult, ALU.add, [ps, s, bW[kind]], [o])
                p.stt(lr[:], ps[:, 256:288], s[:, 1:2], bW[kind][:, 2304:2336], ALU.mult, ALU.add, [ps, s, bW[kind]], [lr])
        qk = o[:, 0:640]
        p.tt("pool", sq[:], qk, qk, ALU.mult, [o], [sq])
        p.op("dve", lambda e: e.tensor_reduce(out=s[:, 2:12], in_=sq[:].rearrange("p (h d) -> p h d", d=64), axis=AX.X, op=ALU.add), [sq], [s])
        p.rsqrt(s, s[:, 2:12], s, s[:, 2:12], 1.0 / 64, epst)
        qk3 = qk.rearrange("p (h d) -> p h d", d=64)
        p.tt("dve", qk3, qk3, s[:, 2:12].unsqueeze(2).broadcast_to([128, 10, 64]), ALU.mult, [o, s], [o])
        p.tt("pool", qk3, qk3, gains[:], ALU.mult, [o, gains], [o])
        x1 = qk3[:, :, 0:32]
        x2 = qk3[:, :, 32:64]
        cb = cst[bi][:, 0:32].unsqueeze(1).broadcast_to([128, 10, 32])
        sb_ = cst[bi][:, 32:64].unsqueeze(1).broadcast_to([128, 10, 32])
        t3 = sq[:, 0:320].rearrange("p (h d) -> p h d", d=32)
        t4 = sq[:, 320:640].rearrange("p (h d) -> p h d", d=32)
        p.tt("dve", tmp[:], x2, sb_, ALU.mult, [o, cst[bi]], [tmp])
        p.tt("pool", t3, x1, sb_, ALU.mult, [o, cst[bi]], [sq])
        p.tt("dve", x1, x1, cb, ALU.mult, [o, cst[bi]], [o])
        p.tt("dve", x1, x1, tmp[:], ALU.subtract, [o, tmp], [o])
        p.tt("dve", x2, x2, cb, ALU.mult, [o, cst[bi]], [o])
        p.tt("dve", x2, x2, t3, ALU.add, [o, sq], [o])
        p.act(o[:, 1536:1664], o[:, 1536:1664], AF.Identity, [o], [o], scale=32 ** -0.5)
        p.act(o[:, 2048:2304], o[:, 2048:2304], AF.Silu, [o], [o])
        p.tr(pst[0:32, 0:128], lr[:], ident[:], [lr, ident], [pst])
        p.act(lrT[:], pst[0:32, 0:128], AF.Identity, [pst], [lrT])
        p.mm(psg[:, 0:256], lrT[:], Wblk[:], True, True, [lrT, Wblk], [psg])
        p.tt("dve", gz[:], psg[:, 0:256], gbb[:], ALU.add, [psg, gbb], [gz])
        p.stt(ga[:], gz[:], -1.0, gz[:], ALU.mult, ALU.min, [gz], [ga])
        p.act(ga[:], ga[:], AF.Exp, [ga], [ga])
        p.act(ga[:], ga[:], AF.Ln, [ga], [ga], bias=1.0)
        p.ts("dve", gz[:], gz[:], 0.0, ALU.min, [gz], [gz], s2=1.0 / 16, op1=ALU.mult)
        p.stt(o[:, 2304:2560], ga[:], -1.0 / 16, gz[:], ALU.mult, ALU.add, [ga, gz], [o])
        p.dma("sp", out[rows, :], o[:], [o], [out])
    return p


def fm(v):
    return np.ascontiguousarray(v.reshape(-1, 128).T)


def tok_split(lat, ctx):
    F = lat.shape[-1]
    outs = []
    for i in range(NCORES):
        b, j = divmod(i, 4)
        a = np.zeros((NT1 * 128, F), lat.dtype)
        a[:2048] = lat[b, j * 2048:(j + 1) * 2048]
        if j < 2:
            a[2048:] = ctx[b, j * 128:(j + 1) * 128]
        outs.append(a)
    return outs


def tok_merge(per_core):
    F = per_core[0].shape[-1]
    lat = np.zeros((2, SEQ, F), per_core[0].dtype)
    ctx = np.zeros((2, CTX, F), per_core[0].dtype)
    for i in range(NCORES):
        b, j = divmod(i, 4)
        lat[b, j * 2048:(j + 1) * 2048] = per_core[i][:2048]
        if j < 2:
            ctx[b, j * 128:(j + 1) * 128] = per_core[i][2048:]
    return lat, ctx


def rope_table():
    t = np.arange(SEQ)
    row = (t // 64).astype(np.float32)
    col = (t % 64).astype(np.float32)
    inv = np.power(np.float32(10000.0), -np.arange(16, dtype=np.float32) / np.float32(16)).astype(np.float32)
    ang = np.concatenate([row[:, None] * inv, col[:, None] * inv], axis=-1).astype(np.float32)
    return np.concatenate([np.cos(ang), np.sin(ang)], axis=-1).astype(np.float32)


_PROGS = {}


def get_prog(name, builder):
    return builder()


def run_l1(h_lat, h_ctx, mod, layer, w):
    xs = tok_split(h_lat, h_ctx)
    rope = rope_table()
    one = np.concatenate([np.ones((128, 32), np.float32), np.zeros((128, 32), np.float32)], axis=1)
    ident = np.eye(128, dtype=np.float32)
    in_maps = []
    for i in range(NCORES):
        b, j = divmod(i, 4)
        x = xs[i]
        vecs = np.stack([fm(w["norm1_g"][layer]), fm(mod[b, layer, 1]), fm(mod[b, layer, 0]),
                         fm(mod[2, layer, 1]), fm(mod[2, layer, 0])], axis=-1)
        cs = np.concatenate([rope[j * 2048:(j + 1) * 2048], one], axis=0)
        in_maps.append({
            "x_tm": x,
            "x_fm": np.ascontiguousarray(x.T.reshape(8, 128, NT1 * 128).transpose(1, 0, 2)),
            "w_in": w["w_in"][layer],
            "vecs": np.ascontiguousarray(vecs),
            "qkg": np.concatenate([w["q_norm_g"][layer], w["k_norm_g"][layer]])[None, :],
            "cs": cs,
            "gw": w["gla_gate_w"][layer],
            "gb": w["gla_gate_b"][layer].reshape(1, 256),
            "ident": ident,
        })
    res = run(build_l1(), in_maps)
    return tok_merge([r["out"] for r in res])


NQT = 33
NKT = 66


def build_l2():
    p = Prog()
    qT_d = p.dram_in("qT", [64, NQT, 512])
    kT_d = p.dram_in("kT", [64, NKT * 128])
    vA_d = p.dram_in("vA", [128, NKT, 65])
    ident_d = p.dram_in("ident", [128, 128])
    out = p.dram_out("out", [NQT * 128, 256])

    qT = p.sb("qT", [64, NQT, 512], BF16)
    kT = p.sb("kT", [64, NKT * 128], BF16)
    vA = p.sb("vA", [128, NKT, 65], BF16)
    ident = p.sb("ident", [128, 128])
    p.dma("sp", ident[:], ident_d[:], [ident_d], [ident])
    p.dma("pool", kT[:, 0:4096], kT_d[:, 0:4096], [kT_d], [kT], max_dma_last_dim=4096)
    p.dma("pool", kT[:, 4096:], kT_d[:, 4096:], [kT_d], [kT], max_dma_last_dim=4096)
    p.dma("pool", vA[:], vA_d[:], [vA_d], [vA], max_dma_last_dim=4096)
    for a in range(0, NQT, 3):
        p.dma("pool", qT[:, a:a + 3, :], qT_d[:, a:a + 3, :], [qT_d], [qT], max_dma_last_dim=4096)

    psS = [p.ps("psS%d" % i) for i in range(3)]
    psO = [p.ps("psO%d" % i) for i in range(2)]
    psT = [p.ps("psT%d" % i) for i in range(2)]
    pT = [p.sb("pT%d" % i, [128, 512], BF16) for i in range(3)]
    oT = [p.sb("oT%d" % i, [65, 512]) for i in range(2)]
    ot = [p.sb("ot%d" % i, [128, 256]) for i in range(2)]
    rec = p.sb("rec", [128, 8])
    it = 0
    for qt in range(NQT):
        kts = list(range(NKT)) if qt < 32 else [64, 65]
        po = psO[qt % 2]
        for j, kt in enumerate(kts):
            ps = psS[it % 3]
            pt = pT[it % 3]
            it += 1
            p.mm(ps[:], kT[:, kt * 128:(kt + 1) * 128], qT[:, qt, :], True, True, [kT, qT], [ps])
            p.act(pt[:], ps[:], AF.Exp, [ps], [pt])
            p.mm(po[0:65, :], vA[:, kt, :], pt[:], j == 0, j == len(kts) - 1, [vA, pt], [po])
        o_sb = oT[qt % 2]
        p.op("dve", lambda e: e.tensor_copy(out=o_sb[:], in_=po[0:65, :]), [po], [o_sb])
        o_t = ot[qt % 2]
        for h in range(4):
            pst = psT[h % 2]
            p.tr(pst[:, 0:65], o_sb[:, h * 128:(h + 1) * 128], ident[0:65, 0:65], [o_sb, ident], [pst])
            rc = rec[:, (qt % 2) * 4 + h:(qt % 2) * 4 + h + 1]
            p.op("dve", lambda e: e.reciprocal(out=rc, in_=pst[:, 64:65]), [pst], [rec])
            p.ts("dve", o_t[:, h * 64:(h + 1) * 64], pst[:, 0:64], rc, ALU.mult, [pst, rec], [o_t])
        p.dma("sp", out[qt * 128:(qt + 1) * 128, :], o_t[:], [o_t], [out])
    return p


def run_l2(lat, ctx):
    ident = np.eye(128, dtype=np.float32)
    in_maps = []
    for i in range(NCORES):
        b, g, qh = i // 4, (i // 2) % 2, i % 2
        ql = lat[b, qh * 4096:(qh + 1) * 4096, 0:512].reshape(32, 128, 8, 64)[:, :, 4 * g:4 * g + 4, :]
        qc = ctx[b, qh * 128:(qh + 1) * 128, 0:512].reshape(1, 128, 8, 64)[:, :, 4 * g:4 * g + 4, :]
        q = np.concatenate([ql, qc], axis=0)
        qT = np.ascontiguousarray(q.transpose(3, 0, 2, 1)).reshape(64, NQT, 512)
        k_all = np.concatenate([lat[b, :, 512:640], ctx[b, :, 512:640]], axis=0)[:, g * 64:(g + 1) * 64]
        kT = np.ascontiguousarray(k_all.T)
        v_all = np.concatenate([lat[b, :, 640:768], ctx[b, :, 640:768]], axis=0)[:, g * 64:(g + 1) * 64]
        vA = np.ones((128, NKT, 65), np.float32)
        vA[:, :, 0:64] = v_all.reshape(NKT, 128, 64).transpose(1, 0, 2)
        in_maps.append({"qT": qT, "kT": kT, "vA": vA, "ident": ident})
    res = run(build_l2(), in_maps)
    a_lat = np.zeros((2, SEQ, 512), np.float32)
    a_ctx = np.zeros((2, CTX, 512), np.float32)
    for i in range(NCORES):
        b, g, qh = i // 4, (i // 2) % 2, i % 2
        o = res[i]["out"]
        a_lat[b, qh * 4096:(qh + 1) * 4096, g * 256:(g + 1) * 256] = o[:4096]
        a_ctx[b, qh * 128:(qh + 1) * 128, g * 256:(g + 1) * 256] = o[4096:]
    return a_lat, a_ctx


def build_l5a():
    p = Prog()
    ntok = NT1 * 128
    y_d = p.dram_in("y", [ntok, D])
    sr_d = p.dram_in("sr", [ntok, 256])
    yb_d = p.dram_in("yb", [ntok, 256])
    h_d = p.dram_in("h", [ntok, D])
    w_d = p.dram_in("w_out", [D, D])
    og_d = p.dram_in("og", [128, 8])
    g1_d = p.dram_in("g1", [2, D])
    ident_d = p.dram_in("ident", [128, 128])
    out = p.dram_out("out", [ntok, D])

    W = p.sb("W", [128, 8, D], BF16)
    og = p.sb("og", [128, 8])
    g1 = [p.sb("g1_%d" % k, [128, D]) for k in range(2)]
    ident = p.sb("ident", [128, 128])
    epst = p.sb("eps", [128, 1])
    p.op("dve", lambda e: e.memset(epst[:], EPS), [], [epst])
    p.dma("sp", og[:], og_d[:], [og_d], [og])
    p.dma("sp", ident[:], ident_d[:], [ident_d], [ident])
    for k in range(2):
        p.dma("sp", g1[k][:], g1_d[k:k + 1, :].partition_broadcast(128)[:, 0, :], [g1_d], [g1[k]])
    for c in range(8):
        p.dma("pool", W[:, c, :], w_d[c * 128:(c + 1) * 128, :], [w_d], [W], max_dma_last_dim=4096)
    NB = 2
    yt = [p.sb("yt%d" % i, [128, D]) for i in range(NB)]
    ht = [p.sb("ht%d" % i, [128, D]) for i in range(NB)]
    srt = [p.sb("srt%d" % i, [128, 256]) for i in range(NB)]
    ybt = [p.sb("ybt%d" % i, [128, 256]) for i in range(NB)]
    ho = [p.sb("ho%d" % i, [128, D]) for i in range(NB)]
    yT = [p.sb("yT%d" % i, [128, 8, 128], BF16) for i in range(NB)]
    sq = p.sb("sq", [128, D])
    st = [p.sb("st%d" % i, [128, 16]) for i in range(NB)]
    pst = [p.ps("pst%d" % i) for i in range(2)]
    pso = [p.ps("pso%d" % i) for i in range(4)]
    for i in range(NT1):
        kind = 0 if i < 16 else 1
        bi = i % NB
        rows = slice(i * 128, (i + 1) * 128)
        y, h, sr, s = yt[bi], ht[bi], srt[bi], st[bi]
        p.dma("sp", y[:], y_d[rows, :], [y_d], [y])
        p.dma("sp", sr[:], sr_d[rows, :], [sr_d], [sr])
        p.dma("sp", h[:], h_d[rows, :], [h_d], [h])
        p.dma("sp", ybt[bi][:], yb_d[rows, :], [yb_d], [ybt[bi]])
        p.tt("pool", y[:, 768:1024], y[:, 768:1024], ybt[bi][:], ALU.add, [y, ybt[bi]], [y])
        p.tt("pool", sq[:], y[:], y[:], ALU.mult, [y], [sq])
        p.op("dve", lambda e: e.tensor_reduce(out=s[:], in_=sq[:].rearrange("p (h d) -> p h d", d=64), axis=AX.X, op=ALU.add), [sq], [s])
        p.rsqrt(s, s[:], s, s[:], 1.0 / 64, epst)
        y3 = y[:].rearrange("p (h d) -> p h d", d=64)
        p.tt("dve", y3, y3, s[:].unsqueeze(2).broadcast_to([128, 16, 64]), ALU.mult, [y, s], [y])
        p.tt("pool", y[:, 768:1024], y[:, 768:1024], sr[:], ALU.mult, [y, sr], [y])
        for c in range(8):
            ps = pst[c // 4]
            blk = ps[:, (c % 4) * 128:(c % 4 + 1) * 128]
            p.tr(blk, y[:, c * 128:(c + 1) * 128], ident[:], [y, ident], [ps])
            p.act(yT[bi][:, c, :], blk, AF.Identity, [ps, og], [yT[bi]], scale=og[:, c:c + 1])
        for hf in range(2):
            ps = pso[(2 * i + hf) % 4]
            cols = slice(hf * 512, (hf + 1) * 512)
            for c in range(8):
                p.mm(ps[:], yT[bi][:, c, :], W[:, c, cols], c == 0, c == 7, [yT[bi], W], [ps])
            p.tt("dve", ho[bi][:, cols], ps[:], g1[kind][:, cols], ALU.mult, [ps, g1[kind]], [ho[bi]])
            p.tt("pool", ho[bi][:, cols], ho[bi][:, cols], h[:, cols], ALU.add, [ho[bi], h], [ho[bi]])
        p.dma("sp", out[rows, :], ho[bi][:], [ho[bi]], [out])
    return p


def run_l5a(y_lat, y_ctx, yb_lat, yb_ctx, sr_lat, sr_ctx, h_lat, h_ctx, mod, layer, w):
    ys = tok_split(y_lat, y_ctx)
    ybs = tok_split(yb_lat, yb_ctx)
    srs = tok_split(sr_lat, sr_ctx)
    hs = tok_split(h_lat, h_ctx)
    ident = np.eye(128, dtype=np.float32)
    in_maps = []
    for i in range(NCORES):
        b = i // 4
        in_maps.append({"y": ys[i], "yb": ybs[i], "sr": srs[i], "h": hs[i], "w_out": w["w_out"][layer],
                        "og": fm(w["out_norm_g"][layer]),
                        "g1": np.ascontiguousarray(np.stack([mod[b, layer, 2], mod[2, layer, 2]])),
                        "ident": ident})
    res = run(build_l5a(), in_maps)
    return tok_merge([r["out"] for r in res])


NF = FFN // 128


def build_l5b(final):
    p = Prog()
    ntok = NT1 * 128
    h_d = p.dram_in("h", [ntok, D])
    w1_d = p.dram_in("w1", [D, FFN])
    w3_d = p.dram_in("w3", [D, FFN])
    w2_d = p.dram_in("w2", [FFN, D])
    vec_d = p.dram_in("vecs", [128, 8, 5])
    g2_d = p.dram_in("g2", [3, D])
    ident_d = p.dram_in("ident", [128, 128])
    out = p.dram_out("out", [ntok, D])

    W1 = p.sb("W1", [128, 8, FFN], BF16)
    W3 = p.sb("W3", [128, 8, FFN], BF16)
    W2 = p.sb("W2", [128, NF, D], BF16)
    V = p.sb("V", [128, 8, 5])
    A = p.sb("A", [128, 8, 2])
    g2 = [p.sb("g2_%d" % k, [128, D]) for k in range(3 if final else 2)]
    ident = p.sb("ident", [128, 128])
    epst = p.sb("eps", [128, 1])
    p.op("dve", lambda e: e.memset(epst[:], EPS), [], [epst])
    p.dma("sp", V[:], vec_d[:], [vec_d], [V])
    p.dma("sp", ident[:], ident_d[:], [ident_d], [ident])
    for k in range(len(g2)):
        p.dma("sp", g2[k][:], g2_d[k:k + 1, :].partition_broadcast(128)[:, 0, :], [g2_d], [g2[k]])
    for c in range(8):
        p.dma("pool", W1[:, c, :], w1_d[c * 128:(c + 1) * 128, :], [w1_d], [W1], max_dma_last_dim=4096)
        p.dma("pool", W3[:, c, :], w3_d[c * 128:(c + 1) * 128, :], [w3_d], [W3], max_dma_last_dim=4096)
    for f in range(NF):
        p.dma("pool", W2[:, f, :], w2_d[f * 128:(f + 1) * 128, :], [w2_d], [W2], max_dma_last_dim=4096)
    for k in range(2):
        p.stt(A[:, :, k], V[:, :, 1 + 2 * k], 1.0, V[:, :, 0], ALU.add, ALU.mult, [V], [A])

    G = 2
    hts = [p.sb("ht%d" % i, [128, D]) for i in range(2 * G)]
    xs = p.sb("xs", [128, D])
    junk = p.sb("junk", [128, D])
    u2T = [p.sb("u2T%d" % i, [128, 8, G * 128], BF16) for i in range(2)]
    hidT = p.sb("hidT", [128, NF, G * 128], BF16)
    sil = [p.sb("sil%d" % i, [128, G * 128]) for i in range(2)]
    ho = [p.sb("ho%d" % i, [128, D]) for i in range(2)]
    st = p.sb("st", [128, 8])
    pst = [p.ps("pst%d" % i) for i in range(2)]
    psu = [p.ps("psu%d" % i) for i in range(4)]
    psd = [p.ps("psd%d" % i) for i in range(2)]
    groups = [list(range(a, min(a + G, 16))) for a in range(0, 16, G)] + [[16]]
    nd = 0
    for gi, tiles in enumerate(groups):
        kind = 0 if tiles[0] < 16 else 1
        N = len(tiles) * 128
        u = u2T[gi % 2]
        for j, i in enumerate(tiles):
            h = hts[(gi % 2) * G + j]
            rows = slice(i * 128, (i + 1) * 128)
            p.dma("sp", h[:], h_d[rows, :], [h_d], [h])
            sc = st[:, 2 * j:2 * j + 1]
            sr_ = st[:, 2 * j + 1:2 * j + 2]
            p.act(junk[:], h[:], AF.Square, [h], [junk, st], accum_out=sc)
            p.rsqrt(st, sr_, st, sc, 1.0 / D, epst)
            p.act(xs[:], h[:], AF.Identity, [h, st], [xs], scale=sr_)
            for c in range(8):
                ps = pst[c // 4]
                blk = ps[:, (c % 4) * 128:(c % 4 + 1) * 128]
                p.tr(blk, xs[:, c * 128:(c + 1) * 128], ident[:], [xs, ident], [ps])
                p.act(u[:, c, j * 128:(j + 1) * 128], blk, AF.Identity, [ps, A, V], [u],
                      scale=A[:, c, kind:kind + 1], bias=V[:, c, 2 + 2 * kind:3 + 2 * kind])
        for f in range(NF):
            ps1 = psu[(2 * f) % 4]
            ps3 = psu[(2 * f + 1) % 4]
            fc = slice(f * 128, (f + 1) * 128)
            for c in range(8):
                p.mm(ps1[:, 0:N], W1[:, c, fc], u[:, c, 0:N], c == 0, c == 7, [W1, u], [ps1])
            for c in range(8):
                p.mm(ps3[:, 0:N], W3[:, c, fc], u[:, c, 0:N], c == 0, c == 7, [W3, u], [ps3])
            s_ = sil[f % 2]
            p.act(s_[:, 0:N], ps1[:, 0:N], AF.Silu, [ps1], [s_])
            p.tt("dve", hidT[:, f, 0:N], s_[:, 0:N], ps3[:, 0:N], ALU.mult, [s_, ps3], [hidT])
        for j, i in enumerate(tiles):
            h = hts[(gi % 2) * G + j]
            rows = slice(i * 128, (i + 1) * 128)
            o = ho[nd % 2]
            nd += 1
            for hf in range(2):
                ps = psd[hf]
                cols = slice(hf * 512, (hf + 1) * 512)
                for f in range(NF):
                    p.mm(ps[:], hidT[:, f, j * 128:(j + 1) * 128], W2[:, f, cols], f == 0, f == NF - 1, [hidT, W2], [ps])
                p.tt("dve", o[:, cols], ps[:], g2[kind][:, cols], ALU.mult, [ps, g2[kind]], [o])
                p.tt("pool", o[:, cols], o[:, cols], h[:, cols], ALU.add, [o, h], [o])
            if final:
                sc = st[:, 4:5]
                sr_ = st[:, 5:6]
                p.act(junk[:], o[:], AF.Square, [o], [junk, st], accum_out=sc)
                p.rsqrt(st, sr_, st, sc, 1.0 / D, epst)
                p.stt(o[:], o[:], sr_, g2[2][:], ALU.mult, ALU.mult, [o, st, g2[2]], [o])
            p.dma("sp", out[rows, :], o[:], [o], [out])
    return p


def run_l5b(h_lat, h_ctx, mod, layer, w, final):
    hs = tok_split(h_lat, h_ctx)
    ident = np.eye(128, dtype=np.float32)
    in_maps = []
    for i in range(NCORES):
        b = i // 4
        vecs = np.stack([fm(w["norm2_g"][layer]), fm(mod[b, layer, 4]), fm(mod[b, layer, 3]),
                         fm(mod[2, layer, 4]), fm(mod[2, layer, 3])], axis=-1)
        in_maps.append({"h": hs[i], "w1": w["ffn_w1"][layer], "w3": w["ffn_w3"][layer], "w2": w["ffn_w2"][layer],
                        "vecs": np.ascontiguousarray(vecs),
                        "g2": np.ascontiguousarray(np.stack([mod[b, layer, 5], mod[2, layer, 5], w["final_norm_g"]])),
                        "ident": ident})
    res = run(build_l5b(final), in_maps)
    return tok_merge([r["out"] for r in res])


TG = SEQ + CTX
NCH = TG // 64
NPR = TG // 128
SEGL = 1408
NSEG = TG // SEGL


def build_l4(stage=4):
    p = Prog()
    qkg_d = p.dram_in("qkg", [2, 3, 32, TG])
    v_d = p.dram_in("v", [2, 128, NPR, 64])
    mask_d = p.dram_in("mask", [128, 128])
    ident_d = p.dram_in("ident", [128, 128])
    out = p.dram_out("out", [2, 64, TG])

    ident = p.sb("ident", [128, 128])
    mask = p.sb("mask", [128, 128])
    ones = p.sb("ones", [32, 1])
    p.dma("sp", ident[:], ident_d[:], [ident_d], [ident])
    p.dma("sp", mask[:], mask_d[:], [mask_d], [mask])
    p.op("dve", lambda e: e.memset(ones[:], 1.0), [], [ones])

    qd = p.sb("qd", [32, TG], BF16)
    kd = p.sb("kd", [32, TG], BF16)
    ktT = [p.sb("ktT%d" % i, [128, NPR, 32], BF16) for i in range(2)]
    pm_d = p.dram_in("pm", [128, 2])
    pm = p.sb("pm", [128, 2])
    p.dma("sp", pm[:], pm_d[:], [pm_d], [pm])
    vt = p.sb("vt", [128, NPR, 64], BF16)
    STm = p.sb("STm", [128, NPR, 128], BF16)
    Sbf = p.sb("Sbf", [32, NCH + 1, 64], BF16)
    dc = p.sb("dc", [32, NCH])
    oT = p.sb("oT", [64, TG])
    Scur = [p.sb("Scur%d" % i, [32, 64]) for i in range(2)]
    gseg = p.sb("gseg", [32, SEGL])
    qseg = p.sb("qseg", [32, SEGL])
    kseg = p.sb("kseg", [32, SEGL])
    Gs = [p.sb("Gs%d" % i, [32, SEGL]) for i in range(2)]
    At = p.sb("At", [32, SEGL])
    Bt = p.sb("Bt", [32, SEGL])
    Sc = p.sb("Sc", [32, 22])
    tmpc = p.sb("tmpc", [32, 22])
    psT = [p.ps("psT%d" % i) for i in range(2)]
    psS = [p.ps("psS%d" % i) for i in range(2)]
    psK = [p.ps("psK%d" % i) for i in range(2)]
    psO = [p.ps("psO%d" % i) for i in range(2)]

    for d in range(2):
        p.dma("pool", vt[:], v_d[d], [v_d], [vt], max_dma_last_dim=4096)
        for s in range(NSEG):
            seg = slice(s * SEGL, (s + 1) * SEGL)
            G = Gs[s % 2]
            Gp = Gs[(s + 1) % 2]
            p.dma("sp", qseg[:], qkg_d[d, 0, :, seg], [qkg_d], [qseg])
            p.dma("sp", kseg[:], qkg_d[d, 1, :, seg], [qkg_d], [kseg])
            p.dma("sp", gseg[:], qkg_d[d, 2, :, seg], [qkg_d], [gseg])
            init = 0.0 if s == 0 else Gp[:, SEGL - 1:SEGL]
            rd = [gseg, ones] + ([] if s == 0 else [Gp])
            p.op("dve", lambda e: e.tensor_tensor_scan(out=G[:], data0=ones[:, 0:1].broadcast_to([32, SEGL]), data1=gseg[:],
                                                       initial=init, op0=ALU.mult, op1=ALU.add), rd, [G])
            G3 = G[:].rearrange("p (c j) -> p c j", j=64)
            Ec = G3[:, :, 63]
            if s == 0:
                p.op("dve", lambda e: e.memset(Sc[:, 0:1], 0.0), [], [Sc])
            else:
                p.op("dve", lambda e: e.tensor_copy(out=Sc[:, 0:1], in_=Gp[:, SEGL - 1:SEGL]), [Gp], [Sc])
            p.op("dve", lambda e: e.tensor_copy(out=Sc[:, 1:22], in_=G3[:, 0:21, 63]), [G], [Sc])
            A3 = At[:].rearrange("p (c j) -> p c j", j=64)
            p.tt("dve", A3, G3, Sc[:].unsqueeze(2).broadcast_to([32, 22, 64]), ALU.subtract, [G, Sc], [At])
            p.act(Bt[:], At[:], AF.Exp, [At], [Bt])
            p.tt("dve", qd[:, seg], qseg[:], Bt[:], ALU.mult, [qseg, Bt], [qd])
            p.act(Bt[:], At[:], AF.Exp, [At], [Bt], scale=-1.0)
            p.tt("dve", kd[:, seg], kseg[:], Bt[:], ALU.mult, [kseg, Bt], [kd])
            p.tt("dve", tmpc[:], Ec, Sc[:], ALU.subtract, [G, Sc], [tmpc])
            p.act(dc[:, s * 22:(s + 1) * 22], tmpc[:], AF.Exp, [tmpc], [dc])
            p.tt("dve", A3, Ec.unsqueeze(2).broadcast_to([32, 22, 64]), G3, ALU.subtract, [G], [At])
            p.act(At[:], At[:], AF.Exp, [At], [At])
            p.tt("dve", Bt[:], kseg[:], At[:], ALU.mult, [kseg, At], [Bt])
            ps = psT[s % 2]
            for j in range(11):
                p.tr(ps[:, j * 32:(j + 1) * 32], Bt[:, j * 128:(j + 1) * 128], ident[0:32, 0:32], [Bt, ident], [ps])
            for hf in range(2):
                p.act(ktT[hf][:, s * 11:(s + 1) * 11, :], ps[:, 0:352].rearrange("p (a b) -> p a b", b=32), AF.Identity,
                      [ps, pm], [ktT[hf]], scale=pm[:, hf:hf + 1])
        if stage < 4:
            p.op("dve", lambda e: e.memset(oT[:], 0.0), [], [oT])
        for g0 in (range(0, NPR, 4) if stage >= 2 else []):
            n = min(4, NPR - g0)
            ps = psS[(g0 // 4) % 2]
            for j in range(n):
                pr = g0 + j
                tok = slice(pr * 128, (pr + 1) * 128)
                p.mm(ps[:, j * 128:(j + 1) * 128], kd[:, tok], qd[:, tok], True, True, [kd, qd], [ps])
            p.tt("dve", STm[:, g0:g0 + n, :], ps[:, 0:n * 128].rearrange("p (a b) -> p a b", b=128),
                 mask[:].unsqueeze(1).broadcast_to([128, n, 128]), ALU.mult, [ps, mask], [STm])
        p.op("dve", lambda e: e.memset(Scur[0][:], 0.0), [], [Scur[0]])
        p.op("dve", lambda e: e.memset(Sbf[:, 0, :], 0.0), [], [Sbf])
        for c0 in (range(0, NCH, 8) if stage >= 3 else []):
            n = min(8, NCH - c0)
            ps = psK[(c0 // 8) % 2]
            for j in range(n):
                c = c0 + j
                pr, hf = divmod(c, 2)
                p.mm(ps[0:32, j * 64:(j + 1) * 64], ktT[hf][:, pr, :], vt[:, pr, :], True, True, [ktT[hf], vt], [ps])
            for j in range(n):
                c = c0 + j
                sa, sb_ = Scur[c % 2], Scur[(c + 1) % 2]
                p.stt(sb_[:], sa[:], dc[:, c:c + 1], ps[0:32, j * 64:(j + 1) * 64], ALU.mult, ALU.add, [sa, dc, ps], [sb_])
                p.act(Sbf[:, c + 1, :], sb_[:], AF.Identity, [sb_], [Sbf])
        for g0 in (range(0, NPR, 4) if stage >= 4 else []):
            n = min(4, NPR - g0)
            ps = psO[(g0 // 4) % 2]
            for j in range(n):
                pr = g0 + j
                cs_ = slice(j * 128, (j + 1) * 128)
                p.mm(ps[0:64, cs_], vt[:, pr, :], STm[:, pr, :], True, False, [vt, STm], [ps])
                for hf in range(2):
                    c = 2 * pr + hf
                    tok = slice(c * 64, (c + 1) * 64)
                    p.mm(ps[0:64, j * 128 + hf * 64:j * 128 + (hf + 1) * 64], Sbf[:, c, :], qd[:, tok], False, hf == 1, [Sbf, qd], [ps])
            p.act(oT[:, g0 * 128:(g0 + n) * 128], ps[0:64, 0:n * 128], AF.Identity, [ps], [oT])
        p.dma("sp", out[d], oT[:], [oT], [out])
    return p


L4_PM = np.stack([(np.arange(128) < 64), (np.arange(128) >= 64)], axis=1).astype(np.float32)


def run_l4(lat, ctx):
    ident = np.eye(128, dtype=np.float32)
    j = np.arange(128)
    mask = ((j[:, None] // 64 == j[None, :] // 64) & (j[None, :] >= j[:, None])).astype(np.float32)
    in_maps = []
    for i in range(NCORES):
        b, hd = divmod(i, 4)
        qkg = np.zeros((2, 3, 32, TG), np.float32)
        v = np.zeros((2, 128, NPR, 64), np.float32)
        for d in range(2):
            def seq(c0, w):
                a, l = ctx[b, :, c0:c0 + w], lat[b, :, c0:c0 + w]
                if d == 1:
                    a, l = a[::-1], l[::-1]
                return np.concatenate([a, l], axis=0)
            qkg[d, 0] = seq(1536 + hd * 32, 32).T
            qkg[d, 1] = seq(1664 + hd * 32, 32).T
            qkg[d, 2] = seq((2304 if d == 0 else 2432) + hd * 32, 32).T
            v[d] = seq(1792 + hd * 64, 64).reshape(NPR, 128, 64).transpose(1, 0, 2)
        in_maps.append({"qkg": qkg, "v": v, "mask": mask, "ident": ident, "pm": L4_PM})
    res = run(build_l4(), in_maps)
    f_lat = np.zeros((2, SEQ, 256), np.float32)
    b_lat = np.zeros((2, SEQ, 256), np.float32)
    f_ctx = np.zeros((2, CTX, 256), np.float32)
    b_ctx = np.zeros((2, CTX, 256), np.float32)
    for i in range(NCORES):
        b, hd = divmod(i, 4)
        o = res[i]["out"]
        cols = slice(hd * 64, (hd + 1) * 64)
        f_ctx[b, :, cols] = o[0].T[:CTX]
        f_lat[b, :, cols] = o[0].T[CTX:]
        b_ctx[b, :, cols] = o[1].T[:CTX][::-1]
        b_lat[b, :, cols] = o[1].T[CTX:][::-1]
    return f_lat, b_lat, f_ctx, b_ctx


NFFT = 16384
PI = math.pi


def hy_consts():
    n1 = np.arange(64)[:, None]
    k = np.arange(128)[None, :]
    th = 2 * np.pi * n1 * k / 128
    F1 = np.concatenate([np.cos(th), -np.sin(th)], axis=1)
    n2 = np.arange(128)[:, None]
    tw = 2 * np.pi * n2 * k / NFFT
    TWf = np.stack([np.cos(tw), -np.sin(tw)], axis=1)
    th2 = 2 * np.pi * n2 * k / 128
    F2 = np.stack([np.cos(th2), -np.sin(th2), np.sin(th2)], axis=1)
    G2 = np.stack([np.concatenate([np.cos(th2), np.sin(th2)], axis=1),
                   np.concatenate([-np.sin(th2), np.cos(th2)], axis=1)], axis=1)
    TWi = np.stack([np.cos(tw.T), np.sin(tw.T)], axis=1)
    k1 = np.arange(128)[:, None]
    n1r = np.arange(64)[None, :]
    th1 = 2 * np.pi * k1 * n1r / 128
    G1 = np.stack([np.cos(th1), -np.sin(th1)], axis=1) / NFFT
    f32 = lambda a: np.ascontiguousarray(a, dtype=np.float32)
    return {"F1": f32(F1), "TWf": f32(TWf), "F2": f32(F2), "G2": f32(G2), "TWi": f32(TWi), "G1": f32(G1)}


def hy_features(n):
    t = np.linspace(0.0, 1.0, n, dtype=np.float32)[:, None]
    omega = (np.float32(2.0 * math.pi / n) * np.arange(n, dtype=np.float32)).astype(np.float32)
    bands = np.linspace(1e-4, 15, 16, dtype=np.float32)
    phase = (omega[:, None] * bands[None, :]).astype(np.float32)
    z = np.concatenate([t, np.cos(phase), -np.sin(phase)], axis=-1).astype(np.float32)
    mn, mx = math.log(1e-2) / 1.5, math.log(1e-2) / 0.3
    deltas = np.abs(np.linspace(mn, mx, 256, dtype=np.float32))
    window = np.exp(-t * deltas[None, :]).astype(np.float32)
    return np.ascontiguousarray(z.T), window


def build_l3(debug=False):
    p = Prog()
    x_d = p.dram_in("x", [3, 64, SEQ])
    xc_d = p.dram_in("xc", [3, 64, CTX])
    cw_d = p.dram_in("cw", [64, 3, 4])
    hb_d = p.dram_in("hb", [2, 64])
    hbc_d = p.dram_in("hbc", [64, 2])
    w1_d = p.dram_in("fw1", [33, 64])
    w2_d = p.dram_in("fw2", [64, 64])
    w3_d = p.dram_in("fw3", [64, 4, 64])
    fb_d = p.dram_in("fb", [64, 3])
    zl_d = p.dram_in("zl", [33, SEQ])
    zc_d = p.dram_in("zc", [33, CTX])
    wl_d = p.dram_in("wl", [64, 64, 128])
    wc_d = p.dram_in("wc", [64, CTX])
    cd = {k: p.dram_in(k, list(v.shape)) for k, v in hy_consts().items()}
    scr = Tile(p.nc.dram_tensor("scr", [3, 64, SEQ], F32, kind="Internal").ap())
    yl_d = p.dram_out("yl", [64, 64, 128])
    yc_d = p.dram_out("yc", [64, CTX])

    BG = [p.sb("BG%d" % i, [128, SEQ]) for i in range(3)]
    Hs = p.sb("Hs", [128, 2, 64, 128], BF16)
    F1 = p.sb("F1", [64, 256], BF16)
    TWf = p.sb("TWf", [128, 2, 128])
    F2 = p.sb("F2", [128, 3, 128], BF16)
    G2 = p.sb("G2", [128, 2, 256], BF16)
    TWi = p.sb("TWi", [128, 2, 128])
    G1 = p.sb("G1", [128, 2, 64], BF16)
    for nm, t in (("F1", F1), ("F2", F2), ("G2", G2), ("G1", G1)):
        p.dma("pool", t[:], cd[nm][:], [cd[nm]], [t])
    p.dma("sp", TWf[:], cd["TWf"][:], [cd["TWf"]], [TWf])
    p.dma("sp", TWi[:], cd["TWi"][:], [cd["TWi"]], [TWi])
    cw = p.sb("cw", [64, 3, 4])
    hbr = p.sb("hbr", [64, 2, 64])
    hbc = p.sb("hbc", [64, 2])
    w1 = p.sb("fw1", [33, 64])
    w2 = p.sb("fw2", [64, 64])
    w3 = p.sb("fw3", [64, 4, 64], BF16)
    w3f = p.sb("fw3f", [64, 4, 64])
    fb = p.sb("fb", [64, 3])
    frb = p.sb("frb", [64, 2])
    wc = p.sb("wc", [64, CTX])
    p.dma("sp", cw[:], cw_d[:], [cw_d], [cw])
    p.dma("sp", hbc[:], hbc_d[:], [hbc_d], [hbc])
    for o in range(2):
        p.dma("sp", hbr[:, o, :], hb_d[o:o + 1, :].partition_broadcast(64)[:, 0, :], [hb_d], [hbr])
    p.dma("sp", w1[:], w1_d[:], [w1_d], [w1])
    p.dma("sp", w2[:], w2_d[:], [w2_d], [w2])
    p.dma("sp", w3f[:], w3_d[:], [w3_d], [w3f])
    p.dma("pool", w3[:], w3_d[:], [w3_d], [w3])
    p.dma("sp", fb[:], fb_d[:], [fb_d], [fb])
    p.dma("sp", wc[:], wc_d[:], [wc_d], [wc])
    for j in range(2):
        p.tt("dve", frb[:, j:j + 1], fb[:, 0:1], fb[:, 1 + j:2 + j], ALU.mult, [fb], [frb])
    PS = [p.ps("P%d" % i) for i in range(8)]
    PA, PX, PB, PY = PS[0:2], PS[2:4], PS[4:6], PS[6:8]

    def short_conv(xt, xap, ut, uap, g, n):
        p.act(uap, xap, AF.Identity, [xt, cw], [ut], scale=cw[:, g, 1:2], bias=cw[:, g, 3:4])
        p.stt(uap[:, 1:n], xap[:, 0:n - 1], cw[:, g, 0:1], uap[:, 1:n], ALU.mult, ALU.add, [xt, cw, ut], [ut])
        p.stt(uap[:, 0:n - 1], xap[:, 1:n], cw[:, g, 2:3], uap[:, 0:n - 1], ALU.mult, ALU.add, [xt, cw, ut], [ut])

    for g in range(3):
        p.dma("sp", BG[0][0:64, :], x_d[g], [x_d], [BG[0]])
        short_conv(BG[0], BG[0][0:64, :], BG[1], BG[1][0:64, :], g, SEQ)
        p.dma("sp", scr[g], BG[1][0:64, :], [BG[1]], [scr])
    uc = p.sb("uc", [64, 3, CTX])
    xct = p.sb("xct", [64, 3, CTX])
    p.dma("sp", xct[:], xc_d[:].rearrange("g c t -> c g t"), [xc_d], [xct])
    for g in range(3):
        short_conv(xct, xct[:, g, :], uc, uc[:, g, :], g, CTX)

    def wrap_sin(dst_t, dst_ap, ps_ap, ps_t, j, n, arg_t):
        a = arg_t[0:64, 0:n]
        p.ts("dve", a, ps_ap, fb[:, 0:1], ALU.mult, [ps_t, fb, frb], [arg_t], s2=frb[:, j:j + 1], op1=ALU.add)
        w_ = wrp[0:64, 0:n]
        for bound, period in ((3 * PI, 4 * PI), (PI, 2 * PI)):
            p.ts("dve", w_, a, bound, ALU.is_gt, [arg_t], [wrp], s2=-period, op1=ALU.mult)
            p.tt("dve", a, a, w_, ALU.add, [arg_t, wrp], [arg_t])
            p.ts("dve", w_, a, -bound, ALU.is_lt, [arg_t], [wrp], s2=period, op1=ALU.mult)
            p.tt("dve", a, a, w_, ALU.add, [arg_t, wrp], [arg_t])
        p.act(dst_ap, a, AF.Sin, [arg_t], [dst_t])

    argt = p.sb("argt", [64, 512])
    wrp = p.sb("wrp", [64, 512])
    h1c = p.sb("h1c", [64, 512])
    zch = [p.sb("zch%d" % i, [33, 512]) for i in range(2)]

    def mlp(z_dram, n, dst_t, dst_fn):
        for q in range(0, n, 512):
            m = min(512, n - q)
            zt = zch[(q // 512) % 2]
            p.dma("sp", zt[:, 0:m], z_dram[:, q:q + m], [z_dram], [zt])
            p.mm(PA[0][0:64, 0:m], w1[:], zt[:, 0:m], True, True, [w1, zt], [PA[0]])
            wrap_sin(h1c, h1c[:, 0:m], PA[0][0:64, 0:m], PA[0], 0, m, argt)
            p.mm(PA[1][0:64, 0:m], w2[:], h1c[:, 0:m], True, True, [w2, h1c], [PA[1]])
            wrap_sin(dst_t, dst_fn(q, m), PA[1][0:64, 0:m], PA[1], 1, m, argt)

    h2 = BG[1][:].bitcast(BF16)
    mlp(zl_d, SEQ, BG[1], lambda q, m: h2[0:64, q:q + m])
    h2c = p.sb("h2c", [64, CTX])
    mlp(zc_d, CTX, h2c, lambda q, m: h2c[:, q:q + m])

    hfc = p.sb("hfc", [64, 4, CTX])
    for blk in range(4):
        p.mm(PX[0][0:64, 0:CTX], w3f[:, blk, :], h2c[:], True, True, [w3f, h2c], [PX[0]])
        p.tt("dve", hfc[:, blk, :], PX[0][0:64, 0:CTX], wc[:], ALU.mult, [PX[0], wc], [hfc])
    zc1 = p.sb("zc1", [64, CTX])
    yct = p.sb("yct", [64, CTX])

    def ctx_conv(zt, zap, o, gate_ap, out_t, out_ap):
        p.ts("dve", yct[:], zap, hbc[:, o:o + 1], ALU.mult, [zt, hbc], [yct])
        for q in range(CTX):
            p.stt(yct[:, q:CTX], zap[:, 0:CTX - q], hfc[:, 2 * o, q:q + 1], yct[:, q:CTX], ALU.mult, ALU.add, [zt, hfc, yct], [yct])
        for q in range(1, CTX):
            p.stt(yct[:, 0:CTX - q], zap[:, q:CTX], hfc[:, 2 * o + 1, q:q + 1], yct[:, 0:CTX - q], ALU.mult, ALU.add, [zt, hfc, yct], [yct])
        p.tt("dve", out_ap, yct[:], gate_ap, ALU.mult, [yct, uc], [out_t])

    ctx_conv(uc, uc[:, 0, :], 0, uc[:, 1, :], zc1, zc1[:])
    ctx_conv(zc1, zc1[:], 1, uc[:, 2, :], zc1, zc1[:])
    p.dma("sp", yc_d[:], zc1[:], [zc1], [yc_d])

    Af = [p.sb("Af%d" % i, [128, 512]) for i in range(2)]
    tm = [p.sb("tm%d" % i, [128, 512]) for i in range(4)]
    Apr = [p.sb("Apr%d" % i, [128, 4, 128], BF16) for i in range(2)]
    Api = [p.sb("Api%d" % i, [128, 4, 128], BF16) for i in range(2)]
    Xf = [p.sb("Xf%d" % i, [128, 512]) for i in range(4)]
    Yr = [p.sb("Yr%d" % i, [128, 4, 128], BF16) for i in range(2)]
    Yi = [p.sb("Yi%d" % i, [128, 4, 128], BF16) for i in range(2)]
    cnt = {"f": 0, "e": 0}

    def eng():
        cnt["e"] += 1
        return "dve" if cnt["e"] % 2 else "pool"

    def cmul(src_t, sr, si, tw_t, twr, twi, dr_t, dr, di_t, di, t_a, t_b):
        e1, e2 = eng(), eng()
        p.tt(e1, t_a[0], sr, twr, ALU.mult, [src_t, tw_t], [t_a[1]])
        p.tt(e2, t_b[0], si, twi, ALU.mult, [src_t, tw_t], [t_b[1]])
        p.tt(e1, dr, t_a[0], t_b[0], ALU.subtract, [t_a[1], t_b[1]], [dr_t])
        p.tt(e1, t_a[0], sr, twi, ALU.mult, [src_t, tw_t], [t_a[1]])
        p.tt(e2, t_b[0], si, twr, ALU.mult, [src_t, tw_t], [t_b[1]])
        p.tt(e2, di, t_a[0], t_b[0], ALU.add, [t_a[1], t_b[1]], [di_t])

    def stage12(src_t, lhs_fn, rhs_t, rhs0, rhs1, lhs2_fn, TW, dr_t, di_t, src2_t=None):
        b = cnt["f"] % 2
        cnt["f"] += 1
        for hf in range(2):
            ps = (PA if rhs1 is None else PB)[hf]
            for k in range(2):
                c = 2 * hf + k
                cols = slice(k * 256, (k + 1) * 256)
                if rhs1 is None:
                    p.mm(ps[:, cols], lhs_fn(c), rhs0, True, True, [src_t, rhs_t], [ps])
                else:
                    p.mm(ps[:, cols], lhs_fn(c), rhs0, True, False, [src_t, rhs_t], [ps])
                    p.mm(ps[:, cols], lhs2_fn(c), rhs1, False, True, [src2_t, rhs_t], [ps])
            af = Af[hf]
            p.act(af[:], ps[:], AF.Identity, [ps], [af])
            a4 = af[:].rearrange("p (k r j) -> p k r j", k=2, r=2)
            twr = TW[:, 0, :].unsqueeze(1).broadcast_to([128, 2, 128])
            twi = TW[:, 1, :].unsqueeze(1).broadcast_to([128, 2, 128])
            ta = tm[2 * hf][:, 0:256].rearrange("p (k j) -> p k j", k=2)
            tb = tm[2 * hf + 1][:, 0:256].rearrange("p (k j) -> p k j", k=2)
            cmul(af, a4[:, :, 0, :], a4[:, :, 1, :], TW, twr, twi,
                 dr_t, dr_t[:, 2 * hf:2 * hf + 2, :], di_t, di_t[:, 2 * hf:2 * hf + 2, :],
                 (ta, tm[2 * hf]), (tb, tm[2 * hf + 1]))

    def fwd_batch(src_t, lhs_fn):
        b = cnt["f"] % 2
        ar, ai = Apr[b], Api[b]
        stage12(src_t, lhs_fn, F1, F1[:], None, None, TWf, ar, ai)
        arf = ar[:].rearrange("p c j -> p (c j)")
        aif = ai[:].rearrange("p c j -> p (c j)")
        p.mm(PX[0][:], F2[:, 0, :], arf, True, False, [F2, ar], [PX[0]])
        p.mm(PX[0][:], F2[:, 2, :], aif, False, True, [F2, ai], [PX[0]])
        p.mm(PX[1][:], F2[:, 0, :], aif, True, False, [F2, ai], [PX[1]])
        p.mm(PX[1][:], F2[:, 1, :], arf, False, True, [F2, ar], [PX[1]])

    def inv_batch(yr, yi, py):
        b = cnt["f"] % 2
        br, bi = Apr[b], Api[b]
        stage12(yr, lambda c: yr[:, c, :], G2, G2[:, 0, :], G2[:, 1, :], lambda c: yi[:, c, :], TWi, br, bi, src2_t=yi)
        p.mm(py[0:64, :], G1[:, 0, :], br[:].rearrange("p c j -> p (c j)"), True, False, [G1, br], [py])
        p.mm(py[0:64, :], G1[:, 1, :], bi[:].rearrange("p c j -> p (c j)"), False, True, [G1, bi], [py])

    win = BG[0]
    hf_t = BG[2]
    hfv = BG[2][:].bitcast(BF16)[0:64, :].rearrange("p (s c j) -> p s c j", s=2, c=64)
    p.dma("sp", win[0:64, :], wl_d[:].rearrange("p c j -> p (c j)"), [wl_d], [win])
    win3 = win[0:64, :].rearrange("p (c j) -> p c j", j=128)
    scr2 = Tile(p.nc.dram_tensor("scr2", [64, 64, 128], F32, kind="Internal").ap())
    zf = [p.sb("zf%d" % i, [64, 4, 128]) for i in range(2)]
    zb = [p.sb("zb%d" % i, [64, 4, 128], BF16) for i in range(2)]
    xgb = [p.sb("xgb%d" % i, [64, 4, 128]) for i in range(2)]
    for o in range(2):
        for n2 in range(128):
            ps = PY[n2 % 2]
            p.mm(ps[0:64, 0:128], h2[0:64, n2:SEQ:128], w3[:, 2 * o:2 * o + 2, :].rearrange("p s c -> p (s c)"), True, True, [BG[1], w3], [ps])
            p.tt("dve", hfv[:, :, :, n2], ps[0:64, 0:128].rearrange("p (s c) -> p s c", s=2),
                 win3[:, :, n2].unsqueeze(1).broadcast_to([64, 2, 64]), ALU.mult, [ps, win], [hf_t])
        p.op("dve", lambda e: e.memset(hfv[0:1, 1, :, 0], 0.0), [], [hf_t])
        for bt in range(16):
            c0 = 4 * bt
            fwd_batch(hf_t, lambda c: hfv[:, 0, c0 + c, :])
            xr, xi = Xf[0], Xf[1]
            p.act(xr[:], PX[0][:], AF.Identity, [PX[0]], [xr])
            p.act(xi[:], PX[1][:], AF.Identity, [PX[1]], [xi])
            fwd_batch(hf_t, lambda c: hfv[:, 1, c0 + c, :])
            p.tt("dve", Hs[:, 0, c0:c0 + 4, :], xr[:].rearrange("p (c j) -> p c j", c=4),
                 PX[0][:].rearrange("p (c j) -> p c j", c=4), ALU.add, [xr, PX[0]], [Hs])
            p.tt("dve", Hs[:, 1, c0:c0 + 4, :], xi[:].rearrange("p (c j) -> p c j", c=4),
                 PX[1][:].rearrange("p (c j) -> p c j", c=4), ALU.subtract, [xi, PX[1]], [Hs])
        if debug:
            dh = p.dram_out("dbg_hf", [64, 2, 64, 128])
            dH = p.dram_out("dbg_H", [128, 2, 64, 128])
            dh2 = p.dram_out("dbg_h2", [64, SEQ])
            p.dma("pool", dh[:], hfv, [hf_t], [dh])
            p.dma("pool", dH[:], Hs[:], [Hs], [dH])
            p.dma("pool", dh2[:], h2[0:64, 0:SEQ], [BG[1]], [dh2], max_dma_last_dim=2048)
            return p
        for bt in range(16):
            c0 = 4 * bt
            b = bt % 2
            z_, zb_, g_ = zf[b], zb[b], xgb[b]
            if o == 0:
                p.dma("sp", z_[:], scr[0, c0:c0 + 4, :].rearrange("c (p j) -> p c j", j=128), [scr], [z_])
            else:
                p.dma("sp", z_[:], scr2[:, c0:c0 + 4, :], [scr2], [z_])
            p.dma("sp", g_[:], scr[1 + o, c0:c0 + 4, :].rearrange("c (p j) -> p c j", j=128), [scr], [g_])
            p.op("pool", lambda e: e.tensor_copy(out=zb_[:], in_=z_[:]), [z_], [zb_])
            p.tt("pool", z_[:], z_[:], hbr[:, o, c0:c0 + 4].unsqueeze(2).broadcast_to([64, 4, 128]), ALU.mult, [z_, hbr], [z_])
            fwd_batch(zb_, lambda c: zb_[:, c, :])
            xr, xi = Xf[2], Xf[3]
            p.act(xr[:], PX[0][:], AF.Identity, [PX[0]], [xr])
            p.act(xi[:], PX[1][:], AF.Identity, [PX[1]], [xi])
            x4r = xr[:].rearrange("p (c j) -> p c j", c=4)
            x4i = xi[:].rearrange("p (c j) -> p c j", c=4)
            ta = tm[0][:].rearrange("p (c j) -> p c j", c=4)
            tb = tm[1][:].rearrange("p (c j) -> p c j", c=4)
            e1, e2 = eng(), eng()
            p.tt(e1, ta, x4r, Hs[:, 0, c0:c0 + 4, :], ALU.mult, [xr, Hs], [tm[0]])
            p.tt(e2, tb, x4i, Hs[:, 1, c0:c0 + 4, :], ALU.mult, [xi, Hs], [tm[1]])
            p.tt(e1, Yr[b][:], ta, tb, ALU.subtract, [tm[0], tm[1]], [Yr[b]])
            p.tt(e1, ta, x4r, Hs[:, 1, c0:c0 + 4, :], ALU.mult, [xr, Hs], [tm[0]])
            p.tt(e2, tb, x4i, Hs[:, 0, c0:c0 + 4, :], ALU.mult, [xi, Hs], [tm[1]])
            p.tt(e2, Yi[b][:], ta, tb, ALU.add, [tm[0], tm[1]], [Yi[b]])
            py = PY[bt % 2]
            inv_batch(Yr[b], Yi[b], py)
            p.tt("dve", z_[:], py[0:64, :].rearrange("p (c j) -> p c j", c=4), z_[:], ALU.add, [py, z_], [z_])
            p.tt("pool", z_[:], z_[:], g_[:], ALU.mult, [z_, g_], [z_])
            if o == 0:
                p.dma("sp", scr2[:, c0:c0 + 4, :], z_[:], [z_], [scr2])
            else:
                p.dma("sp", yl_d[:, c0:c0 + 4, :], z_[:], [z_], [yl_d])
    return p


def run_l3(lat, ctx, layer, w):
    consts = hy_consts()
    zl, win_l = hy_features(SEQ)
    zc, win_c = hy_features(CTX)
    in_maps = []
    for i in range(NCORES):
        b, cg = divmod(i, 4)
        ch = slice(cg * 64, (cg + 1) * 64)
        cols = [768 + g * 256 + cg * 64 for g in range(3)]
        x = np.stack([lat[b, :, c:c + 64].T for c in cols])
        xc = np.stack([ctx[b, :, c:c + 64].T for c in cols])
        cwf = w["hy_conv_w"][layer].reshape(3, 3, 256)[:, :, ch]
        cbf = w["hy_conv_b"][layer].reshape(3, 256)[:, ch]
        cw = np.concatenate([cwf.transpose(2, 1, 0), cbf.T[:, :, None]], axis=2)
        hb = w["hy_bias"][layer][:, ch]
        w3 = w["filt_w3"][layer].reshape(64, 4, 256)[:, :, ch]
        fb = np.stack([w["filt_freq"][layer], w["filt_b1"][layer], w["filt_b2"][layer]], axis=1)
        wl = win_l[:, ch].reshape(64, 128, 64).transpose(0, 2, 1)
        m = {"x": x, "xc": xc, "cw": cw, "hb": hb, "hbc": hb.T, "fw1": w["filt_w1"][layer], "fw2": w["filt_w2"][layer],
             "fw3": w3, "fb": fb, "zl": zl, "zc": zc, "wl": wl, "wc": win_c[:, ch].T}
        m.update(consts)
        in_maps.append({k: np.ascontiguousarray(v, dtype=np.float32) for k, v in m.items()})
    res = run(build_l3(), in_maps)
    y_lat = np.zeros((2, SEQ, 256), np.float32)
    y_ctx = np.zeros((2, CTX, 256), np.float32)
    for i in range(NCORES):
        b, cg = divmod(i, 4)
        ch = slice(cg * 64, (cg + 1) * 64)
        y_lat[b, :, ch] = res[i]["yl"].transpose(0, 2, 1).reshape(SEQ, 64)
        y_ctx[b, :, ch] = res[i]["yc"].T
    return y_lat, y_ctx


def kernel(**inputs):
    w = {k: np.ascontiguousarray(np.asarray(v, dtype=np.float32)) for k, v in inputs.items()}
    mod = run_ada(w["c"], w["c_ctx"], w["ada_w"], w["ada_b"])
    h_lat, h_ctx = w["x"], w["ctx"]
    for layer in range(DEPTH):
        l1_lat, l1_ctx = run_l1(h_lat, h_ctx, mod, layer, w)
        a_lat, a_ctx = run_l2(l1_lat, l1_ctx)
        hy_lat, hy_ctx = run_l3(l1_lat, l1_ctx, layer, w)
        f_lat, b_lat, f_ctx, b_ctx = run_l4(l1_lat, l1_ctx)
        y_lat = np.concatenate([a_lat, hy_lat, f_lat], axis=-1)
        y_ctx = np.concatenate([a_ctx, hy_ctx, f_ctx], axis=-1)
        h_lat, h_ctx = run_l5a(y_lat, y_ctx, b_lat, b_ctx, l1_lat[:, :, 2048:2304], l1_ctx[:, :, 2048:2304],
                               h_lat, h_ctx, mod, layer, w)
        h_lat, h_ctx = run_l5b(h_lat, h_ctx, mod, layer, w, layer == DEPTH - 1)
    return np.ascontiguousarray(h_lat, dtype=np.float32)


TT = SEQ + CTX
NTT = TT // 128


class Stage:
    def __init__(self, p):
        self.p = p
        self.cms = []

    def sb(self, name, shape, dtype=F32):
        p = self.p
        p.uid = getattr(p, "uid", 0) + 1
        cm = p.nc.sbuf_tensor("%s_%d" % (name, p.uid), list(shape), dtype)
        h = cm.__enter__()
        self.cms.append(cm)
        return Tile(h)

    def close(self):
        p = self.p
        tot = {sid: c for sid, c in p.cnt.items() if c > 0}
        for e in p.eng:
            p._wait(e, dict(tot))
        for cm in reversed(self.cms):
            cm.__exit__(None, None, None)
        self.cms = []


def scratch(p, name, shape):
    return Tile(p.nc.dram_tensor(name, list(shape), F32, kind="Internal").ap())


def emit_ada(p, PS, cT_d, adaw_d, adab_d, MOD):
    st = Stage(p)
    cs = st.sb("cs", [128, 8, 2])
    Wc = [st.sb("Wc%d" % i, [128, 8, 512]) for i in range(3)]
    bb = st.sb("bb", [2, 6 * D])
    res = st.sb("res", [2, 6 * D])
    p.dma("sp", cs[:], cT_d[:], [cT_d], [cs])
    p.act(cs[:], cs[:], AF.Silu, [cs], [cs])
    k = 0
    for l in range(DEPTH):
        p.dma("sp", bb[:], adab_d[l:l + 1, :].partition_broadcast(2)[:, 0, :], [adab_d], [bb])
        for j in range(6 * D // 512):
            W = Wc[k % 3]
            q = "sp" if k % 2 == 0 else "act"
            p.dma(q, W[:], adaw_d[l, :, j * 512:(j + 1) * 512].rearrange("(c p) n -> p c n", p=128), [adaw_d], [W])
            ps = PS[k % 2]
            k += 1
            for c in range(8):
                p.mm(ps[0:2, :], cs[:, c, :], W[:, c, :], c == 0, c == 7, [cs, W], [ps])
            p.tt("dve", res[:, j * 512:(j + 1) * 512], ps[0:2, :], bb[:, j * 512:(j + 1) * 512], ALU.add, [ps, bb], [res])
        p.dma("sp", MOD[l], res[:], [res], [MOD])
    st.close()


def load_modT(p, st, PS, MOD, l, ident):
    raw = st.sb("modraw", [96, 128])
    modT = st.sb("modT", [128, 96])
    p.dma("sp", raw[:], MOD[l].rearrange("r (s p) -> (r s) p", p=128), [MOD], [raw])
    p.tr(PS[7][:, 0:96], raw[:], ident[0:96, 0:96], [raw, ident], [PS[7]])
    p.act(modT[:], PS[7][:, 0:96], AF.Identity, [PS[7]], [modT])
    return modT


def emit_l1(p, PS, l, Hin, S1, MOD, wd, cs_d, ident_d):
    st = Stage(p)
    sb = st.sb
    W = sb("W", [128, 8, IN_W], BF16)
    A = sb("A", [128, 8, 2])
    Brep = sb("Brep", [128, 8, 128], BF16)
    bW = [sb("bW%d" % k, [128, IN_W]) for k in range(2)]
    gains = sb("gains", [128, 10, 64])
    Wblk = sb("Wblk", [32, 256])
    gbb = sb("gbb", [128, 256])
    ident = sb("ident", [128, 128])
    epst = sb("eps", [128, 1])
    ng = sb("ng", [128, 8])
    pss, pst, psg = PS[0:5], PS[5:7], PS[7]
    p.op("dve", lambda e: e.memset(epst[:], EPS), [], [epst])
    p.op("dve", lambda e: e.memset(Wblk[:], 0.0), [], [Wblk])
    p.dma("sp", ident[:], ident_d[:], [ident_d], [ident])
    p.dma("sp", ng[:], wd["norm1_g"][l], [wd["norm1_g"]], [ng])
    p.dma("sp", gbb[:], wd["gla_gate_b"][l:l + 1, :].partition_broadcast(128)[:, 0, :], [wd["gla_gate_b"]], [gbb])
    p.dma("sp", Wblk[0:16, 0:128], wd["gla_gate_w"][l, 0], [wd["gla_gate_w"]], [Wblk])
    p.dma("sp", Wblk[16:32, 128:256], wd["gla_gate_w"][l, 1], [wd["gla_gate_w"]], [Wblk])
    for h in range(10):
        src = wd["qkg"][l:l + 1, 0:64] if h < 8 else wd["qkg"][l:l + 1, 64:128]
        p.dma("sp", gains[:, h, :], src.partition_broadcast(128)[:, 0, :], [wd["qkg"]], [gains])
    p.ts("dve", gains[:, 0:8, :], gains[:, 0:8, :], 0.125, ALU.mult, [gains], [gains])
    for c in range(8):
        p.dma("pool", W[:, c, :], wd["w_in"][l, c * 128:(c + 1) * 128, :], [wd["w_in"]], [W], max_dma_last_dim=4096)
    modT = load_modT(p, st, PS, MOD, l, ident)
    for k in range(2):
        sc = modT[:, k * 48 + 8:k * 48 + 16]
        sh = modT[:, k * 48 + 0:k * 48 + 8]
        p.stt(A[:, :, k], sc, 1.0, ng[:], ALU.add, ALU.mult, [modT, ng], [A])
        p.op("dve", lambda e: e.tensor_copy(out=Brep[:], in_=sh.unsqueeze(2).broadcast_to([128, 8, 128])), [modT], [Brep])
        for gi, (a, b) in enumerate(GROUPS1):
            ps = pss[gi]
            for c in range(8):
                p.mm(ps[:, 0:b - a], Brep[:, c, :], W[:, c, a:b], c == 0, c == 7, [Brep, W], [ps])
            p.act(bW[k][:, a:b], ps[:, 0:b - a], AF.Identity, [ps], [bW[k]])
    NB = 2
    xt = [sb("xt%d" % i, [128, D]) for i in range(NB)]
    xa = [sb("xa%d" % i, [128, 8, 128], BF16) for i in range(NB)]
    cst = [sb("cst%d" % i, [128, 64]) for i in range(NB)]
    O = [sb("O%d" % i, [128, OUT1]) for i in range(NB)]
    junk = sb("junk", [128, D])
    sq = sb("sq", [128, 640])
    tmp = sb("tmp", [128, 10, 32])
    stt_ = [sb("st%d" % i, [128, 16]) for i in range(NB)]
    lrs = [sb("lr%d" % i, [128, 32]) for i in range(NB)]
    lrT = sb("lrT", [32, 128])
    gz = sb("gz", [128, 256])
    ga = sb("ga", [128, 256])
    pic = [0]

    def phaseA(i):
        kind = 0 if i < 64 else 1
        bi = i % NB
        rows = slice(i * 128, (i + 1) * 128)
        p.dma("sp", xt[bi][:], Hin[rows, :], [Hin], [xt[bi]])
        p.dma("sp", cst[bi][:], cs_d[rows, :], [cs_d], [cst[bi]])
        s = stt_[bi]
        p.act(junk[:], xt[bi][:], AF.Square, [xt[bi]], [junk, s], accum_out=s[:, 0:1])
        p.rsqrt(s, s[:, 1:2], s, s[:, 0:1], 1.0 / D, epst)
        for c in range(8):
            pt = pst[c // 4]
            blk = pt[:, (c % 4) * 128:(c % 4 + 1) * 128]
            p.tr(blk, xt[bi][:, c * 128:(c + 1) * 128], ident[:], [xt[bi], ident], [pt])
            p.act(xa[bi][:, c, :], blk, AF.Identity, [pt, A], [xa[bi]], scale=A[:, c, kind:kind + 1])
        o = O[bi]
        for gi, (a, b) in enumerate(GROUPS1):
            ps = pss[pic[0] % 5]
            pic[0] += 1
            for c in range(8):
                p.mm(ps[:, 0:b - a], xa[bi][:, c, :], W[:, c, a:b], c == 0, c == 7, [xa[bi], W], [ps])
            if gi < 4:
                p.stt(o[:, a:b], ps[:, 0:b - a], s[:, 1:2], bW[kind][:, a:b], ALU.mult, ALU.add, [ps, s, bW[kind]], [o])
            else:
                p.stt(o[:, 2048:2304], ps[:, 0:256], s[:, 1:2], bW[kind][:, 2048:2304], ALU.mult, ALU.add, [ps, s, bW[kind]], [o])
                p.stt(lrs[bi][:], ps[:, 256:288], s[:, 1:2], bW[kind][:, 2304:2336], ALU.mult, ALU.add, [ps, s, bW[kind]], [lrs[bi]])

    def phaseB(i):
        bi = i % NB
        rows = slice(i * 128, (i + 1) * 128)
        s = stt_[bi]
        o = O[bi]
        lr = lrs[bi]
        qk = o[:, 0:640]
        p.tt("pool", sq[:], qk, qk, ALU.mult, [o], [sq])
        p.op("dve", lambda e: e.tensor_reduce(out=s[:, 2:12], in_=sq[:].rearrange("p (h d) -> p h d", d=64), axis=AX.X, op=ALU.add), [sq], [s])
        p.rsqrt(s, s[:, 2:12], s, s[:, 2:12], 1.0 / 64, epst)
        qk3 = qk.rearrange("p (h d) -> p h d", d=64)
        p.tt("dve", qk3, qk3, s[:, 2:12].unsqueeze(2).broadcast_to([128, 10, 64]), ALU.mult, [o, s], [o])
        p.tt("pool", qk3, qk3, gains[:], ALU.mult, [o, gains], [o])
        x1 = qk3[:, :, 0:32]
        x2 = qk3[:, :, 32:64]
        cb = cst[bi][:, 0:32].unsqueeze(1).broadcast_to([128, 10, 32])
        sb_ = cst[bi][:, 32:64].unsqueeze(1).broadcast_to([128, 10, 32])
        t3 = sq[:, 0:320].rearrange("p (h d) -> p h d", d=32)
        p.tt("dve", tmp[:], x2, sb_, ALU.mult, [o, cst[bi]], [tmp])
        p.tt("pool", t3, x1, sb_, ALU.mult, [o, cst[bi]], [sq])
        p.tt("dve", x1, x1, cb, ALU.mult, [o, cst[bi]], [o])
        p.tt("dve", x1, x1, tmp[:], ALU.subtract, [o, tmp], [o])
        p.tt("dve", x2, x2, cb, ALU.mult, [o, cst[bi]], [o])
        p.tt("dve", x2, x2, t3, ALU.add, [o, sq], [o])
        p.act(o[:, 1536:1664], o[:, 1536:1664], AF.Identity, [o], [o], scale=32 ** -0.5)
        p.act(o[:, 2048:2304], o[:, 2048:2304], AF.Silu, [o], [o])
        p.tr(psg[0:32, 0:128], lr[:], ident[:], [lr, ident], [psg])
        p.act(lrT[:], psg[0:32, 0:128], AF.Identity, [psg], [lrT])
        p.mm(psg[:, 128:384], lrT[:], Wblk[:], True, True, [lrT, Wblk], [psg])
        p.tt("dve", gz[:], psg[:, 128:384], gbb[:], ALU.add, [psg, gbb], [gz])
        p.stt(ga[:], gz[:], -1.0, gz[:], ALU.mult, ALU.min, [gz], [ga])
        p.act(ga[:], ga[:], AF.Exp, [ga], [ga])
        p.act(ga[:], ga[:], AF.Ln, [ga], [ga], bias=1.0)
        p.ts("dve", gz[:], gz[:], 0.0, ALU.min, [gz], [gz], s2=1.0 / 16, op1=ALU.mult)
        p.stt(o[:, 2304:2560], ga[:], -1.0 / 16, gz[:], ALU.mult, ALU.add, [ga, gz], [o])
        p.dma("sp", S1[rows, :], o[:], [o], [S1])

    phaseA(0)
    for i in range(NTT):
        if i + 1 < NTT:
            phaseA(i + 1)
        phaseB(i)
    st.close()


def emit_l2(p, PS, S1, SY, ident_d, PS2):
    st = Stage(p)
    sb = st.sb
    ident = sb("ident", [128, 128])
    p.dma("sp", ident[:], ident_d[:], [ident_d], [ident])
    kT2 = sb("kT2", [128, TT], BF16)
    vA = [sb("vA%d" % g, [128, NTT, 65], BF16) for g in range(2)]
    kin = [sb("kin%d" % i, [128, 128]) for i in range(2)]
    qin = [sb("qin%d" % i, [128, 512]) for i in range(2)]
    q2 = [sb("q2_%d" % i, [128, 512], BF16) for i in range(2)]
    qrs = [sb("qr%d" % i, [128, 512]) for i in range(2)]
    psS = PS2[0:3]
    psO = PS2[3]
    for g in range(2):
        p.op("dve", lambda e: e.memset(vA[g][:, :, 64:65], 1.0), [], [vA[g]])
        p.dma("pool", vA[g][:, :, 0:64], S1[:, 640 + g * 64:704 + g * 64].rearrange("(t p) c -> p t c", p=128), [S1], [vA[g]])
    for t in range(NTT):
        ki = kin[t % 2]
        p.dma("sp", ki[:], S1[t * 128:(t + 1) * 128, 512:640], [S1], [ki])
        pt = psS[t % 3]
        p.tr(pt[:, 0:128], ki[:], ident[:], [ki, ident], [pt])
        p.act(kT2[:, t * 128:(t + 1) * 128], pt[:, 0:128], AF.Identity, [pt], [kT2])
    LA = 2
    pT = [sb("pT%d" % i, [128, 1024], BF16) for i in range(LA + 2)]
    oT = [sb("oT%d" % i, [65, 1024]) for i in range(2)]
    ot = [sb("ot%d" % i, [128, 512]) for i in range(2)]
    rec = sb("rec", [128, 16])
    it = 0
    for qt in range(NTT):
        kts = list(range(NTT)) if qt < 64 else [64, 65]
        qi = qin[qt % 2]
        p.dma("sp", qi[:], S1[qt * 128:(qt + 1) * 128, 0:512], [S1], [qi])
        o_t = ot[qt % 2]
        q_ = q2[qt % 2]
        pt = psS[it % 3]
        qr = qrs[qt % 2]
        p.op("pool", lambda e: e.tensor_copy(out=qr[:].rearrange("p (h g d) -> p h g d", h=4, g=2),
                                             in_=qi[:].rearrange("p (g h d) -> p h g d", g=2, h=4)), [qi], [qr])
        for h in range(4):
            p.tr(pt[:, h * 128:(h + 1) * 128], qr[:, h * 128:(h + 1) * 128], ident[:], [qr, ident], [pt])
        p.act(q_[:], pt[:, 0:512], AF.Identity, [pt], [q_])

        def pv(pend, last):
            k0, kt, ptile = pend
            for g in range(2):
                p.mm(psO[0:65, g * 512:(g + 1) * 512], vA[g][:, kt, :], ptile[:, g * 512:(g + 1) * 512], k0 == 0, last,
                     [vA[g], ptile], [psO])

        pend = []
        for k0, kt in enumerate(kts):
            ps = psS[it % 3]
            pt_ = pT[it % (LA + 2)]
            it += 1
            for g in range(2):
                rows = slice(g * 64, (g + 1) * 64)
                p.mm(ps[:, g * 512:(g + 1) * 512], kT2[rows, kt * 128:(kt + 1) * 128], q_[rows, :], True, True, [kT2, q_], [ps])
            p.act(pt_[:], ps[:], AF.Exp, [ps], [pt_])
            pend.append((k0, kt, pt_))
            if len(pend) > LA:
                pv(pend.pop(0), False)
        while pend:
            x_ = pend.pop(0)
            pv(x_, len(pend) == 0)
        o_sb = oT[qt % 2]
        p.op("dve", lambda e: e.tensor_copy(out=o_sb[:], in_=psO[0:65, :]), [psO], [o_sb])
        for hh in range(8):
            pb = psS[(it + 1 + hh % 2) % 3]
            ptt = pb[:, (hh // 2 % 2) * 512:(hh // 2 % 2) * 512 + 65]
            p.tr(ptt, o_sb[:, hh * 128:(hh + 1) * 128], ident[0:65, 0:65], [o_sb, ident], [pb])
            rc = rec[:, hh + (qt % 2) * 8:hh + (qt % 2) * 8 + 1]
            p.op("dve", lambda e: e.reciprocal(out=rc, in_=ptt[:, 64:65]), [pb], [rec])
            p.ts("dve", o_t[:, hh * 64:(hh + 1) * 64], ptt[:, 0:64], rc, ALU.mult, [pb, rec], [o_t])
        p.dma("sp", SY[qt * 128:(qt + 1) * 128, :], o_t[:], [o_t], [SY])
    st.close()


def emit_l3(p, PS, l, S1, YL, YC, wd, cd, scr, scr2):
    st = Stage(p)
    sb = st.sb
    BGa = sb("BGa", [128, SEQ])
    BGb = sb("BGb", [128, SEQ])
    H2 = sb("H2", [64, SEQ], BF16)
    Hs = sb("Hs", [128, 2, 64, 128], BF16)
    F1 = sb("F1", [64, 256], BF16)
    TWf = sb("TWf", [128, 2, 128])
    F2 = sb("F2", [128, 3, 128], BF16)
    G2 = sb("G2", [128, 2, 256], BF16)
    TWi = sb("TWi", [128, 2, 128])
    G1 = sb("G1", [128, 2, 64], BF16)
    ident = sb("ident", [128, 128])
    for nm, t in (("F1", F1), ("F2", F2), ("G2", G2), ("G1", G1)):
        p.dma("pool", t[:], cd[nm][:], [cd[nm]], [t])
    p.dma("sp", TWf[:], cd["TWf"][:], [cd["TWf"]], [TWf])
    p.dma("sp", TWi[:], cd["TWi"][:], [cd["TWi"]], [TWi])
    p.dma("sp", ident[:], cd["ident"][:], [cd["ident"]], [ident])
    w1 = sb("fw1", [33, 64])
    w2 = sb("fw2", [64, 64])
    fb = sb("fb", [64, 3])
    frb = sb("frb", [64, 2])
    p.dma("sp", w1[:], wd["filt_w1"][l], [wd["filt_w1"]], [w1])
    p.dma("sp", w2[:], wd["filt_w2"][l], [wd["filt_w2"]], [w2])
    p.dma("sp", fb[:], wd["fb"][l], [wd["fb"]], [fb])
    for j in range(2):
        p.tt("dve", frb[:, j:j + 1], fb[:, 0:1], fb[:, 1 + j:2 + j], ALU.mult, [fb], [frb])
    PA, PX, PB, PY = PS[0:2], PS[2:4], PS[4:6], PS[6:8]
    argt = sb("argt", [64, 512])
    wrp = sb("wrp", [64, 512])
    h1c = sb("h1c", [64, 512])
    zch = [sb("zch%d" % i, [33, 512]) for i in range(2)]
    h2c = sb("h2c", [64, CTX])

    def wrap_sin(dst_t, dst_ap, ps_ap, ps_t, j, n):
        a = argt[0:64, 0:n]
        p.ts("dve", a, ps_ap, fb[:, 0:1], ALU.mult, [ps_t, fb, frb], [argt], s2=frb[:, j:j + 1], op1=ALU.add)
        w_ = wrp[0:64, 0:n]
        for bound, period in ((3 * PI, 4 * PI), (PI, 2 * PI)):
            p.ts("dve", w_, a, bound, ALU.is_gt, [argt], [wrp], s2=-period, op1=ALU.mult)
            p.tt("dve", a, a, w_, ALU.add, [argt, wrp], [argt])
            p.ts("dve", w_, a, -bound, ALU.is_lt, [argt], [wrp], s2=period, op1=ALU.mult)
            p.tt("dve", a, a, w_, ALU.add, [argt, wrp], [argt])
        p.act(dst_ap, a, AF.Sin, [argt], [dst_t])

    def mlp(z_dram, n, dst_t):
        for q in range(0, n, 512):
            m = min(512, n - q)
            zt = zch[(q // 512) % 2]
            p.dma("sp", zt[:, 0:m], z_dram[:, q:q + m], [z_dram], [zt])
            p.mm(PA[0][0:64, 0:m], w1[:], zt[:, 0:m], True, True, [w1, zt], [PA[0]])
            wrap_sin(h1c, h1c[:, 0:m], PA[0][0:64, 0:m], PA[0], 0, m)
            p.mm(PA[1][0:64, 0:m], w2[:], h1c[:, 0:m], True, True, [w2, h1c], [PA[1]])
            wrap_sin(dst_t, dst_t[:, q:q + m], PA[1][0:64, 0:m], PA[1], 1, m)

    mlp(cd["zl"], SEQ, H2)
    mlp(cd["zc"], CTX, h2c)

    cw = sb("cw", [64, 3, 4])
    hbr = sb("hbr", [64, 2, 64])
    hbc = sb("hbc", [64, 2])
    w3 = sb("fw3", [64, 4, 64], BF16)
    w3f = sb("fw3f", [64, 4, 64])
    wc = sb("wc", [64, CTX])
    uc = sb("uc", [64, 3, CTX])
    xct = sb("xct", [64, 3, CTX])
    hfc = sb("hfc", [64, 4, CTX])
    zc1 = sb("zc1", [64, CTX])
    yct = sb("yct", [64, CTX])
    xin = [sb("xin%d" % i, [128, 64]) for i in range(3)]
    Af = [sb("Af%d" % i, [128, 512]) for i in range(2)]
    tm = [sb("tm%d" % i, [128, 512]) for i in range(4)]
    tms = [[sb("tms%d_%d" % (a, b), [128, 256]) for b in range(4)] for a in range(4)]
    Apr = [sb("Apr%d" % i, [128, 4, 128], BF16) for i in range(2)]
    Api = [sb("Api%d" % i, [128, 4, 128], BF16) for i in range(2)]
    Xf = [sb("Xf%d" % i, [128, 512]) for i in range(4)]
    Yr = [sb("Yr%d" % i, [128, 4, 128], BF16) for i in range(2)]
    Yi = [sb("Yi%d" % i, [128, 4, 128], BF16) for i in range(2)]
    zf = [sb("zf%d" % i, [64, 4, 128]) for i in range(2)]
    zb = [sb("zb%d" % i, [64, 4, 128], BF16) for i in range(2)]
    xgb = [sb("xgb%d" % i, [64, 4, 128]) for i in range(2)]
    cnt = {"f": 0, "e": 0, "x": 0}

    def eng():
        cnt["e"] += 1
        return "dve" if cnt["e"] % 2 else "pool"

    def cmul(src_t, sr, si, tw_t, twr, twi, dr_t, dr, di_t, di, T4, e2):
        (a1, A1), (b1, B1), (a2, A2), (b2, B2) = T4
        p.tt("dve", a1, sr, twr, ALU.mult, [src_t, tw_t], [A1])
        p.tt("dve", b1, si, twi, ALU.mult, [src_t, tw_t], [B1])
        p.tt("dve", dr, a1, b1, ALU.subtract, [A1, B1], [dr_t])
        p.tt(e2, a2, sr, twi, ALU.mult, [src_t, tw_t], [A2])
        p.tt(e2, b2, si, twr, ALU.mult, [src_t, tw_t], [B2])
        p.tt(e2, di, a2, b2, ALU.add, [A2, B2], [di_t])

    def stage12(src_t, lhs_fn, rhs_t, rhs0, rhs1, lhs2_fn, TW, dr_t, di_t, src2_t=None):
        cnt["f"] += 1
        for hf in range(2):
            ps = (PA if rhs1 is None else PB)[hf]
            for k in range(2):
                c = 2 * hf + k
                cols = slice(k * 256, (k + 1) * 256)
                if rhs1 is None:
                    p.mm(ps[:, cols], lhs_fn(c), rhs0, True, True, [src_t, rhs_t], [ps])
                else:
                    p.mm(ps[:, cols], lhs_fn(c), rhs0, True, False, [src_t, rhs_t], [ps])
                    p.mm(ps[:, cols], lhs2_fn(c), rhs1, False, True, [src2_t, rhs_t], [ps])
            af = Af[hf]
            p.act(af[:], ps[:], AF.Identity, [ps], [af])
            a4 = af[:].rearrange("p (k r j) -> p k r j", k=2, r=2)
            twr = TW[:, 0, :].unsqueeze(1).broadcast_to([128, 2, 128])
            twi = TW[:, 1, :].unsqueeze(1).broadcast_to([128, 2, 128])
            tset = tms[(cnt["f"] % 2) * 2 + hf]
            T4 = [(t_[:].rearrange("p (k j) -> p k j", k=2), t_) for t_ in tset]
            cmul(af, a4[:, :, 0, :], a4[:, :, 1, :], TW, twr, twi,
                 dr_t, dr_t[:, 2 * hf:2 * hf + 2, :], di_t, di_t[:, 2 * hf:2 * hf + 2, :],
                 T4, "pool" if hf == 1 else "dve")

    def fwd_batch(src_t, lhs_fn):
        b = cnt["f"] % 2
        ar, ai = Apr[b], Api[b]
        stage12(src_t, lhs_fn, F1, F1[:], None, None, TWf, ar, ai)
        arf = ar[:].rearrange("p c j -> p (c j)")
        aif = ai[:].rearrange("p c j -> p (c j)")
        p.mm(PX[0][:], F2[:, 0, :], arf, True, False, [F2, ar], [PX[0]])
        p.mm(PX[0][:], F2[:, 2, :], aif, False, True, [F2, ai], [PX[0]])
        p.mm(PX[1][:], F2[:, 0, :], aif, True, False, [F2, ai], [PX[1]])
        p.mm(PX[1][:], F2[:, 1, :], arf, False, True, [F2, ar], [PX[1]])

    def inv_batch(yr, yi, py):
        b = cnt["f"] % 2
        br, bi = Apr[b], Api[b]
        stage12(yr, lambda c: yr[:, c, :], G2, G2[:, 0, :], G2[:, 1, :], lambda c: yi[:, c, :], TWi, br, bi, src2_t=yi)
        p.mm(py[0:64, :], G1[:, 0, :], br[:].rearrange("p c j -> p (c j)"), True, False, [G1, br], [py])
        p.mm(py[0:64, :], G1[:, 1, :], bi[:].rearrange("p c j -> p (c j)"), False, True, [G1, bi], [py])

    def short_conv(xt, xap, ut, uap, g, n):
        p.act(uap, xap, AF.Identity, [xt, cw], [ut], scale=cw[:, g, 1:2], bias=cw[:, g, 3:4])
        p.stt(uap[:, 1:n], xap[:, 0:n - 1], cw[:, g, 0:1], uap[:, 1:n], ALU.mult, ALU.add, [xt, cw, ut], [ut])
        p.stt(uap[:, 0:n - 1], xap[:, 1:n], cw[:, g, 2:3], uap[:, 0:n - 1], ALU.mult, ALU.add, [xt, cw, ut], [ut])

    def ctx_conv(zt, zap, o, gate_ap, out_t, out_ap):
        p.ts("dve", yct[:], zap, hbc[:, o:o + 1], ALU.mult, [zt, hbc], [yct])
        for q in range(CTX):
            p.stt(yct[:, q:CTX], zap[:, 0:CTX - q], hfc[:, 2 * o, q:q + 1], yct[:, q:CTX], ALU.mult, ALU.add, [zt, hfc, yct], [yct])
        for q in range(1, CTX):
            p.stt(yct[:, 0:CTX - q], zap[:, q:CTX], hfc[:, 2 * o + 1, q:q + 1], yct[:, 0:CTX - q], ALU.mult, ALU.add, [zt, hfc, yct], [yct])
        p.tt("dve", out_ap, yct[:], gate_ap, ALU.mult, [yct, uc], [out_t])

    hfv = BGb[:].bitcast(BF16)[0:64, :].rearrange("p (s c j) -> p s c j", s=2, c=64)
    win3 = BGa[0:64, :].rearrange("p (c j) -> p c j", j=128)
    for cg in range(4):
        p.dma("sp", cw[:], wd["cw"][l, cg], [wd["cw"]], [cw])
        p.dma("sp", hbc[:], wd["hbc"][l, cg], [wd["hbc"]], [hbc])
        for o in range(2):
            p.dma("sp", hbr[:, o, :], wd["hb"][l, cg, o:o + 1, :].partition_broadcast(64)[:, 0, :], [wd["hb"]], [hbr])
        p.dma("sp", w3f[:], wd["fw3"][l, cg], [wd["fw3"]], [w3f])
        p.dma("pool", w3[:], wd["fw3"][l, cg], [wd["fw3"]], [w3])
        p.dma("sp", wc[:], cd["wc"][cg], [cd["wc"]], [wc])
        for g in range(3):
            c0 = 768 + g * 256 + cg * 64
            for t in range(NTT):
                xi = xin[cnt["x"] % 3]
                cnt["x"] += 1
                p.dma("sp", xi[:], S1[t * 128:(t + 1) * 128, c0:c0 + 64], [S1], [xi])
                pt = PA[(t // 4) % 2]
                p.tr(pt[0:64, (t % 4) * 128:(t % 4 + 1) * 128], xi[:], ident[:], [xi, ident], [pt])
                if t % 4 == 3 and t < 64:
                    p.act(BGa[0:64, (t - 3) * 128:(t + 1) * 128], pt[0:64, :], AF.Identity, [pt], [BGa])
                if t == 65:
                    p.act(xct[:, g, :], pt[0:64, 0:256], AF.Identity, [pt], [xct])
            short_conv(BGa, BGa[0:64, :], BGb, BGb[0:64, :], g, SEQ)
            p.dma("sp", scr[g], BGb[0:64, :], [BGb], [scr])
            short_conv(xct, xct[:, g, :], uc, uc[:, g, :], g, CTX)
        for blk in range(4):
            p.mm(PX[0][0:64, 0:CTX], w3f[:, blk, :], h2c[:], True, True, [w3f, h2c], [PX[0]])
            p.tt("dve", hfc[:, blk, :], PX[0][0:64, 0:CTX], wc[:], ALU.mult, [PX[0], wc], [hfc])
        ctx_conv(uc, uc[:, 0, :], 0, uc[:, 1, :], zc1, zc1[:])
        ctx_conv(zc1, zc1[:], 1, uc[:, 2, :], zc1, zc1[:])
        p.dma("sp", YC[cg], zc1[:], [zc1], [YC])
        p.dma("sp", BGa[0:64, :], cd["wl"][cg].rearrange("p c j -> p (c j)"), [cd["wl"]], [BGa])
        for o in range(2):
            for n2 in range(128):
                ps = PY[n2 % 2]
                p.mm(ps[0:64, 0:128], H2[:, n2:SEQ:128], w3[:, 2 * o:2 * o + 2, :].rearrange("p s c -> p (s c)"), True, True, [H2, w3], [ps])
                p.tt("dve", hfv[:, :, :, n2], ps[0:64, 0:128].rearrange("p (s c) -> p s c", s=2),
                     win3[:, :, n2].unsqueeze(1).broadcast_to([64, 2, 64]), ALU.mult, [ps, BGa], [BGb])
            p.op("dve", lambda e: e.memset(hfv[0:1, 1, :, 0], 0.0), [], [BGb])
            for bt in range(16):
                c0 = 4 * bt
                fwd_batch(BGb, lambda c: hfv[:, 0, c0 + c, :])
                xr, xi_ = Xf[0], Xf[1]
                p.act(xr[:], PX[0][:], AF.Identity, [PX[0]], [xr])
                p.act(xi_[:], PX[1][:], AF.Identity, [PX[1]], [xi_])
                fwd_batch(BGb, lambda c: hfv[:, 1, c0 + c, :])
                p.tt("dve", Hs[:, 0, c0:c0 + 4, :], xr[:].rearrange("p (c j) -> p c j", c=4),
                     PX[0][:].rearrange("p (c j) -> p c j", c=4), ALU.add, [xr, PX[0]], [Hs])
                p.tt("dve", Hs[:, 1, c0:c0 + 4, :], xi_[:].rearrange("p (c j) -> p c j", c=4),
                     PX[1][:].rearrange("p (c j) -> p c j", c=4), ALU.subtract, [xi_, PX[1]], [Hs])
            for bt in range(16):
                c0 = 4 * bt
                b = bt % 2
                z_, zb_, g_ = zf[b], zb[b], xgb[b]
                if o == 0:
                    p.dma("sp", z_[:], scr[0, c0:c0 + 4, :].rearrange("c (p j) -> p c j", j=128), [scr], [z_])
                else:
                    p.dma("sp", z_[:], scr2[:, c0:c0 + 4, :], [scr2], [z_])
                p.dma("sp", g_[:], scr[1 + o, c0:c0 + 4, :].rearrange("c (p j) -> p c j", j=128), [scr], [g_])
                p.op("pool", lambda e: e.tensor_copy(out=zb_[:], in_=z_[:]), [z_], [zb_])
                p.tt("pool", z_[:], z_[:], hbr[:, o, c0:c0 + 4].unsqueeze(2).broadcast_to([64, 4, 128]), ALU.mult, [z_, hbr], [z_])
                fwd_batch(zb_, lambda c: zb_[:, c, :])
                xr, xi_ = Xf[2], Xf[3]
                p.act(xr[:], PX[0][:], AF.Identity, [PX[0]], [xr])
                p.act(xi_[:], PX[1][:], AF.Identity, [PX[1]], [xi_])
                x4r = xr[:].rearrange("p (c j) -> p c j", c=4)
                x4i = xi_[:].rearrange("p (c j) -> p c j", c=4)
                ta = tm[0][:].rearrange("p (c j) -> p c j", c=4)
                tb = tm[1][:].rearrange("p (c j) -> p c j", c=4)
                tc_ = tm[2][:].rearrange("p (c j) -> p c j", c=4)
                td_ = tm[3][:].rearrange("p (c j) -> p c j", c=4)
                p.tt("dve", ta, x4r, Hs[:, 0, c0:c0 + 4, :], ALU.mult, [xr, Hs], [tm[0]])
                p.tt("dve", tb, x4i, Hs[:, 1, c0:c0 + 4, :], ALU.mult, [xi_, Hs], [tm[1]])
                p.tt("dve", Yr[b][:], ta, tb, ALU.subtract, [tm[0], tm[1]], [Yr[b]])
                p.tt("pool", tc_, x4r, Hs[:, 1, c0:c0 + 4, :], ALU.mult, [xr, Hs], [tm[2]])
                p.tt("pool", td_, x4i, Hs[:, 0, c0:c0 + 4, :], ALU.mult, [xi_, Hs], [tm[3]])
                p.tt("pool", Yi[b][:], tc_, td_, ALU.add, [tm[2], tm[3]], [Yi[b]])
                py = PY[bt % 2]
                inv_batch(Yr[b], Yi[b], py)
                p.tt("dve", z_[:], py[0:64, :].rearrange("p (c j) -> p c j", c=4), z_[:], ALU.add, [py, z_], [z_])
                p.tt("pool", z_[:], z_[:], g_[:], ALU.mult, [z_, g_], [z_])
                if o == 0:
                    p.dma("sp", scr2[:, c0:c0 + 4, :], z_[:], [z_], [scr2])
                else:
                    p.dma("sp", YL[cg, :, c0:c0 + 4, :], z_[:], [z_], [YL])
    st.close()


def emit_l4(p, PS, S1, GO, cd):
    st = Stage(p)
    sb = st.sb
    ident = sb("ident", [128, 128])
    mask = sb("mask", [128, 128])
    Jm = sb("Jm", [128, 128], BF16)
    pm = sb("pm", [128, 2])
    ones = sb("ones", [32, 1])
    p.dma("sp", ident[:], cd["ident"][:], [cd["ident"]], [ident])
    p.dma("sp", mask[:], cd["mask"][:], [cd["mask"]], [mask])
    p.dma("pool", Jm[:], cd["J"][:], [cd["J"]], [Jm])
    p.dma("sp", pm[:], cd["pm"][:], [cd["pm"]], [pm])
    p.op("dve", lambda e: e.memset(ones[:], 1.0), [], [ones])
    qd = sb("qd", [32, TG], BF16)
    kd = sb("kd", [32, TG], BF16)
    ktT = [sb("ktT%d" % i, [128, NPR, 32], BF16) for i in range(2)]
    vnat = sb("vnat", [128, NPR, 64], BF16)
    vt = sb("vt", [128, NPR, 64], BF16)
    STm = sb("STm", [128, NPR, 128], BF16)
    Sbf = sb("Sbf", [32, NCH + 1, 64], BF16)
    dc = sb("dc", [32, NCH])
    oT = sb("oT", [64, TG])
    Scur = [sb("Scur%d" % i, [32, 64]) for i in range(2)]
    seg3 = sb("seg3", [32, 3, SEGL])
    gin = [sb("gin%d" % i, [128, 3, 32]) for i in range(3)]
    Gs = [sb("Gs%d" % i, [32, SEGL]) for i in range(2)]
    At = sb("At", [32, SEGL])
    Bt = sb("Bt", [32, SEGL])
    Sc = sb("Sc", [32, 22])
    tmpc = sb("tmpc", [32, 22])
    psT, psS, psK, psO = PS[0:2], PS[2:4], PS[4:6], PS[6:8]
    ng = 0
    for hd in range(4):
        p.dma("pool", vnat[:], S1[:, 1792 + hd * 64:1856 + hd * 64].rearrange("(t p) c -> p t c", p=128), [S1], [vnat], max_dma_last_dim=4096)
        for d in range(2):
            def gtile(j):
                return (64 + j if j < 2 else j - 2) if d == 0 else 65 - j
            if d == 0:
                p.op("pool", lambda e: e.tensor_copy(out=vt[:, 0:2, :], in_=vnat[:, 64:66, :]), [vnat], [vt])
                p.op("pool", lambda e: e.tensor_copy(out=vt[:, 2:66, :], in_=vnat[:, 0:64, :]), [vnat], [vt])
            else:
                for j0 in range(0, NPR, 8):
                    n = min(8, NPR - j0)
                    ps = psK[(j0 // 8) % 2]
                    for j in range(n):
                        p.mm(ps[:, j * 64:(j + 1) * 64], Jm[:], vnat[:, gtile(j0 + j), :], True, True, [Jm, vnat], [ps])
                    p.act(vt[:, j0:j0 + n, :], ps[:, 0:n * 64].rearrange("p (a b) -> p a b", b=64), AF.Identity, [ps], [vt])
            gcol = (2304 if d == 0 else 2432) + hd * 32
            for s in range(NSEG):
                seg = slice(s * SEGL, (s + 1) * SEGL)
                for jj in range(11):
                    j = s * 11 + jj
                    gt = gtile(j)
                    gi = gin[ng % 3]
                    ps = psT[ng % 2]
                    ng += 1
                    rows = slice(gt * 128, (gt + 1) * 128)
                    p.dma("sp", gi[:, 0:2, :], S1[rows, 1536 + hd * 32:1536 + hd * 32 + 256].rearrange("p (a c) -> p a c", a=2)[:, :, 0:32], [S1], [gi])
                    p.dma("sp", gi[:, 2, :], S1[rows, gcol:gcol + 32], [S1], [gi])
                    for a in range(3):
                        p.tr(ps[0:32, a * 128:(a + 1) * 128], gi[:, a, :], ident[:], [gi, ident], [ps])
                    dst = seg3[:, :, jj * 128:(jj + 1) * 128]
                    if d == 1:
                        dst = dst[:, :, ::-1]
                    p.act(dst, ps[0:32, 0:384].rearrange("p (a j) -> p a j", a=3), AF.Identity, [ps], [seg3])
                qseg, kseg, gseg = seg3[:, 0, :], seg3[:, 1, :], seg3[:, 2, :]
                G = Gs[s % 2]
                Gp = Gs[(s + 1) % 2]
                init = 0.0 if s == 0 else Gp[:, SEGL - 1:SEGL]
                rd = [seg3, ones] + ([] if s == 0 else [Gp])
                p.op("dve", lambda e: e.tensor_tensor_scan(out=G[:], data0=ones[:, 0:1].broadcast_to([32, SEGL]), data1=gseg,
                                                           initial=init, op0=ALU.mult, op1=ALU.add), rd, [G])
                G3 = G[:].rearrange("p (c j) -> p c j", j=64)
                Ec = G3[:, :, 63]
                if s == 0:
                    p.op("dve", lambda e: e.memset(Sc[:, 0:1], 0.0), [], [Sc])
                else:
                    p.op("dve", lambda e: e.tensor_copy(out=Sc[:, 0:1], in_=Gp[:, SEGL - 1:SEGL]), [Gp], [Sc])
                p.op("dve", lambda e: e.tensor_copy(out=Sc[:, 1:22], in_=G3[:, 0:21, 63]), [G], [Sc])
                A3 = At[:].rearrange("p (c j) -> p c j", j=64)
                p.tt("dve", A3, G3, Sc[:].unsqueeze(2).broadcast_to([32, 22, 64]), ALU.subtract, [G, Sc], [At])
                p.act(Bt[:], At[:], AF.Exp, [At], [Bt])
                p.tt("dve", qd[:, seg], qseg, Bt[:], ALU.mult, [seg3, Bt], [qd])
                p.act(Bt[:], At[:], AF.Exp, [At], [Bt], scale=-1.0)
                p.tt("dve", kd[:, seg], kseg, Bt[:], ALU.mult, [seg3, Bt], [kd])
                p.tt("dve", tmpc[:], Ec, Sc[:], ALU.subtract, [G, Sc], [tmpc])
                p.act(dc[:, s * 22:(s + 1) * 22], tmpc[:], AF.Exp, [tmpc], [dc])
                p.tt("dve", A3, Ec.unsqueeze(2).broadcast_to([32, 22, 64]), G3, ALU.subtract, [G], [At])
                p.act(At[:], At[:], AF.Exp, [At], [At])
                p.tt("dve", Bt[:], kseg, At[:], ALU.mult, [seg3, At], [Bt])
                ps = psS[s % 2]
                for j in range(11):
                    p.tr(ps[:, j * 32:(j + 1) * 32], Bt[:, j * 128:(j + 1) * 128], ident[0:32, 0:32], [Bt, ident], [ps])
                for hf in range(2):
                    p.act(ktT[hf][:, s * 11:(s + 1) * 11, :], ps[:, 0:352].rearrange("p (a b) -> p a b", b=32), AF.Identity,
                          [ps, pm], [ktT[hf]], scale=pm[:, hf:hf + 1])
            for g0 in range(0, NPR, 4):
                n = min(4, NPR - g0)
                ps = psS[(g0 // 4) % 2]
                for j in range(n):
                    tok = slice((g0 + j) * 128, (g0 + j + 1) * 128)
                    p.mm(ps[:, j * 128:(j + 1) * 128], kd[:, tok], qd[:, tok], True, True, [kd, qd], [ps])
                p.tt("dve", STm[:, g0:g0 + n, :], ps[:, 0:n * 128].rearrange("p (a b) -> p a b", b=128),
                     mask[:].unsqueeze(1).broadcast_to([128, n, 128]), ALU.mult, [ps, mask], [STm])
            p.op("dve", lambda e: e.memset(Scur[0][:], 0.0), [], [Scur[0]])
            p.op("dve", lambda e: e.memset(Sbf[:, 0, :], 0.0), [], [Sbf])
            for c0 in range(0, NCH, 8):
                n = min(8, NCH - c0)
                ps = psK[(c0 // 8) % 2]
                for j in range(n):
                    pr, hf = divmod(c0 + j, 2)
                    p.mm(ps[0:32, j * 64:(j + 1) * 64], ktT[hf][:, pr, :], vt[:, pr, :], True, True, [ktT[hf], vt], [ps])
                for j in range(n):
                    c = c0 + j
                    sa, sb_ = Scur[c % 2], Scur[(c + 1) % 2]
                    p.stt(sb_[:], sa[:], dc[:, c:c + 1], ps[0:32, j * 64:(j + 1) * 64], ALU.mult, ALU.add, [sa, dc, ps], [sb_])
                    p.act(Sbf[:, c + 1, :], sb_[:], AF.Identity, [sb_], [Sbf])
            for g0 in range(0, NPR, 4):
                n = min(4, NPR - g0)
                ps = psO[(g0 // 4) % 2]
                for j in range(n):
                    pr = g0 + j
                    cs_ = slice(j * 128, (j + 1) * 128)
                    p.mm(ps[0:64, cs_], vt[:, pr, :], STm[:, pr, :], True, False, [vt, STm], [ps])
                    for hf in range(2):
                        c = 2 * pr + hf
                        tok = slice(c * 64, (c + 1) * 64)
                        p.mm(ps[0:64, j * 128 + hf * 64:j * 128 + (hf + 1) * 64], Sbf[:, c, :], qd[:, tok], False, hf == 1, [Sbf, qd], [ps])
                p.act(oT[:, g0 * 128:(g0 + n) * 128], ps[0:64, 0:n * 128], AF.Identity, [ps], [oT])
            p.dma("sp", GO[hd, d], oT[:], [oT], [GO])
    st.close()


def emit_l5a(p, PS, l, SY, YL, YC, GO, S1, Hin, H1, MOD, wd, ident_d):
    st = Stage(p)
    sb = st.sb
    W = sb("W", [128, 8, D], BF16)
    og = sb("og", [128, 8])
    g1 = [sb("g1_%d" % k, [128, D]) for k in range(2)]
    ident = sb("ident", [128, 128])
    epst = sb("eps", [128, 1])
    p.op("dve", lambda e: e.memset(epst[:], EPS), [], [epst])
    p.dma("sp", og[:], wd["out_norm_g"][l], [wd["out_norm_g"]], [og])
    p.dma("sp", ident[:], ident_d[:], [ident_d], [ident])
    for k in range(2):
        p.dma("sp", g1[k][:], MOD[l, k:k + 1, 2 * D:3 * D].partition_broadcast(128)[:, 0, :], [MOD], [g1[k]])
    for c in range(8):
        p.dma("pool", W[:, c, :], wd["w_out"][l, c * 128:(c + 1) * 128, :], [wd["w_out"]], [W], max_dma_last_dim=4096)
    NB = 2
    yt = [sb("yt%d" % i, [128, D]) for i in range(NB)]
    ht = [sb("ht%d" % i, [128, D]) for i in range(NB)]
    srt = [sb("srt%d" % i, [128, 256]) for i in range(NB)]
    hin = [sb("hin%d" % i, [64, 4, 128]) for i in range(NB)]
    gf = [sb("gf%d" % i, [64, 4, 128]) for i in range(NB)]
    gb = [sb("gb%d" % i, [64, 4, 128]) for i in range(NB)]
    ho = [sb("ho%d" % i, [128, D]) for i in range(NB)]
    yT = [sb("yT%d" % i, [128, 8, 128], BF16) for i in range(NB)]
    sq = sb("sq", [128, D])
    st_ = [sb("st%d" % i, [128, 16]) for i in range(NB)]
    pst, pso, psx = PS[0:2], PS[2:6], PS[6:8]
    def phaseA(i):
        bi = i % NB
        rows = slice(i * 128, (i + 1) * 128)
        y, h, sr, s = yt[bi], ht[bi], srt[bi], st_[bi]
        p.dma("sp", y[:, 0:512], SY[rows, :], [SY], [y])
        p.dma("sp", sr[:], S1[rows, 2048:2304], [S1], [sr])
        p.dma("sp", h[:], Hin[rows, :], [Hin], [h])
        if i < 64:
            p.dma("sp", hin[bi][:], YL[:, i, :, :].rearrange("g c j -> c g j"), [YL], [hin[bi]])
            f0, b0 = 256 + i * 128, 256 + (63 - i) * 128
        else:
            c = i - 64
            p.dma("sp", hin[bi][:], YC[:, :, c * 128:(c + 1) * 128].rearrange("g c j -> c g j"), [YC], [hin[bi]])
            f0, b0 = c * 128, (1 - c) * 128
        p.dma("sp", gf[bi][:], GO[:, 0, :, f0:f0 + 128].rearrange("h v j -> v h j"), [GO], [gf[bi]])
        p.dma("sp", gb[bi][:], GO[:, 1, :, b0:b0 + 128].rearrange("h v j -> v h j"), [GO], [gb[bi]])
        p.tt("pool", gf[bi][:], gf[bi][:], gb[bi][:, :, ::-1], ALU.add, [gf[bi], gb[bi]], [gf[bi]])
        for a in range(4):
            p.tr(psx[0][:, a * 64:(a + 1) * 64], hin[bi][:, a, :], ident[0:64, 0:64], [hin[bi], ident], [psx[0]])
            p.tr(psx[1][:, a * 64:(a + 1) * 64], gf[bi][:, a, :], ident[0:64, 0:64], [gf[bi], ident], [psx[1]])
        p.act(y[:, 512:768], psx[0][:, 0:256], AF.Identity, [psx[0]], [y])
        p.act(y[:, 768:1024], psx[1][:, 0:256], AF.Identity, [psx[1]], [y])
        p.tt("pool", sq[:], y[:], y[:], ALU.mult, [y], [sq])
        p.op("dve", lambda e: e.tensor_reduce(out=s[:], in_=sq[:].rearrange("p (h d) -> p h d", d=64), axis=AX.X, op=ALU.add), [sq], [s])
        p.rsqrt(s, s[:], s, s[:], 1.0 / 64, epst)
        y3 = y[:].rearrange("p (h d) -> p h d", d=64)
        p.tt("dve", y3, y3, s[:].unsqueeze(2).broadcast_to([128, 16, 64]), ALU.mult, [y, s], [y])
        p.tt("pool", y[:, 768:1024], y[:, 768:1024], sr[:], ALU.mult, [y, sr], [y])
        for c in range(8):
            ps = pst[c // 4]
            blk = ps[:, (c % 4) * 128:(c % 4 + 1) * 128]
            p.tr(blk, y[:, c * 128:(c + 1) * 128], ident[:], [y, ident], [ps])
            p.act(yT[bi][:, c, :], blk, AF.Identity, [ps, og], [yT[bi]], scale=og[:, c:c + 1])

    def phaseB(i):
        kind = 0 if i < 64 else 1
        bi = i % NB
        rows = slice(i * 128, (i + 1) * 128)
        h = ht[bi]
        for hf in range(2):
            ps = pso[(2 * i + hf) % 4]
            cols = slice(hf * 512, (hf + 1) * 512)
            for c in range(8):
                p.mm(ps[:], yT[bi][:, c, :], W[:, c, cols], c == 0, c == 7, [yT[bi], W], [ps])
            p.tt("dve", ho[bi][:, cols], ps[:], g1[kind][:, cols], ALU.mult, [ps, g1[kind]], [ho[bi]])
            p.tt("pool", ho[bi][:, cols], ho[bi][:, cols], h[:, cols], ALU.add, [ho[bi], h], [ho[bi]])
        p.dma("sp", H1[rows, :], ho[bi][:], [ho[bi]], [H1])

    phaseA(0)
    for i in range(NTT):
        if i + 1 < NTT:
            phaseA(i + 1)
        phaseB(i)
    st.close()


def emit_l5b(p, PS, l, H1, Hout, MOD, wd, ident_d, final):
    st = Stage(p)
    sb = st.sb
    W1 = sb("W1", [128, 8, FFN], BF16)
    W3 = sb("W3", [128, 8, FFN], BF16)
    W2 = sb("W2", [128, NF, D], BF16)
    A = sb("A", [128, 8, 2])
    ng = sb("ng", [128, 8])
    g2 = [sb("g2_%d" % k, [128, D]) for k in range(3 if final else 2)]
    ident = sb("ident", [128, 128])
    epst = sb("eps", [128, 1])
    p.op("dve", lambda e: e.memset(epst[:], EPS), [], [epst])
    p.dma("sp", ident[:], ident_d[:], [ident_d], [ident])
    p.dma("sp", ng[:], wd["norm2_g"][l], [wd["norm2_g"]], [ng])
    for k in range(2):
        p.dma("sp", g2[k][:], MOD[l, k:k + 1, 5 * D:6 * D].partition_broadcast(128)[:, 0, :], [MOD], [g2[k]])
    if final:
        p.dma("sp", g2[2][:], wd["final_norm_g"][:].partition_broadcast(128)[:, 0, :], [wd["final_norm_g"]], [g2[2]])
    for c in range(8):
        p.dma("pool", W1[:, c, :], wd["ffn_w1"][l, c * 128:(c + 1) * 128, :], [wd["ffn_w1"]], [W1], max_dma_last_dim=4096)
        p.dma("pool", W3[:, c, :], wd["ffn_w3"][l, c * 128:(c + 1) * 128, :], [wd["ffn_w3"]], [W3], max_dma_last_dim=4096)
    for f in range(NF):
        p.dma("pool", W2[:, f, :], wd["ffn_w2"][l, f * 128:(f + 1) * 128, :], [wd["ffn_w2"]], [W2], max_dma_last_dim=4096)
    modT = load_modT(p, st, PS, MOD, l, ident)
    for k in range(2):
        p.stt(A[:, :, k], modT[:, k * 48 + 32:k * 48 + 40], 1.0, ng[:], ALU.add, ALU.mult, [modT, ng], [A])
    G = 2
    hts = [sb("ht%d" % i, [128, D]) for i in range(2 * G)]
    xs = sb("xs", [128, D])
    junk = sb("junk", [128, D])
    u2T = [sb("u2T%d" % i, [128, 8, G * 128], BF16) for i in range(2)]
    hidT = sb("hidT", [128, NF, G * 128], BF16)
    sil = [sb("sil%d" % i, [128, G * 128]) for i in range(2)]
    ho = [sb("ho%d" % i, [128, D]) for i in range(2)]
    st_ = sb("st", [128, 8])
    pst, psu, psd = PS[0:2], PS[2:6], PS[6:8]
    groups = [list(range(a, a + G)) for a in range(0, 64, G)] + [[64, 65]]
    nd = 0
    for gi, tiles in enumerate(groups):
        kind = 0 if tiles[0] < 64 else 1
        N = len(tiles) * 128
        u = u2T[gi % 2]
        for j, i in enumerate(tiles):
            h = hts[(gi % 2) * G + j]
            rows = slice(i * 128, (i + 1) * 128)
            p.dma("sp", h[:], H1[rows, :], [H1], [h])
            sc = st_[:, 2 * j:2 * j + 1]
            sr_ = st_[:, 2 * j + 1:2 * j + 2]
            p.act(junk[:], h[:], AF.Square, [h], [junk, st_], accum_out=sc)
            p.rsqrt(st_, sr_, st_, sc, 1.0 / D, epst)
            p.act(xs[:], h[:], AF.Identity, [h, st_], [xs], scale=sr_)
            for c in range(8):
                ps = pst[c // 4]
                blk = ps[:, (c % 4) * 128:(c % 4 + 1) * 128]
                p.tr(blk, xs[:, c * 128:(c + 1) * 128], ident[:], [xs, ident], [ps])
                p.act(u[:, c, j * 128:(j + 1) * 128], blk, AF.Identity, [ps, A, modT], [u],
                      scale=A[:, c, kind:kind + 1], bias=modT[:, kind * 48 + 24 + c:kind * 48 + 25 + c])
        for f in range(NF):
            ps1 = psu[(2 * f) % 4]
            ps3 = psu[(2 * f + 1) % 4]
            fc = slice(f * 128, (f + 1) * 128)
            for c in range(8):
                p.mm(ps1[:, 0:N], W1[:, c, fc], u[:, c, 0:N], c == 0, c == 7, [W1, u], [ps1])
            for c in range(8):
                p.mm(ps3[:, 0:N], W3[:, c, fc], u[:, c, 0:N], c == 0, c == 7, [W3, u], [ps3])
            s_ = sil[f % 2]
            p.act(s_[:, 0:N], ps1[:, 0:N], AF.Silu, [ps1], [s_])
            p.tt("dve", hidT[:, f, 0:N], s_[:, 0:N], ps3[:, 0:N], ALU.mult, [s_, ps3], [hidT])
        for j, i in enumerate(tiles):
            h = hts[(gi % 2) * G + j]
            rows = slice(i * 128, (i + 1) * 128)
            o = ho[nd % 2]
            nd += 1
            for hf in range(2):
                ps = psd[hf]
                cols = slice(hf * 512, (hf + 1) * 512)
                for f in range(NF):
                    p.mm(ps[:], hidT[:, f, j * 128:(j + 1) * 128], W2[:, f, cols], f == 0, f == NF - 1, [hidT, W2], [ps])
                p.tt("dve", o[:, cols], ps[:], g2[kind][:, cols], ALU.mult, [ps, g2[kind]], [o])
                p.tt("pool", o[:, cols], o[:, cols], h[:, cols], ALU.add, [o, h], [o])
            if final:
                sc = st_[:, 4:5]
                sr_ = st_[:, 5:6]
                p.act(junk[:], o[:], AF.Square, [o], [junk, st_], accum_out=sc)
                p.rsqrt(st_, sr_, st_, sc, 1.0 / D, epst)
                p.stt(o[:], o[:], sr_, g2[2][:], ALU.mult, ALU.mult, [o, st_, g2[2]], [o])
            p.dma("sp", Hout[rows, :], o[:], [o], [Hout])
    st.close()


FUSED_W = ["ada_w", "ada_b", "w_in", "qkg", "gla_gate_w", "gla_gate_b", "norm1_g", "norm2_g", "out_norm_g", "w_out",
           "ffn_w1", "ffn_w3", "ffn_w2", "final_norm_g", "cw", "hbc", "hb", "fw3", "filt_w1", "filt_w2", "fb"]
FUSED_C = ["cs", "ident", "mask", "J", "pm", "zl", "zc", "wl", "wc", "F1", "TWf", "F2", "G2", "TWi", "G1"]


def fused_host_inputs(w):
    rope = rope_table()
    one = np.concatenate([np.ones((CTX, 32), np.float32), np.zeros((CTX, 32), np.float32)], axis=1)
    j = np.arange(128)
    zl, win_l = hy_features(SEQ)
    zc, win_c = hy_features(CTX)
    consts = hy_consts()
    consts.update({
        "cs": np.concatenate([rope, one], axis=0),
        "ident": np.eye(128, dtype=np.float32),
        "mask": ((j[:, None] // 64 == j[None, :] // 64) & (j[None, :] >= j[:, None])).astype(np.float32),
        "J": np.eye(128, dtype=np.float32)[::-1].copy(),
        "pm": L4_PM,
        "zl": zl, "zc": zc,
        "wl": win_l.reshape(64, 128, 4, 64).transpose(2, 0, 3, 1),
        "wc": win_c.reshape(CTX, 4, 64).transpose(1, 2, 0),
    })
    fmL = lambda a: np.stack([fm(a[l]) for l in range(DEPTH)])
    cwf = w["hy_conv_w"].reshape(DEPTH, 3, 3, 4, 64)
    cbf = w["hy_conv_b"].reshape(DEPTH, 3, 4, 64)
    cw = np.concatenate([cwf.transpose(0, 3, 4, 2, 1), cbf.transpose(0, 2, 3, 1)[..., None]], axis=-1)
    hb = w["hy_bias"].reshape(DEPTH, 2, 4, 64).transpose(0, 2, 1, 3)
    ws = {
        "ada_w": w["ada_w"], "ada_b": w["ada_b"], "w_in": w["w_in"],
        "qkg": np.concatenate([w["q_norm_g"], w["k_norm_g"]], axis=1),
        "gla_gate_w": w["gla_gate_w"], "gla_gate_b": w["gla_gate_b"].reshape(DEPTH, 256),
        "norm1_g": fmL(w["norm1_g"]), "norm2_g": fmL(w["norm2_g"]), "out_norm_g": fmL(w["out_norm_g"]),
        "w_out": w["w_out"], "ffn_w1": w["ffn_w1"], "ffn_w3": w["ffn_w3"], "ffn_w2": w["ffn_w2"],
        "final_norm_g": w["final_norm_g"].reshape(1, D),
        "cw": cw, "hbc": hb.transpose(0, 1, 3, 2), "hb": hb,
        "fw3": w["filt_w3"].reshape(DEPTH, 64, 4, 4, 64).transpose(0, 3, 1, 2, 4),
        "filt_w1": w["filt_w1"], "filt_w2": w["filt_w2"],
        "fb": np.stack([w["filt_freq"], w["filt_b1"], w["filt_b2"]], axis=-1),
    }
    shared = {k: np.ascontiguousarray(v, dtype=np.float32) for k, v in {**ws, **consts}.items()}
    zeros = {k: np.zeros_like(v) for k, v in shared.items()}
    zeros["H0"] = np.zeros((TT, D), np.float32)
    zeros["cT"] = np.zeros((128, 8, 2), np.float32)
    in_maps = []
    for i in range(NCORES):
        b = i // 4
        if i % 4 != 0:
            in_maps.append(zeros)
            continue
        m = dict(shared)
        m["H0"] = np.ascontiguousarray(np.concatenate([w["x"][b], w["ctx"][b]], axis=0))
        cvec = np.stack([w["c"][b], w["c_ctx"]], axis=0)
        m["cT"] = np.ascontiguousarray(cvec.T.reshape(8, 128, 2).transpose(1, 0, 2))
        in_maps.append(m)
    return in_maps


def build_fused(shapes, nlayers=DEPTH, final=True, dump=()):
    p = Prog()
    H0 = p.dram_in("H0", [TT, D])
    cT = p.dram_in("cT", [128, 8, 2])
    wd = {k: p.dram_in(k, shapes[k]) for k in FUSED_W}
    cd = {k: p.dram_in(k, shapes[k]) for k in FUSED_C}
    out = p.dram_out("out", [TT, D])
    MOD = scratch(p, "MOD", [DEPTH, 2, 6 * D])
    S1 = scratch(p, "S1", [TT, OUT1])
    SY = scratch(p, "SY", [TT, 512])
    YL = scratch(p, "YL", [4, 64, 64, 128])
    YC = scratch(p, "YC", [4, 64, CTX])
    GO = scratch(p, "GO", [4, 2, 64, TT])
    H1 = scratch(p, "H1", [TT, D])
    HA = scratch(p, "HA", [TT, D])
    scr = scratch(p, "scr", [3, 64, SEQ])
    scr2 = scratch(p, "scr2", [64, 64, 128])
    PS2 = [p.ps("Q%d" % i, (128, 1024)) for i in range(4)]
    PS = [Tile(PS2[i // 2].h[:, (i % 2) * 512:(i % 2 + 1) * 512]) for i in range(8)]
    emit_ada(p, PS, cT, wd["ada_w"], wd["ada_b"], MOD)
    Hin = H0
    for l in range(nlayers):
        last = l == nlayers - 1
        emit_l1(p, PS, l, Hin, S1, MOD, wd, cd["cs"], cd["ident"])
        emit_l2(p, PS, S1, SY, cd["ident"], PS2)
        emit_l3(p, PS, l, S1, YL, YC, wd, cd, scr, scr2)
        emit_l4(p, PS, S1, GO, cd)
        emit_l5a(p, PS, l, SY, YL, YC, GO, S1, Hin, H1, MOD, wd, cd["ident"])
        emit_l5b(p, PS, l, H1, out if last else HA, MOD, wd, cd["ident"], final and last)
        Hin = HA
    for name in dump:
        src = {"S1": S1, "SY": SY, "YL": YL, "YC": YC, "GO": GO, "H1": H1, "MOD": MOD}[name]
        shp = list(src.h.shape)
        d = p.dram_out("dump_" + name, shp)
        flat = lambda ap: ap.rearrange(" ".join("abcd"[:len(shp)]) + " -> " + ("(" + " ".join("abcd"[:len(shp) - 1]) + ") " + "abcd"[len(shp) - 1] if len(shp) > 2 else "a b"))
        p.dma("sp", flat(d[:]), flat(src[:]), [src], [d])
    return p


def kernel_fused(inputs, nlayers=DEPTH, final=True, dump=()):
    w = {k: np.ascontiguousarray(np.asarray(v, dtype=np.float32)) for k, v in inputs.items()}
    in_maps = fused_host_inputs(w)
    shapes = {k: list(v.shape) for k, v in in_maps[0].items()}
    res = run(build_fused(shapes, nlayers, final, dump), in_maps)
    return res


def kernel(**inputs):
    res = kernel_fused(inputs)
    return np.ascontiguousarray(np.stack([res[0]["out"][:SEQ], res[4]["out"][:SEQ]], axis=0), dtype=np.float32)
```

```python
import math
import numpy as np
import concourse.bass as bass
import concourse.mybir as mybir
from concourse.bass_utils import run_bass_kernel_spmd

F32 = mybir.dt.float32
BF16 = mybir.dt.bfloat16
AF = mybir.ActivationFunctionType
ALU = mybir.AluOpType
AX = mybir.AxisListType

NCORES = 8
D = 1024
SEQ = 8192
CTX = 256
DEPTH = 4
IN_W = 2336
FFN = 2816
EPS = 1e-6
NSLOT = 8


class Tile:
    def __init__(self, h):
        self.h = h
        self.w = None
        self.r = {}

    def __getitem__(self, idx):
        return self.h[idx]


class Prog:
    def __init__(self, self_sync=True):
        self.nc = bass.Bass("TRN2", target_bir_lowering=False)
        nc = self.nc
        self.eng = {"pe": nc.tensor, "dve": nc.vector, "act": nc.scalar, "pool": nc.gpsimd, "sp": nc.sync}
        self.sems = {}
        self.cnt = {}
        self.known = {e: {} for e in self.eng}
        self.dslot = {"sp": 0, "act": 0, "pool": 0}
        self.self_sync = self_sync
        self._sem_cms = []
        self.n_inst = 0

    def _sem(self, sid):
        if sid not in self.sems:
            cm = self.nc.semaphore(sid)
            self._sem_cms.append(cm)
            self.sems[sid] = cm.__enter__()
            self.cnt[sid] = 0
        return self.sems[sid]

    def sb(self, name, shape, dtype=F32):
        return Tile(self.nc.alloc_sbuf_tensor("sb_" + name, list(shape), dtype))

    def ps(self, name, shape=(128, 512), dtype=F32):
        return Tile(self.nc.alloc_psum_tensor("ps_" + name, list(shape), dtype))

    def dram_in(self, name, shape, dtype=F32):
        return Tile(self.nc.dram_tensor(name, list(shape), dtype, kind="ExternalInput").ap())

    def dram_out(self, name, shape, dtype=F32):
        return Tile(self.nc.dram_tensor(name, list(shape), dtype, kind="ExternalOutput").ap())

    def _deps(self, eng, reads, writes):
        deps = {}

        def add(sid, v):
            if deps.get(sid, 0) < v:
                deps[sid] = v

        for t in reads:
            if t.w:
                add(*t.w)
        for t in writes:
            if t.w:
                add(*t.w)
            for sid, v in t.r.items():
                add(sid, v)
        if eng == "pe" or not self.self_sync:
            deps.pop("c_" + eng, None)
        return deps

    def _wait(self, eng, deps):
        e = self.eng[eng]
        kn = self.known[eng]
        for sid, val in deps.items():
            if kn.get(sid, 0) >= val:
                continue
            e.wait_ge(self.sems[sid], val)
            kn[sid] = val

    def _mark(self, tok, reads, writes):
        sid, v = tok
        for t in reads:
            if t.r.get(sid, 0) < v:
                t.r[sid] = v
        for t in writes:
            t.w = tok
            t.r = {}

    def op(self, eng, fn, reads=(), writes=()):
        self._wait(eng, self._deps(eng, reads, writes))
        inst = fn(self.eng[eng])
        sid = "c_" + eng
        sem = self._sem(sid)
        self.cnt[sid] += 1
        inst.then_inc(sem, 1)
        self._mark((sid, self.cnt[sid]), reads, writes)
        self.n_inst += 1
        return inst

    def dma(self, q, out, in_, reads=(), writes=(), **kw):
        deps = self._deps(q, reads, writes)
        slot = self.dslot[q]
        self.dslot[q] = (slot + 1) % NSLOT
        sid = "d_%s_%d" % (q, slot)
        sem = self._sem(sid)
        if self.cnt[sid] > 0 and deps.get(sid, 0) < self.cnt[sid]:
            deps[sid] = self.cnt[sid]
        self._wait(q, deps)
        inst = self.eng[q].dma_start(out=out, in_=in_, **kw)
        self.cnt[sid] += 16
        inst.then_inc(sem, 16)
        self._mark((sid, self.cnt[sid]), reads, writes)
        self.n_inst += 1
        return inst

    def finish(self):
        deps = {sid: c for sid, c in self.cnt.items() if sid.startswith("d_") and c > 0}
        self._wait("sp", deps)
        return self.nc

    def tt(self, eng, out, in0, in1, op, reads, writes):
        return self.op(eng, lambda e: e.tensor_tensor(out=out, in0=in0, in1=in1, op=op), reads, writes)

    def ts(self, eng, out, in0, s1, op0, reads, writes, s2=None, op1=None):
        if op1 is None:
            return self.op(eng, lambda e: e.tensor_scalar(out=out, in0=in0, scalar1=s1, scalar2=None, op0=op0), reads, writes)
        return self.op(eng, lambda e: e.tensor_scalar(out=out, in0=in0, scalar1=s1, scalar2=s2, op0=op0, op1=op1), reads, writes)

    def stt(self, out, in0, scalar, in1, op0, op1, reads, writes):
        return self.op("dve", lambda e: e.scalar_tensor_tensor(out=out, in0=in0, scalar=scalar, in1=in1, op0=op0, op1=op1), reads, writes)

    def act(self, out, in_, func, reads, writes, bias=None, scale=None, accum_out=None):
        kw = {}
        if bias is not None:
            kw["bias"] = bias
        if scale is not None:
            kw["scale"] = scale
        if accum_out is not None:
            kw["accum_out"] = accum_out
        return self.op("act", lambda e: e.activation(out=out, in_=in_, func=func, **kw), reads, writes)

    def mm(self, out, lhsT, rhs, start, stop, reads, writes):
        return self.op("pe", lambda e: e.matmul(out, lhsT, rhs, start=start, stop=stop), reads, writes)

    def tr(self, out, in_, ident, reads, writes):
        return self.op("pe", lambda e: e.transpose(out, in_, ident), reads, writes)

    def rsqrt(self, out_t, out_ap, in_t, in_ap, scale, eps_t):
        self.act(out_ap, in_ap, AF.Sqrt, [in_t, eps_t], [out_t], bias=eps_t[0:in_ap.shape[0], 0:1], scale=scale)
        self.op("dve", lambda e: e.reciprocal(out=out_ap, in_=out_ap), [out_t], [out_t])


def run(prog, in_maps):
    nc = prog.finish()
    res = run_bass_kernel_spmd(nc, in_maps, core_ids=list(range(len(in_maps))))
    return res.results


ADA_COLS = DEPTH * 6 * D // NCORES


def build_ada():
    p = Prog()
    cT = p.dram_in("cT", [128, 8, 3])
    w = p.dram_in("w", [D, ADA_COLS])
    b = p.dram_in("b", [1, ADA_COLS])
    out = p.dram_out("mod", [3, ADA_COLS])
    W = p.sb("W", [128, 8, ADA_COLS])
    cs = p.sb("cs", [128, 8, 3])
    bb = p.sb("bb", [3, ADA_COLS])
    res = p.sb("res", [3, ADA_COLS])
    p.dma("sp", cs[:], cT[:], [cT], [cs])
    p.dma("sp", bb[:], b[:].partition_broadcast(3)[:, 0, :], [b], [bb])
    for c in range(8):
        p.dma("sp" if c % 2 == 0 else "pool", W[:, c, :], w[c * 128:(c + 1) * 128, :], [w], [W])
    p.act(cs[:], cs[:], AF.Silu, [cs], [cs])
    pss = [p.ps("ps%d" % i) for i in range(2)]
    for j in range(ADA_COLS // 512):
        ps = pss[j % 2]
        for c in range(8):
            p.mm(ps[0:3, :], cs[:, c, :], W[:, c, j * 512:(j + 1) * 512], c == 0, c == 7, [cs, W], [ps])
        p.tt("dve", res[:, j * 512:(j + 1) * 512], ps[0:3, :], bb[:, j * 512:(j + 1) * 512], ALU.add, [ps, bb], [res])
    p.dma("sp", out[:], res[:], [res], [out])
    return p


def run_ada(c, c_ctx, ada_w, ada_b):
    cvec = np.concatenate([c, c_ctx[None, :]], axis=0)
    cT = np.ascontiguousarray(cvec.T.reshape(8, 128, 3).transpose(1, 0, 2))
    wall = np.ascontiguousarray(ada_w.transpose(1, 0, 2).reshape(D, DEPTH * 6 * D))
    ball = ada_b.reshape(1, DEPTH * 6 * D)
    in_maps = []
    for i in range(NCORES):
        sl = slice(i * ADA_COLS, (i + 1) * ADA_COLS)
        in_maps.append({"cT": cT, "w": np.ascontiguousarray(wall[:, sl]), "b": np.ascontiguousarray(ball[:, sl])})
    res = run(build_ada(), in_maps)
    mod = np.concatenate([r["mod"] for r in res], axis=1)
    return mod.reshape(3, DEPTH, 6, D)


NT1 = 17
OUT1 = 2560
GROUPS1 = [(0, 512), (512, 1024), (1024, 1536), (1536, 2048), (2048, 2336)]


def build_l1():
    p = Prog()
    ntok = NT1 * 128
    x_tm = p.dram_in("x_tm", [ntok, D])
    x_fm = p.dram_in("x_fm", [128, 8, ntok])
    w_in = p.dram_in("w_in", [D, IN_W])
    vecs = p.dram_in("vecs", [128, 8, 5])
    qkg = p.dram_in("qkg", [1, 128])
    cs_d = p.dram_in("cs", [ntok, 64])
    gw = p.dram_in("gw", [2, 16, 128])
    gb = p.dram_in("gb", [1, 256])
    ident_d = p.dram_in("ident", [128, 128])
    out = p.dram_out("out", [ntok, OUT1])

    W = p.sb("W", [128, 8, IN_W], BF16)
    V = p.sb("V", [128, 8, 5])
    A = p.sb("A", [128, 8, 2])
    Brep = p.sb("Brep", [128, 8, 128], BF16)
    bW = [p.sb("bW%d" % k, [128, IN_W]) for k in range(2)]
    gains = p.sb("gains", [128, 10, 64])
    Wblk = p.sb("Wblk", [32, 256])
    gbb = p.sb("gbb", [128, 256])
    ident = p.sb("ident", [128, 128])
    epst = p.sb("eps", [128, 1])
    pss = [p.ps("ps%d" % i) for i in range(6)]
    pst = p.ps("pst")
    psg = p.ps("psg")

    p.op("dve", lambda e: e.memset(epst[:], EPS), [], [epst])
    p.op("dve", lambda e: e.memset(Wblk[:], 0.0), [], [Wblk])
    p.dma("sp", V[:], vecs[:], [vecs], [V])
    p.dma("sp", ident[:], ident_d[:], [ident_d], [ident])
    p.dma("sp", gbb[:], gb[:].partition_broadcast(128)[:, 0, :], [gb], [gbb])
    p.dma("sp", Wblk[0:16, 0:128], gw[0], [gw], [Wblk])
    p.dma("sp", Wblk[16:32, 128:256], gw[1], [gw], [Wblk])
    for h in range(10):
        src = qkg[:, 0:64] if h < 8 else qkg[:, 64:128]
        p.dma("sp", gains[:, h, :], src.partition_broadcast(128)[:, 0, :], [qkg], [gains])
    p.ts("dve", gains[:, 0:8, :], gains[:, 0:8, :], 0.125, ALU.mult, [gains], [gains])
    for c in range(8):
        p.dma("pool", W[:, c, :], w_in[c * 128:(c + 1) * 128, :], [w_in], [W], max_dma_last_dim=4096)
    for k in range(2):
        p.stt(A[:, :, k], V[:, :, 1 + 2 * k], 1.0, V[:, :, 0], ALU.add, ALU.mult, [V], [A])
        p.op("dve", lambda e: e.tensor_copy(out=Brep[:], in_=V[:, :, 2 + 2 * k].unsqueeze(2).broadcast_to([128, 8, 128])), [V], [Brep])
        for gi, (a, b) in enumerate(GROUPS1):
            ps = pss[gi]
            for c in range(8):
                p.mm(ps[:, 0:b - a], Brep[:, c, :], W[:, c, a:b], c == 0, c == 7, [Brep, W], [ps])
            p.act(bW[k][:, a:b], ps[:, 0:b - a], AF.Identity, [ps], [bW[k]])

    NB = 2
    xt = [p.sb("xt%d" % i, [128, D]) for i in range(NB)]
    xf = [p.sb("xf%d" % i, [128, 8, 128]) for i in range(NB)]
    xa = [p.sb("xa%d" % i, [128, 8, 128], BF16) for i in range(NB)]
    cst = [p.sb("cst%d" % i, [128, 64]) for i in range(NB)]
    O = [p.sb("O%d" % i, [128, OUT1]) for i in range(NB)]
    junk = p.sb("junk", [128, D])
    sq = p.sb("sq", [128, 640])
    tmp = p.sb("tmp", [128, 10, 32])
    st = [p.sb("st%d" % i, [128, 16]) for i in range(NB)]
    lr = p.sb("lr", [128, 32])
    lrT = p.sb("lrT", [32, 128])
    gz = p.sb("gz", [128, 256])
    ga = p.sb("ga", [128, 256])
    pi = 0
    for i in range(NT1):
        kind = 0 if i < 16 else 1
        bi = i % NB
        rows = slice(i * 128, (i + 1) * 128)
        p.dma("sp", xt[bi][:], x_tm[rows, :], [x_tm], [xt[bi]])
        p.dma("sp", xf[bi][:], x_fm[:, :, rows], [x_fm], [xf[bi]])
        p.dma("sp", cst[bi][:], cs_d[rows, :], [cs_d], [cst[bi]])
        s = st[bi]
        p.act(junk[:], xt[bi][:], AF.Square, [xt[bi]], [junk, s], accum_out=s[:, 0:1])
        p.rsqrt(s, s[:, 1:2], s, s[:, 0:1], 1.0 / D, epst)
        p.tt("dve", xa[bi][:], xf[bi][:], A[:, :, kind].unsqueeze(2).broadcast_to([128, 8, 128]), ALU.mult, [xf[bi], A], [xa[bi]])
        o = O[bi]
        for gi, (a, b) in enumerate(GROUPS1):
            ps = pss[pi % 6]
            pi += 1
            for c in range(8):
                p.mm(ps[:, 0:b - a], xa[bi][:, c, :], W[:, c, a:b], c == 0, c == 7, [xa[bi], W], [ps])
            if gi < 4:
                p.stt(o[:, a:b], ps[:, 0:b - a], s[:, 1:2], bW[kind][:, a:b], ALU.mult, ALU.add, [ps, s, bW[kind]], [o])
            else:
                p.stt(o[:, 2048:2304], ps[:, 0:256], s[:, 1:2], bW[kind][:, 2048:2304], ALU.mult, ALU.add, [ps, s, bW[kind]], [o])
                p.stt(lr[:], ps[:, 256:288], s[:, 1:2], bW[kind][:, 2304:2336], ALU.mult, ALU.add, [ps, s, bW[kind]], [lr])
        qk = o[:, 0:640]
        p.tt("pool", sq[:], qk, qk, ALU.mult, [o], [sq])
        p.op("dve", lambda e: e.tensor_reduce(out=s[:, 2:12], in_=sq[:].rearrange("p (h d) -> p h d", d=64), axis=AX.X, op=ALU.add), [sq], [s])
        p.rsqrt(s, s[:, 2:12], s, s[:, 2:12], 1.0 / 64, epst)
        qk3 = qk.rearrange("p (h d) -> p h d", d=64)
        p.tt("dve", qk3, qk3, s[:, 2:12].unsqueeze(2).broadcast_to([128, 10, 64]), ALU.mult, [o, s], [o])
        p.tt("pool", qk3, qk3, gains[:], ALU.mult, [o, gains], [o])
        x1 = qk3[:, :, 0:32]
        x2 = qk3[:, :, 32:64]
        cb = cst[bi][:, 0:32].unsqueeze(1).broadcast_to([128, 10, 32])
        sb_ = cst[bi][:, 32:64].unsqueeze(1).broadcast_to([128, 10, 32])
        t3 = sq[:, 0:320].rearrange("p (h d) -> p h d", d=32)
        t4 = sq[:, 320:640].rearrange("p (h d) -> p h d", d=32)
        p.tt("dve", tmp[:], x2, sb_, ALU.mult, [o, cst[bi]], [tmp])
        p.tt("pool", t3, x1, sb_, ALU.mult, [o, cst[bi]], [sq])
        p.tt("dve", x1, x1, cb, ALU.mult, [o, cst[bi]], [o])
        p.tt("dve", x1, x1, tmp[:], ALU.subtract, [o, tmp], [o])
        p.tt("dve", x2, x2, cb, ALU.mult, [o, cst[bi]], [o])
        p.tt("dve", x2, x2, t3, ALU.add, [o, sq], [o])
        p.act(o[:, 1536:1664], o[:, 1536:1664], AF.Identity, [o], [o], scale=32 ** -0.5)
        p.act(o[:, 2048:2304], o[:, 2048:2304], AF.Silu, [o], [o])
        p.tr(pst[0:32, 0:128], lr[:], ident[:], [lr, ident], [pst])
        p.act(lrT[:], pst[0:32, 0:128], AF.Identity, [pst], [lrT])
        p.mm(psg[:, 0:256], lrT[:], Wblk[:], True, True, [lrT, Wblk], [psg])
        p.tt("dve", gz[:], psg[:, 0:256], gbb[:], ALU.add, [psg, gbb], [gz])
        p.stt(ga[:], gz[:], -1.0, gz[:], ALU.mult, ALU.min, [gz], [ga])
        p.act(ga[:], ga[:], AF.Exp, [ga], [ga])
        p.act(ga[:], ga[:], AF.Ln, [ga], [ga], bias=1.0)
        p.ts("dve", gz[:], gz[:], 0.0, ALU.min, [gz], [gz], s2=1.0 / 16, op1=ALU.mult)
        p.stt(o[:, 2304:2560], ga[:], -1.0 / 16, gz[:], ALU.mult, ALU.add, [ga, gz], [o])
        p.dma("sp", out[rows, :], o[:], [o], [out])
    return p


def fm(v):
    return np.ascontiguousarray(v.reshape(-1, 128).T)


def tok_split(lat, ctx):
    F = lat.shape[-1]
    outs = []
    for i in range(NCORES):
        b, j = divmod(i, 4)
        a = np.zeros((NT1 * 128, F), lat.dtype)
        a[:2048] = lat[b, j * 2048:(j + 1) * 2048]
        if j < 2:
            a[2048:] = ctx[b, j * 128:(j + 1) * 128]
        outs.append(a)
    return outs


def tok_merge(per_core):
    F = per_core[0].shape[-1]
    lat = np.zeros((2, SEQ, F), per_core[0].dtype)
    ctx = np.zeros((2, CTX, F), per_core[0].dtype)
    for i in range(NCORES):
        b, j = divmod(i, 4)
        lat[b, j * 2048:(j + 1) * 2048] = per_core[i][:2048]
        if j < 2:
            ctx[b, j * 128:(j + 1) * 128] = per_core[i][2048:]
    return lat, ctx


def rope_table():
    t = np.arange(SEQ)
    row = (t // 64).astype(np.float32)
    col = (t % 64).astype(np.float32)
    inv = np.power(np.float32(10000.0), -np.arange(16, dtype=np.float32) / np.float32(16)).astype(np.float32)
    ang = np.concatenate([row[:, None] * inv, col[:, None] * inv], axis=-1).astype(np.float32)
    return np.concatenate([np.cos(ang), np.sin(ang)], axis=-1).astype(np.float32)


_PROGS = {}


def get_prog(name, builder):
    return builder()


def run_l1(h_lat, h_ctx, mod, layer, w):
    xs = tok_split(h_lat, h_ctx)
    rope = rope_table()
    one = np.concatenate([np.ones((128, 32), np.float32), np.zeros((128, 32), np.float32)], axis=1)
    ident = np.eye(128, dtype=np.float32)
    in_maps = []
    for i in range(NCORES):
        b, j = divmod(i, 4)
        x = xs[i]
        vecs = np.stack([fm(w["norm1_g"][layer]), fm(mod[b, layer, 1]), fm(mod[b, layer, 0]),
                         fm(mod[2, layer, 1]), fm(mod[2, layer, 0])], axis=-1)
        cs = np.concatenate([rope[j * 2048:(j + 1) * 2048], one], axis=0)
        in_maps.append({
            "x_tm": x,
            "x_fm": np.ascontiguousarray(x.T.reshape(8, 128, NT1 * 128).transpose(1, 0, 2)),
            "w_in": w["w_in"][layer],
            "vecs": np.ascontiguousarray(vecs),
            "qkg": np.concatenate([w["q_norm_g"][layer], w["k_norm_g"][layer]])[None, :],
            "cs": cs,
            "gw": w["gla_gate_w"][layer],
            "gb": w["gla_gate_b"][layer].reshape(1, 256),
            "ident": ident,
        })
    res = run(build_l1(), in_maps)
    return tok_merge([r["out"] for r in res])


NQT = 33
NKT = 66


def build_l2():
    p = Prog()
    qT_d = p.dram_in("qT", [64, NQT, 512])
    kT_d = p.dram_in("kT", [64, NKT * 128])
    vA_d = p.dram_in("vA", [128, NKT, 65])
    ident_d = p.dram_in("ident", [128, 128])
    out = p.dram_out("out", [NQT * 128, 256])

    qT = p.sb("qT", [64, NQT, 512], BF16)
    kT = p.sb("kT", [64, NKT * 128], BF16)
    vA = p.sb("vA", [128, NKT, 65], BF16)
    ident = p.sb("ident", [128, 128])
    p.dma("sp", ident[:], ident_d[:], [ident_d], [ident])
    p.dma("pool", kT[:, 0:4096], kT_d[:, 0:4096], [kT_d], [kT], max_dma_last_dim=4096)
    p.dma("pool", kT[:, 4096:], kT_d[:, 4096:], [kT_d], [kT], max_dma_last_dim=4096)
    p.dma("pool", vA[:], vA_d[:], [vA_d], [vA], max_dma_last_dim=4096)
    for a in range(0, NQT, 3):
        p.dma("pool", qT[:, a:a + 3, :], qT_d[:, a:a + 3, :], [qT_d], [qT], max_dma_last_dim=4096)

    psS = [p.ps("psS%d" % i) for i in range(3)]
    psO = [p.ps("psO%d" % i) for i in range(2)]
    psT = [p.ps("psT%d" % i) for i in range(2)]
    pT = [p.sb("pT%d" % i, [128, 512], BF16) for i in range(3)]
    oT = [p.sb("oT%d" % i, [65, 512]) for i in range(2)]
    ot = [p.sb("ot%d" % i, [128, 256]) for i in range(2)]
    rec = p.sb("rec", [128, 8])
    it = 0
    for qt in range(NQT):
        kts = list(range(NKT)) if qt < 32 else [64, 65]
        po = psO[qt % 2]
        for j, kt in enumerate(kts):
            ps = psS[it % 3]
            pt = pT[it % 3]
            it += 1
            p.mm(ps[:], kT[:, kt * 128:(kt + 1) * 128], qT[:, qt, :], True, True, [kT, qT], [ps])
            p.act(pt[:], ps[:], AF.Exp, [ps], [pt])
            p.mm(po[0:65, :], vA[:, kt, :], pt[:], j == 0, j == len(kts) - 1, [vA, pt], [po])
        o_sb = oT[qt % 2]
        p.op("dve", lambda e: e.tensor_copy(out=o_sb[:], in_=po[0:65, :]), [po], [o_sb])
        o_t = ot[qt % 2]
        for h in range(4):
            pst = psT[h % 2]
            p.tr(pst[:, 0:65], o_sb[:, h * 128:(h + 1) * 128], ident[0:65, 0:65], [o_sb, ident], [pst])
            rc = rec[:, (qt % 2) * 4 + h:(qt % 2) * 4 + h + 1]
            p.op("dve", lambda e: e.reciprocal(out=rc, in_=pst[:, 64:65]), [pst], [rec])
            p.ts("dve", o_t[:, h * 64:(h + 1) * 64], pst[:, 0:64], rc, ALU.mult, [pst, rec], [o_t])
        p.dma("sp", out[qt * 128:(qt + 1) * 128, :], o_t[:], [o_t], [out])
    return p


def run_l2(lat, ctx):
    ident = np.eye(128, dtype=np.float32)
    in_maps = []
    for i in range(NCORES):
        b, g, qh = i // 4, (i // 2) % 2, i % 2
        ql = lat[b, qh * 4096:(qh + 1) * 4096, 0:512].reshape(32, 128, 8, 64)[:, :, 4 * g:4 * g + 4, :]
        qc = ctx[b, qh * 128:(qh + 1) * 128, 0:512].reshape(1, 128, 8, 64)[:, :, 4 * g:4 * g + 4, :]
        q = np.concatenate([ql, qc], axis=0)
        qT = np.ascontiguousarray(q.transpose(3, 0, 2, 1)).reshape(64, NQT, 512)
        k_all = np.concatenate([lat[b, :, 512:640], ctx[b, :, 512:640]], axis=0)[:, g * 64:(g + 1) * 64]
        kT = np.ascontiguousarray(k_all.T)
        v_all = np.concatenate([lat[b, :, 640:768], ctx[b, :, 640:768]], axis=0)[:, g * 64:(g + 1) * 64]
        vA = np.ones((128, NKT, 65), np.float32)
        vA[:, :, 0:64] = v_all.reshape(NKT, 128, 64).transpose(1, 0, 2)
        in_maps.append({"qT": qT, "kT": kT, "vA": vA, "ident": ident})
    res = run(build_l2(), in_maps)
    a_lat = np.zeros((2, SEQ, 512), np.float32)
    a_ctx = np.zeros((2, CTX, 512), np.float32)
    for i in range(NCORES):
        b, g, qh = i // 4, (i // 2) % 2, i % 2
        o = res[i]["out"]
        a_lat[b, qh * 4096:(qh + 1) * 4096, g * 256:(g + 1) * 256] = o[:4096]
        a_ctx[b, qh * 128:(qh + 1) * 128, g * 256:(g + 1) * 256] = o[4096:]
    return a_lat, a_ctx


def build_l5a():
    p = Prog()
    ntok = NT1 * 128
    y_d = p.dram_in("y", [ntok, D])
    sr_d = p.dram_in("sr", [ntok, 256])
    yb_d = p.dram_in("yb", [ntok, 256])
    h_d = p.dram_in("h", [ntok, D])
    w_d = p.dram_in("w_out", [D, D])
    og_d = p.dram_in("og", [128, 8])
    g1_d = p.dram_in("g1", [2, D])
    ident_d = p.dram_in("ident", [128, 128])
    out = p.dram_out("out", [ntok, D])

    W = p.sb("W", [128, 8, D], BF16)
    og = p.sb("og", [128, 8])
    g1 = [p.sb("g1_%d" % k, [128, D]) for k in range(2)]
    ident = p.sb("ident", [128, 128])
    epst = p.sb("eps", [128, 1])
    p.op("dve", lambda e: e.memset(epst[:], EPS), [], [epst])
    p.dma("sp", og[:], og_d[:], [og_d], [og])
    p.dma("sp", ident[:], ident_d[:], [ident_d], [ident])
    for k in range(2):
        p.dma("sp", g1[k][:], g1_d[k:k + 1, :].partition_broadcast(128)[:, 0, :], [g1_d], [g1[k]])
    for c in range(8):
        p.dma("pool", W[:, c, :], w_d[c * 128:(c + 1) * 128, :], [w_d], [W], max_dma_last_dim=4096)
    NB = 2
    yt = [p.sb("yt%d" % i, [128, D]) for i in range(NB)]
    ht = [p.sb("ht%d" % i, [128, D]) for i in range(NB)]
    srt = [p.sb("srt%d" % i, [128, 256]) for i in range(NB)]
    ybt = [p.sb("ybt%d" % i, [128, 256]) for i in range(NB)]
    ho = [p.sb("ho%d" % i, [128, D]) for i in range(NB)]
    yT = [p.sb("yT%d" % i, [128, 8, 128], BF16) for i in range(NB)]
    sq = p.sb("sq", [128, D])
    st = [p.sb("st%d" % i, [128, 16]) for i in range(NB)]
    pst = [p.ps("pst%d" % i) for i in range(2)]
    pso = [p.ps("pso%d" % i) for i in range(4)]
    for i in range(NT1):
        kind = 0 if i < 16 else 1
        bi = i % NB
        rows = slice(i * 128, (i + 1) * 128)
        y, h, sr, s = yt[bi], ht[bi], srt[bi], st[bi]
        p.dma("sp", y[:], y_d[rows, :], [y_d], [y])
        p.dma("sp", sr[:], sr_d[rows, :], [sr_d], [sr])
        p.dma("sp", h[:], h_d[rows, :], [h_d], [h])
        p.dma("sp", ybt[bi][:], yb_d[rows, :], [yb_d], [ybt[bi]])
        p.tt("pool", y[:, 768:1024], y[:, 768:1024], ybt[bi][:], ALU.add, [y, ybt[bi]], [y])
        p.tt("pool", sq[:], y[:], y[:], ALU.mult, [y], [sq])
        p.op("dve", lambda e: e.tensor_reduce(out=s[:], in_=sq[:].rearrange("p (h d) -> p h d", d=64), axis=AX.X, op=ALU.add), [sq], [s])
        p.rsqrt(s, s[:], s, s[:], 1.0 / 64, epst)
        y3 = y[:].rearrange("p (h d) -> p h d", d=64)
        p.tt("dve", y3, y3, s[:].unsqueeze(2).broadcast_to([128, 16, 64]), ALU.mult, [y, s], [y])
        p.tt("pool", y[:, 768:1024], y[:, 768:1024], sr[:], ALU.mult, [y, sr], [y])
        for c in range(8):
            ps = pst[c // 4]
            blk = ps[:, (c % 4) * 128:(c % 4 + 1) * 128]
            p.tr(blk, y[:, c * 128:(c + 1) * 128], ident[:], [y, ident], [ps])
            p.act(yT[bi][:, c, :], blk, AF.Identity, [ps, og], [yT[bi]], scale=og[:, c:c + 1])
        for hf in range(2):
            ps = pso[(2 * i + hf) % 4]
            cols = slice(hf * 512, (hf + 1) * 512)
            for c in range(8):
                p.mm(ps[:], yT[bi][:, c, :], W[:, c, cols], c == 0, c == 7, [yT[bi], W], [ps])
            p.tt("dve", ho[bi][:, cols], ps[:], g1[kind][:, cols], ALU.mult, [ps, g1[kind]], [ho[bi]])
            p.tt("pool", ho[bi][:, cols], ho[bi][:, cols], h[:, cols], ALU.add, [ho[bi], h], [ho[bi]])
        p.dma("sp", out[rows, :], ho[bi][:], [ho[bi]], [out])
    return p


def run_l5a(y_lat, y_ctx, yb_lat, yb_ctx, sr_lat, sr_ctx, h_lat, h_ctx, mod, layer, w):
    ys = tok_split(y_lat, y_ctx)
    ybs = tok_split(yb_lat, yb_ctx)
    srs = tok_split(sr_lat, sr_ctx)
    hs = tok_split(h_lat, h_ctx)
    ident = np.eye(128, dtype=np.float32)
    in_maps = []
    for i in range(NCORES):
        b = i // 4
        in_maps.append({"y": ys[i], "yb": ybs[i], "sr": srs[i], "h": hs[i], "w_out": w["w_out"][layer],
                        "og": fm(w["out_norm_g"][layer]),
                        "g1": np.ascontiguousarray(np.stack([mod[b, layer, 2], mod[2, layer, 2]])),
                        "ident": ident})
    res = run(build_l5a(), in_maps)
    return tok_merge([r["out"] for r in res])


NF = FFN // 128


def build_l5b(final):
    p = Prog()
    ntok = NT1 * 128
    h_d = p.dram_in("h", [ntok, D])
    w1_d = p.dram_in("w1", [D, FFN])
    w3_d = p.dram_in("w3", [D, FFN])
    w2_d = p.dram_in("w2", [FFN, D])
    vec_d = p.dram_in("vecs", [128, 8, 5])
    g2_d = p.dram_in("g2", [3, D])
    ident_d = p.dram_in("ident", [128, 128])
    out = p.dram_out("out", [ntok, D])

    W1 = p.sb("W1", [128, 8, FFN], BF16)
    W3 = p.sb("W3", [128, 8, FFN], BF16)
    W2 = p.sb("W2", [128, NF, D], BF16)
    V = p.sb("V", [128, 8, 5])
    A = p.sb("A", [128, 8, 2])
    g2 = [p.sb("g2_%d" % k, [128, D]) for k in range(3 if final else 2)]
    ident = p.sb("ident", [128, 128])
    epst = p.sb("eps", [128, 1])
    p.op("dve", lambda e: e.memset(epst[:], EPS), [], [epst])
    p.dma("sp", V[:], vec_d[:], [vec_d], [V])
    p.dma("sp", ident[:], ident_d[:], [ident_d], [ident])
    for k in range(len(g2)):
        p.dma("sp", g2[k][:], g2_d[k:k + 1, :].partition_broadcast(128)[:, 0, :], [g2_d], [g2[k]])
    for c in range(8):
        p.dma("pool", W1[:, c, :], w1_d[c * 128:(c + 1) * 128, :], [w1_d], [W1], max_dma_last_dim=4096)
        p.dma("pool", W3[:, c, :], w3_d[c * 128:(c + 1) * 128, :], [w3_d], [W3], max_dma_last_dim=4096)
    for f in range(NF):
        p.dma("pool", W2[:, f, :], w2_d[f * 128:(f + 1) * 128, :], [w2_d], [W2], max_dma_last_dim=4096)
    for k in range(2):
        p.stt(A[:, :, k], V[:, :, 1 + 2 * k], 1.0, V[:, :, 0], ALU.add, ALU.mult, [V], [A])

    G = 2
    hts = [p.sb("ht%d" % i, [128, D]) for i in range(2 * G)]
    xs = p.sb("xs", [128, D])
    junk = p.sb("junk", [128, D])
    u2T = [p.sb("u2T%d" % i, [128, 8, G * 128], BF16) for i in range(2)]
    hidT = p.sb("hidT", [128, NF, G * 128], BF16)
    sil = [p.sb("sil%d" % i, [128, G * 128]) for i in range(2)]
    ho = [p.sb("ho%d" % i, [128, D]) for i in range(2)]
    st = p.sb("st", [128, 8])
    pst = [p.ps("pst%d" % i) for i in range(2)]
    psu = [p.ps("psu%d" % i) for i in range(4)]
    psd = [p.ps("psd%d" % i) for i in range(2)]
    groups = [list(range(a, min(a + G, 16))) for a in range(0, 16, G)] + [[16]]
    nd = 0
    for gi, tiles in enumerate(groups):
        kind = 0 if tiles[0] < 16 else 1
        N = len(tiles) * 128
        u = u2T[gi % 2]
        for j, i in enumerate(tiles):
            h = hts[(gi % 2) * G + j]
            rows = slice(i * 128, (i + 1) * 128)
            p.dma("sp", h[:], h_d[rows, :], [h_d], [h])
            sc = st[:, 2 * j:2 * j + 1]
            sr_ = st[:, 2 * j + 1:2 * j + 2]
            p.act(junk[:], h[:], AF.Square, [h], [junk, st], accum_out=sc)
            p.rsqrt(st, sr_, st, sc, 1.0 / D, epst)
            p.act(xs[:], h[:], AF.Identity, [h, st], [xs], scale=sr_)
            for c in range(8):
                ps = pst[c // 4]
                blk = ps[:, (c % 4) * 128:(c % 4 + 1) * 128]
                p.tr(blk, xs[:, c * 128:(c + 1) * 128], ident[:], [xs, ident], [ps])
                p.act(u[:, c, j * 128:(j + 1) * 128], blk, AF.Identity, [ps, A, V], [u],
                      scale=A[:, c, kind:kind + 1], bias=V[:, c, 2 + 2 * kind:3 + 2 * kind])
        for f in range(NF):
            ps1 = psu[(2 * f) % 4]
            ps3 = psu[(2 * f + 1) % 4]
            fc = slice(f * 128, (f + 1) * 128)
            for c in range(8):
                p.mm(ps1[:, 0:N], W1[:, c, fc], u[:, c, 0:N], c == 0, c == 7, [W1, u], [ps1])
            for c in range(8):
                p.mm(ps3[:, 0:N], W3[:, c, fc], u[:, c, 0:N], c == 0, c == 7, [W3, u], [ps3])
            s_ = sil[f % 2]
            p.act(s_[:, 0:N], ps1[:, 0:N], AF.Silu, [ps1], [s_])
            p.tt("dve", hidT[:, f, 0:N], s_[:, 0:N], ps3[:, 0:N], ALU.mult, [s_, ps3], [hidT])
        for j, i in enumerate(tiles):
            h = hts[(gi % 2) * G + j]
            rows = slice(i * 128, (i + 1) * 128)
            o = ho[nd % 2]
            nd += 1
            for hf in range(2):
                ps = psd[hf]
                cols = slice(hf * 512, (hf + 1) * 512)
                for f in range(NF):
                    p.mm(ps[:], hidT[:, f, j * 128:(j + 1) * 128], W2[:, f, cols], f == 0, f == NF - 1, [hidT, W2], [ps])
                p.tt("dve", o[:, cols], ps[:], g2[kind][:, cols], ALU.mult, [ps, g2[kind]], [o])
                p.tt("pool", o[:, cols], o[:, cols], h[:, cols], ALU.add, [o, h], [o])
            if final:
                sc = st[:, 4:5]
                sr_ = st[:, 5:6]
                p.act(junk[:], o[:], AF.Square, [o], [junk, st], accum_out=sc)
                p.rsqrt(st, sr_, st, sc, 1.0 / D, epst)
                p.stt(o[:], o[:], sr_, g2[2][:], ALU.mult, ALU.mult, [o, st, g2[2]], [o])
            p.dma("sp", out[rows, :], o[:], [o], [out])
    return p


def run_l5b(h_lat, h_ctx, mod, layer, w, final):
    hs = tok_split(h_lat, h_ctx)
    ident = np.eye(128, dtype=np.float32)
    in_maps = []
    for i in range(NCORES):
        b = i // 4
        vecs = np.stack([fm(w["norm2_g"][layer]), fm(mod[b, layer, 4]), fm(mod[b, layer, 3]),
                         fm(mod[2, layer, 4]), fm(mod[2, layer, 3])], axis=-1)
        in_maps.append({"h": hs[i], "w1": w["ffn_w1"][layer], "w3": w["ffn_w3"][layer], "w2": w["ffn_w2"][layer],
                        "vecs": np.ascontiguousarray(vecs),
                        "g2": np.ascontiguousarray(np.stack([mod[b, layer, 5], mod[2, layer, 5], w["final_norm_g"]])),
                        "ident": ident})
    res = run(build_l5b(final), in_maps)
    return tok_merge([r["out"] for r in res])


TG = SEQ + CTX
NCH = TG // 64
NPR = TG // 128
SEGL = 1408
NSEG = TG // SEGL


def build_l4(stage=4):
    p = Prog()
    qkg_d = p.dram_in("qkg", [2, 3, 32, TG])
    v_d = p.dram_in("v", [2, 128, NPR, 64])
    mask_d = p.dram_in("mask", [128, 128])
    ident_d = p.dram_in("ident", [128, 128])
    out = p.dram_out("out", [2, 64, TG])

    ident = p.sb("ident", [128, 128])
    mask = p.sb("mask", [128, 128])
    ones = p.sb("ones", [32, 1])
    p.dma("sp", ident[:], ident_d[:], [ident_d], [ident])
    p.dma("sp", mask[:], mask_d[:], [mask_d], [mask])
    p.op("dve", lambda e: e.memset(ones[:], 1.0), [], [ones])

    qd = p.sb("qd", [32, TG], BF16)
    kd = p.sb("kd", [32, TG], BF16)
    ktT = [p.sb("ktT%d" % i, [128, NPR, 32], BF16) for i in range(2)]
    pm_d = p.dram_in("pm", [128, 2])
    pm = p.sb("pm", [128, 2])
    p.dma("sp", pm[:], pm_d[:], [pm_d], [pm])
    vt = p.sb("vt", [128, NPR, 64], BF16)
    STm = p.sb("STm", [128, NPR, 128], BF16)
    Sbf = p.sb("Sbf", [32, NCH + 1, 64], BF16)
    dc = p.sb("dc", [32, NCH])
    oT = p.sb("oT", [64, TG])
    Scur = [p.sb("Scur%d" % i, [32, 64]) for i in range(2)]
    gseg = p.sb("gseg", [32, SEGL])
    qseg = p.sb("qseg", [32, SEGL])
    kseg = p.sb("kseg", [32, SEGL])
    Gs = [p.sb("Gs%d" % i, [32, SEGL]) for i in range(2)]
    At = p.sb("At", [32, SEGL])
    Bt = p.sb("Bt", [32, SEGL])
    Sc = p.sb("Sc", [32, 22])
    tmpc = p.sb("tmpc", [32, 22])
    psT = [p.ps("psT%d" % i) for i in range(2)]
    psS = [p.ps("psS%d" % i) for i in range(2)]
    psK = [p.ps("psK%d" % i) for i in range(2)]
    psO = [p.ps("psO%d" % i) for i in range(2)]

    for d in range(2):
        p.dma("pool", vt[:], v_d[d], [v_d], [vt], max_dma_last_dim=4096)
        for s in range(NSEG):
            seg = slice(s * SEGL, (s + 1) * SEGL)
            G = Gs[s % 2]
            Gp = Gs[(s + 1) % 2]
            p.dma("sp", qseg[:], qkg_d[d, 0, :, seg], [qkg_d], [qseg])
            p.dma("sp", kseg[:], qkg_d[d, 1, :, seg], [qkg_d], [kseg])
            p.dma("sp", gseg[:], qkg_d[d, 2, :, seg], [qkg_d], [gseg])
            init = 0.0 if s == 0 else Gp[:, SEGL - 1:SEGL]
            rd = [gseg, ones] + ([] if s == 0 else [Gp])
            p.op("dve", lambda e: e.tensor_tensor_scan(out=G[:], data0=ones[:, 0:1].broadcast_to([32, SEGL]), data1=gseg[:],
                                                       initial=init, op0=ALU.mult, op1=ALU.add), rd, [G])
            G3 = G[:].rearrange("p (c j) -> p c j", j=64)
            Ec = G3[:, :, 63]
            if s == 0:
                p.op("dve", lambda e: e.memset(Sc[:, 0:1], 0.0), [], [Sc])
            else:
                p.op("dve", lambda e: e.tensor_copy(out=Sc[:, 0:1], in_=Gp[:, SEGL - 1:SEGL]), [Gp], [Sc])
            p.op("dve", lambda e: e.tensor_copy(out=Sc[:, 1:22], in_=G3[:, 0:21, 63]), [G], [Sc])
            A3 = At[:].rearrange("p (c j) -> p c j", j=64)
            p.tt("dve", A3, G3, Sc[:].unsqueeze(2).broadcast_to([32, 22, 64]), ALU.subtract, [G, Sc], [At])
            p.act(Bt[:], At[:], AF.Exp, [At], [Bt])
            p.tt("dve", qd[:, seg], qseg[:], Bt[:], ALU.mult, [qseg, Bt], [qd])
            p.act(Bt[:], At[:], AF.Exp, [At], [Bt], scale=-1.0)
            p.tt("dve", kd[:, seg], kseg[:], Bt[:], ALU.mult, [kseg, Bt], [kd])
            p.tt("dve", tmpc[:], Ec, Sc[:], ALU.subtract, [G, Sc], [tmpc])
            p.act(dc[:, s * 22:(s + 1) * 22], tmpc[:], AF.Exp, [tmpc], [dc])
            p.tt("dve", A3, Ec.unsqueeze(2).broadcast_to([32, 22, 64]), G3, ALU.subtract, [G], [At])
            p.act(At[:], At[:], AF.Exp, [At], [At])
            p.tt("dve", Bt[:], kseg[:], At[:], ALU.mult, [kseg, At], [Bt])
            ps = psT[s % 2]
            for j in range(11):
                p.tr(ps[:, j * 32:(j + 1) * 32], Bt[:, j * 128:(j + 1) * 128], ident[0:32, 0:32], [Bt, ident], [ps])
            for hf in range(2):
                p.act(ktT[hf][:, s * 11:(s + 1) * 11, :], ps[:, 0:352].rearrange("p (a b) -> p a b", b=32), AF.Identity,
                      [ps, pm], [ktT[hf]], scale=pm[:, hf:hf + 1])
        if stage < 4:
            p.op("dve", lambda e: e.memset(oT[:], 0.0), [], [oT])
        for g0 in (range(0, NPR, 4) if stage >= 2 else []):
            n = min(4, NPR - g0)
            ps = psS[(g0 // 4) % 2]
            for j in range(n):
                pr = g0 + j
                tok = slice(pr * 128, (pr + 1) * 128)
                p.mm(ps[:, j * 128:(j + 1) * 128], kd[:, tok], qd[:, tok], True, True, [kd, qd], [ps])
            p.tt("dve", STm[:, g0:g0 + n, :], ps[:, 0:n * 128].rearrange("p (a b) -> p a b", b=128),
                 mask[:].unsqueeze(1).broadcast_to([128, n, 128]), ALU.mult, [ps, mask], [STm])
        p.op("dve", lambda e: e.memset(Scur[0][:], 0.0), [], [Scur[0]])
        p.op("dve", lambda e: e.memset(Sbf[:, 0, :], 0.0), [], [Sbf])
        for c0 in (range(0, NCH, 8) if stage >= 3 else []):
            n = min(8, NCH - c0)
            ps = psK[(c0 // 8) % 2]
            for j in range(n):
                c = c0 + j
                pr, hf = divmod(c, 2)
                p.mm(ps[0:32, j * 64:(j + 1) * 64], ktT[hf][:, pr, :], vt[:, pr, :], True, True, [ktT[hf], vt], [ps])
            for j in range(n):
                c = c0 + j
                sa, sb_ = Scur[c % 2], Scur[(c + 1) % 2]
                p.stt(sb_[:], sa[:], dc[:, c:c + 1], ps[0:32, j * 64:(j + 1) * 64], ALU.mult, ALU.add, [sa, dc, ps], [sb_])
                p.act(Sbf[:, c + 1, :], sb_[:], AF.Identity, [sb_], [Sbf])
        for g0 in (range(0, NPR, 4) if stage >= 4 else []):
            n = min(4, NPR - g0)
            ps = psO[(g0 // 4) % 2]
            for j in range(n):
                pr = g0 + j
                cs_ = slice(j * 128, (j + 1) * 128)
                p.mm(ps[0:64, cs_], vt[:, pr, :], STm[:, pr, :], True, False, [vt, STm], [ps])
                for hf in range(2):
                    c = 2 * pr + hf
                    tok = slice(c * 64, (c + 1) * 64)
                    p.mm(ps[0:64, j * 128 + hf * 64:j * 128 + (hf + 1) * 64], Sbf[:, c, :], qd[:, tok], False, hf == 1, [Sbf, qd], [ps])
            p.act(oT[:, g0 * 128:(g0 + n) * 128], ps[0:64, 0:n * 128], AF.Identity, [ps], [oT])
        p.dma("sp", out[d], oT[:], [oT], [out])
    return p


L4_PM = np.stack([(np.arange(128) < 64), (np.arange(128) >= 64)], axis=1).astype(np.float32)


def run_l4(lat, ctx):
    ident = np.eye(128, dtype=np.float32)
    j = np.arange(128)
    mask = ((j[:, None] // 64 == j[None, :] // 64) & (j[None, :] >= j[:, None])).astype(np.float32)
    in_maps = []
    for i in range(NCORES):
        b, hd = divmod(i, 4)
        qkg = np.zeros((2, 3, 32, TG), np.float32)
        v = np.zeros((2, 128, NPR, 64), np.float32)
        for d in range(2):
            def seq(c0, w):
                a, l = ctx[b, :, c0:c0 + w], lat[b, :, c0:c0 + w]
                if d == 1:
                    a, l = a[::-1], l[::-1]
                return np.concatenate([a, l], axis=0)
            qkg[d, 0] = seq(1536 + hd * 32, 32).T
            qkg[d, 1] = seq(1664 + hd * 32, 32).T
            qkg[d, 2] = seq((2304 if d == 0 else 2432) + hd * 32, 32).T
            v[d] = seq(1792 + hd * 64, 64).reshape(NPR, 128, 64).transpose(1, 0, 2)
        in_maps.append({"qkg": qkg, "v": v, "mask": mask, "ident": ident, "pm": L4_PM})
    res = run(build_l4(), in_maps)
    f_lat = np.zeros((2, SEQ, 256), np.float32)
    b_lat = np.zeros((2, SEQ, 256), np.float32)
    f_ctx = np.zeros((2, CTX, 256), np.float32)
    b_ctx = np.zeros((2, CTX, 256), np.float32)
    for i in range(NCORES):
        b, hd = divmod(i, 4)
        o = res[i]["out"]
        cols = slice(hd * 64, (hd + 1) * 64)
        f_ctx[b, :, cols] = o[0].T[:CTX]
        f_lat[b, :, cols] = o[0].T[CTX:]
        b_ctx[b, :, cols] = o[1].T[:CTX][::-1]
        b_lat[b, :, cols] = o[1].T[CTX:][::-1]
    return f_lat, b_lat, f_ctx, b_ctx


NFFT = 16384
PI = math.pi


def hy_consts():
    n1 = np.arange(64)[:, None]
    k = np.arange(128)[None, :]
    th = 2 * np.pi * n1 * k / 128
    F1 = np.concatenate([np.cos(th), -np.sin(th)], axis=1)
    n2 = np.arange(128)[:, None]
    tw = 2 * np.pi * n2 * k / NFFT
    TWf = np.stack([np.cos(tw), -np.sin(tw)], axis=1)
    th2 = 2 * np.pi * n2 * k / 128
    F2 = np.stack([np.cos(th2), -np.sin(th2), np.sin(th2)], axis=1)
    G2 = np.stack([np.concatenate([np.cos(th2), np.sin(th2)], axis=1),
                   np.concatenate([-np.sin(th2), np.cos(th2)], axis=1)], axis=1)
    TWi = np.stack([np.cos(tw.T), np.sin(tw.T)], axis=1)
    k1 = np.arange(128)[:, None]
    n1r = np.arange(64)[None, :]
    th1 = 2 * np.pi * k1 * n1r / 128
    G1 = np.stack([np.cos(th1), -np.sin(th1)], axis=1) / NFFT
    f32 = lambda a: np.ascontiguousarray(a, dtype=np.float32)
    return {"F1": f32(F1), "TWf": f32(TWf), "F2": f32(F2), "G2": f32(G2), "TWi": f32(TWi), "G1": f32(G1)}


def hy_features(n):
    t = np.linspace(0.0, 1.0, n, dtype=np.float32)[:, None]
    omega = (np.float32(2.0 * math.pi / n) * np.arange(n, dtype=np.float32)).astype(np.float32)
    bands = np.linspace(1e-4, 15, 16, dtype=np.float32)
    phase = (omega[:, None] * bands[None, :]).astype(np.float32)
    z = np.concatenate([t, np.cos(phase), -np.sin(phase)], axis=-1).astype(np.float32)
    mn, mx = math.log(1e-2) / 1.5, math.log(1e-2) / 0.3
    deltas = np.abs(np.linspace(mn, mx, 256, dtype=np.float32))
    window = np.exp(-t * deltas[None, :]).astype(np.float32)
    return np.ascontiguousarray(z.T), window


def build_l3(debug=False):
    p = Prog()
    x_d = p.dram_in("x", [3, 64, SEQ])
    xc_d = p.dram_in("xc", [3, 64, CTX])
    cw_d = p.dram_in("cw", [64, 3, 4])
    hb_d = p.dram_in("hb", [2, 64])
    hbc_d = p.dram_in("hbc", [64, 2])
    w1_d = p.dram_in("fw1", [33, 64])
    w2_d = p.dram_in("fw2", [64, 64])
    w3_d = p.dram_in("fw3", [64, 4, 64])
    fb_d = p.dram_in("fb", [64, 3])
    zl_d = p.dram_in("zl", [33, SEQ])
    zc_d = p.dram_in("zc", [33, CTX])
    wl_d = p.dram_in("wl", [64, 64, 128])
    wc_d = p.dram_in("wc", [64, CTX])
    cd = {k: p.dram_in(k, list(v.shape)) for k, v in hy_consts().items()}
    scr = Tile(p.nc.dram_tensor("scr", [3, 64, SEQ], F32, kind="Internal").ap())
    yl_d = p.dram_out("yl", [64, 64, 128])
    yc_d = p.dram_out("yc", [64, CTX])

    BG = [p.sb("BG%d" % i, [128, SEQ]) for i in range(3)]
    Hs = p.sb("Hs", [128, 2, 64, 128], BF16)
    F1 = p.sb("F1", [64, 256], BF16)
    TWf = p.sb("TWf", [128, 2, 128])
    F2 = p.sb("F2", [128, 3, 128], BF16)
    G2 = p.sb("G2", [128, 2, 256], BF16)
    TWi = p.sb("TWi", [128, 2, 128])
    G1 = p.sb("G1", [128, 2, 64], BF16)
    for nm, t in (("F1", F1), ("F2", F2), ("G2", G2), ("G1", G1)):
        p.dma("pool", t[:], cd[nm][:], [cd[nm]], [t])
    p.dma("sp", TWf[:], cd["TWf"][:], [cd["TWf"]], [TWf])
    p.dma("sp", TWi[:], cd["TWi"][:], [cd["TWi"]], [TWi])
    cw = p.sb("cw", [64, 3, 4])
    hbr = p.sb("hbr", [64, 2, 64])
    hbc = p.sb("hbc", [64, 2])
    w1 = p.sb("fw1", [33, 64])
    w2 = p.sb("fw2", [64, 64])
    w3 = p.sb("fw3", [64, 4, 64], BF16)
    w3f = p.sb("fw3f", [64, 4, 64])
    fb = p.sb("fb", [64, 3])
    frb = p.sb("frb", [64, 2])
    wc = p.sb("wc", [64, CTX])
    p.dma("sp", cw[:], cw_d[:], [cw_d], [cw])
    p.dma("sp", hbc[:], hbc_d[:], [hbc_d], [hbc])
    for o in range(2):
        p.dma("sp", hbr[:, o, :], hb_d[o:o + 1, :].partition_broadcast(64)[:, 0, :], [hb_d], [hbr])
    p.dma("sp", w1[:], w1_d[:], [w1_d], [w1])
    p.dma("sp", w2[:], w2_d[:], [w2_d], [w2])
    p.dma("sp", w3f[:], w3_d[:], [w3_d], [w3f])
    p.dma("pool", w3[:], w3_d[:], [w3_d], [w3])
    p.dma("sp", fb[:], fb_d[:], [fb_d], [fb])
    p.dma("sp", wc[:], wc_d[:], [wc_d], [wc])
    for j in range(2):
        p.tt("dve", frb[:, j:j + 1], fb[:, 0:1], fb[:, 1 + j:2 + j], ALU.mult, [fb], [frb])
    PS = [p.ps("P%d" % i) for i in range(8)]
    PA, PX, PB, PY = PS[0:2], PS[2:4], PS[4:6], PS[6:8]

    def short_conv(xt, xap, ut, uap, g, n):
        p.act(uap, xap, AF.Identity, [xt, cw], [ut], scale=cw[:, g, 1:2], bias=cw[:, g, 3:4])
        p.stt(uap[:, 1:n], xap[:, 0:n - 1], cw[:, g, 0:1], uap[:, 1:n], ALU.mult, ALU.add, [xt, cw, ut], [ut])
        p.stt(uap[:, 0:n - 1], xap[:, 1:n], cw[:, g, 2:3], uap[:, 0:n - 1], ALU.mult, ALU.add, [xt, cw, ut], [ut])

    for g in range(3):
        p.dma("sp", BG[0][0:64, :], x_d[g], [x_d], [BG[0]])
        short_conv(BG[0], BG[0][0:64, :], BG[1], BG[1][0:64, :], g, SEQ)
        p.dma("sp", scr[g], BG[1][0:64, :], [BG[1]], [scr])
    uc = p.sb("uc", [64, 3, CTX])
    xct = p.sb("xct", [64, 3, CTX])
    p.dma("sp", xct[:], xc_d[:].rearrange("g c t -> c g t"), [xc_d], [xct])
    for g in range(3):
        short_conv(xct, xct[:, g, :], uc, uc[:, g, :], g, CTX)

    def wrap_sin(dst_t, dst_ap, ps_ap, ps_t, j, n, arg_t):
        a = arg_t[0:64, 0:n]
        p.ts("dve", a, ps_ap, fb[:, 0:1], ALU.mult, [ps_t, fb, frb], [arg_t], s2=frb[:, j:j + 1], op1=ALU.add)
        w_ = wrp[0:64, 0:n]
        for bound, period in ((3 * PI, 4 * PI), (PI, 2 * PI)):
            p.ts("dve", w_, a, bound, ALU.is_gt, [arg_t], [wrp], s2=-period, op1=ALU.mult)
            p.tt("dve", a, a, w_, ALU.add, [arg_t, wrp], [arg_t])
            p.ts("dve", w_, a, -bound, ALU.is_lt, [arg_t], [wrp], s2=period, op1=ALU.mult)
            p.tt("dve", a, a, w_, ALU.add, [arg_t, wrp], [arg_t])
        p.act(dst_ap, a, AF.Sin, [arg_t], [dst_t])

    argt = p.sb("argt", [64, 512])
    wrp = p.sb("wrp", [64, 512])
    h1c = p.sb("h1c", [64, 512])
    zch = [p.sb("zch%d" % i, [33, 512]) for i in range(2)]

    def mlp(z_dram, n, dst_t, dst_fn):
        for q in range(0, n, 512):
            m = min(512, n - q)
            zt = zch[(q // 512) % 2]
            p.dma("sp", zt[:, 0:m], z_dram[:, q:q + m], [z_dram], [zt])
            p.mm(PA[0][0:64, 0:m], w1[:], zt[:, 0:m], True, True, [w1, zt], [PA[0]])
            wrap_sin(h1c, h1c[:, 0:m], PA[0][0:64, 0:m], PA[0], 0, m, argt)
            p.mm(PA[1][0:64, 0:m], w2[:], h1c[:, 0:m], True, True, [w2, h1c], [PA[1]])
            wrap_sin(dst_t, dst_fn(q, m), PA[1][0:64, 0:m], PA[1], 1, m, argt)

    h2 = BG[1][:].bitcast(BF16)
    mlp(zl_d, SEQ, BG[1], lambda q, m: h2[0:64, q:q + m])
    h2c = p.sb("h2c", [64, CTX])
    mlp(zc_d, CTX, h2c, lambda q, m: h2c[:, q:q + m])

    hfc = p.sb("hfc", [64, 4, CTX])
    for blk in range(4):
        p.mm(PX[0][0:64, 0:CTX], w3f[:, blk, :], h2c[:], True, True, [w3f, h2c], [PX[0]])
        p.tt("dve", hfc[:, blk, :], PX[0][0:64, 0:CTX], wc[:], ALU.mult, [PX[0], wc], [hfc])
    zc1 = p.sb("zc1", [64, CTX])
    yct = p.sb("yct", [64, CTX])

    def ctx_conv(zt, zap, o, gate_ap, out_t, out_ap):
        p.ts("dve", yct[:], zap, hbc[:, o:o + 1], ALU.mult, [zt, hbc], [yct])
        for q in range(CTX):
            p.stt(yct[:, q:CTX], zap[:, 0:CTX - q], hfc[:, 2 * o, q:q + 1], yct[:, q:CTX], ALU.mult, ALU.add, [zt, hfc, yct], [yct])
        for q in range(1, CTX):
            p.stt(yct[:, 0:CTX - q], zap[:, q:CTX], hfc[:, 2 * o + 1, q:q + 1], yct[:, 0:CTX - q], ALU.mult, ALU.add, [zt, hfc, yct], [yct])
        p.tt("dve", out_ap, yct[:], gate_ap, ALU.mult, [yct, uc], [out_t])

    ctx_conv(uc, uc[:, 0, :], 0, uc[:, 1, :], zc1, zc1[:])
    ctx_conv(zc1, zc1[:], 1, uc[:, 2, :], zc1, zc1[:])
    p.dma("sp", yc_d[:], zc1[:], [zc1], [yc_d])

    Af = [p.sb("Af%d" % i, [128, 512]) for i in range(2)]
    tm = [p.sb("tm%d" % i, [128, 512]) for i in range(4)]
    Apr = [p.sb("Apr%d" % i, [128, 4, 128], BF16) for i in range(2)]
    Api = [p.sb("Api%d" % i, [128, 4, 128], BF16) for i in range(2)]
    Xf = [p.sb("Xf%d" % i, [128, 512]) for i in range(4)]
    Yr = [p.sb("Yr%d" % i, [128, 4, 128], BF16) for i in range(2)]
    Yi = [p.sb("Yi%d" % i, [128, 4, 128], BF16) for i in range(2)]
    cnt = {"f": 0, "e": 0}

    def eng():
        cnt["e"] += 1
        return "dve" if cnt["e"] % 2 else "pool"

    def cmul(src_t, sr, si, tw_t, twr, twi, dr_t, dr, di_t, di, t_a, t_b):
        e1, e2 = eng(), eng()
        p.tt(e1, t_a[0], sr, twr, ALU.mult, [src_t, tw_t], [t_a[1]])
        p.tt(e2, t_b[0], si, twi, ALU.mult, [src_t, tw_t], [t_b[1]])
        p.tt(e1, dr, t_a[0], t_b[0], ALU.subtract, [t_a[1], t_b[1]], [dr_t])
        p.tt(e1, t_a[0], sr, twi, ALU.mult, [src_t, tw_t], [t_a[1]])
        p.tt(e2, t_b[0], si, twr, ALU.mult, [src_t, tw_t], [t_b[1]])
        p.tt(e2, di, t_a[0], t_b[0], ALU.add, [t_a[1], t_b[1]], [di_t])

    def stage12(src_t, lhs_fn, rhs_t, rhs0, rhs1, lhs2_fn, TW, dr_t, di_t, src2_t=None):
        b = cnt["f"] % 2
        cnt["f"] += 1
        for hf in range(2):
            ps = (PA if rhs1 is None else PB)[hf]
            for k in range(2):
                c = 2 * hf + k
                cols = slice(k * 256, (k + 1) * 256)
                if rhs1 is None:
                    p.mm(ps[:, cols], lhs_fn(c), rhs0, True, True, [src_t, rhs_t], [ps])
                else:
                    p.mm(ps[:, cols], lhs_fn(c), rhs0, True, False, [src_t, rhs_t], [ps])
                    p.mm(ps[:, cols], lhs2_fn(c), rhs1, False, True, [src2_t, rhs_t], [ps])
            af = Af[hf]
            p.act(af[:], ps[:], AF.Identity, [ps], [af])
            a4 = af[:].rearrange("p (k r j) -> p k r j", k=2, r=2)
            twr = TW[:, 0, :].unsqueeze(1).broadcast_to([128, 2, 128])
            twi = TW[:, 1, :].unsqueeze(1).broadcast_to([128, 2, 128])
            ta = tm[2 * hf][:, 0:256].rearrange("p (k j) -> p k j", k=2)
            tb = tm[2 * hf + 1][:, 0:256].rearrange("p (k j) -> p k j", k=2)
            cmul(af, a4[:, :, 0, :], a4[:, :, 1, :], TW, twr, twi,
                 dr_t, dr_t[:, 2 * hf:2 * hf + 2, :], di_t, di_t[:, 2 * hf:2 * hf + 2, :],
                 (ta, tm[2 * hf]), (tb, tm[2 * hf + 1]))

    def fwd_batch(src_t, lhs_fn):
        b = cnt["f"] % 2
        ar, ai = Apr[b], Api[b]
        stage12(src_t, lhs_fn, F1, F1[:], None, None, TWf, ar, ai)
        arf = ar[:].rearrange("p c j -> p (c j)")
        aif = ai[:].rearrange("p c j -> p (c j)")
        p.mm(PX[0][:], F2[:, 0, :], arf, True, False, [F2, ar], [PX[0]])
        p.mm(PX[0][:], F2[:, 2, :], aif, False, True, [F2, ai], [PX[0]])
        p.mm(PX[1][:], F2[:, 0, :], aif, True, False, [F2, ai], [PX[1]])
        p.mm(PX[1][:], F2[:, 1, :], arf, False, True, [F2, ar], [PX[1]])

    def inv_batch(yr, yi, py):
        b = cnt["f"] % 2
        br, bi = Apr[b], Api[b]
        stage12(yr, lambda c: yr[:, c, :], G2, G2[:, 0, :], G2[:, 1, :], lambda c: yi[:, c, :], TWi, br, bi, src2_t=yi)
        p.mm(py[0:64, :], G1[:, 0, :], br[:].rearrange("p c j -> p (c j)"), True, False, [G1, br], [py])
        p.mm(py[0:64, :], G1[:, 1, :], bi[:].rearrange("p c j -> p (c j)"), False, True, [G1, bi], [py])

    win = BG[0]
    hf_t = BG[2]
    hfv = BG[2][:].bitcast(BF16)[0:64, :].rearrange("p (s c j) -> p s c j", s=2, c=64)
    p.dma("sp", win[0:64, :], wl_d[:].rearrange("p c j -> p (c j)"), [wl_d], [win])
    win3 = win[0:64, :].rearrange("p (c j) -> p c j", j=128)
    scr2 = Tile(p.nc.dram_tensor("scr2", [64, 64, 128], F32, kind="Internal").ap())
    zf = [p.sb("zf%d" % i, [64, 4, 128]) for i in range(2)]
    zb = [p.sb("zb%d" % i, [64, 4, 128], BF16) for i in range(2)]
    xgb = [p.sb("xgb%d" % i, [64, 4, 128]) for i in range(2)]
    for o in range(2):
        for n2 in range(128):
            ps = PY[n2 % 2]
            p.mm(ps[0:64, 0:128], h2[0:64, n2:SEQ:128], w3[:, 2 * o:2 * o + 2, :].rearrange("p s c -> p (s c)"), True, True, [BG[1], w3], [ps])
            p.tt("dve", hfv[:, :, :, n2], ps[0:64, 0:128].rearrange("p (s c) -> p s c", s=2),
                 win3[:, :, n2].unsqueeze(1).broadcast_to([64, 2, 64]), ALU.mult, [ps, win], [hf_t])
        p.op("dve", lambda e: e.memset(hfv[0:1, 1, :, 0], 0.0), [], [hf_t])
        for bt in range(16):
            c0 = 4 * bt
            fwd_batch(hf_t, lambda c: hfv[:, 0, c0 + c, :])
            xr, xi = Xf[0], Xf[1]
            p.act(xr[:], PX[0][:], AF.Identity, [PX[0]], [xr])
            p.act(xi[:], PX[1][:], AF.Identity, [PX[1]], [xi])
            fwd_batch(hf_t, lambda c: hfv[:, 1, c0 + c, :])
            p.tt("dve", Hs[:, 0, c0:c0 + 4, :], xr[:].rearrange("p (c j) -> p c j", c=4),
                 PX[0][:].rearrange("p (c j) -> p c j", c=4), ALU.add, [xr, PX[0]], [Hs])
            p.tt("dve", Hs[:, 1, c0:c0 + 4, :], xi[:].rearrange("p (c j) -> p c j", c=4),
                 PX[1][:].rearrange("p (c j) -> p c j", c=4), ALU.subtract, [xi, PX[1]], [Hs])
        if debug:
            dh = p.dram_out("dbg_hf", [64, 2, 64, 128])
            dH = p.dram_out("dbg_H", [128, 2, 64, 128])
            dh2 = p.dram_out("dbg_h2", [64, SEQ])
            p.dma("pool", dh[:], hfv, [hf_t], [dh])
            p.dma("pool", dH[:], Hs[:], [Hs], [dH])
            p.dma("pool", dh2[:], h2[0:64, 0:SEQ], [BG[1]], [dh2], max_dma_last_dim=2048)
            return p
        for bt in range(16):
            c0 = 4 * bt
            b = bt % 2
            z_, zb_, g_ = zf[b], zb[b], xgb[b]
            if o == 0:
                p.dma("sp", z_[:], scr[0, c0:c0 + 4, :].rearrange("c (p j) -> p c j", j=128), [scr], [z_])
            else:
                p.dma("sp", z_[:], scr2[:, c0:c0 + 4, :], [scr2], [z_])
            p.dma("sp", g_[:], scr[1 + o, c0:c0 + 4, :].rearrange("c (p j) -> p c j", j=128), [scr], [g_])
            p.op("pool", lambda e: e.tensor_copy(out=zb_[:], in_=z_[:]), [z_], [zb_])
            p.tt("pool", z_[:], z_[:], hbr[:, o, c0:c0 + 4].unsqueeze(2).broadcast_to([64, 4, 128]), ALU.mult, [z_, hbr], [z_])
            fwd_batch(zb_, lambda c: zb_[:, c, :])
            xr, xi = Xf[2], Xf[3]
            p.act(xr[:], PX[0][:], AF.Identity, [PX[0]], [xr])
            p.act(xi[:], PX[1][:], AF.Identity, [PX[1]], [xi])
            x4r = xr[:].rearrange("p (c j) -> p c j", c=4)
            x4i = xi[:].rearrange("p (c j) -> p c j", c=4)
            ta = tm[0][:].rearrange("p (c j) -> p c j", c=4)
            tb = tm[1][:].rearrange("p (c j) -> p c j", c=4)
            e1, e2 = eng(), eng()
            p.tt(e1, ta, x4r, Hs[:, 0, c0:c0 + 4, :], ALU.mult, [xr, Hs], [tm[0]])
            p.tt(e2, tb, x4i, Hs[:, 1, c0:c0 + 4, :], ALU.mult, [xi, Hs], [tm[1]])
            p.tt(e1, Yr[b][:], ta, tb, ALU.subtract, [tm[0], tm[1]], [Yr[b]])
            p.tt(e1, ta, x4r, Hs[:, 1, c0:c0 + 4, :], ALU.mult, [xr, Hs], [tm[0]])
            p.tt(e2, tb, x4i, Hs[:, 0, c0:c0 + 4, :], ALU.mult, [xi, Hs], [tm[1]])
            p.tt(e2, Yi[b][:], ta, tb, ALU.add, [tm[0], tm[1]], [Yi[b]])
            py = PY[bt % 2]
            inv_batch(Yr[b], Yi[b], py)
            p.tt("dve", z_[:], py[0:64, :].rearrange("p (c j) -> p c j", c=4), z_[:], ALU.add, [py, z_], [z_])
            p.tt("pool", z_[:], z_[:], g_[:], ALU.mult, [z_, g_], [z_])
            if o == 0:
                p.dma("sp", scr2[:, c0:c0 + 4, :], z_[:], [z_], [scr2])
            else:
                p.dma("sp", yl_d[:, c0:c0 + 4, :], z_[:], [z_], [yl_d])
    return p


def run_l3(lat, ctx, layer, w):
    consts = hy_consts()
    zl, win_l = hy_features(SEQ)
    zc, win_c = hy_features(CTX)
    in_maps = []
    for i in range(NCORES):
        b, cg = divmod(i, 4)
        ch = slice(cg * 64, (cg + 1) * 64)
        cols = [768 + g * 256 + cg * 64 for g in range(3)]
        x = np.stack([lat[b, :, c:c + 64].T for c in cols])
        xc = np.stack([ctx[b, :, c:c + 64].T for c in cols])
        cwf = w["hy_conv_w"][layer].reshape(3, 3, 256)[:, :, ch]
        cbf = w["hy_conv_b"][layer].reshape(3, 256)[:, ch]
        cw = np.concatenate([cwf.transpose(2, 1, 0), cbf.T[:, :, None]], axis=2)
        hb = w["hy_bias"][layer][:, ch]
        w3 = w["filt_w3"][layer].reshape(64, 4, 256)[:, :, ch]
        fb = np.stack([w["filt_freq"][layer], w["filt_b1"][layer], w["filt_b2"][layer]], axis=1)
        wl = win_l[:, ch].reshape(64, 128, 64).transpose(0, 2, 1)
        m = {"x": x, "xc": xc, "cw": cw, "hb": hb, "hbc": hb.T, "fw1": w["filt_w1"][layer], "fw2": w["filt_w2"][layer],
             "fw3": w3, "fb": fb, "zl": zl, "zc": zc, "wl": wl, "wc": win_c[:, ch].T}
        m.update(consts)
        in_maps.append({k: np.ascontiguousarray(v, dtype=np.float32) for k, v in m.items()})
    res = run(build_l3(), in_maps)
    y_lat = np.zeros((2, SEQ, 256), np.float32)
    y_ctx = np.zeros((2, CTX, 256), np.float32)
    for i in range(NCORES):
        b, cg = divmod(i, 4)
        ch = slice(cg * 64, (cg + 1) * 64)
        y_lat[b, :, ch] = res[i]["yl"].transpose(0, 2, 1).reshape(SEQ, 64)
        y_ctx[b, :, ch] = res[i]["yc"].T
    return y_lat, y_ctx


def kernel(**inputs):
    w = {k: np.ascontiguousarray(np.asarray(v, dtype=np.float32)) for k, v in inputs.items()}
    mod = run_ada(w["c"], w["c_ctx"], w["ada_w"], w["ada_b"])
    h_lat, h_ctx = w["x"], w["ctx"]
    for layer in range(DEPTH):
        l1_lat, l1_ctx = run_l1(h_lat, h_ctx, mod, layer, w)
        a_lat, a_ctx = run_l2(l1_lat, l1_ctx)
        hy_lat, hy_ctx = run_l3(l1_lat, l1_ctx, layer, w)
        f_lat, b_lat, f_ctx, b_ctx = run_l4(l1_lat, l1_ctx)
        y_lat = np.concatenate([a_lat, hy_lat, f_lat], axis=-1)
        y_ctx = np.concatenate([a_ctx, hy_ctx, f_ctx], axis=-1)
        h_lat, h_ctx = run_l5a(y_lat, y_ctx, b_lat, b_ctx, l1_lat[:, :, 2048:2304], l1_ctx[:, :, 2048:2304],
                               h_lat, h_ctx, mod, layer, w)
        h_lat, h_ctx = run_l5b(h_lat, h_ctx, mod, layer, w, layer == DEPTH - 1)
    return np.ascontiguousarray(h_lat, dtype=np.float32)


TT = SEQ + CTX
NTT = TT // 128


class Stage:
    def __init__(self, p):
        self.p = p
        self.cms = []

    def sb(self, name, shape, dtype=F32):
        p = self.p
        p.uid = getattr(p, "uid", 0) + 1
        cm = p.nc.sbuf_tensor("%s_%d" % (name, p.uid), list(shape), dtype)
        h = cm.__enter__()
        self.cms.append(cm)
        return Tile(h)

    def close(self):
        p = self.p
        tot = {sid: c for sid, c in p.cnt.items() if c > 0}
        for e in p.eng:
            p._wait(e, dict(tot))
        for cm in reversed(self.cms):
            cm.__exit__(None, None, None)
        self.cms = []


def scratch(p, name, shape):
    return Tile(p.nc.dram_tensor(name, list(shape), F32, kind="Internal").ap())


def emit_ada(p, PS, cT_d, adaw_d, adab_d, MOD):
    st = Stage(p)
    cs = st.sb("cs", [128, 8, 2])
    Wc = [st.sb("Wc%d" % i, [128, 8, 512]) for i in range(3)]
    bb = st.sb("bb", [2, 6 * D])
    res = st.sb("res", [2, 6 * D])
    p.dma("sp", cs[:], cT_d[:], [cT_d], [cs])
    p.act(cs[:], cs[:], AF.Silu, [cs], [cs])
    k = 0
    for l in range(DEPTH):
        p.dma("sp", bb[:], adab_d[l:l + 1, :].partition_broadcast(2)[:, 0, :], [adab_d], [bb])
        for j in range(6 * D // 512):
            W = Wc[k % 3]
            q = "sp" if k % 2 == 0 else "act"
            p.dma(q, W[:], adaw_d[l, :, j * 512:(j + 1) * 512].rearrange("(c p) n -> p c n", p=128), [adaw_d], [W])
            ps = PS[k % 2]
            k += 1
            for c in range(8):
                p.mm(ps[0:2, :], cs[:, c, :], W[:, c, :], c == 0, c == 7, [cs, W], [ps])
            p.tt("dve", res[:, j * 512:(j + 1) * 512], ps[0:2, :], bb[:, j * 512:(j + 1) * 512], ALU.add, [ps, bb], [res])
        p.dma("sp", MOD[l], res[:], [res], [MOD])
    st.close()


def load_modT(p, st, PS, MOD, l, ident):
    raw = st.sb("modraw", [96, 128])
    modT = st.sb("modT", [128, 96])
    p.dma("sp", raw[:], MOD[l].rearrange("r (s p) -> (r s) p", p=128), [MOD], [raw])
    p.tr(PS[7][:, 0:96], raw[:], ident[0:96, 0:96], [raw, ident], [PS[7]])
    p.act(modT[:], PS[7][:, 0:96], AF.Identity, [PS[7]], [modT])
    return modT


def emit_l1(p, PS, l, Hin, S1, MOD, wd, cs_d, ident_d):
    st = Stage(p)
    sb = st.sb
    W = sb("W", [128, 8, IN_W], BF16)
    A = sb("A", [128, 8, 2])
    Brep = sb("Brep", [128, 8, 128], BF16)
    bW = [sb("bW%d" % k, [128, IN_W]) for k in range(2)]
    gains = sb("gains", [128, 10, 64])
    Wblk = sb("Wblk", [32, 256])
    gbb = sb("gbb", [128, 256])
    ident = sb("ident", [128, 128])
    epst = sb("eps", [128, 1])
    ng = sb("ng", [128, 8])
    pss, pst, psg = PS[0:5], PS[5:7], PS[7]
    p.op("dve", lambda e: e.memset(epst[:], EPS), [], [epst])
    p.op("dve", lambda e: e.memset(Wblk[:], 0.0), [], [Wblk])
    p.dma("sp", ident[:], ident_d[:], [ident_d], [ident])
    p.dma("sp", ng[:], wd["norm1_g"][l], [wd["norm1_g"]], [ng])
    p.dma("sp", gbb[:], wd["gla_gate_b"][l:l + 1, :].partition_broadcast(128)[:, 0, :], [wd["gla_gate_b"]], [gbb])
    p.dma("sp", Wblk[0:16, 0:128], wd["gla_gate_w"][l, 0], [wd["gla_gate_w"]], [Wblk])
    p.dma("sp", Wblk[16:32, 128:256], wd["gla_gate_w"][l, 1], [wd["gla_gate_w"]], [Wblk])
    for h in range(10):
        src = wd["qkg"][l:l + 1, 0:64] if h < 8 else wd["qkg"][l:l + 1, 64:128]
        p.dma("sp", gains[:, h, :], src.partition_broadcast(128)[:, 0, :], [wd["qkg"]], [gains])
    p.ts("dve", gains[:, 0:8, :], gains[:, 0:8, :], 0.125, ALU.mult, [gains], [gains])
    for c in range(8):
        p.dma("pool", W[:, c, :], wd["w_in"][l, c * 128:(c + 1) * 128, :], [wd["w_in"]], [W], max_dma_last_dim=4096)
    modT = load_modT(p, st, PS, MOD, l, ident)
    for k in range(2):
        sc = modT[:, k * 48 + 8:k * 48 + 16]
        sh = modT[:, k * 48 + 0:k * 48 + 8]
        p.stt(A[:, :, k], sc, 1.0, ng[:], ALU.add, ALU.mult, [modT, ng], [A])
        p.op("dve", lambda e: e.tensor_copy(out=Brep[:], in_=sh.unsqueeze(2).broadcast_to([128, 8, 128])), [modT], [Brep])
        for gi, (a, b) in enumerate(GROUPS1):
            ps = pss[gi]
            for c in range(8):
                p.mm(ps[:, 0:b - a], Brep[:, c, :], W[:, c, a:b], c == 0, c == 7, [Brep, W], [ps])
            p.act(bW[k][:, a:b], ps[:, 0:b - a], AF.Identity, [ps], [bW[k]])
    NB = 2
    xt = [sb("xt%d" % i, [128, D]) for i in range(NB)]
    xa = [sb("xa%d" % i, [128, 8, 128], BF16) for i in range(NB)]
    cst = [sb("cst%d" % i, [128, 64]) for i in range(NB)]
    O = [sb("O%d" % i, [128, OUT1]) for i in range(NB)]
    junk = sb("junk", [128, D])
    sq = sb("sq", [128, 640])
    tmp = sb("tmp", [128, 10, 32])
    stt_ = [sb("st%d" % i, [128, 16]) for i in range(NB)]
    lrs = [sb("lr%d" % i, [128, 32]) for i in range(NB)]
    lrT = sb("lrT", [32, 128])
    gz = sb("gz", [128, 256])
    ga = sb("ga", [128, 256])
    pic = [0]

    def phaseA(i):
        kind = 0 if i < 64 else 1
        bi = i % NB
        rows = slice(i * 128, (i + 1) * 128)
        p.dma("sp", xt[bi][:], Hin[rows, :], [Hin], [xt[bi]])
        p.dma("sp", cst[bi][:], cs_d[rows, :], [cs_d], [cst[bi]])
        s = stt_[bi]
        p.act(junk[:], xt[bi][:], AF.Square, [xt[bi]], [junk, s], accum_out=s[:, 0:1])
        p.rsqrt(s, s[:, 1:2], s, s[:, 0:1], 1.0 / D, epst)
        for c in range(8):
            pt = pst[c // 4]
            blk = pt[:, (c % 4) * 128:(c % 4 + 1) * 128]
            p.tr(blk, xt[bi][:, c * 128:(c + 1) * 128], ident[:], [xt[bi], ident], [pt])
            p.act(xa[bi][:, c, :], blk, AF.Identity, [pt, A], [xa[bi]], scale=A[:, c, kind:kind + 1])
        o = O[bi]
        for gi, (a, b) in enumerate(GROUPS1):
            ps = pss[pic[0] % 5]
            pic[0] += 1
            for c in range(8):
                p.mm(ps[:, 0:b - a], xa[bi][:, c, :], W[:, c, a:b], c == 0, c == 7, [xa[bi], W], [ps])
            if gi < 4:
                p.stt(o[:, a:b], ps[:, 0:b - a], s[:, 1:2], bW[kind][:, a:b], ALU.mult, ALU.add, [ps, s, bW[kind]], [o])
            else:
                p.stt(o[:, 2048:2304], ps[:, 0:256], s[:, 1:2], bW[kind][:, 2048:2304], ALU.mult, ALU.add, [ps, s, bW[kind]], [o])
                p.stt(lrs[bi][:], ps[:, 256:288], s[:, 1:2], bW[kind][:, 2304:2336], ALU.mult, ALU.add, [ps, s, bW[kind]], [lrs[bi]])

    def phaseB(i):
        bi = i % NB
        rows = slice(i * 128, (i + 1) * 128)
        s = stt_[bi]
        o = O[bi]
        lr = lrs[bi]
        qk = o[:, 0:640]
        p.tt("pool", sq[:], qk, qk, ALU.mult, [o], [sq])
        p.op("dve", lambda e: e.tensor_reduce(out=s[:, 2:12], in_=sq[:].rearrange("p (h d) -> p h d", d=64), axis=AX.X, op=ALU.add), [sq], [s])
        p.rsqrt(s, s[:, 2:12], s, s[:, 2:12], 1.0 / 64, epst)
        qk3 = qk.rearrange("p (h d) -> p h d", d=64)
        p.tt("dve", qk3, qk3, s[:, 2:12].unsqueeze(2).broadcast_to([128, 10, 64]), ALU.mult, [o, s], [o])
        p.tt("pool", qk3, qk3, gains[:], ALU.mult, [o, gains], [o])
        x1 = qk3[:, :, 0:32]
        x2 = qk3[:, :, 32:64]
        cb = cst[bi][:, 0:32].unsqueeze(1).broadcast_to([128, 10, 32])
        sb_ = cst[bi][:, 32:64].unsqueeze(1).broadcast_to([128, 10, 32])
        t3 = sq[:, 0:320].rearrange("p (h d) -> p h d", d=32)
        p.tt("dve", tmp[:], x2, sb_, ALU.mult, [o, cst[bi]], [tmp])
        p.tt("pool", t3, x1, sb_, ALU.mult, [o, cst[bi]], [sq])
        p.tt("dve", x1, x1, cb, ALU.mult, [o, cst[bi]], [o])
        p.tt("dve", x1, x1, tmp[:], ALU.subtract, [o, tmp], [o])
        p.tt("dve", x2, x2, cb, ALU.mult, [o, cst[bi]], [o])
        p.tt("dve", x2, x2, t3, ALU.add, [o, sq], [o])
        p.act(o[:, 1536:1664], o[:, 1536:1664], AF.Identity, [o], [o], scale=32 ** -0.5)
        p.act(o[:, 2048:2304], o[:, 2048:2304], AF.Silu, [o], [o])
        p.tr(psg[0:32, 0:128], lr[:], ident[:], [lr, ident], [psg])
        p.act(lrT[:], psg[0:32, 0:128], AF.Identity, [psg], [lrT])
        p.mm(psg[:, 128:384], lrT[:], Wblk[:], True, True, [lrT, Wblk], [psg])
        p.tt("dve", gz[:], psg[:, 128:384], gbb[:], ALU.add, [psg, gbb], [gz])
        p.stt(ga[:], gz[:], -1.0, gz[:], ALU.mult, ALU.min, [gz], [ga])
        p.act(ga[:], ga[:], AF.Exp, [ga], [ga])
        p.act(ga[:], ga[:], AF.Ln, [ga], [ga], bias=1.0)
        p.ts("dve", gz[:], gz[:], 0.0, ALU.min, [gz], [gz], s2=1.0 / 16, op1=ALU.mult)
        p.stt(o[:, 2304:2560], ga[:], -1.0 / 16, gz[:], ALU.mult, ALU.add, [ga, gz], [o])
        p.dma("sp", S1[rows, :], o[:], [o], [S1])

    phaseA(0)
    for i in range(NTT):
        if i + 1 < NTT:
            phaseA(i + 1)
        phaseB(i)
    st.close()


def emit_l2(p, PS, S1, SY, ident_d, PS2):
    st = Stage(p)
    sb = st.sb
    ident = sb("ident", [128, 128])
    p.dma("sp", ident[:], ident_d[:], [ident_d], [ident])
    kT2 = sb("kT2", [128, TT], BF16)
    vA = [sb("vA%d" % g, [128, NTT, 65], BF16) for g in range(2)]
    kin = [sb("kin%d" % i, [128, 128]) for i in range(2)]
    qin = [sb("qin%d" % i, [128, 512]) for i in range(2)]
    q2 = [sb("q2_%d" % i, [128, 512], BF16) for i in range(2)]
    qrs = [sb("qr%d" % i, [128, 512]) for i in range(2)]
    psS = PS2[0:3]
    psO = PS2[3]
    for g in range(2):
        p.op("dve", lambda e: e.memset(vA[g][:, :, 64:65], 1.0), [], [vA[g]])
        p.dma("pool", vA[g][:, :, 0:64], S1[:, 640 + g * 64:704 + g * 64].rearrange("(t p) c -> p t c", p=128), [S1], [vA[g]])
    for t in range(NTT):
        ki = kin[t % 2]
        p.dma("sp", ki[:], S1[t * 128:(t + 1) * 128, 512:640], [S1], [ki])
        pt = psS[t % 3]
        p.tr(pt[:, 0:128], ki[:], ident[:], [ki, ident], [pt])
        p.act(kT2[:, t * 128:(t + 1) * 128], pt[:, 0:128], AF.Identity, [pt], [kT2])
    LA = 2
    pT = [sb("pT%d" % i, [128, 1024], BF16) for i in range(LA + 2)]
    oT = [sb("oT%d" % i, [65, 1024]) for i in range(2)]
    ot = [sb("ot%d" % i, [128, 512]) for i in range(2)]
    rec = sb("rec", [128, 16])
    it = 0
    for qt in range(NTT):
        kts = list(range(NTT)) if qt < 64 else [64, 65]
        qi = qin[qt % 2]
        p.dma("sp", qi[:], S1[qt * 128:(qt + 1) * 128, 0:512], [S1], [qi])
        o_t = ot[qt % 2]
        q_ = q2[qt % 2]
        pt = psS[it % 3]
        qr = qrs[qt % 2]
        p.op("pool", lambda e: e.tensor_copy(out=qr[:].rearrange("p (h g d) -> p h g d", h=4, g=2),
                                             in_=qi[:].rearrange("p (g h d) -> p h g d", g=2, h=4)), [qi], [qr])
        for h in range(4):
            p.tr(pt[:, h * 128:(h + 1) * 128], qr[:, h * 128:(h + 1) * 128], ident[:], [qr, ident], [pt])
        p.act(q_[:], pt[:, 0:512], AF.Identity, [pt], [q_])

        def pv(pend, last):
            k0, kt, ptile = pend
            for g in range(2):
                p.mm(psO[0:65, g * 512:(g + 1) * 512], vA[g][:, kt, :], ptile[:, g * 512:(g + 1) * 512], k0 == 0, last,
                     [vA[g], ptile], [psO])

        pend = []
        for k0, kt in enumerate(kts):
            ps = psS[it % 3]
            pt_ = pT[it % (LA + 2)]
            it += 1
            for g in range(2):
                rows = slice(g * 64, (g + 1) * 64)
                p.mm(ps[:, g * 512:(g + 1) * 512], kT2[rows, kt * 128:(kt + 1) * 128], q_[rows, :], True, True, [kT2, q_], [ps])
            p.act(pt_[:], ps[:], AF.Exp, [ps], [pt_])
            pend.append((k0, kt, pt_))
            if len(pend) > LA:
                pv(pend.pop(0), False)
        while pend:
            x_ = pend.pop(0)
            pv(x_, len(pend) == 0)
        o_sb = oT[qt % 2]
        p.op("dve", lambda e: e.tensor_copy(out=o_sb[:], in_=psO[0:65, :]), [psO], [o_sb])
        for hh in range(8):
            pb = psS[(it + 1 + hh % 2) % 3]
            ptt = pb[:, (hh // 2 % 2) * 512:(hh // 2 % 2) * 512 + 65]
            p.tr(ptt, o_sb[:, hh * 128:(hh + 1) * 128], ident[0:65, 0:65], [o_sb, ident], [pb])
            rc = rec[:, hh + (qt % 2) * 8:hh + (qt % 2) * 8 + 1]
            p.op("dve", lambda e: e.reciprocal(out=rc, in_=ptt[:, 64:65]), [pb], [rec])
            p.ts("dve", o_t[:, hh * 64:(hh + 1) * 64], ptt[:, 0:64], rc, ALU.mult, [pb, rec], [o_t])
        p.dma("sp", SY[qt * 128:(qt + 1) * 128, :], o_t[:], [o_t], [SY])
    st.close()


def emit_l3(p, PS, l, S1, YL, YC, wd, cd, scr, scr2):
    st = Stage(p)
    sb = st.sb
    BGa = sb("BGa", [128, SEQ])
    BGb = sb("BGb", [128, SEQ])
    H2 = sb("H2", [64, SEQ], BF16)
    Hs = sb("Hs", [128, 2, 64, 128], BF16)
    F1 = sb("F1", [64, 256], BF16)
    TWf = sb("TWf", [128, 2, 128])
    F2 = sb("F2", [128, 3, 128], BF16)
    G2 = sb("G2", [128, 2, 256], BF16)
    TWi = sb("TWi", [128, 2, 128])
    G1 = sb("G1", [128, 2, 64], BF16)
    ident = sb("ident", [128, 128])
    for nm, t in (("F1", F1), ("F2", F2), ("G2", G2), ("G1", G1)):
        p.dma("pool", t[:], cd[nm][:], [cd[nm]], [t])
    p.dma("sp", TWf[:], cd["TWf"][:], [cd["TWf"]], [TWf])
    p.dma("sp", TWi[:], cd["TWi"][:], [cd["TWi"]], [TWi])
    p.dma("sp", ident[:], cd["ident"][:], [cd["ident"]], [ident])
    w1 = sb("fw1", [33, 64])
    w2 = sb("fw2", [64, 64])
    fb = sb("fb", [64, 3])
    frb = sb("frb", [64, 2])
    p.dma("sp", w1[:], wd["filt_w1"][l], [wd["filt_w1"]], [w1])
    p.dma("sp", w2[:], wd["filt_w2"][l], [wd["filt_w2"]], [w2])
    p.dma("sp", fb[:], wd["fb"][l], [wd["fb"]], [fb])
    for j in range(2):
        p.tt("dve", frb[:, j:j + 1], fb[:, 0:1], fb[:, 1 + j:2 + j], ALU.mult, [fb], [frb])
    PA, PX, PB, PY = PS[0:2], PS[2:4], PS[4:6], PS[6:8]
    argt = sb("argt", [64, 512])
    wrp = sb("wrp", [64, 512])
    h1c = sb("h1c", [64, 512])
    zch = [sb("zch%d" % i, [33, 512]) for i in range(2)]
    h2c = sb("h2c", [64, CTX])

    def wrap_sin(dst_t, dst_ap, ps_ap, ps_t, j, n):
        a = argt[0:64, 0:n]
        p.ts("dve", a, ps_ap, fb[:, 0:1], ALU.mult, [ps_t, fb, frb], [argt], s2=frb[:, j:j + 1], op1=ALU.add)
        w_ = wrp[0:64, 0:n]
        for bound, period in ((3 * PI, 4 * PI), (PI, 2 * PI)):
            p.ts("dve", w_, a, bound, ALU.is_gt, [argt], [wrp], s2=-period, op1=ALU.mult)
            p.tt("dve", a, a, w_, ALU.add, [argt, wrp], [argt])
            p.ts("dve", w_, a, -bound, ALU.is_lt, [argt], [wrp], s2=period, op1=ALU.mult)
            p.tt("dve", a, a, w_, ALU.add, [argt, wrp], [argt])
        p.act(dst_ap, a, AF.Sin, [argt], [dst_t])

    def mlp(z_dram, n, dst_t):
        for q in range(0, n, 512):
            m = min(512, n - q)
            zt = zch[(q // 512) % 2]
            p.dma("sp", zt[:, 0:m], z_dram[:, q:q + m], [z_dram], [zt])
            p.mm(PA[0][0:64, 0:m], w1[:], zt[:, 0:m], True, True, [w1, zt], [PA[0]])
            wrap_sin(h1c, h1c[:, 0:m], PA[0][0:64, 0:m], PA[0], 0, m)
            p.mm(PA[1][0:64, 0:m], w2[:], h1c[:, 0:m], True, True, [w2, h1c], [PA[1]])
            wrap_sin(dst_t, dst_t[:, q:q + m], PA[1][0:64, 0:m], PA[1], 1, m)

    mlp(cd["zl"], SEQ, H2)
    mlp(cd["zc"], CTX, h2c)

    cw = sb("cw", [64, 3, 4])
    hbr = sb("hbr", [64, 2, 64])
    hbc = sb("hbc", [64, 2])
    w3 = sb("fw3", [64, 4, 64], BF16)
    w3f = sb("fw3f", [64, 4, 64])
    wc = sb("wc", [64, CTX])
    uc = sb("uc", [64, 3, CTX])
    xct = sb("xct", [64, 3, CTX])
    hfc = sb("hfc", [64, 4, CTX])
    zc1 = sb("zc1", [64, CTX])
    yct = sb("yct", [64, CTX])
    yct2 = sb("yct2", [64, CTX])
    xin = [sb("xin%d" % i, [128, 64]) for i in range(3)]
    Af = [sb("Af%d" % i, [128, 512]) for i in range(2)]
    tm = [sb("tm%d" % i, [128, 512]) for i in range(4)]
    tms = [[sb("tms%d_%d" % (a, b), [128, 256]) for b in range(4)] for a in range(4)]
    Apr = [sb("Apr%d" % i, [128, 4, 128], BF16) for i in range(2)]
    Api = [sb("Api%d" % i, [128, 4, 128], BF16) for i in range(2)]
    Xf = [sb("Xf%d" % i, [128, 512]) for i in range(4)]
    Yr = [sb("Yr%d" % i, [128, 4, 128], BF16) for i in range(2)]
    Yi = [sb("Yi%d" % i, [128, 4, 128], BF16) for i in range(2)]
    zf = [sb("zf%d" % i, [64, 4, 128]) for i in range(2)]
    zb = [sb("zb%d" % i, [64, 4, 128], BF16) for i in range(2)]
    xgb = [sb("xgb%d" % i, [64, 4, 128]) for i in range(2)]
    cnt = {"f": 0, "e": 0, "x": 0}

    def eng():
        cnt["e"] += 1
        return "dve" if cnt["e"] % 2 else "pool"

    def cmul(src_t, sr, si, tw_t, twr, twi, dr_t, dr, di_t, di, T4, e2):
        (a1, A1), (b1, B1), (a2, A2), (b2, B2) = T4
        p.tt("dve", a1, sr, twr, ALU.mult, [src_t, tw_t], [A1])
        p.tt("dve", b1, si, twi, ALU.mult, [src_t, tw_t], [B1])
        p.tt("dve", dr, a1, b1, ALU.subtract, [A1, B1], [dr_t])
        p.tt(e2, a2, sr, twi, ALU.mult, [src_t, tw_t], [A2])
        p.tt(e2, b2, si, twr, ALU.mult, [src_t, tw_t], [B2])
        p.tt(e2, di, a2, b2, ALU.add, [A2, B2], [di_t])

    def stage12(src_t, lhs_fn, rhs_t, rhs0, rhs1, lhs2_fn, TW, dr_t, di_t, src2_t=None):
        cnt["f"] += 1
        for hf in range(2):
            ps = (PA if rhs1 is None else PB)[hf]
            for k in range(2):
                c = 2 * hf + k
                cols = slice(k * 256, (k + 1) * 256)
                if rhs1 is None:
                    p.mm(ps[:, cols], lhs_fn(c), rhs0, True, True, [src_t, rhs_t], [ps])
                else:
                    p.mm(ps[:, cols], lhs_fn(c), rhs0, True, False, [src_t, rhs_t], [ps])
                    p.mm(ps[:, cols], lhs2_fn(c), rhs1, False, True, [src2_t, rhs_t], [ps])
            af = Af[hf]
            p.act(af[:], ps[:], AF.Identity, [ps], [af])
            a4 = af[:].rearrange("p (k r j) -> p k r j", k=2, r=2)
            twr = TW[:, 0, :].unsqueeze(1).broadcast_to([128, 2, 128])
            twi = TW[:, 1, :].unsqueeze(1).broadcast_to([128, 2, 128])
            tset = tms[(cnt["f"] % 2) * 2 + hf]
            T4 = [(t_[:].rearrange("p (k j) -> p k j", k=2), t_) for t_ in tset]
            cmul(af, a4[:, :, 0, :], a4[:, :, 1, :], TW, twr, twi,
                 dr_t, dr_t[:, 2 * hf:2 * hf + 2, :], di_t, di_t[:, 2 * hf:2 * hf + 2, :],
                 T4, "pool" if hf == 1 else "dve")

    def fwd_batch(src_t, lhs_fn):
        b = cnt["f"] % 2
        ar, ai = Apr[b], Api[b]
        stage12(src_t, lhs_fn, F1, F1[:], None, None, TWf, ar, ai)
        arf = ar[:].rearrange("p c j -> p (c j)")
        aif = ai[:].rearrange("p c j -> p (c j)")
        p.mm(PX[0][:], F2[:, 0, :], arf, True, False, [F2, ar], [PX[0]])
        p.mm(PX[0][:], F2[:, 2, :], aif, False, True, [F2, ai], [PX[0]])
        p.mm(PX[1][:], F2[:, 0, :], aif, True, False, [F2, ai], [PX[1]])
        p.mm(PX[1][:], F2[:, 1, :], arf, False, True, [F2, ar], [PX[1]])

    def inv_batch(yr, yi, py):
        b = cnt["f"] % 2
        br, bi = Apr[b], Api[b]
        stage12(yr, lambda c: yr[:, c, :], G2, G2[:, 0, :], G2[:, 1, :], lambda c: yi[:, c, :], TWi, br, bi, src2_t=yi)
        p.mm(py[0:64, :], G1[:, 0, :], br[:].rearrange("p c j -> p (c j)"), True, False, [G1, br], [py])
        p.mm(py[0:64, :], G1[:, 1, :], bi[:].rearrange("p c j -> p (c j)"), False, True, [G1, bi], [py])

    def short_conv(xt, xap, ut, uap, g, n):
        p.act(uap, xap, AF.Identity, [xt, cw], [ut], scale=cw[:, g, 1:2], bias=cw[:, g, 3:4])
        p.stt(uap[:, 1:n], xap[:, 0:n - 1], cw[:, g, 0:1], uap[:, 1:n], ALU.mult, ALU.add, [xt, cw, ut], [ut])
        p.stt(uap[:, 0:n - 1], xap[:, 1:n], cw[:, g, 2:3], uap[:, 0:n - 1], ALU.mult, ALU.add, [xt, cw, ut], [ut])

    def ctx_conv(zt, zap, o, gate_ap, out_t, out_ap):
        p.ts("dve", yct[:], zap, hbc[:, o:o + 1], ALU.mult, [zt, hbc], [yct])
        p.op("dve", lambda e: e.memset(yct2[:], 0.0), [], [yct2])
        accs = (yct, yct2)
        k = 0
        for q in range(CTX):
            a_ = accs[k % 2]
            k += 1
            p.stt(a_[:, q:CTX], zap[:, 0:CTX - q], hfc[:, 2 * o, q:q + 1], a_[:, q:CTX], ALU.mult, ALU.add, [zt, hfc, a_], [a_])
        for q in range(1, CTX):
            a_ = accs[k % 2]
            k += 1
            p.stt(a_[:, 0:CTX - q], zap[:, q:CTX], hfc[:, 2 * o + 1, q:q + 1], a_[:, 0:CTX - q], ALU.mult, ALU.add, [zt, hfc, a_], [a_])
        p.tt("dve", yct[:], yct[:], yct2[:], ALU.add, [yct, yct2], [yct])
        p.tt("dve", out_ap, yct[:], gate_ap, ALU.mult, [yct, uc], [out_t])

    hfv = BGb[:].bitcast(BF16)[0:64, :].rearrange("p (s c j) -> p s c j", s=2, c=64)
    win3 = BGa[0:64, :].rearrange("p (c j) -> p c j", j=128)
    for cg in range(4):
        p.dma("sp", cw[:], wd["cw"][l, cg], [wd["cw"]], [cw])
        p.dma("sp", hbc[:], wd["hbc"][l, cg], [wd["hbc"]], [hbc])
        for o in range(2):
            p.dma("sp", hbr[:, o, :], wd["hb"][l, cg, o:o + 1, :].partition_broadcast(64)[:, 0, :], [wd["hb"]], [hbr])
        p.dma("sp", w3f[:], wd["fw3"][l, cg], [wd["fw3"]], [w3f])
        p.dma("pool", w3[:], wd["fw3"][l, cg], [wd["fw3"]], [w3])
        p.dma("sp", wc[:], cd["wc"][cg], [cd["wc"]], [wc])
        for g in range(3):
            c0 = 768 + g * 256 + cg * 64
            for t in range(NTT):
                xi = xin[cnt["x"] % 3]
                cnt["x"] += 1
                p.dma("sp", xi[:], S1[t * 128:(t + 1) * 128, c0:c0 + 64], [S1], [xi])
                pt = PA[(t // 4) % 2]
                p.tr(pt[0:64, (t % 4) * 128:(t % 4 + 1) * 128], xi[:], ident[:], [xi, ident], [pt])
                if t % 4 == 3 and t < 64:
                    p.act(BGa[0:64, (t - 3) * 128:(t + 1) * 128], pt[0:64, :], AF.Identity, [pt], [BGa])
                if t == 65:
                    p.act(xct[:, g, :], pt[0:64, 0:256], AF.Identity, [pt], [xct])
            short_conv(BGa, BGa[0:64, :], BGb, BGb[0:64, :], g, SEQ)
            p.dma("sp", scr[g], BGb[0:64, :], [BGb], [scr])
            short_conv(xct, xct[:, g, :], uc, uc[:, g, :], g, CTX)
        for blk in range(4):
            p.mm(PX[0][0:64, 0:CTX], w3f[:, blk, :], h2c[:], True, True, [w3f, h2c], [PX[0]])
            p.tt("dve", hfc[:, blk, :], PX[0][0:64, 0:CTX], wc[:], ALU.mult, [PX[0], wc], [hfc])
        ctx_conv(uc, uc[:, 0, :], 0, uc[:, 1, :], zc1, zc1[:])
        ctx_conv(zc1, zc1[:], 1, uc[:, 2, :], zc1, zc1[:])
        p.dma("sp", YC[cg], zc1[:], [zc1], [YC])
        p.dma("sp", BGa[0:64, :], cd["wl"][cg].rearrange("p c j -> p (c j)"), [cd["wl"]], [BGa])
        for o in range(2):
            for n2 in range(128):
                ps = PY[n2 % 2]
                p.mm(ps[0:64, 0:128], H2[:, n2:SEQ:128], w3[:, 2 * o:2 * o + 2, :].rearrange("p s c -> p (s c)"), True, True, [H2, w3], [ps])
                p.tt("dve", hfv[:, :, :, n2], ps[0:64, 0:128].rearrange("p (s c) -> p s c", s=2),
                     win3[:, :, n2].unsqueeze(1).broadcast_to([64, 2, 64]), ALU.mult, [ps, BGa], [BGb])
            p.op("dve", lambda e: e.memset(hfv[0:1, 1, :, 0], 0.0), [], [BGb])
            for bt in range(16):
                c0 = 4 * bt
                fwd_batch(BGb, lambda c: hfv[:, 0, c0 + c, :])
                xr, xi_ = Xf[0], Xf[1]
                p.act(xr[:], PX[0][:], AF.Identity, [PX[0]], [xr])
                p.act(xi_[:], PX[1][:], AF.Identity, [PX[1]], [xi_])
                fwd_batch(BGb, lambda c: hfv[:, 1, c0 + c, :])
                p.tt("dve", Hs[:, 0, c0:c0 + 4, :], xr[:].rearrange("p (c j) -> p c j", c=4),
                     PX[0][:].rearrange("p (c j) -> p c j", c=4), ALU.add, [xr, PX[0]], [Hs])
                p.tt("dve", Hs[:, 1, c0:c0 + 4, :], xi_[:].rearrange("p (c j) -> p c j", c=4),
                     PX[1][:].rearrange("p (c j) -> p c j", c=4), ALU.subtract, [xi_, PX[1]], [Hs])
            for bt in range(16):
                c0 = 4 * bt
                b = bt % 2
                z_, zb_, g_ = zf[b], zb[b], xgb[b]
                if o == 0:
                    p.dma("sp", z_[:], scr[0, c0:c0 + 4, :].rearrange("c (p j) -> p c j", j=128), [scr], [z_])
                else:
                    p.dma("sp", z_[:], scr2[:, c0:c0 + 4, :], [scr2], [z_])
                p.dma("sp", g_[:], scr[1 + o, c0:c0 + 4, :].rearrange("c (p j) -> p c j", j=128), [scr], [g_])
                p.op("pool", lambda e: e.tensor_copy(out=zb_[:], in_=z_[:]), [z_], [zb_])
                p.tt("pool", z_[:], z_[:], hbr[:, o, c0:c0 + 4].unsqueeze(2).broadcast_to([64, 4, 128]), ALU.mult, [z_, hbr], [z_])
                fwd_batch(zb_, lambda c: zb_[:, c, :])
                xr, xi_ = Xf[2], Xf[3]
                p.act(xr[:], PX[0][:], AF.Identity, [PX[0]], [xr])
                p.act(xi_[:], PX[1][:], AF.Identity, [PX[1]], [xi_])
                x4r = xr[:].rearrange("p (c j) -> p c j", c=4)
                x4i = xi_[:].rearrange("p (c j) -> p c j", c=4)
                ta = tm[0][:].rearrange("p (c j) -> p c j", c=4)
                tb = tm[1][:].rearrange("p (c j) -> p c j", c=4)
                tc_ = tm[2][:].rearrange("p (c j) -> p c j", c=4)
                td_ = tm[3][:].rearrange("p (c j) -> p c j", c=4)
                p.tt("dve", ta, x4r, Hs[:, 0, c0:c0 + 4, :], ALU.mult, [xr, Hs], [tm[0]])
                p.tt("dve", tb, x4i, Hs[:, 1, c0:c0 + 4, :], ALU.mult, [xi_, Hs], [tm[1]])
                p.tt("dve", Yr[b][:], ta, tb, ALU.subtract, [tm[0], tm[1]], [Yr[b]])
                p.tt("pool", tc_, x4r, Hs[:, 1, c0:c0 + 4, :], ALU.mult, [xr, Hs], [tm[2]])
                p.tt("pool", td_, x4i, Hs[:, 0, c0:c0 + 4, :], ALU.mult, [xi_, Hs], [tm[3]])
                p.tt("pool", Yi[b][:], tc_, td_, ALU.add, [tm[2], tm[3]], [Yi[b]])
                py = PY[bt % 2]
                inv_batch(Yr[b], Yi[b], py)
                p.tt("dve", z_[:], py[0:64, :].rearrange("p (c j) -> p c j", c=4), z_[:], ALU.add, [py, z_], [z_])
                p.tt("pool", z_[:], z_[:], g_[:], ALU.mult, [z_, g_], [z_])
                if o == 0:
                    p.dma("sp", scr2[:, c0:c0 + 4, :], z_[:], [z_], [scr2])
                else:
                    p.dma("sp", YL[cg, :, c0:c0 + 4, :], z_[:], [z_], [YL])
    st.close()


def emit_l4(p, PS, S1, GO, cd):
    st = Stage(p)
    sb = st.sb
    ident = sb("ident", [128, 128])
    mask = sb("mask", [128, 128])
    Jm = sb("Jm", [128, 128], BF16)
    pm = sb("pm", [128, 2])
    ones = sb("ones", [32, 1])
    p.dma("sp", ident[:], cd["ident"][:], [cd["ident"]], [ident])
    p.dma("sp", mask[:], cd["mask"][:], [cd["mask"]], [mask])
    p.dma("pool", Jm[:], cd["J"][:], [cd["J"]], [Jm])
    p.dma("sp", pm[:], cd["pm"][:], [cd["pm"]], [pm])
    p.op("dve", lambda e: e.memset(ones[:], 1.0), [], [ones])
    qd = sb("qd", [32, TG], BF16)
    kd = sb("kd", [32, TG], BF16)
    ktT = [sb("ktT%d" % i, [128, NPR, 32], BF16) for i in range(2)]
    vnat = sb("vnat", [128, NPR, 64], BF16)
    vt = sb("vt", [128, NPR, 64], BF16)
    STm = sb("STm", [128, NPR, 128], BF16)
    Sbf = sb("Sbf", [32, NCH + 1, 64], BF16)
    dc = sb("dc", [32, NCH])
    oT = sb("oT", [64, TG])
    Scur = [sb("Scur%d" % i, [32, 64]) for i in range(2)]
    seg3 = sb("seg3", [32, 3, SEGL])
    gin = [sb("gin%d" % i, [128, 3, 32]) for i in range(3)]
    Gs = [sb("Gs%d" % i, [32, SEGL]) for i in range(2)]
    At = sb("At", [32, SEGL])
    Bt = sb("Bt", [32, SEGL])
    Sc = sb("Sc", [32, 22])
    tmpc = sb("tmpc", [32, 22])
    psT, psS, psK, psO = PS[0:2], PS[2:4], PS[4:6], PS[6:8]
    ng = 0
    for hd in range(4):
        p.dma("pool", vnat[:], S1[:, 1792 + hd * 64:1856 + hd * 64].rearrange("(t p) c -> p t c", p=128), [S1], [vnat], max_dma_last_dim=4096)
        for d in range(2):
            def gtile(j):
                return (64 + j if j < 2 else j - 2) if d == 0 else 65 - j
            if d == 0:
                p.op("pool", lambda e: e.tensor_copy(out=vt[:, 0:2, :], in_=vnat[:, 64:66, :]), [vnat], [vt])
                p.op("pool", lambda e: e.tensor_copy(out=vt[:, 2:66, :], in_=vnat[:, 0:64, :]), [vnat], [vt])
            else:
                for j0 in range(0, NPR, 8):
                    n = min(8, NPR - j0)
                    ps = psK[(j0 // 8) % 2]
                    for j in range(n):
                        p.mm(ps[:, j * 64:(j + 1) * 64], Jm[:], vnat[:, gtile(j0 + j), :], True, True, [Jm, vnat], [ps])
                    p.act(vt[:, j0:j0 + n, :], ps[:, 0:n * 64].rearrange("p (a b) -> p a b", b=64), AF.Identity, [ps], [vt])
            gcol = (2304 if d == 0 else 2432) + hd * 32
            for s in range(NSEG):
                seg = slice(s * SEGL, (s + 1) * SEGL)
                for jj in range(11):
                    j = s * 11 + jj
                    gt = gtile(j)
                    gi = gin[ng % 3]
                    ps = psT[ng % 2]
                    ng += 1
                    rows = slice(gt * 128, (gt + 1) * 128)
                    p.dma("sp", gi[:, 0:2, :], S1[rows, 1536 + hd * 32:1536 + hd * 32 + 256].rearrange("p (a c) -> p a c", a=2)[:, :, 0:32], [S1], [gi])
                    p.dma("sp", gi[:, 2, :], S1[rows, gcol:gcol + 32], [S1], [gi])
                    for a in range(3):
                        p.tr(ps[0:32, a * 128:(a + 1) * 128], gi[:, a, :], ident[:], [gi, ident], [ps])
                    dst = seg3[:, :, jj * 128:(jj + 1) * 128]
                    if d == 1:
                        dst = dst[:, :, ::-1]
                    p.act(dst, ps[0:32, 0:384].rearrange("p (a j) -> p a j", a=3), AF.Identity, [ps], [seg3])
                qseg, kseg, gseg = seg3[:, 0, :], seg3[:, 1, :], seg3[:, 2, :]
                G = Gs[s % 2]
                Gp = Gs[(s + 1) % 2]
                init = 0.0 if s == 0 else Gp[:, SEGL - 1:SEGL]
                rd = [seg3, ones] + ([] if s == 0 else [Gp])
                p.op("dve", lambda e: e.tensor_tensor_scan(out=G[:], data0=ones[:, 0:1].broadcast_to([32, SEGL]), data1=gseg,
                                                           initial=init, op0=ALU.mult, op1=ALU.add), rd, [G])
                G3 = G[:].rearrange("p (c j) -> p c j", j=64)
                Ec = G3[:, :, 63]
                if s == 0:
                    p.op("dve", lambda e: e.memset(Sc[:, 0:1], 0.0), [], [Sc])
                else:
                    p.op("dve", lambda e: e.tensor_copy(out=Sc[:, 0:1], in_=Gp[:, SEGL - 1:SEGL]), [Gp], [Sc])
                p.op("dve", lambda e: e.tensor_copy(out=Sc[:, 1:22], in_=G3[:, 0:21, 63]), [G], [Sc])
                A3 = At[:].rearrange("p (c j) -> p c j", j=64)
                p.tt("dve", A3, G3, Sc[:].unsqueeze(2).broadcast_to([32, 22, 64]), ALU.subtract, [G, Sc], [At])
                p.act(Bt[:], At[:], AF.Exp, [At], [Bt])
                p.tt("dve", qd[:, seg], qseg, Bt[:], ALU.mult, [seg3, Bt], [qd])
                p.act(Bt[:], At[:], AF.Exp, [At], [Bt], scale=-1.0)
                p.tt("dve", kd[:, seg], kseg, Bt[:], ALU.mult, [seg3, Bt], [kd])
                p.tt("dve", tmpc[:], Ec, Sc[:], ALU.subtract, [G, Sc], [tmpc])
                p.act(dc[:, s * 22:(s + 1) * 22], tmpc[:], AF.Exp, [tmpc], [dc])
                p.tt("dve", A3, Ec.unsqueeze(2).broadcast_to([32, 22, 64]), G3, ALU.subtract, [G], [At])
                p.act(At[:], At[:], AF.Exp, [At], [At])
                p.tt("dve", Bt[:], kseg, At[:], ALU.mult, [seg3, At], [Bt])
                ps = psS[s % 2]
                for j in range(11):
                    p.tr(ps[:, j * 32:(j + 1) * 32], Bt[:, j * 128:(j + 1) * 128], ident[0:32, 0:32], [Bt, ident], [ps])
                for hf in range(2):
                    p.act(ktT[hf][:, s * 11:(s + 1) * 11, :], ps[:, 0:352].rearrange("p (a b) -> p a b", b=32), AF.Identity,
                          [ps, pm], [ktT[hf]], scale=pm[:, hf:hf + 1])
            for g0 in range(0, NPR, 4):
                n = min(4, NPR - g0)
                ps = psS[(g0 // 4) % 2]
                for j in range(n):
                    tok = slice((g0 + j) * 128, (g0 + j + 1) * 128)
                    p.mm(ps[:, j * 128:(j + 1) * 128], kd[:, tok], qd[:, tok], True, True, [kd, qd], [ps])
                p.tt("dve", STm[:, g0:g0 + n, :], ps[:, 0:n * 128].rearrange("p (a b) -> p a b", b=128),
                     mask[:].unsqueeze(1).broadcast_to([128, n, 128]), ALU.mult, [ps, mask], [STm])
            p.op("dve", lambda e: e.memset(Scur[0][:], 0.0), [], [Scur[0]])
            p.op("dve", lambda e: e.memset(Sbf[:, 0, :], 0.0), [], [Sbf])
            for c0 in range(0, NCH, 8):
                n = min(8, NCH - c0)
                ps = psK[(c0 // 8) % 2]
                for j in range(n):
                    pr, hf = divmod(c0 + j, 2)
                    p.mm(ps[0:32, j * 64:(j + 1) * 64], ktT[hf][:, pr, :], vt[:, pr, :], True, True, [ktT[hf], vt], [ps])
                for j in range(n):
                    c = c0 + j
                    sa, sb_ = Scur[c % 2], Scur[(c + 1) % 2]
                    p.stt(sb_[:], sa[:], dc[:, c:c + 1], ps[0:32, j * 64:(j + 1) * 64], ALU.mult, ALU.add, [sa, dc, ps], [sb_])
                    p.act(Sbf[:, c + 1, :], sb_[:], AF.Identity, [sb_], [Sbf])
            for g0 in range(0, NPR, 4):
                n = min(4, NPR - g0)
                ps = psO[(g0 // 4) % 2]
                for j in range(n):
                    pr = g0 + j
                    cs_ = slice(j * 128, (j + 1) * 128)
                    p.mm(ps[0:64, cs_], vt[:, pr, :], STm[:, pr, :], True, False, [vt, STm], [ps])
                    for hf in range(2):
                        c = 2 * pr + hf
                        tok = slice(c * 64, (c + 1) * 64)
                        p.mm(ps[0:64, j * 128 + hf * 64:j * 128 + (hf + 1) * 64], Sbf[:, c, :], qd[:, tok], False, hf == 1, [Sbf, qd], [ps])
                p.act(oT[:, g0 * 128:(g0 + n) * 128], ps[0:64, 0:n * 128], AF.Identity, [ps], [oT])
            p.dma("sp", GO[hd, d], oT[:], [oT], [GO])
    st.close()


def emit_l5a(p, PS, l, SY, YL, YC, GO, S1, Hin, H1, MOD, wd, ident_d):
    st = Stage(p)
    sb = st.sb
    W = sb("W", [128, 8, D], BF16)
    og = sb("og", [128, 8])
    g1 = [sb("g1_%d" % k, [128, D]) for k in range(2)]
    ident = sb("ident", [128, 128])
    epst = sb("eps", [128, 1])
    p.op("dve", lambda e: e.memset(epst[:], EPS), [], [epst])
    p.dma("sp", og[:], wd["out_norm_g"][l], [wd["out_norm_g"]], [og])
    p.dma("sp", ident[:], ident_d[:], [ident_d], [ident])
    for k in range(2):
        p.dma("sp", g1[k][:], MOD[l, k:k + 1, 2 * D:3 * D].partition_broadcast(128)[:, 0, :], [MOD], [g1[k]])
    for c in range(8):
        p.dma("pool", W[:, c, :], wd["w_out"][l, c * 128:(c + 1) * 128, :], [wd["w_out"]], [W], max_dma_last_dim=4096)
    NB = 2
    yt = [sb("yt%d" % i, [128, D]) for i in range(NB)]
    ht = [sb("ht%d" % i, [128, D]) for i in range(NB)]
    srt = [sb("srt%d" % i, [128, 256]) for i in range(NB)]
    hin = [sb("hin%d" % i, [64, 4, 128]) for i in range(NB)]
    gf = [sb("gf%d" % i, [64, 4, 128]) for i in range(NB)]
    gb = [sb("gb%d" % i, [64, 4, 128]) for i in range(NB)]
    ho = [sb("ho%d" % i, [128, D]) for i in range(NB)]
    yT = [sb("yT%d" % i, [128, 8, 128], BF16) for i in range(NB)]
    sq = sb("sq", [128, D])
    st_ = [sb("st%d" % i, [128, 16]) for i in range(NB)]
    pst, pso, psx = PS[0:2], PS[2:6], PS[6:8]
    def phaseA(i):
        bi = i % NB
        rows = slice(i * 128, (i + 1) * 128)
        y, h, sr, s = yt[bi], ht[bi], srt[bi], st_[bi]
        p.dma("sp", y[:, 0:512], SY[rows, :], [SY], [y])
        p.dma("sp", sr[:], S1[rows, 2048:2304], [S1], [sr])
        p.dma("sp", h[:], Hin[rows, :], [Hin], [h])
        if i < 64:
            p.dma("sp", hin[bi][:], YL[:, i, :, :].rearrange("g c j -> c g j"), [YL], [hin[bi]])
            f0, b0 = 256 + i * 128, 256 + (63 - i) * 128
        else:
            c = i - 64
            p.dma("sp", hin[bi][:], YC[:, :, c * 128:(c + 1) * 128].rearrange("g c j -> c g j"), [YC], [hin[bi]])
            f0, b0 = c * 128, (1 - c) * 128
        p.dma("sp", gf[bi][:], GO[:, 0, :, f0:f0 + 128].rearrange("h v j -> v h j"), [GO], [gf[bi]])
        p.dma("sp", gb[bi][:], GO[:, 1, :, b0:b0 + 128].rearrange("h v j -> v h j"), [GO], [gb[bi]])
        p.tt("pool", gf[bi][:], gf[bi][:], gb[bi][:, :, ::-1], ALU.add, [gf[bi], gb[bi]], [gf[bi]])
        for a in range(4):
            p.tr(psx[0][:, a * 64:(a + 1) * 64], hin[bi][:, a, :], ident[0:64, 0:64], [hin[bi], ident], [psx[0]])
            p.tr(psx[1][:, a * 64:(a + 1) * 64], gf[bi][:, a, :], ident[0:64, 0:64], [gf[bi], ident], [psx[1]])
        p.act(y[:, 512:768], psx[0][:, 0:256], AF.Identity, [psx[0]], [y])
        p.act(y[:, 768:1024], psx[1][:, 0:256], AF.Identity, [psx[1]], [y])
        p.tt("pool", sq[:], y[:], y[:], ALU.mult, [y], [sq])
        p.op("dve", lambda e: e.tensor_reduce(out=s[:], in_=sq[:].rearrange("p (h d) -> p h d", d=64), axis=AX.X, op=ALU.add), [sq], [s])
        p.rsqrt(s, s[:], s, s[:], 1.0 / 64, epst)
        y3 = y[:].rearrange("p (h d) -> p h d", d=64)
        p.tt("dve", y3, y3, s[:].unsqueeze(2).broadcast_to([128, 16, 64]), ALU.mult, [y, s], [y])
        p.tt("pool", y[:, 768:1024], y[:, 768:1024], sr[:], ALU.mult, [y, sr], [y])
        for c in range(8):
            ps = pst[c // 4]
            blk = ps[:, (c % 4) * 128:(c % 4 + 1) * 128]
            p.tr(blk, y[:, c * 128:(c + 1) * 128], ident[:], [y, ident], [ps])
            p.act(yT[bi][:, c, :], blk, AF.Identity, [ps, og], [yT[bi]], scale=og[:, c:c + 1])

    def phaseB(i):
        kind = 0 if i < 64 else 1
        bi = i % NB
        rows = slice(i * 128, (i + 1) * 128)
        h = ht[bi]
        for hf in range(2):
            ps = pso[(2 * i + hf) % 4]
            cols = slice(hf * 512, (hf + 1) * 512)
            for c in range(8):
                p.mm(ps[:], yT[bi][:, c, :], W[:, c, cols], c == 0, c == 7, [yT[bi], W], [ps])
            p.tt("dve", ho[bi][:, cols], ps[:], g1[kind][:, cols], ALU.mult, [ps, g1[kind]], [ho[bi]])
            p.tt("pool", ho[bi][:, cols], ho[bi][:, cols], h[:, cols], ALU.add, [ho[bi], h], [ho[bi]])
        p.dma("sp", H1[rows, :], ho[bi][:], [ho[bi]], [H1])

    phaseA(0)
    for i in range(NTT):
        if i + 1 < NTT:
            phaseA(i + 1)
        phaseB(i)
    st.close()


def emit_l5b(p, PS, l, H1, Hout, MOD, wd, ident_d, final):
    st = Stage(p)
    sb = st.sb
    W1 = sb("W1", [128, 8, FFN], BF16)
    W3 = sb("W3", [128, 8, FFN], BF16)
    W2 = sb("W2", [128, NF, D], BF16)
    A = sb("A", [128, 8, 2])
    ng = sb("ng", [128, 8])
    g2 = [sb("g2_%d" % k, [128, D]) for k in range(3 if final else 2)]
    ident = sb("ident", [128, 128])
    epst = sb("eps", [128, 1])
    p.op("dve", lambda e: e.memset(epst[:], EPS), [], [epst])
    p.dma("sp", ident[:], ident_d[:], [ident_d], [ident])
    p.dma("sp", ng[:], wd["norm2_g"][l], [wd["norm2_g"]], [ng])
    for k in range(2):
        p.dma("sp", g2[k][:], MOD[l, k:k + 1, 5 * D:6 * D].partition_broadcast(128)[:, 0, :], [MOD], [g2[k]])
    if final:
        p.dma("sp", g2[2][:], wd["final_norm_g"][:].partition_broadcast(128)[:, 0, :], [wd["final_norm_g"]], [g2[2]])
    for c in range(8):
        p.dma("pool", W1[:, c, :], wd["ffn_w1"][l, c * 128:(c + 1) * 128, :], [wd["ffn_w1"]], [W1], max_dma_last_dim=4096)
        p.dma("pool", W3[:, c, :], wd["ffn_w3"][l, c * 128:(c + 1) * 128, :], [wd["ffn_w3"]], [W3], max_dma_last_dim=4096)
    for f in range(NF):
        p.dma("pool", W2[:, f, :], wd["ffn_w2"][l, f * 128:(f + 1) * 128, :], [wd["ffn_w2"]], [W2], max_dma_last_dim=4096)
    modT = load_modT(p, st, PS, MOD, l, ident)
    for k in range(2):
        p.stt(A[:, :, k], modT[:, k * 48 + 32:k * 48 + 40], 1.0, ng[:], ALU.add, ALU.mult, [modT, ng], [A])
    G = 2
    hts = [sb("ht%d" % i, [128, D]) for i in range(2 * G)]
    xs = sb("xs", [128, D])
    junk = sb("junk", [128, D])
    u2T = [sb("u2T%d" % i, [128, 8, G * 128], BF16) for i in range(2)]
    hidT = sb("hidT", [128, NF, G * 128], BF16)
    sil = [sb("sil%d" % i, [128, G * 128]) for i in range(2)]
    ho = [sb("ho%d" % i, [128, D]) for i in range(2)]
    st_ = sb("st", [128, 8])
    pst, psu, psd = PS[0:2], PS[2:6], PS[6:8]
    groups = [list(range(a, a + G)) for a in range(0, 64, G)] + [[64, 65]]
    nd = 0
    for gi, tiles in enumerate(groups):
        kind = 0 if tiles[0] < 64 else 1
        N = len(tiles) * 128
        u = u2T[gi % 2]
        for j, i in enumerate(tiles):
            h = hts[(gi % 2) * G + j]
            rows = slice(i * 128, (i + 1) * 128)
            p.dma("sp", h[:], H1[rows, :], [H1], [h])
            sc = st_[:, 2 * j:2 * j + 1]
            sr_ = st_[:, 2 * j + 1:2 * j + 2]
            p.act(junk[:], h[:], AF.Square, [h], [junk, st_], accum_out=sc)
            p.rsqrt(st_, sr_, st_, sc, 1.0 / D, epst)
            p.act(xs[:], h[:], AF.Identity, [h, st_], [xs], scale=sr_)
            for c in range(8):
                ps = pst[c // 4]
                blk = ps[:, (c % 4) * 128:(c % 4 + 1) * 128]
                p.tr(blk, xs[:, c * 128:(c + 1) * 128], ident[:], [xs, ident], [ps])
                p.act(u[:, c, j * 128:(j + 1) * 128], blk, AF.Identity, [ps, A, modT], [u],
                      scale=A[:, c, kind:kind + 1], bias=modT[:, kind * 48 + 24 + c:kind * 48 + 25 + c])
        for f in range(NF):
            ps1 = psu[(2 * f) % 4]
            ps3 = psu[(2 * f + 1) % 4]
            fc = slice(f * 128, (f + 1) * 128)
            for c in range(8):
                p.mm(ps1[:, 0:N], W1[:, c, fc], u[:, c, 0:N], c == 0, c == 7, [W1, u], [ps1])
            for c in range(8):
                p.mm(ps3[:, 0:N], W3[:, c, fc], u[:, c, 0:N], c == 0, c == 7, [W3, u], [ps3])
            s_ = sil[f % 2]
            p.act(s_[:, 0:N], ps1[:, 0:N], AF.Silu, [ps1], [s_])
            p.tt("dve", hidT[:, f, 0:N], s_[:, 0:N], ps3[:, 0:N], ALU.mult, [s_, ps3], [hidT])
        for j, i in enumerate(tiles):
            h = hts[(gi % 2) * G + j]
            rows = slice(i * 128, (i + 1) * 128)
            o = ho[nd % 2]
            nd += 1
            for hf in range(2):
                ps = psd[hf]
                cols = slice(hf * 512, (hf + 1) * 512)
                for f in range(NF):
                    p.mm(ps[:], hidT[:, f, j * 128:(j + 1) * 128], W2[:, f, cols], f == 0, f == NF - 1, [hidT, W2], [ps])
                p.tt("dve", o[:, cols], ps[:], g2[kind][:, cols], ALU.mult, [ps, g2[kind]], [o])
                p.tt("pool", o[:, cols], o[:, cols], h[:, cols], ALU.add, [o, h], [o])
            if final:
                sc = st_[:, 4:5]
                sr_ = st_[:, 5:6]
                p.act(junk[:], o[:], AF.Square, [o], [junk, st_], accum_out=sc)
                p.rsqrt(st_, sr_, st_, sc, 1.0 / D, epst)
                p.stt(o[:], o[:], sr_, g2[2][:], ALU.mult, ALU.mult, [o, st_, g2[2]], [o])
            p.dma("sp", Hout[rows, :], o[:], [o], [Hout])
    st.close()


FUSED_W = ["ada_w", "ada_b", "w_in", "qkg", "gla_gate_w", "gla_gate_b", "norm1_g", "norm2_g", "out_norm_g", "w_out",
           "ffn_w1", "ffn_w3", "ffn_w2", "final_norm_g", "cw", "hbc", "hb", "fw3", "filt_w1", "filt_w2", "fb"]
FUSED_C = ["cs", "ident", "mask", "J", "pm", "zl", "zc", "wl", "wc", "F1", "TWf", "F2", "G2", "TWi", "G1"]


def fused_host_inputs(w):
    rope = rope_table()
    one = np.concatenate([np.ones((CTX, 32), np.float32), np.zeros((CTX, 32), np.float32)], axis=1)
    j = np.arange(128)
    zl, win_l = hy_features(SEQ)
    zc, win_c = hy_features(CTX)
    consts = hy_consts()
    consts.update({
        "cs": np.concatenate([rope, one], axis=0),
        "ident": np.eye(128, dtype=np.float32),
        "mask": ((j[:, None] // 64 == j[None, :] // 64) & (j[None, :] >= j[:, None])).astype(np.float32),
        "J": np.eye(128, dtype=np.float32)[::-1].copy(),
        "pm": L4_PM,
        "zl": zl, "zc": zc,
        "wl": win_l.reshape(64, 128, 4, 64).transpose(2, 0, 3, 1),
        "wc": win_c.reshape(CTX, 4, 64).transpose(1, 2, 0),
    })
    fmL = lambda a: np.stack([fm(a[l]) for l in range(DEPTH)])
    cwf = w["hy_conv_w"].reshape(DEPTH, 3, 3, 4, 64)
    cbf = w["hy_conv_b"].reshape(DEPTH, 3, 4, 64)
    cw = np.concatenate([cwf.transpose(0, 3, 4, 2, 1), cbf.transpose(0, 2, 3, 1)[..., None]], axis=-1)
    hb = w["hy_bias"].reshape(DEPTH, 2, 4, 64).transpose(0, 2, 1, 3)
    ws = {
        "ada_w": w["ada_w"], "ada_b": w["ada_b"], "w_in": w["w_in"],
        "qkg": np.concatenate([w["q_norm_g"], w["k_norm_g"]], axis=1),
        "gla_gate_w": w["gla_gate_w"], "gla_gate_b": w["gla_gate_b"].reshape(DEPTH, 256),
        "norm1_g": fmL(w["norm1_g"]), "norm2_g": fmL(w["norm2_g"]), "out_norm_g": fmL(w["out_norm_g"]),
        "w_out": w["w_out"], "ffn_w1": w["ffn_w1"], "ffn_w3": w["ffn_w3"], "ffn_w2": w["ffn_w2"],
        "final_norm_g": w["final_norm_g"].reshape(1, D),
        "cw": cw, "hbc": hb.transpose(0, 1, 3, 2), "hb": hb,
        "fw3": w["filt_w3"].reshape(DEPTH, 64, 4, 4, 64).transpose(0, 3, 1, 2, 4),
        "filt_w1": w["filt_w1"], "filt_w2": w["filt_w2"],
        "fb": np.stack([w["filt_freq"], w["filt_b1"], w["filt_b2"]], axis=-1),
    }
    shared = {k: np.ascontiguousarray(v, dtype=np.float32) for k, v in {**ws, **consts}.items()}
    in_maps = []
    for i in range(NCORES):
        b = i // 4
        m = dict(shared)
        m["H0"] = np.ascontiguousarray(np.concatenate([w["x"][b], w["ctx"][b]], axis=0))
        cvec = np.stack([w["c"][b], w["c_ctx"]], axis=0)
        m["cT"] = np.ascontiguousarray(cvec.T.reshape(8, 128, 2).transpose(1, 0, 2))
        in_maps.append(m)
    return in_maps


def build_fused(shapes, nlayers=DEPTH, final=True, dump=()):
    p = Prog()
    H0 = p.dram_in("H0", [TT, D])
    cT = p.dram_in("cT", [128, 8, 2])
    wd = {k: p.dram_in(k, shapes[k]) for k in FUSED_W}
    cd = {k: p.dram_in(k, shapes[k]) for k in FUSED_C}
    out = p.dram_out("out", [TT, D])
    MOD = scratch(p, "MOD", [DEPTH, 2, 6 * D])
    S1 = scratch(p, "S1", [TT, OUT1])
    SY = scratch(p, "SY", [TT, 512])
    YL = scratch(p, "YL", [4, 64, 64, 128])
    YC = scratch(p, "YC", [4, 64, CTX])
    GO = scratch(p, "GO", [4, 2, 64, TT])
    H1 = scratch(p, "H1", [TT, D])
    HA = scratch(p, "HA", [TT, D])
    scr = scratch(p, "scr", [3, 64, SEQ])
    scr2 = scratch(p, "scr2", [64, 64, 128])
    PS2 = [p.ps("Q%d" % i, (128, 1024)) for i in range(4)]
    PS = [Tile(PS2[i // 2].h[:, (i % 2) * 512:(i % 2 + 1) * 512]) for i in range(8)]
    emit_ada(p, PS, cT, wd["ada_w"], wd["ada_b"], MOD)
    Hin = H0
    for l in range(nlayers):
        last = l == nlayers - 1
        emit_l1(p, PS, l, Hin, S1, MOD, wd, cd["cs"], cd["ident"])
        emit_l2(p, PS, S1, SY, cd["ident"], PS2)
        emit_l3(p, PS, l, S1, YL, YC, wd, cd, scr, scr2)
        emit_l4(p, PS, S1, GO, cd)
        emit_l5a(p, PS, l, SY, YL, YC, GO, S1, Hin, H1, MOD, wd, cd["ident"])
        emit_l5b(p, PS, l, H1, out if last else HA, MOD, wd, cd["ident"], final and last)
        Hin = HA
    for name in dump:
        src = {"S1": S1, "SY": SY, "YL": YL, "YC": YC, "GO": GO, "H1": H1, "MOD": MOD}[name]
        shp = list(src.h.shape)
        d = p.dram_out("dump_" + name, shp)
        flat = lambda ap: ap.rearrange(" ".join("abcd"[:len(shp)]) + " -> " + ("(" + " ".join("abcd"[:len(shp) - 1]) + ") " + "abcd"[len(shp) - 1] if len(shp) > 2 else "a b"))
        p.dma("sp", flat(d[:]), flat(src[:]), [src], [d])
    return p


def kernel_fused(inputs, nlayers=DEPTH, final=True, dump=()):
    w = {k: np.ascontiguousarray(np.asarray(v, dtype=np.float32)) for k, v in inputs.items()}
    in_maps = fused_host_inputs(w)
    shapes = {k: list(v.shape) for k, v in in_maps[0].items()}
    res = run(build_fused(shapes, nlayers, final, dump), in_maps)
    return res


def kernel(**inputs):
    res = kernel_fused(inputs)
    return np.ascontiguousarray(np.stack([res[0]["out"][:SEQ], res[4]["out"][:SEQ]], axis=0), dtype=np.float32)
```

```python
import math
import numpy as np
import concourse.bass as bass
import concourse.mybir as mybir
from concourse.bass_utils import run_bass_kernel_spmd

F32 = mybir.dt.float32
BF16 = mybir.dt.bfloat16
AF = mybir.ActivationFunctionType
ALU = mybir.AluOpType
AX = mybir.AxisListType

NCORES = 8
D = 1024
SEQ = 8192
CTX = 256
DEPTH = 4
IN_W = 2336
FFN = 2816
EPS = 1e-6
NSLOT = 8


class Tile:
    def __init__(self, h):
        self.h = h
        self.w = None
        self.r = {}

    def __getitem__(self, idx):
        return self.h[idx]


class Prog:
    def __init__(self, self_sync=True):
        self.nc = bass.Bass("TRN2", target_bir_lowering=False)
        nc = self.nc
        self.eng = {"pe": nc.tensor, "dve": nc.vector, "act": nc.scalar, "pool": nc.gpsimd, "sp": nc.sync}
        self.sems = {}
        self.cnt = {}
        self.known = {e: {} for e in self.eng}
        self.dslot = {"sp": 0, "act": 0, "pool": 0}
        self.self_sync = self_sync
        self._sem_cms = []
        self.n_inst = 0

    def _sem(self, sid):
        if sid not in self.sems:
            cm = self.nc.semaphore(sid)
            self._sem_cms.append(cm)
            self.sems[sid] = cm.__enter__()
            self.cnt[sid] = 0
        return self.sems[sid]

    def sb(self, name, shape, dtype=F32):
        return Tile(self.nc.alloc_sbuf_tensor("sb_" + name, list(shape), dtype))

    def ps(self, name, shape=(128, 512), dtype=F32):
        return Tile(self.nc.alloc_psum_tensor("ps_" + name, list(shape), dtype))

    def dram_in(self, name, shape, dtype=F32):
        return Tile(self.nc.dram_tensor(name, list(shape), dtype, kind="ExternalInput").ap())

    def dram_out(self, name, shape, dtype=F32):
        return Tile(self.nc.dram_tensor(name, list(shape), dtype, kind="ExternalOutput").ap())

    def _deps(self, eng, reads, writes):
        deps = {}

        def add(sid, v):
            if deps.get(sid, 0) < v:
                deps[sid] = v

        for t in reads:
            if t.w:
                add(*t.w)
        for t in writes:
            if t.w:
                add(*t.w)
            for sid, v in t.r.items():
                add(sid, v)
        if eng == "pe" or not self.self_sync:
            deps.pop("c_" + eng, None)
        return deps

    def _wait(self, eng, deps):
        e = self.eng[eng]
        kn = self.known[eng]
        for sid, val in deps.items():
            if kn.get(sid, 0) >= val:
                continue
            e.wait_ge(self.sems[sid], val)
            kn[sid] = val

    def _mark(self, tok, reads, writes):
        sid, v = tok
        for t in reads:
            if t.r.get(sid, 0) < v:
                t.r[sid] = v
        for t in writes:
            t.w = tok
            t.r = {}

    def op(self, eng, fn, reads=(), writes=()):
        self._wait(eng, self._deps(eng, reads, writes))
        inst = fn(self.eng[eng])
        sid = "c_" + eng
        sem = self._sem(sid)
        self.cnt[sid] += 1
        inst.then_inc(sem, 1)
        self._mark((sid, self.cnt[sid]), reads, writes)
        self.n_inst += 1
        return inst

    def dma(self, q, out, in_, reads=(), writes=(), **kw):
        deps = self._deps(q, reads, writes)
        slot = self.dslot[q]
        self.dslot[q] = (slot + 1) % NSLOT
        sid = "d_%s_%d" % (q, slot)
        sem = self._sem(sid)
        if self.cnt[sid] > 0 and deps.get(sid, 0) < self.cnt[sid]:
            deps[sid] = self.cnt[sid]
        self._wait(q, deps)
        inst = self.eng[q].dma_start(out=out, in_=in_, **kw)
        self.cnt[sid] += 16
        inst.then_inc(sem, 16)
        self._mark((sid, self.cnt[sid]), reads, writes)
        self.n_inst += 1
        return inst

    def finish(self):
        deps = {sid: c for sid, c in self.cnt.items() if sid.startswith("d_") and c > 0}
        self._wait("sp", deps)
        return self.nc

    def tt(self, eng, out, in0, in1, op, reads, writes):
        return self.op(eng, lambda e: e.tensor_tensor(out=out, in0=in0, in1=in1, op=op), reads, writes)

    def ts(self, eng, out, in0, s1, op0, reads, writes, s2=None, op1=None):
        if op1 is None:
            return self.op(eng, lambda e: e.tensor_scalar(out=out, in0=in0, scalar1=s1, scalar2=None, op0=op0), reads, writes)
        return self.op(eng, lambda e: e.tensor_scalar(out=out, in0=in0, scalar1=s1, scalar2=s2, op0=op0, op1=op1), reads, writes)

    def stt(self, out, in0, scalar, in1, op0, op1, reads, writes):
        return self.op("dve", lambda e: e.scalar_tensor_tensor(out=out, in0=in0, scalar=scalar, in1=in1, op0=op0, op1=op1), reads, writes)

    def act(self, out, in_, func, reads, writes, bias=None, scale=None, accum_out=None):
        kw = {}
        if bias is not None:
            kw["bias"] = bias
        if scale is not None:
            kw["scale"] = scale
        if accum_out is not None:
            kw["accum_out"] = accum_out
        return self.op("act", lambda e: e.activation(out=out, in_=in_, func=func, **kw), reads, writes)

    def mm(self, out, lhsT, rhs, start, stop, reads, writes):
        return self.op("pe", lambda e: e.matmul(out, lhsT, rhs, start=start, stop=stop), reads, writes)

    def tr(self, out, in_, ident, reads, writes):
        return self.op("pe", lambda e: e.transpose(out, in_, ident), reads, writes)

    def rsqrt(self, out_t, out_ap, in_t, in_ap, scale, eps_t):
        self.act(out_ap, in_ap, AF.Sqrt, [in_t, eps_t], [out_t], bias=eps_t[0:in_ap.shape[0], 0:1], scale=scale)
        self.op("dve", lambda e: e.reciprocal(out=out_ap, in_=out_ap), [out_t], [out_t])


def run(prog, in_maps):
    nc = prog.finish()
    res = run_bass_kernel_spmd(nc, in_maps, core_ids=list(range(len(in_maps))))
    return res.results


ADA_COLS = DEPTH * 6 * D // NCORES


def build_ada():
    p = Prog()
    cT = p.dram_in("cT", [128, 8, 3])
    w = p.dram_in("w", [D, ADA_COLS])
    b = p.dram_in("b", [1, ADA_COLS])
    out = p.dram_out("mod", [3, ADA_COLS])
    W = p.sb("W", [128, 8, ADA_COLS])
    cs = p.sb("cs", [128, 8, 3])
    bb = p.sb("bb", [3, ADA_COLS])
    res = p.sb("res", [3, ADA_COLS])
    p.dma("sp", cs[:], cT[:], [cT], [cs])
    p.dma("sp", bb[:], b[:].partition_broadcast(3)[:, 0, :], [b], [bb])
    for c in range(8):
        p.dma("sp" if c % 2 == 0 else "pool", W[:, c, :], w[c * 128:(c + 1) * 128, :], [w], [W])
    p.act(cs[:], cs[:], AF.Silu, [cs], [cs])
    pss = [p.ps("ps%d" % i) for i in range(2)]
    for j in range(ADA_COLS // 512):
        ps = pss[j % 2]
        for c in range(8):
            p.mm(ps[0:3, :], cs[:, c, :], W[:, c, j * 512:(j + 1) * 512], c == 0, c == 7, [cs, W], [ps])
        p.tt("dve", res[:, j * 512:(j + 1) * 512], ps[0:3, :], bb[:, j * 512:(j + 1) * 512], ALU.add, [ps, bb], [res])
    p.dma("sp", out[:], res[:], [res], [out])
    return p


def run_ada(c, c_ctx, ada_w, ada_b):
    cvec = np.concatenate([c, c_ctx[None, :]], axis=0)
    cT = np.ascontiguousarray(cvec.T.reshape(8, 128, 3).transpose(1, 0, 2))
    wall = np.ascontiguousarray(ada_w.transpose(1, 0, 2).reshape(D, DEPTH * 6 * D))
    ball = ada_b.reshape(1, DEPTH * 6 * D)
    in_maps = []
    for i in range(NCORES):
        sl = slice(i * ADA_COLS, (i + 1) * ADA_COLS)
        in_maps.append({"cT": cT, "w": np.ascontiguousarray(wall[:, sl]), "b": np.ascontiguousarray(ball[:, sl])})
    res = run(build_ada(), in_maps)
    mod = np.concatenate([r["mod"] for r in res], axis=1)
    return mod.reshape(3, DEPTH, 6, D)


NT1 = 17
OUT1 = 2560
GROUPS1 = [(0, 512), (512, 1024), (1024, 1536), (1536, 2048), (2048, 2336)]


def build_l1():
    p = Prog()
    ntok = NT1 * 128
    x_tm = p.dram_in("x_tm", [ntok, D])
    x_fm = p.dram_in("x_fm", [128, 8, ntok])
    w_in = p.dram_in("w_in", [D, IN_W])
    vecs = p.dram_in("vecs", [128, 8, 5])
    qkg = p.dram_in("qkg", [1, 128])
    cs_d = p.dram_in("cs", [ntok, 64])
    gw = p.dram_in("gw", [2, 16, 128])
    gb = p.dram_in("gb", [1, 256])
    ident_d = p.dram_in("ident", [128, 128])
    out = p.dram_out("out", [ntok, OUT1])

    W = p.sb("W", [128, 8, IN_W], BF16)
    V = p.sb("V", [128, 8, 5])
    A = p.sb("A", [128, 8, 2])
    Brep = p.sb("Brep", [128, 8, 128], BF16)
    bW = [p.sb("bW%d" % k, [128, IN_W]) for k in range(2)]
    gains = p.sb("gains", [128, 10, 64])
    Wblk = p.sb("Wblk", [32, 256])
    gbb = p.sb("gbb", [128, 256])
    ident = p.sb("ident", [128, 128])
    epst = p.sb("eps", [128, 1])
    pss = [p.ps("ps%d" % i) for i in range(6)]
    pst = p.ps("pst")
    psg = p.ps("psg")

    p.op("dve", lambda e: e.memset(epst[:], EPS), [], [epst])
    p.op("dve", lambda e: e.memset(Wblk[:], 0.0), [], [Wblk])
    p.dma("sp", V[:], vecs[:], [vecs], [V])
    p.dma("sp", ident[:], ident_d[:], [ident_d], [ident])
    p.dma("sp", gbb[:], gb[:].partition_broadcast(128)[:, 0, :], [gb], [gbb])
    p.dma("sp", Wblk[0:16, 0:128], gw[0], [gw], [Wblk])
    p.dma("sp", Wblk[16:32, 128:256], gw[1], [gw], [Wblk])
    for h in range(10):
        src = qkg[:, 0:64] if h < 8 else qkg[:, 64:128]
        p.dma("sp", gains[:, h, :], src.partition_broadcast(128)[:, 0, :], [qkg], [gains])
    p.ts("dve", gains[:, 0:8, :], gains[:, 0:8, :], 0.125, ALU.mult, [gains], [gains])
    for c in range(8):
        p.dma("pool", W[:, c, :], w_in[c * 128:(c + 1) * 128, :], [w_in], [W], max_dma_last_dim=4096)
    for k in range(2):
        p.stt(A[:, :, k], V[:, :, 1 + 2 * k], 1.0, V[:, :, 0], ALU.add, ALU.mult, [V], [A])
        p.op("dve", lambda e: e.tensor_copy(out=Brep[:], in_=V[:, :, 2 + 2 * k].unsqueeze(2).broadcast_to([128, 8, 128])), [V], [Brep])
        for gi, (a, b) in enumerate(GROUPS1):
            ps = pss[gi]
            for c in range(8):
                p.mm(ps[:, 0:b - a], Brep[:, c, :], W[:, c, a:b], c == 0, c == 7, [Brep, W], [ps])
            p.act(bW[k][:, a:b], ps[:, 0:b - a], AF.Identity, [ps], [bW[k]])

    NB = 2
    xt = [p.sb("xt%d" % i, [128, D]) for i in range(NB)]
    xf = [p.sb("xf%d" % i, [128, 8, 128]) for i in range(NB)]
    xa = [p.sb("xa%d" % i, [128, 8, 128], BF16) for i in range(NB)]
    cst = [p.sb("cst%d" % i, [128, 64]) for i in range(NB)]
    O = [p.sb("O%d" % i, [128, OUT1]) for i in range(NB)]
    junk = p.sb("junk", [128, D])
    sq = p.sb("sq", [128, 640])
    tmp = p.sb("tmp", [128, 10, 32])
    st = [p.sb("st%d" % i, [128, 16]) for i in range(NB)]
    lr = p.sb("lr", [128, 32])
    lrT = p.sb("lrT", [32, 128])
    gz = p.sb("gz", [128, 256])
    ga = p.sb("ga", [128, 256])
    pi = 0
    for i in range(NT1):
        kind = 0 if i < 16 else 1
        bi = i % NB
        rows = slice(i * 128, (i + 1) * 128)
        p.dma("sp", xt[bi][:], x_tm[rows, :], [x_tm], [xt[bi]])
        p.dma("sp", xf[bi][:], x_fm[:, :, rows], [x_fm], [xf[bi]])
        p.dma("sp", cst[bi][:], cs_d[rows, :], [cs_d], [cst[bi]])
        s = st[bi]
        p.act(junk[:], xt[bi][:], AF.Square, [xt[bi]], [junk, s], accum_out=s[:, 0:1])
        p.rsqrt(s, s[:, 1:2], s, s[:, 0:1], 1.0 / D, epst)
        p.tt("dve", xa[bi][:], xf[bi][:], A[:, :, kind].unsqueeze(2).broadcast_to([128, 8, 128]), ALU.mult, [xf[bi], A], [xa[bi]])
        o = O[bi]
        for gi, (a, b) in enumerate(GROUPS1):
            ps = pss[pi % 6]
            pi += 1
            for c in range(8):
                p.mm(ps[:, 0:b - a], xa[bi][:, c, :], W[:, c, a:b], c == 0, c == 7, [xa[bi], W], [ps])
            if gi < 4:
                p.stt(o[:, a:b], ps[:, 0:b - a], s[:, 1:2], bW[kind][:, a:b], ALU.mult, ALU.add, [ps, s, bW[kind]], [o])
            else:
                p.stt(o[:, 2048:2304], ps[:, 0:256], s[:, 1:2], bW[kind][:, 2048:2304], ALU.mult, ALU.add, [ps, s, bW[kind]], [o])
                p.stt(lr[:], ps[:, 256:288], s[:, 1:2], bW[kind][:, 2304:2336], ALU.mult, ALU.add, [ps, s, bW[kind]], [lr])
        qk = o[:, 0:640]
        p.tt("pool", sq[:], qk, qk, ALU.mult, [o], [sq])
        p.op("dve", lambda e: e.tensor_reduce(out=s[:, 2:12], in_=sq[:].rearrange("p (h d) -> p h d", d=64), axis=AX.X, op=ALU.add), [sq], [s])
        p.rsqrt(s, s[:, 2:12], s, s[:, 2:12], 1.0 / 64, epst)
        qk3 = qk.rearrange("p (h d) -> p h d", d=64)
        p.tt("dve", qk3, qk3, s[:, 2:12].unsqueeze(2).broadcast_to([128, 10, 64]), ALU.mult, [o, s], [o])
        p.tt("pool", qk3, qk3, gains[:], ALU.mult, [o, gains], [o])
        x1 = qk3[:, :, 0:32]
        x2 = qk3[:, :, 32:64]
        cb = cst[bi][:, 0:32].unsqueeze(1).broadcast_to([128, 10, 32])
        sb_ = cst[bi][:, 32:64].unsqueeze(1).broadcast_to([128, 10, 32])
        t3 = sq[:, 0:320].rearrange("p (h d) -> p h d", d=32)
        t4 = sq[:, 320:640].rearrange("p (h d) -> p h d", d=32)
        p.tt("dve", tmp[:], x2, sb_, ALU.mult, [o, cst[bi]], [tmp])
        p.tt("pool", t3, x1, sb_, ALU.mult, [o, cst[bi]], [sq])
        p.tt("dve", x1, x1, cb, ALU.mult, [o, cst[bi]], [o])
        p.tt("dve", x1, x1, tmp[:], ALU.subtract, [o, tmp], [o])
        p.tt("dve", x2, x2, cb, ALU.mult, [o, cst[bi]], [o])
        p.tt("dve", x2, x2, t3, ALU.add, [o, sq], [o])
        p.act(o[:, 1536:1664], o[:, 1536:1664], AF.Identity, [o], [o], scale=32 ** -0.5)
        p.act(o[:, 2048:2304], o[:, 2048:2304], AF.Silu, [o], [o])
        p.tr(pst[0:32, 0:128], lr[:], ident[:], [lr, ident], [pst])
        p.act(lrT[:], pst[0:32, 0:128], AF.Identity, [pst], [lrT])
        p.mm(psg[:, 0:256], lrT[:], Wblk[:], True, True, [lrT, Wblk], [psg])
        p.tt("dve", gz[:], psg[:, 0:256], gbb[:], ALU.add, [psg, gbb], [gz])
        p.stt(ga[:], gz[:], -1.0, gz[:], ALU.mult, ALU.min, [gz], [ga])
        p.act(ga[:], ga[:], AF.Exp, [ga], [ga])
        p.act(ga[:], ga[:], AF.Ln, [ga], [ga], bias=1.0)
        p.ts("dve", gz[:], gz[:], 0.0, ALU.min, [gz], [gz], s2=1.0 / 16, op1=ALU.mult)
        p.stt(o[:, 2304:2560], ga[:], -1.0 / 16, gz[:], ALU.mult, ALU.add, [ga, gz], [o])
        p.dma("sp", out[rows, :], o[:], [o], [out])
    return p


def fm(v):
    return np.ascontiguousarray(v.reshape(-1, 128).T)


def tok_split(lat, ctx):
    F = lat.shape[-1]
    outs = []
    for i in range(NCORES):
        b, j = divmod(i, 4)
        a = np.zeros((NT1 * 128, F), lat.dtype)
        a[:2048] = lat[b, j * 2048:(j + 1) * 2048]
        if j < 2:
            a[2048:] = ctx[b, j * 128:(j + 1) * 128]
        outs.append(a)
    return outs


def tok_merge(per_core):
    F = per_core[0].shape[-1]
    lat = np.zeros((2, SEQ, F), per_core[0].dtype)
    ctx = np.zeros((2, CTX, F), per_core[0].dtype)
    for i in range(NCORES):
        b, j = divmod(i, 4)
        lat[b, j * 2048:(j + 1) * 2048] = per_core[i][:2048]
        if j < 2:
            ctx[b, j * 128:(j + 1) * 128] = per_core[i][2048:]
    return lat, ctx


def rope_table():
    t = np.arange(SEQ)
    row = (t // 64).astype(np.float32)
    col = (t % 64).astype(np.float32)
    inv = np.power(np.float32(10000.0), -np.arange(16, dtype=np.float32) / np.float32(16)).astype(np.float32)
    ang = np.concatenate([row[:, None] * inv, col[:, None] * inv], axis=-1).astype(np.float32)
    return np.concatenate([np.cos(ang), np.sin(ang)], axis=-1).astype(np.float32)


_PROGS = {}


def get_prog(name, builder):
    return builder()


def run_l1(h_lat, h_ctx, mod, layer, w):
    xs = tok_split(h_lat, h_ctx)
    rope = rope_table()
    one = np.concatenate([np.ones((128, 32), np.float32), np.zeros((128, 32), np.float32)], axis=1)
    ident = np.eye(128, dtype=np.float32)
    in_maps = []
    for i in range(NCORES):
        b, j = divmod(i, 4)
        x = xs[i]
        vecs = np.stack([fm(w["norm1_g"][layer]), fm(mod[b, layer, 1]), fm(mod[b, layer, 0]),
                         fm(mod[2, layer, 1]), fm(mod[2, layer, 0])], axis=-1)
        cs = np.concatenate([rope[j * 2048:(j + 1) * 2048], one], axis=0)
        in_maps.append({
            "x_tm": x,
            "x_fm": np.ascontiguousarray(x.T.reshape(8, 128, NT1 * 128).transpose(1, 0, 2)),
            "w_in": w["w_in"][layer],
            "vecs": np.ascontiguousarray(vecs),
            "qkg": np.concatenate([w["q_norm_g"][layer], w["k_norm_g"][layer]])[None, :],
            "cs": cs,
            "gw": w["gla_gate_w"][layer],
            "gb": w["gla_gate_b"][layer].reshape(1, 256),
            "ident": ident,
        })
    res = run(build_l1(), in_maps)
    return tok_merge([r["out"] for r in res])


NQT = 33
NKT = 66


def build_l2():
    p = Prog()
    qT_d = p.dram_in("qT", [64, NQT, 512])
    kT_d = p.dram_in("kT", [64, NKT * 128])
    vA_d = p.dram_in("vA", [128, NKT, 65])
    ident_d = p.dram_in("ident", [128, 128])
    out = p.dram_out("out", [NQT * 128, 256])

    qT = p.sb("qT", [64, NQT, 512], BF16)
    kT = p.sb("kT", [64, NKT * 128], BF16)
    vA = p.sb("vA", [128, NKT, 65], BF16)
    ident = p.sb("ident", [128, 128])
    p.dma("sp", ident[:], ident_d[:], [ident_d], [ident])
    p.dma("pool", kT[:, 0:4096], kT_d[:, 0:4096], [kT_d], [kT], max_dma_last_dim=4096)
    p.dma("pool", kT[:, 4096:], kT_d[:, 4096:], [kT_d], [kT], max_dma_last_dim=4096)
    p.dma("pool", vA[:], vA_d[:], [vA_d], [vA], max_dma_last_dim=4096)
    for a in range(0, NQT, 3):
        p.dma("pool", qT[:, a:a + 3, :], qT_d[:, a:a + 3, :], [qT_d], [qT], max_dma_last_dim=4096)

    psS = [p.ps("psS%d" % i) for i in range(3)]
    psO = [p.ps("psO%d" % i) for i in range(2)]
    psT = [p.ps("psT%d" % i) for i in range(2)]
    pT = [p.sb("pT%d" % i, [128, 512], BF16) for i in range(3)]
    oT = [p.sb("oT%d" % i, [65, 512]) for i in range(2)]
    ot = [p.sb("ot%d" % i, [128, 256]) for i in range(2)]
    rec = p.sb("rec", [128, 8])
    it = 0
    for qt in range(NQT):
        kts = list(range(NKT)) if qt < 32 else [64, 65]
        po = psO[qt % 2]
        for j, kt in enumerate(kts):
            ps = psS[it % 3]
            pt = pT[it % 3]
            it += 1
            p.mm(ps[:], kT[:, kt * 128:(kt + 1) * 128], qT[:, qt, :], True, True, [kT, qT], [ps])
            p.act(pt[:], ps[:], AF.Exp, [ps], [pt])
            p.mm(po[0:65, :], vA[:, kt, :], pt[:], j == 0, j == len(kts) - 1, [vA, pt], [po])
        o_sb = oT[qt % 2]
        p.op("dve", lambda e: e.tensor_copy(out=o_sb[:], in_=po[0:65, :]), [po], [o_sb])
        o_t = ot[qt % 2]
        for h in range(4):
            pst = psT[h % 2]
            p.tr(pst[:, 0:65], o_sb[:, h * 128:(h + 1) * 128], ident[0:65, 0:65], [o_sb, ident], [pst])
            rc = rec[:, (qt % 2) * 4 + h:(qt % 2) * 4 + h + 1]
            p.op("dve", lambda e: e.reciprocal(out=rc, in_=pst[:, 64:65]), [pst], [rec])
            p.ts("dve", o_t[:, h * 64:(h + 1) * 64], pst[:, 0:64], rc, ALU.mult, [pst, rec], [o_t])
        p.dma("sp", out[qt * 128:(qt + 1) * 128, :], o_t[:], [o_t], [out])
    return p


def run_l2(lat, ctx):
    ident = np.eye(128, dtype=np.float32)
    in_maps = []
    for i in range(NCORES):
        b, g, qh = i // 4, (i // 2) % 2, i % 2
        ql = lat[b, qh * 4096:(qh + 1) * 4096, 0:512].reshape(32, 128, 8, 64)[:, :, 4 * g:4 * g + 4, :]
        qc = ctx[b, qh * 128:(qh + 1) * 128, 0:512].reshape(1, 128, 8, 64)[:, :, 4 * g:4 * g + 4, :]
        q = np.concatenate([ql, qc], axis=0)
        qT = np.ascontiguousarray(q.transpose(3, 0, 2, 1)).reshape(64, NQT, 512)
        k_all = np.concatenate([lat[b, :, 512:640], ctx[b, :, 512:640]], axis=0)[:, g * 64:(g + 1) * 64]
        kT = np.ascontiguousarray(k_all.T)
        v_all = np.concatenate([lat[b, :, 640:768], ctx[b, :, 640:768]], axis=0)[:, g * 64:(g + 1) * 64]
        vA = np.ones((128, NKT, 65), np.float32)
        vA[:, :, 0:64] = v_all.reshape(NKT, 128, 64).transpose(1, 0, 2)
        in_maps.append({"qT": qT, "kT": kT, "vA": vA, "ident": ident})
    res = run(build_l2(), in_maps)
    a_lat = np.zeros((2, SEQ, 512), np.float32)
    a_ctx = np.zeros((2, CTX, 512), np.float32)
    for i in range(NCORES):
        b, g, qh = i // 4, (i // 2) % 2, i % 2
        o = res[i]["out"]
        a_lat[b, qh * 4096:(qh + 1) * 4096, g * 256:(g + 1) * 256] = o[:4096]
        a_ctx[b, qh * 128:(qh + 1) * 128, g * 256:(g + 1) * 256] = o[4096:]
    return a_lat, a_ctx


def build_l5a():
    p = Prog()
    ntok = NT1 * 128
    y_d = p.dram_in("y", [ntok, D])
    sr_d = p.dram_in("sr", [ntok, 256])
    yb_d = p.dram_in("yb", [ntok, 256])
    h_d = p.dram_in("h", [ntok, D])
    w_d = p.dram_in("w_out", [D, D])
    og_d = p.dram_in("og", [128, 8])
    g1_d = p.dram_in("g1", [2, D])
    ident_d = p.dram_in("ident", [128, 128])
    out = p.dram_out("out", [ntok, D])

    W = p.sb("W", [128, 8, D], BF16)
    og = p.sb("og", [128, 8])
    g1 = [p.sb("g1_%d" % k, [128, D]) for k in range(2)]
    ident = p.sb("ident", [128, 128])
    epst = p.sb("eps", [128, 1])
    p.op("dve", lambda e: e.memset(epst[:], EPS), [], [epst])
    p.dma("sp", og[:], og_d[:], [og_d], [og])
    p.dma("sp", ident[:], ident_d[:], [ident_d], [ident])
    for k in range(2):
        p.dma("sp", g1[k][:], g1_d[k:k + 1, :].partition_broadcast(128)[:, 0, :], [g1_d], [g1[k]])
    for c in range(8):
        p.dma("pool", W[:, c, :], w_d[c * 128:(c + 1) * 128, :], [w_d], [W], max_dma_last_dim=4096)
    NB = 2
    yt = [p.sb("yt%d" % i, [128, D]) for i in range(NB)]
    ht = [p.sb("ht%d" % i, [128, D]) for i in range(NB)]
    srt = [p.sb("srt%d" % i, [128, 256]) for i in range(NB)]
    ybt = [p.sb("ybt%d" % i, [128, 256]) for i in range(NB)]
    ho = [p.sb("ho%d" % i, [128, D]) for i in range(NB)]
    yT = [p.sb("yT%d" % i, [128, 8, 128], BF16) for i in range(NB)]
    sq = p.sb("sq", [128, D])
    st = [p.sb("st%d" % i, [128, 16]) for i in range(NB)]
    pst = [p.ps("pst%d" % i) for i in range(2)]
    pso = [p.ps("pso%d" % i) for i in range(4)]
    for i in range(NT1):
        kind = 0 if i < 16 else 1
        bi = i % NB
        rows = slice(i * 128, (i + 1) * 128)
        y, h, sr, s = yt[bi], ht[bi], srt[bi], st[bi]
        p.dma("sp", y[:], y_d[rows, :], [y_d], [y])
        p.dma("sp", sr[:], sr_d[rows, :], [sr_d], [sr])
        p.dma("sp", h[:], h_d[rows, :], [h_d], [h])
        p.dma("sp", ybt[bi][:], yb_d[rows, :], [yb_d], [ybt[bi]])
        p.tt("pool", y[:, 768:1024], y[:, 768:1024], ybt[bi][:], ALU.add, [y, ybt[bi]], [y])
        p.tt("pool", sq[:], y[:], y[:], ALU.mult, [y], [sq])
        p.op("dve", lambda e: e.tensor_reduce(out=s[:], in_=sq[:].rearrange("p (h d) -> p h d", d=64), axis=AX.X, op=ALU.add), [sq], [s])
        p.rsqrt(s, s[:], s, s[:], 1.0 / 64, epst)
        y3 = y[:].rearrange("p (h d) -> p h d", d=64)
        p.tt("dve", y3, y3, s[:].unsqueeze(2).broadcast_to([128, 16, 64]), ALU.mult, [y, s], [y])
        p.tt("pool", y[:, 768:1024], y[:, 768:1024], sr[:], ALU.mult, [y, sr], [y])
        for c in range(8):
            ps = pst[c // 4]
            blk = ps[:, (c % 4) * 128:(c % 4 + 1) * 128]
            p.tr(blk, y[:, c * 128:(c + 1) * 128], ident[:], [y, ident], [ps])
            p.act(yT[bi][:, c, :], blk, AF.Identity, [ps, og], [yT[bi]], scale=og[:, c:c + 1])
        for hf in range(2):
            ps = pso[(2 * i + hf) % 4]
            cols = slice(hf * 512, (hf + 1) * 512)
            for c in range(8):
                p.mm(ps[:], yT[bi][:, c, :], W[:, c, cols], c == 0, c == 7, [yT[bi], W], [ps])
            p.tt("dve", ho[bi][:, cols], ps[:], g1[kind][:, cols], ALU.mult, [ps, g1[kind]], [ho[bi]])
            p.tt("pool", ho[bi][:, cols], ho[bi][:, cols], h[:, cols], ALU.add, [ho[bi], h], [ho[bi]])
        p.dma("sp", out[rows, :], ho[bi][:], [ho[bi]], [out])
    return p


def run_l5a(y_lat, y_ctx, yb_lat, yb_ctx, sr_lat, sr_ctx, h_lat, h_ctx, mod, layer, w):
    ys = tok_split(y_lat, y_ctx)
    ybs = tok_split(yb_lat, yb_ctx)
    srs = tok_split(sr_lat, sr_ctx)
    hs = tok_split(h_lat, h_ctx)
    ident = np.eye(128, dtype=np.float32)
    in_maps = []
    for i in range(NCORES):
        b = i // 4
        in_maps.append({"y": ys[i], "yb": ybs[i], "sr": srs[i], "h": hs[i], "w_out": w["w_out"][layer],
                        "og": fm(w["out_norm_g"][layer]),
                        "g1": np.ascontiguousarray(np.stack([mod[b, layer, 2], mod[2, layer, 2]])),
                        "ident": ident})
    res = run(build_l5a(), in_maps)
    return tok_merge([r["out"] for r in res])


NF = FFN // 128


def build_l5b(final):
    p = Prog()
    ntok = NT1 * 128
    h_d = p.dram_in("h", [ntok, D])
    w1_d = p.dram_in("w1", [D, FFN])
    w3_d = p.dram_in("w3", [D, FFN])
    w2_d = p.dram_in("w2", [FFN, D])
    vec_d = p.dram_in("vecs", [128, 8, 5])
    g2_d = p.dram_in("g2", [3, D])
    ident_d = p.dram_in("ident", [128, 128])
    out = p.dram_out("out", [ntok, D])

    W1 = p.sb("W1", [128, 8, FFN], BF16)
    W3 = p.sb("W3", [128, 8, FFN], BF16)
    W2 = p.sb("W2", [128, NF, D], BF16)
    V = p.sb("V", [128, 8, 5])
    A = p.sb("A", [128, 8, 2])
    g2 = [p.sb("g2_%d" % k, [128, D]) for k in range(3 if final else 2)]
    ident = p.sb("ident", [128, 128])
    epst = p.sb("eps", [128, 1])
    p.op("dve", lambda e: e.memset(epst[:], EPS), [], [epst])
    p.dma("sp", V[:], vec_d[:], [vec_d], [V])
    p.dma("sp", ident[:], ident_d[:], [ident_d], [ident])
    for k in range(len(g2)):
        p.dma("sp", g2[k][:], g2_d[k:k + 1, :].partition_broadcast(128)[:, 0, :], [g2_d], [g2[k]])
    for c in range(8):
        p.dma("pool", W1[:, c, :], w1_d[c * 128:(c + 1) * 128, :], [w1_d], [W1], max_dma_last_dim=4096)
        p.dma("pool", W3[:, c, :], w3_d[c * 128:(c + 1) * 128, :], [w3_d], [W3], max_dma_last_dim=4096)
    for f in range(NF):
        p.dma("pool", W2[:, f, :], w2_d[f * 128:(f + 1) * 128, :], [w2_d], [W2], max_dma_last_dim=4096)
    for k in range(2):
        p.stt(A[:, :, k], V[:, :, 1 + 2 * k], 1.0, V[:, :, 0], ALU.add, ALU.mult, [V], [A])

    G = 2
    hts = [p.sb("ht%d" % i, [128, D]) for i in range(2 * G)]
    xs = p.sb("xs", [128, D])
    junk = p.sb("junk", [128, D])
    u2T = [p.sb("u2T%d" % i, [128, 8, G * 128], BF16) for i in range(2)]
    hidT = p.sb("hidT", [128, NF, G * 128], BF16)
    sil = [p.sb("sil%d" % i, [128, G * 128]) for i in range(2)]
    ho = [p.sb("ho%d" % i, [128, D]) for i in range(2)]
    st = p.sb("st", [128, 8])
    pst = [p.ps("pst%d" % i) for i in range(2)]
    psu = [p.ps("psu%d" % i) for i in range(4)]
    psd = [p.ps("psd%d" % i) for i in range(2)]
    groups = [list(range(a, min(a + G, 16))) for a in range(0, 16, G)] + [[16]]
    nd = 0
    for gi, tiles in enumerate(groups):
        kind = 0 if tiles[0] < 16 else 1
        N = len(tiles) * 128
        u = u2T[gi % 2]
        for j, i in enumerate(tiles):
            h = hts[(gi % 2) * G + j]
            rows = slice(i * 128, (i + 1) * 128)
            p.dma("sp", h[:], h_d[rows, :], [h_d], [h])
            sc = st[:, 2 * j:2 * j + 1]
            sr_ = st[:, 2 * j + 1:2 * j + 2]
            p.act(junk[:], h[:], AF.Square, [h], [junk, st], accum_out=sc)
            p.rsqrt(st, sr_, st, sc, 1.0 / D, epst)
            p.act(xs[:], h[:], AF.Identity, [h, st], [xs], scale=sr_)
            for c in range(8):
                ps = pst[c // 4]
                blk = ps[:, (c % 4) * 128:(c % 4 + 1) * 128]
                p.tr(blk, xs[:, c * 128:(c + 1) * 128], ident[:], [xs, ident], [ps])
                p.act(u[:, c, j * 128:(j + 1) * 128], blk, AF.Identity, [ps, A, V], [u],
                      scale=A[:, c, kind:kind + 1], bias=V[:, c, 2 + 2 * kind:3 + 2 * kind])
        for f in range(NF):
            ps1 = psu[(2 * f) % 4]
            ps3 = psu[(2 * f + 1) % 4]
            fc = slice(f * 128, (f + 1) * 128)
            for c in range(8):
                p.mm(ps1[:, 0:N], W1[:, c, fc], u[:, c, 0:N], c == 0, c == 7, [W1, u], [ps1])
            for c in range(8):
                p.mm(ps3[:, 0:N], W3[:, c, fc], u[:, c, 0:N], c == 0, c == 7, [W3, u], [ps3])
            s_ = sil[f % 2]
            p.act(s_[:, 0:N], ps1[:, 0:N], AF.Silu, [ps1], [s_])
            p.tt("dve", hidT[:, f, 0:N], s_[:, 0:N], ps3[:, 0:N], ALU.mult, [s_, ps3], [hidT])
        for j, i in enumerate(tiles):
            h = hts[(gi % 2) * G + j]
            rows = slice(i * 128, (i + 1) * 128)
            o = ho[nd % 2]
            nd += 1
            for hf in range(2):
                ps = psd[hf]
                cols = slice(hf * 512, (hf + 1) * 512)
                for f in range(NF):
                    p.mm(ps[:], hidT[:, f, j * 128:(j + 1) * 128], W2[:, f, cols], f == 0, f == NF - 1, [hidT, W2], [ps])
                p.tt("dve", o[:, cols], ps[:], g2[kind][:, cols], ALU.mult, [ps, g2[kind]], [o])
                p.tt("pool", o[:, cols], o[:, cols], h[:, cols], ALU.add, [o, h], [o])
            if final:
                sc = st[:, 4:5]
                sr_ = st[:, 5:6]
                p.act(junk[:], o[:], AF.Square, [o], [junk, st], accum_out=sc)
                p.rsqrt(st, sr_, st, sc, 1.0 / D, epst)
                p.stt(o[:], o[:], sr_, g2[2][:], ALU.mult, ALU.mult, [o, st, g2[2]], [o])
            p.dma("sp", out[rows, :], o[:], [o], [out])
    return p


def run_l5b(h_lat, h_ctx, mod, layer, w, final):
    hs = tok_split(h_lat, h_ctx)
    ident = np.eye(128, dtype=np.float32)
    in_maps = []
    for i in range(NCORES):
        b = i // 4
        vecs = np.stack([fm(w["norm2_g"][layer]), fm(mod[b, layer, 4]), fm(mod[b, layer, 3]),
                         fm(mod[2, layer, 4]), fm(mod[2, layer, 3])], axis=-1)
        in_maps.append({"h": hs[i], "w1": w["ffn_w1"][layer], "w3": w["ffn_w3"][layer], "w2": w["ffn_w2"][layer],
                        "vecs": np.ascontiguousarray(vecs),
                        "g2": np.ascontiguousarray(np.stack([mod[b, layer, 5], mod[2, layer, 5], w["final_norm_g"]])),
                        "ident": ident})
    res = run(build_l5b(final), in_maps)
    return tok_merge([r["out"] for r in res])


TG = SEQ + CTX
NCH = TG // 64
NPR = TG // 128
SEGL = 1408
NSEG = TG // SEGL


def build_l4(stage=4):
    p = Prog()
    qkg_d = p.dram_in("qkg", [2, 3, 32, TG])
    v_d = p.dram_in("v", [2, 128, NPR, 64])
    mask_d = p.dram_in("mask", [128, 128])
    ident_d = p.dram_in("ident", [128, 128])
    out = p.dram_out("out", [2, 64, TG])

    ident = p.sb("ident", [128, 128])
    mask = p.sb("mask", [128, 128])
    ones = p.sb("ones", [32, 1])
    p.dma("sp", ident[:], ident_d[:], [ident_d], [ident])
    p.dma("sp", mask[:], mask_d[:], [mask_d], [mask])
    p.op("dve", lambda e: e.memset(ones[:], 1.0), [], [ones])

    qd = p.sb("qd", [32, TG], BF16)
    kd = p.sb("kd", [32, TG], BF16)
    ktT = [p.sb("ktT%d" % i, [128, NPR, 32], BF16) for i in range(2)]
    pm_d = p.dram_in("pm", [128, 2])
    pm = p.sb("pm", [128, 2])
    p.dma("sp", pm[:], pm_d[:], [pm_d], [pm])
    vt = p.sb("vt", [128, NPR, 64], BF16)
    STm = p.sb("STm", [128, NPR, 128], BF16)
    Sbf = p.sb("Sbf", [32, NCH + 1, 64], BF16)
    dc = p.sb("dc", [32, NCH])
    oT = p.sb("oT", [64, TG])
    Scur = [p.sb("Scur%d" % i, [32, 64]) for i in range(2)]
    gseg = p.sb("gseg", [32, SEGL])
    qseg = p.sb("qseg", [32, SEGL])
    kseg = p.sb("kseg", [32, SEGL])
    Gs = [p.sb("Gs%d" % i, [32, SEGL]) for i in range(2)]
    At = p.sb("At", [32, SEGL])
    Bt = p.sb("Bt", [32, SEGL])
    Sc = p.sb("Sc", [32, 22])
    tmpc = p.sb("tmpc", [32, 22])
    psT = [p.ps("psT%d" % i) for i in range(2)]
    psS = [p.ps("psS%d" % i) for i in range(2)]
    psK = [p.ps("psK%d" % i) for i in range(2)]
    psO = [p.ps("psO%d" % i) for i in range(2)]

    for d in range(2):
        p.dma("pool", vt[:], v_d[d], [v_d], [vt], max_dma_last_dim=4096)
        for s in range(NSEG):
            seg = slice(s * SEGL, (s + 1) * SEGL)
            G = Gs[s % 2]
            Gp = Gs[(s + 1) % 2]
            p.dma("sp", qseg[:], qkg_d[d, 0, :, seg], [qkg_d], [qseg])
            p.dma("sp", kseg[:], qkg_d[d, 1, :, seg], [qkg_d], [kseg])
            p.dma("sp", gseg[:], qkg_d[d, 2, :, seg], [qkg_d], [gseg])
            init = 0.0 if s == 0 else Gp[:, SEGL - 1:SEGL]
            rd = [gseg, ones] + ([] if s == 0 else [Gp])
            p.op("dve", lambda e: e.tensor_tensor_scan(out=G[:], data0=ones[:, 0:1].broadcast_to([32, SEGL]), data1=gseg[:],
                                                       initial=init, op0=ALU.mult, op1=ALU.add), rd, [G])
            G3 = G[:].rearrange("p (c j) -> p c j", j=64)
            Ec = G3[:, :, 63]
            if s == 0:
                p.op("dve", lambda e: e.memset(Sc[:, 0:1], 0.0), [], [Sc])
            else:
                p.op("dve", lambda e: e.tensor_copy(out=Sc[:, 0:1], in_=Gp[:, SEGL - 1:SEGL]), [Gp], [Sc])
            p.op("dve", lambda e: e.tensor_copy(out=Sc[:, 1:22], in_=G3[:, 0:21, 63]), [G], [Sc])
            A3 = At[:].rearrange("p (c j) -> p c j", j=64)
            p.tt("dve", A3, G3, Sc[:].unsqueeze(2).broadcast_to([32, 22, 64]), ALU.subtract, [G, Sc], [At])
            p.act(Bt[:], At[:], AF.Exp, [At], [Bt])
            p.tt("dve", qd[:, seg], qseg[:], Bt[:], ALU.mult, [qseg, Bt], [qd])
            p.act(Bt[:], At[:], AF.Exp, [At], [Bt], scale=-1.0)
            p.tt("dve", kd[:, seg], kseg[:], Bt[:], ALU.mult, [kseg, Bt], [kd])
            p.tt("dve", tmpc[:], Ec, Sc[:], ALU.subtract, [G, Sc], [tmpc])
            p.act(dc[:, s * 22:(s + 1) * 22], tmpc[:], AF.Exp, [tmpc], [dc])
            p.tt("dve", A3, Ec.unsqueeze(2).broadcast_to([32, 22, 64]), G3, ALU.subtract, [G], [At])
            p.act(At[:], At[:], AF.Exp, [At], [At])
            p.tt("dve", Bt[:], kseg[:], At[:], ALU.mult, [kseg, At], [Bt])
            ps = psT[s % 2]
            for j in range(11):
                p.tr(ps[:, j * 32:(j + 1) * 32], Bt[:, j * 128:(j + 1) * 128], ident[0:32, 0:32], [Bt, ident], [ps])
            for hf in range(2):
                p.act(ktT[hf][:, s * 11:(s + 1) * 11, :], ps[:, 0:352].rearrange("p (a b) -> p a b", b=32), AF.Identity,
                      [ps, pm], [ktT[hf]], scale=pm[:, hf:hf + 1])
        if stage < 4:
            p.op("dve", lambda e: e.memset(oT[:], 0.0), [], [oT])
        for g0 in (range(0, NPR, 4) if stage >= 2 else []):
            n = min(4, NPR - g0)
            ps = psS[(g0 // 4) % 2]
            for j in range(n):
                pr = g0 + j
                tok = slice(pr * 128, (pr + 1) * 128)
                p.mm(ps[:, j * 128:(j + 1) * 128], kd[:, tok], qd[:, tok], True, True, [kd, qd], [ps])
            p.tt("dve", STm[:, g0:g0 + n, :], ps[:, 0:n * 128].rearrange("p (a b) -> p a b", b=128),
                 mask[:].unsqueeze(1).broadcast_to([128, n, 128]), ALU.mult, [ps, mask], [STm])
        p.op("dve", lambda e: e.memset(Scur[0][:], 0.0), [], [Scur[0]])
        p.op("dve", lambda e: e.memset(Sbf[:, 0, :], 0.0), [], [Sbf])
        for c0 in (range(0, NCH, 8) if stage >= 3 else []):
            n = min(8, NCH - c0)
            ps = psK[(c0 // 8) % 2]
            for j in range(n):
                c = c0 + j
                pr, hf = divmod(c, 2)
                p.mm(ps[0:32, j * 64:(j + 1) * 64], ktT[hf][:, pr, :], vt[:, pr, :], True, True, [ktT[hf], vt], [ps])
            for j in range(n):
                c = c0 + j
                sa, sb_ = Scur[c % 2], Scur[(c + 1) % 2]
                p.stt(sb_[:], sa[:], dc[:, c:c + 1], ps[0:32, j * 64:(j + 1) * 64], ALU.mult, ALU.add, [sa, dc, ps], [sb_])
                p.act(Sbf[:, c + 1, :], sb_[:], AF.Identity, [sb_], [Sbf])
        for g0 in (range(0, NPR, 4) if stage >= 4 else []):
            n = min(4, NPR - g0)
            ps = psO[(g0 // 4) % 2]
            for j in range(n):
                pr = g0 + j
                cs_ = slice(j * 128, (j + 1) * 128)
                p.mm(ps[0:64, cs_], vt[:, pr, :], STm[:, pr, :], True, False, [vt, STm], [ps])
                for hf in range(2):
                    c = 2 * pr + hf
                    tok = slice(c * 64, (c + 1) * 64)
                    p.mm(ps[0:64, j * 128 + hf * 64:j * 128 + (hf + 1) * 64], Sbf[:, c, :], qd[:, tok], False, hf == 1, [Sbf, qd], [ps])
            p.act(oT[:, g0 * 128:(g0 + n) * 128], ps[0:64, 0:n * 128], AF.Identity, [ps], [oT])
        p.dma("sp", out[d], oT[:], [oT], [out])
    return p


L4_PM = np.stack([(np.arange(128) < 64), (np.arange(128) >= 64)], axis=1).astype(np.float32)


def run_l4(lat, ctx):
    ident = np.eye(128, dtype=np.float32)
    j = np.arange(128)
    mask = ((j[:, None] // 64 == j[None, :] // 64) & (j[None, :] >= j[:, None])).astype(np.float32)
    in_maps = []
    for i in range(NCORES):
        b, hd = divmod(i, 4)
        qkg = np.zeros((2, 3, 32, TG), np.float32)
        v = np.zeros((2, 128, NPR, 64), np.float32)
        for d in range(2):
            def seq(c0, w):
                a, l = ctx[b, :, c0:c0 + w], lat[b, :, c0:c0 + w]
                if d == 1:
                    a, l = a[::-1], l[::-1]
                return np.concatenate([a, l], axis=0)
            qkg[d, 0] = seq(1536 + hd * 32, 32).T
            qkg[d, 1] = seq(1664 + hd * 32, 32).T
            qkg[d, 2] = seq((2304 if d == 0 else 2432) + hd * 32, 32).T
            v[d] = seq(1792 + hd * 64, 64).reshape(NPR, 128, 64).transpose(1, 0, 2)
        in_maps.append({"qkg": qkg, "v": v, "mask": mask, "ident": ident, "pm": L4_PM})
    res = run(build_l4(), in_maps)
    f_lat = np.zeros((2, SEQ, 256), np.float32)
    b_lat = np.zeros((2, SEQ, 256), np.float32)
    f_ctx = np.zeros((2, CTX, 256), np.float32)
    b_ctx = np.zeros((2, CTX, 256), np.float32)
    for i in range(NCORES):
        b, hd = divmod(i, 4)
        o = res[i]["out"]
        cols = slice(hd * 64, (hd + 1) * 64)
        f_ctx[b, :, cols] = o[0].T[:CTX]
        f_lat[b, :, cols] = o[0].T[CTX:]
        b_ctx[b, :, cols] = o[1].T[:CTX][::-1]
        b_lat[b, :, cols] = o[1].T[CTX:][::-1]
    return f_lat, b_lat, f_ctx, b_ctx


NFFT = 16384
PI = math.pi


def hy_consts():
    n1 = np.arange(64)[:, None]
    k = np.arange(128)[None, :]
    th = 2 * np.pi * n1 * k / 128
    F1 = np.concatenate([np.cos(th), -np.sin(th)], axis=1)
    n2 = np.arange(128)[:, None]
    tw = 2 * np.pi * n2 * k / NFFT
    TWf = np.stack([np.cos(tw), -np.sin(tw)], axis=1)
    th2 = 2 * np.pi * n2 * k / 128
    F2 = np.stack([np.cos(th2), -np.sin(th2), np.sin(th2)], axis=1)
    G2 = np.stack([np.concatenate([np.cos(th2), np.sin(th2)], axis=1),
                   np.concatenate([-np.sin(th2), np.cos(th2)], axis=1)], axis=1)
    TWi = np.stack([np.cos(tw.T), np.sin(tw.T)], axis=1)
    k1 = np.arange(128)[:, None]
    n1r = np.arange(64)[None, :]
    th1 = 2 * np.pi * k1 * n1r / 128
    G1 = np.stack([np.cos(th1), -np.sin(th1)], axis=1) / NFFT
    f32 = lambda a: np.ascontiguousarray(a, dtype=np.float32)
    return {"F1": f32(F1), "TWf": f32(TWf), "F2": f32(F2), "G2": f32(G2), "TWi": f32(TWi), "G1": f32(G1)}


def hy_features(n):
    t = np.linspace(0.0, 1.0, n, dtype=np.float32)[:, None]
    omega = (np.float32(2.0 * math.pi / n) * np.arange(n, dtype=np.float32)).astype(np.float32)
    bands = np.linspace(1e-4, 15, 16, dtype=np.float32)
    phase = (omega[:, None] * bands[None, :]).astype(np.float32)
    z = np.concatenate([t, np.cos(phase), -np.sin(phase)], axis=-1).astype(np.float32)
    mn, mx = math.log(1e-2) / 1.5, math.log(1e-2) / 0.3
    deltas = np.abs(np.linspace(mn, mx, 256, dtype=np.float32))
    window = np.exp(-t * deltas[None, :]).astype(np.float32)
    return np.ascontiguousarray(z.T), window


def build_l3(debug=False):
    p = Prog()
    x_d = p.dram_in("x", [3, 64, SEQ])
    xc_d = p.dram_in("xc", [3, 64, CTX])
    cw_d = p.dram_in("cw", [64, 3, 4])
    hb_d = p.dram_in("hb", [2, 64])
    hbc_d = p.dram_in("hbc", [64, 2])
    w1_d = p.dram_in("fw1", [33, 64])
    w2_d = p.dram_in("fw2", [64, 64])
    w3_d = p.dram_in("fw3", [64, 4, 64])
    fb_d = p.dram_in("fb", [64, 3])
    zl_d = p.dram_in("zl", [33, SEQ])
    zc_d = p.dram_in("zc", [33, CTX])
    wl_d = p.dram_in("wl", [64, 64, 128])
    wc_d = p.dram_in("wc", [64, CTX])
    cd = {k: p.dram_in(k, list(v.shape)) for k, v in hy_consts().items()}
    scr = Tile(p.nc.dram_tensor("scr", [3, 64, SEQ], F32, kind="Internal").ap())
    yl_d = p.dram_out("yl", [64, 64, 128])
    yc_d = p.dram_out("yc", [64, CTX])

    BG = [p.sb("BG%d" % i, [128, SEQ]) for i in range(3)]
    Hs = p.sb("Hs", [128, 2, 64, 128], BF16)
    F1 = p.sb("F1", [64, 256], BF16)
    TWf = p.sb("TWf", [128, 2, 128])
    F2 = p.sb("F2", [128, 3, 128], BF16)
    G2 = p.sb("G2", [128, 2, 256], BF16)
    TWi = p.sb("TWi", [128, 2, 128])
    G1 = p.sb("G1", [128, 2, 64], BF16)
    for nm, t in (("F1", F1), ("F2", F2), ("G2", G2), ("G1", G1)):
        p.dma("pool", t[:], cd[nm][:], [cd[nm]], [t])
    p.dma("sp", TWf[:], cd["TWf"][:], [cd["TWf"]], [TWf])
    p.dma("sp", TWi[:], cd["TWi"][:], [cd["TWi"]], [TWi])
    cw = p.sb("cw", [64, 3, 4])
    hbr = p.sb("hbr", [64, 2, 64])
    hbc = p.sb("hbc", [64, 2])
    w1 = p.sb("fw1", [33, 64])
    w2 = p.sb("fw2", [64, 64])
    w3 = p.sb("fw3", [64, 4, 64], BF16)
    w3f = p.sb("fw3f", [64, 4, 64])
    fb = p.sb("fb", [64, 3])
    frb = p.sb("frb", [64, 2])
    wc = p.sb("wc", [64, CTX])
    p.dma("sp", cw[:], cw_d[:], [cw_d], [cw])
    p.dma("sp", hbc[:], hbc_d[:], [hbc_d], [hbc])
    for o in range(2):
        p.dma("sp", hbr[:, o, :], hb_d[o:o + 1, :].partition_broadcast(64)[:, 0, :], [hb_d], [hbr])
    p.dma("sp", w1[:], w1_d[:], [w1_d], [w1])
    p.dma("sp", w2[:], w2_d[:], [w2_d], [w2])
    p.dma("sp", w3f[:], w3_d[:], [w3_d], [w3f])
    p.dma("pool", w3[:], w3_d[:], [w3_d], [w3])
    p.dma("sp", fb[:], fb_d[:], [fb_d], [fb])
    p.dma("sp", wc[:], wc_d[:], [wc_d], [wc])
    for j in range(2):
        p.tt("dve", frb[:, j:j + 1], fb[:, 0:1], fb[:, 1 + j:2 + j], ALU.mult, [fb], [frb])
    PS = [p.ps("P%d" % i) for i in range(8)]
    PA, PX, PB, PY = PS[0:2], PS[2:4], PS[4:6], PS[6:8]

    def short_conv(xt, xap, ut, uap, g, n):
        p.act(uap, xap, AF.Identity, [xt, cw], [ut], scale=cw[:, g, 1:2], bias=cw[:, g, 3:4])
        p.stt(uap[:, 1:n], xap[:, 0:n - 1], cw[:, g, 0:1], uap[:, 1:n], ALU.mult, ALU.add, [xt, cw, ut], [ut])
        p.stt(uap[:, 0:n - 1], xap[:, 1:n], cw[:, g, 2:3], uap[:, 0:n - 1], ALU.mult, ALU.add, [xt, cw, ut], [ut])

    for g in range(3):
        p.dma("sp", BG[0][0:64, :], x_d[g], [x_d], [BG[0]])
        short_conv(BG[0], BG[0][0:64, :], BG[1], BG[1][0:64, :], g, SEQ)
        p.dma("sp", scr[g], BG[1][0:64, :], [BG[1]], [scr])
    uc = p.sb("uc", [64, 3, CTX])
    xct = p.sb("xct", [64, 3, CTX])
    p.dma("sp", xct[:], xc_d[:].rearrange("g c t -> c g t"), [xc_d], [xct])
    for g in range(3):
        short_conv(xct, xct[:, g, :], uc, uc[:, g, :], g, CTX)

    def wrap_sin(dst_t, dst_ap, ps_ap, ps_t, j, n, arg_t):
        a = arg_t[0:64, 0:n]
        p.ts("dve", a, ps_ap, fb[:, 0:1], ALU.mult, [ps_t, fb, frb], [arg_t], s2=frb[:, j:j + 1], op1=ALU.add)
        w_ = wrp[0:64, 0:n]
        for bound, period in ((3 * PI, 4 * PI), (PI, 2 * PI)):
            p.ts("dve", w_, a, bound, ALU.is_gt, [arg_t], [wrp], s2=-period, op1=ALU.mult)
            p.tt("dve", a, a, w_, ALU.add, [arg_t, wrp], [arg_t])
            p.ts("dve", w_, a, -bound, ALU.is_lt, [arg_t], [wrp], s2=period, op1=ALU.mult)
            p.tt("dve", a, a, w_, ALU.add, [arg_t, wrp], [arg_t])
        p.act(dst_ap, a, AF.Sin, [arg_t], [dst_t])

    argt = p.sb("argt", [64, 512])
    wrp = p.sb("wrp", [64, 512])
    h1c = p.sb("h1c", [64, 512])
    zch = [p.sb("zch%d" % i, [33, 512]) for i in range(2)]

    def mlp(z_dram, n, dst_t, dst_fn):
        for q in range(0, n, 512):
            m = min(512, n - q)
            zt = zch[(q // 512) % 2]
            p.dma("sp", zt[:, 0:m], z_dram[:, q:q + m], [z_dram], [zt])
            p.mm(PA[0][0:64, 0:m], w1[:], zt[:, 0:m], True, True, [w1, zt], [PA[0]])
            wrap_sin(h1c, h1c[:, 0:m], PA[0][0:64, 0:m], PA[0], 0, m, argt)
            p.mm(PA[1][0:64, 0:m], w2[:], h1c[:, 0:m], True, True, [w2, h1c], [PA[1]])
            wrap_sin(dst_t, dst_fn(q, m), PA[1][0:64, 0:m], PA[1], 1, m, argt)

    h2 = BG[1][:].bitcast(BF16)
    mlp(zl_d, SEQ, BG[1], lambda q, m: h2[0:64, q:q + m])
    h2c = p.sb("h2c", [64, CTX])
    mlp(zc_d, CTX, h2c, lambda q, m: h2c[:, q:q + m])

    hfc = p.sb("hfc", [64, 4, CTX])
    for blk in range(4):
        p.mm(PX[0][0:64, 0:CTX], w3f[:, blk, :], h2c[:], True, True, [w3f, h2c], [PX[0]])
        p.tt("dve", hfc[:, blk, :], PX[0][0:64, 0:CTX], wc[:], ALU.mult, [PX[0], wc], [hfc])
    zc1 = p.sb("zc1", [64, CTX])
    yct = p.sb("yct", [64, CTX])

    def ctx_conv(zt, zap, o, gate_ap, out_t, out_ap):
        p.ts("dve", yct[:], zap, hbc[:, o:o + 1], ALU.mult, [zt, hbc], [yct])
        for q in range(CTX):
            p.stt(yct[:, q:CTX], zap[:, 0:CTX - q], hfc[:, 2 * o, q:q + 1], yct[:, q:CTX], ALU.mult, ALU.add, [zt, hfc, yct], [yct])
        for q in range(1, CTX):
            p.stt(yct[:, 0:CTX - q], zap[:, q:CTX], hfc[:, 2 * o + 1, q:q + 1], yct[:, 0:CTX - q], ALU.mult, ALU.add, [zt, hfc, yct], [yct])
        p.tt("dve", out_ap, yct[:], gate_ap, ALU.mult, [yct, uc], [out_t])

    ctx_conv(uc, uc[:, 0, :], 0, uc[:, 1, :], zc1, zc1[:])
    ctx_conv(zc1, zc1[:], 1, uc[:, 2, :], zc1, zc1[:])
    p.dma("sp", yc_d[:], zc1[:], [zc1], [yc_d])

    Af = [p.sb("Af%d" % i, [128, 512]) for i in range(2)]
    tm = [p.sb("tm%d" % i, [128, 512]) for i in range(4)]
    Apr = [p.sb("Apr%d" % i, [128, 4, 128], BF16) for i in range(2)]
    Api = [p.sb("Api%d" % i, [128, 4, 128], BF16) for i in range(2)]
    Xf = [p.sb("Xf%d" % i, [128, 512]) for i in range(4)]
    Yr = [p.sb("Yr%d" % i, [128, 4, 128], BF16) for i in range(2)]
    Yi = [p.sb("Yi%d" % i, [128, 4, 128], BF16) for i in range(2)]
    cnt = {"f": 0, "e": 0}

    def eng():
        cnt["e"] += 1
        return "dve" if cnt["e"] % 2 else "pool"

    def cmul(src_t, sr, si, tw_t, twr, twi, dr_t, dr, di_t, di, t_a, t_b):
        e1, e2 = eng(), eng()
        p.tt(e1, t_a[0], sr, twr, ALU.mult, [src_t, tw_t], [t_a[1]])
        p.tt(e2, t_b[0], si, twi, ALU.mult, [src_t, tw_t], [t_b[1]])
        p.tt(e1, dr, t_a[0], t_b[0], ALU.subtract, [t_a[1], t_b[1]], [dr_t])
        p.tt(e1, t_a[0], sr, twi, ALU.mult, [src_t, tw_t], [t_a[1]])
        p.tt(e2, t_b[0], si, twr, ALU.mult, [src_t, tw_t], [t_b[1]])
        p.tt(e2, di, t_a[0], t_b[0], ALU.add, [t_a[1], t_b[1]], [di_t])

    def stage12(src_t, lhs_fn, rhs_t, rhs0, rhs1, lhs2_fn, TW, dr_t, di_t, src2_t=None):
        b = cnt["f"] % 2
        cnt["f"] += 1
        for hf in range(2):
            ps = (PA if rhs1 is None else PB)[hf]
            for k in range(2):
                c = 2 * hf + k
                cols = slice(k * 256, (k + 1) * 256)
                if rhs1 is None:
                    p.mm(ps[:, cols], lhs_fn(c), rhs0, True, True, [src_t, rhs_t], [ps])
                else:
                    p.mm(ps[:, cols], lhs_fn(c), rhs0, True, False, [src_t, rhs_t], [ps])
                    p.mm(ps[:, cols], lhs2_fn(c), rhs1, False, True, [src2_t, rhs_t], [ps])
            af = Af[hf]
            p.act(af[:], ps[:], AF.Identity, [ps], [af])
            a4 = af[:].rearrange("p (k r j) -> p k r j", k=2, r=2)
            twr = TW[:, 0, :].unsqueeze(1).broadcast_to([128, 2, 128])
            twi = TW[:, 1, :].unsqueeze(1).broadcast_to([128, 2, 128])
            ta = tm[2 * hf][:, 0:256].rearrange("p (k j) -> p k j", k=2)
            tb = tm[2 * hf + 1][:, 0:256].rearrange("p (k j) -> p k j", k=2)
            cmul(af, a4[:, :, 0, :], a4[:, :, 1, :], TW, twr, twi,
                 dr_t, dr_t[:, 2 * hf:2 * hf + 2, :], di_t, di_t[:, 2 * hf:2 * hf + 2, :],
                 (ta, tm[2 * hf]), (tb, tm[2 * hf + 1]))

    def fwd_batch(src_t, lhs_fn):
        b = cnt["f"] % 2
        ar, ai = Apr[b], Api[b]
        stage12(src_t, lhs_fn, F1, F1[:], None, None, TWf, ar, ai)
        arf = ar[:].rearrange("p c j -> p (c j)")
        aif = ai[:].rearrange("p c j -> p (c j)")
        p.mm(PX[0][:], F2[:, 0, :], arf, True, False, [F2, ar], [PX[0]])
        p.mm(PX[0][:], F2[:, 2, :], aif, False, True, [F2, ai], [PX[0]])
        p.mm(PX[1][:], F2[:, 0, :], aif, True, False, [F2, ai], [PX[1]])
        p.mm(PX[1][:], F2[:, 1, :], arf, False, True, [F2, ar], [PX[1]])

    def inv_batch(yr, yi, py):
        b = cnt["f"] % 2
        br, bi = Apr[b], Api[b]
        stage12(yr, lambda c: yr[:, c, :], G2, G2[:, 0, :], G2[:, 1, :], lambda c: yi[:, c, :], TWi, br, bi, src2_t=yi)
        p.mm(py[0:64, :], G1[:, 0, :], br[:].rearrange("p c j -> p (c j)"), True, False, [G1, br], [py])
        p.mm(py[0:64, :], G1[:, 1, :], bi[:].rearrange("p c j -> p (c j)"), False, True, [G1, bi], [py])

    win = BG[0]
    hf_t = BG[2]
    hfv = BG[2][:].bitcast(BF16)[0:64, :].rearrange("p (s c j) -> p s c j", s=2, c=64)
    p.dma("sp", win[0:64, :], wl_d[:].rearrange("p c j -> p (c j)"), [wl_d], [win])
    win3 = win[0:64, :].rearrange("p (c j) -> p c j", j=128)
    scr2 = Tile(p.nc.dram_tensor("scr2", [64, 64, 128], F32, kind="Internal").ap())
    zf = [p.sb("zf%d" % i, [64, 4, 128]) for i in range(2)]
    zb = [p.sb("zb%d" % i, [64, 4, 128], BF16) for i in range(2)]
    xgb = [p.sb("xgb%d" % i, [64, 4, 128]) for i in range(2)]
    for o in range(2):
        for n2 in range(128):
            ps = PY[n2 % 2]
            p.mm(ps[0:64, 0:128], h2[0:64, n2:SEQ:128], w3[:, 2 * o:2 * o + 2, :].rearrange("p s c -> p (s c)"), True, True, [BG[1], w3], [ps])
            p.tt("dve", hfv[:, :, :, n2], ps[0:64, 0:128].rearrange("p (s c) -> p s c", s=2),
                 win3[:, :, n2].unsqueeze(1).broadcast_to([64, 2, 64]), ALU.mult, [ps, win], [hf_t])
        p.op("dve", lambda e: e.memset(hfv[0:1, 1, :, 0], 0.0), [], [hf_t])
        for bt in range(16):
            c0 = 4 * bt
            fwd_batch(hf_t, lambda c: hfv[:, 0, c0 + c, :])
            xr, xi = Xf[0], Xf[1]
            p.act(xr[:], PX[0][:], AF.Identity, [PX[0]], [xr])
            p.act(xi[:], PX[1][:], AF.Identity, [PX[1]], [xi])
            fwd_batch(hf_t, lambda c: hfv[:, 1, c0 + c, :])
            p.tt("dve", Hs[:, 0, c0:c0 + 4, :], xr[:].rearrange("p (c j) -> p c j", c=4),
                 PX[0][:].rearrange("p (c j) -> p c j", c=4), ALU.add, [xr, PX[0]], [Hs])
            p.tt("dve", Hs[:, 1, c0:c0 + 4, :], xi[:].rearrange("p (c j) -> p c j", c=4),
                 PX[1][:].rearrange("p (c j) -> p c j", c=4), ALU.subtract, [xi, PX[1]], [Hs])
        if debug:
            dh = p.dram_out("dbg_hf", [64, 2, 64, 128])
            dH = p.dram_out("dbg_H", [128, 2, 64, 128])
            dh2 = p.dram_out("dbg_h2", [64, SEQ])
            p.dma("pool", dh[:], hfv, [hf_t], [dh])
            p.dma("pool", dH[:], Hs[:], [Hs], [dH])
            p.dma("pool", dh2[:], h2[0:64, 0:SEQ], [BG[1]], [dh2], max_dma_last_dim=2048)
            return p
        for bt in range(16):
            c0 = 4 * bt
            b = bt % 2
            z_, zb_, g_ = zf[b], zb[b], xgb[b]
            if o == 0:
                p.dma("sp", z_[:], scr[0, c0:c0 + 4, :].rearrange("c (p j) -> p c j", j=128), [scr], [z_])
            else:
                p.dma("sp", z_[:], scr2[:, c0:c0 + 4, :], [scr2], [z_])
            p.dma("sp", g_[:], scr[1 + o, c0:c0 + 4, :].rearrange("c (p j) -> p c j", j=128), [scr], [g_])
            p.op("pool", lambda e: e.tensor_copy(out=zb_[:], in_=z_[:]), [z_], [zb_])
            p.tt("pool", z_[:], z_[:], hbr[:, o, c0:c0 + 4].unsqueeze(2).broadcast_to([64, 4, 128]), ALU.mult, [z_, hbr], [z_])
            fwd_batch(zb_, lambda c: zb_[:, c, :])
            xr, xi = Xf[2], Xf[3]
            p.act(xr[:], PX[0][:], AF.Identity, [PX[0]], [xr])
            p.act(xi[:], PX[1][:], AF.Identity, [PX[1]], [xi])
            x4r = xr[:].rearrange("p (c j) -> p c j", c=4)
            x4i = xi[:].rearrange("p (c j) -> p c j", c=4)
            ta = tm[0][:].rearrange("p (c j) -> p c j", c=4)
            tb = tm[1][:].rearrange("p (c j) -> p c j", c=4)
            e1, e2 = eng(), eng()
            p.tt(e1, ta, x4r, Hs[:, 0, c0:c0 + 4, :], ALU.mult, [xr, Hs], [tm[0]])
            p.tt(e2, tb, x4i, Hs[:, 1, c0:c0 + 4, :], ALU.mult, [xi, Hs], [tm[1]])
            p.tt(e1, Yr[b][:], ta, tb, ALU.subtract, [tm[0], tm[1]], [Yr[b]])
            p.tt(e1, ta, x4r, Hs[:, 1, c0:c0 + 4, :], ALU.mult, [xr, Hs], [tm[0]])
            p.tt(e2, tb, x4i, Hs[:, 0, c0:c0 + 4, :], ALU.mult, [xi, Hs], [tm[1]])
            p.tt(e2, Yi[b][:], ta, tb, ALU.add, [tm[0], tm[1]], [Yi[b]])
            py = PY[bt % 2]
            inv_batch(Yr[b], Yi[b], py)
            p.tt("dve", z_[:], py[0:64, :].rearrange("p (c j) -> p c j", c=4), z_[:], ALU.add, [py, z_], [z_])
            p.tt("pool", z_[:], z_[:], g_[:], ALU.mult, [z_, g_], [z_])
            if o == 0:
                p.dma("sp", scr2[:, c0:c0 + 4, :], z_[:], [z_], [scr2])
            else:
                p.dma("sp", yl_d[:, c0:c0 + 4, :], z_[:], [z_], [yl_d])
    return p


def run_l3(lat, ctx, layer, w):
    consts = hy_consts()
    zl, win_l = hy_features(SEQ)
    zc, win_c = hy_features(CTX)
    in_maps = []
    for i in range(NCORES):
        b, cg = divmod(i, 4)
        ch = slice(cg * 64, (cg + 1) * 64)
        cols = [768 + g * 256 + cg * 64 for g in range(3)]
        x = np.stack([lat[b, :, c:c + 64].T for c in cols])
        xc = np.stack([ctx[b, :, c:c + 64].T for c in cols])
        cwf = w["hy_conv_w"][layer].reshape(3, 3, 256)[:, :, ch]
        cbf = w["hy_conv_b"][layer].reshape(3, 256)[:, ch]
        cw = np.concatenate([cwf.transpose(2, 1, 0), cbf.T[:, :, None]], axis=2)
        hb = w["hy_bias"][layer][:, ch]
        w3 = w["filt_w3"][layer].reshape(64, 4, 256)[:, :, ch]
        fb = np.stack([w["filt_freq"][layer], w["filt_b1"][layer], w["filt_b2"][layer]], axis=1)
        wl = win_l[:, ch].reshape(64, 128, 64).transpose(0, 2, 1)
        m = {"x": x, "xc": xc, "cw": cw, "hb": hb, "hbc": hb.T, "fw1": w["filt_w1"][layer], "fw2": w["filt_w2"][layer],
             "fw3": w3, "fb": fb, "zl": zl, "zc": zc, "wl": wl, "wc": win_c[:, ch].T}
        m.update(consts)
        in_maps.append({k: np.ascontiguousarray(v, dtype=np.float32) for k, v in m.items()})
    res = run(build_l3(), in_maps)
    y_lat = np.zeros((2, SEQ, 256), np.float32)
    y_ctx = np.zeros((2, CTX, 256), np.float32)
    for i in range(NCORES):
        b, cg = divmod(i, 4)
        ch = slice(cg * 64, (cg + 1) * 64)
        y_lat[b, :, ch] = res[i]["yl"].transpose(0, 2, 1).reshape(SEQ, 64)
        y_ctx[b, :, ch] = res[i]["yc"].T
    return y_lat, y_ctx


def kernel(**inputs):
    w = {k: np.ascontiguousarray(np.asarray(v, dtype=np.float32)) for k, v in inputs.items()}
    mod = run_ada(w["c"], w["c_ctx"], w["ada_w"], w["ada_b"])
    h_lat, h_ctx = w["x"], w["ctx"]
    for layer in range(DEPTH):
        l1_lat, l1_ctx = run_l1(h_lat, h_ctx, mod, layer, w)
        a_lat, a_ctx = run_l2(l1_lat, l1_ctx)
        hy_lat, hy_ctx = run_l3(l1_lat, l1_ctx, layer, w)
        f_lat, b_lat, f_ctx, b_ctx = run_l4(l1_lat, l1_ctx)
        y_lat = np.concatenate([a_lat, hy_lat, f_lat], axis=-1)
        y_ctx = np.concatenate([a_ctx, hy_ctx, f_ctx], axis=-1)
        h_lat, h_ctx = run_l5a(y_lat, y_ctx, b_lat, b_ctx, l1_lat[:, :, 2048:2304], l1_ctx[:, :, 2048:2304],
                               h_lat, h_ctx, mod, layer, w)
        h_lat, h_ctx = run_l5b(h_lat, h_ctx, mod, layer, w, layer == DEPTH - 1)
    return np.ascontiguousarray(h_lat, dtype=np.float32)


TT = SEQ + CTX
NTT = TT // 128


class Stage:
    def __init__(self, p):
        self.p = p
        self.cms = []

    def sb(self, name, shape, dtype=F32):
        p = self.p
        p.uid = getattr(p, "uid", 0) + 1
        cm = p.nc.sbuf_tensor("%s_%d" % (name, p.uid), list(shape), dtype)
        h = cm.__enter__()
        self.cms.append(cm)
        return Tile(h)

    def close(self):
        p = self.p
        tot = {sid: c for sid, c in p.cnt.items() if c > 0}
        for e in p.eng:
            p._wait(e, dict(tot))
        for cm in reversed(self.cms):
            cm.__exit__(None, None, None)
        self.cms = []


def scratch(p, name, shape):
    return Tile(p.nc.dram_tensor(name, list(shape), F32, kind="Internal").ap())


def emit_ada(p, PS, cT_d, adaw_d, adab_d, MOD):
    st = Stage(p)
    cs = st.sb("cs", [128, 8, 2])
    Wc = [st.sb("Wc%d" % i, [128, 8, 512]) for i in range(3)]
    bb = st.sb("bb", [2, 6 * D])
    res = st.sb("res", [2, 6 * D])
    p.dma("sp", cs[:], cT_d[:], [cT_d], [cs])
    p.act(cs[:], cs[:], AF.Silu, [cs], [cs])
    k = 0
    for l in range(DEPTH):
        p.dma("sp", bb[:], adab_d[l:l + 1, :].partition_broadcast(2)[:, 0, :], [adab_d], [bb])
        for j in range(6 * D // 512):
            W = Wc[k % 3]
            q = "sp" if k % 2 == 0 else "act"
            p.dma(q, W[:], adaw_d[l, :, j * 512:(j + 1) * 512].rearrange("(c p) n -> p c n", p=128), [adaw_d], [W])
            ps = PS[k % 2]
            k += 1
            for c in range(8):
                p.mm(ps[0:2, :], cs[:, c, :], W[:, c, :], c == 0, c == 7, [cs, W], [ps])
            p.tt("dve", res[:, j * 512:(j + 1) * 512], ps[0:2, :], bb[:, j * 512:(j + 1) * 512], ALU.add, [ps, bb], [res])
        p.dma("sp", MOD[l], res[:], [res], [MOD])
    st.close()


def load_modT(p, st, PS, MOD, l, ident):
    raw = st.sb("modraw", [96, 128])
    modT = st.sb("modT", [128, 96])
    p.dma("sp", raw[:], MOD[l].rearrange("r (s p) -> (r s) p", p=128), [MOD], [raw])
    p.tr(PS[7][:, 0:96], raw[:], ident[0:96, 0:96], [raw, ident], [PS[7]])
    p.act(modT[:], PS[7][:, 0:96], AF.Identity, [PS[7]], [modT])
    return modT


def emit_l1(p, PS, l, Hin, S1, MOD, wd, cs_d, ident_d):
    st = Stage(p)
    sb = st.sb
    W = sb("W", [128, 8, IN_W], BF16)
    A = sb("A", [128, 8, 2])
    Brep = sb("Brep", [128, 8, 128], BF16)
    bW = [sb("bW%d" % k, [128, IN_W]) for k in range(2)]
    gains = sb("gains", [128, 10, 64])
    Wblk = sb("Wblk", [32, 256])
    gbb = sb("gbb", [128, 256])
    ident = sb("ident", [128, 128])
    epst = sb("eps", [128, 1])
    ng = sb("ng", [128, 8])
    pss, pst, psg = PS[0:5], PS[5:7], PS[7]
    p.op("dve", lambda e: e.memset(epst[:], EPS), [], [epst])
    p.op("dve", lambda e: e.memset(Wblk[:], 0.0), [], [Wblk])
    p.dma("sp", ident[:], ident_d[:], [ident_d], [ident])
    p.dma("sp", ng[:], wd["norm1_g"][l], [wd["norm1_g"]], [ng])
    p.dma("sp", gbb[:], wd["gla_gate_b"][l:l + 1, :].partition_broadcast(128)[:, 0, :], [wd["gla_gate_b"]], [gbb])
    p.dma("sp", Wblk[0:16, 0:128], wd["gla_gate_w"][l, 0], [wd["gla_gate_w"]], [Wblk])
    p.dma("sp", Wblk[16:32, 128:256], wd["gla_gate_w"][l, 1], [wd["gla_gate_w"]], [Wblk])
    for h in range(10):
        src = wd["qkg"][l:l + 1, 0:64] if h < 8 else wd["qkg"][l:l + 1, 64:128]
        p.dma("sp", gains[:, h, :], src.partition_broadcast(128)[:, 0, :], [wd["qkg"]], [gains])
    p.ts("dve", gains[:, 0:8, :], gains[:, 0:8, :], 0.125, ALU.mult, [gains], [gains])
    for c in range(8):
        p.dma("pool", W[:, c, :], wd["w_in"][l, c * 128:(c + 1) * 128, :], [wd["w_in"]], [W], max_dma_last_dim=4096)
    modT = load_modT(p, st, PS, MOD, l, ident)
    for k in range(2):
        sc = modT[:, k * 48 + 8:k * 48 + 16]
        sh = modT[:, k * 48 + 0:k * 48 + 8]
        p.stt(A[:, :, k], sc, 1.0, ng[:], ALU.add, ALU.mult, [modT, ng], [A])
        p.op("dve", lambda e: e.tensor_copy(out=Brep[:], in_=sh.unsqueeze(2).broadcast_to([128, 8, 128])), [modT], [Brep])
        for gi, (a, b) in enumerate(GROUPS1):
            ps = pss[gi]
            for c in range(8):
                p.mm(ps[:, 0:b - a], Brep[:, c, :], W[:, c, a:b], c == 0, c == 7, [Brep, W], [ps])
            p.act(bW[k][:, a:b], ps[:, 0:b - a], AF.Identity, [ps], [bW[k]])
    NB = 2
    xt = [sb("xt%d" % i, [128, D]) for i in range(NB)]
    xa = [sb("xa%d" % i, [128, 8, 128], BF16) for i in range(NB)]
    cst = [sb("cst%d" % i, [128, 64]) for i in range(NB)]
    O = [sb("O%d" % i, [128, OUT1]) for i in range(NB)]
    junk = sb("junk", [128, D])
    sq = sb("sq", [128, 640])
    tmp = sb("tmp", [128, 10, 32])
    stt_ = [sb("st%d" % i, [128, 16]) for i in range(NB)]
    lrs = [sb("lr%d" % i, [128, 32]) for i in range(NB)]
    lrT = sb("lrT", [32, 128])
    gz = sb("gz", [128, 256])
    ga = sb("ga", [128, 256])
    pic = [0]

    def phaseA(i):
        kind = 0 if i < 64 else 1
        bi = i % NB
        rows = slice(i * 128, (i + 1) * 128)
        p.dma("sp", xt[bi][:], Hin[rows, :], [Hin], [xt[bi]])
        p.dma("sp", cst[bi][:], cs_d[rows, :], [cs_d], [cst[bi]])
        s = stt_[bi]
        p.act(junk[:], xt[bi][:], AF.Square, [xt[bi]], [junk, s], accum_out=s[:, 0:1])
        p.rsqrt(s, s[:, 1:2], s, s[:, 0:1], 1.0 / D, epst)
        for c in range(8):
            pt = pst[c // 4]
            blk = pt[:, (c % 4) * 128:(c % 4 + 1) * 128]
            p.tr(blk, xt[bi][:, c * 128:(c + 1) * 128], ident[:], [xt[bi], ident], [pt])
            p.act(xa[bi][:, c, :], blk, AF.Identity, [pt, A], [xa[bi]], scale=A[:, c, kind:kind + 1])
        o = O[bi]
        for gi, (a, b) in enumerate(GROUPS1):
            ps = pss[pic[0] % 5]
            pic[0] += 1
            for c in range(8):
                p.mm(ps[:, 0:b - a], xa[bi][:, c, :], W[:, c, a:b], c == 0, c == 7, [xa[bi], W], [ps])
            if gi < 4:
                p.stt(o[:, a:b], ps[:, 0:b - a], s[:, 1:2], bW[kind][:, a:b], ALU.mult, ALU.add, [ps, s, bW[kind]], [o])
            else:
                p.stt(o[:, 2048:2304], ps[:, 0:256], s[:, 1:2], bW[kind][:, 2048:2304], ALU.mult, ALU.add, [ps, s, bW[kind]], [o])
                p.stt(lrs[bi][:], ps[:, 256:288], s[:, 1:2], bW[kind][:, 2304:2336], ALU.mult, ALU.add, [ps, s, bW[kind]], [lrs[bi]])

    def phaseB(i):
        bi = i % NB
        rows = slice(i * 128, (i + 1) * 128)
        s = stt_[bi]
        o = O[bi]
        lr = lrs[bi]
        qk = o[:, 0:640]
        p.tt("pool", sq[:], qk, qk, ALU.mult, [o], [sq])
        p.op("dve", lambda e: e.tensor_reduce(out=s[:, 2:12], in_=sq[:].rearrange("p (h d) -> p h d", d=64), axis=AX.X, op=ALU.add), [sq], [s])
        p.rsqrt(s, s[:, 2:12], s, s[:, 2:12], 1.0 / 64, epst)
        qk3 = qk.rearrange("p (h d) -> p h d", d=64)
        p.tt("dve", qk3, qk3, s[:, 2:12].unsqueeze(2).broadcast_to([128, 10, 64]), ALU.mult, [o, s], [o])
        p.tt("pool", qk3, qk3, gains[:], ALU.mult, [o, gains], [o])
        x1 = qk3[:, :, 0:32]
        x2 = qk3[:, :, 32:64]
        cb = cst[bi][:, 0:32].unsqueeze(1).broadcast_to([128, 10, 32])
        sb_ = cst[bi][:, 32:64].unsqueeze(1).broadcast_to([128, 10, 32])
        t3 = sq[:, 0:320].rearrange("p (h d) -> p h d", d=32)
        p.tt("dve", tmp[:], x2, sb_, ALU.mult, [o, cst[bi]], [tmp])
        p.tt("pool", t3, x1, sb_, ALU.mult, [o, cst[bi]], [sq])
        p.tt("dve", x1, x1, cb, ALU.mult, [o, cst[bi]], [o])
        p.tt("dve", x1, x1, tmp[:], ALU.subtract, [o, tmp], [o])
        p.tt("dve", x2, x2, cb, ALU.mult, [o, cst[bi]], [o])
        p.tt("dve", x2, x2, t3, ALU.add, [o, sq], [o])
        p.act(o[:, 1536:1664], o[:, 1536:1664], AF.Identity, [o], [o], scale=32 ** -0.5)
        p.act(o[:, 2048:2304], o[:, 2048:2304], AF.Silu, [o], [o])
        p.tr(psg[0:32, 0:128], lr[:], ident[:], [lr, ident], [psg])
        p.act(lrT[:], psg[0:32, 0:128], AF.Identity, [psg], [lrT])
        p.mm(psg[:, 128:384], lrT[:], Wblk[:], True, True, [lrT, Wblk], [psg])
        p.tt("dve", gz[:], psg[:, 128:384], gbb[:], ALU.add, [psg, gbb], [gz])
        p.stt(ga[:], gz[:], -1.0, gz[:], ALU.mult, ALU.min, [gz], [ga])
        p.act(ga[:], ga[:], AF.Exp, [ga], [ga])
        p.act(ga[:], ga[:], AF.Ln, [ga], [ga], bias=1.0)
        p.ts("dve", gz[:], gz[:], 0.0, ALU.min, [gz], [gz], s2=1.0 / 16, op1=ALU.mult)
        p.stt(o[:, 2304:2560], ga[:], -1.0 / 16, gz[:], ALU.mult, ALU.add, [ga, gz], [o])
        p.dma("sp", S1[rows, :], o[:], [o], [S1])

    phaseA(0)
    for i in range(NTT):
        if i + 1 < NTT:
            phaseA(i + 1)
        phaseB(i)
    st.close()


def emit_l2(p, PS, S1, SY, ident_d, PS2):
    st = Stage(p)
    sb = st.sb
    ident = sb("ident", [128, 128])
    p.dma("sp", ident[:], ident_d[:], [ident_d], [ident])
    kT2 = sb("kT2", [128, TT], BF16)
    vA = [sb("vA%d" % g, [128, NTT, 65], BF16) for g in range(2)]
    kin = [sb("kin%d" % i, [128, 128]) for i in range(2)]
    qin = [sb("qin%d" % i, [128, 512]) for i in range(2)]
    q2 = [sb("q2_%d" % i, [128, 512], BF16) for i in range(2)]
    qrs = [sb("qr%d" % i, [128, 512]) for i in range(2)]
    psS = PS2[0:3]
    psO = PS2[3]
    for g in range(2):
        p.op("dve", lambda e: e.memset(vA[g][:, :, 64:65], 1.0), [], [vA[g]])
        p.dma("pool", vA[g][:, :, 0:64], S1[:, 640 + g * 64:704 + g * 64].rearrange("(t p) c -> p t c", p=128), [S1], [vA[g]])
    for t in range(NTT):
        ki = kin[t % 2]
        p.dma("sp", ki[:], S1[t * 128:(t + 1) * 128, 512:640], [S1], [ki])
        pt = psS[t % 3]
        p.tr(pt[:, 0:128], ki[:], ident[:], [ki, ident], [pt])
        p.act(kT2[:, t * 128:(t + 1) * 128], pt[:, 0:128], AF.Identity, [pt], [kT2])
    LA = 2
    pT = [sb("pT%d" % i, [128, 1024], BF16) for i in range(LA + 2)]
    oT = [sb("oT%d" % i, [65, 1024]) for i in range(2)]
    ot = [sb("ot%d" % i, [128, 512]) for i in range(2)]
    rec = sb("rec", [128, 16])
    it = 0
    for qt in range(NTT):
        kts = list(range(NTT)) if qt < 64 else [64, 65]
        qi = qin[qt % 2]
        p.dma("sp", qi[:], S1[qt * 128:(qt + 1) * 128, 0:512], [S1], [qi])
        o_t = ot[qt % 2]
        q_ = q2[qt % 2]
        pt = psS[it % 3]
        qr = qrs[qt % 2]
        p.op("pool", lambda e: e.tensor_copy(out=qr[:].rearrange("p (h g d) -> p h g d", h=4, g=2),
                                             in_=qi[:].rearrange("p (g h d) -> p h g d", g=2, h=4)), [qi], [qr])
        for h in range(4):
            p.tr(pt[:, h * 128:(h + 1) * 128], qr[:, h * 128:(h + 1) * 128], ident[:], [qr, ident], [pt])
        p.act(q_[:], pt[:, 0:512], AF.Identity, [pt], [q_])

        def pv(pend, last):
            k0, kt, ptile = pend
            for g in range(2):
                p.mm(psO[0:65, g * 512:(g + 1) * 512], vA[g][:, kt, :], ptile[:, g * 512:(g + 1) * 512], k0 == 0, last,
                     [vA[g], ptile], [psO])

        pend = []
        for k0, kt in enumerate(kts):
            ps = psS[it % 3]
            pt_ = pT[it % (LA + 2)]
            it += 1
            for g in range(2):
                rows = slice(g * 64, (g + 1) * 64)
                p.mm(ps[:, g * 512:(g + 1) * 512], kT2[rows, kt * 128:(kt + 1) * 128], q_[rows, :], True, True, [kT2, q_], [ps])
            p.act(pt_[:], ps[:], AF.Exp, [ps], [pt_])
            pend.append((k0, kt, pt_))
            if len(pend) > LA:
                pv(pend.pop(0), False)
        while pend:
            x_ = pend.pop(0)
            pv(x_, len(pend) == 0)
        o_sb = oT[qt % 2]
        p.op("dve", lambda e: e.tensor_copy(out=o_sb[:], in_=psO[0:65, :]), [psO], [o_sb])
        for hh in range(8):
            pb = psS[(it + 1 + hh % 2) % 3]
            ptt = pb[:, (hh // 2 % 2) * 512:(hh // 2 % 2) * 512 + 65]
            p.tr(ptt, o_sb[:, hh * 128:(hh + 1) * 128], ident[0:65, 0:65], [o_sb, ident], [pb])
            rc = rec[:, hh + (qt % 2) * 8:hh + (qt % 2) * 8 + 1]
            p.op("dve", lambda e: e.reciprocal(out=rc, in_=ptt[:, 64:65]), [pb], [rec])
            p.ts("dve", o_t[:, hh * 64:(hh + 1) * 64], ptt[:, 0:64], rc, ALU.mult, [pb, rec], [o_t])
        p.dma("sp", SY[qt * 128:(qt + 1) * 128, :], o_t[:], [o_t], [SY])
    st.close()


def emit_l3(p, PS, l, S1, YL, YC, wd, cd, scr, scr2):
    st = Stage(p)
    sb = st.sb
    BGa = sb("BGa", [128, SEQ])
    BGb = sb("BGb", [128, SEQ])
    H2 = sb("H2", [64, SEQ], BF16)
    Hs = sb("Hs", [128, 2, 64, 128], BF16)
    F1 = sb("F1", [64, 256], BF16)
    TWf = sb("TWf", [128, 2, 128])
    F2 = sb("F2", [128, 3, 128], BF16)
    G2 = sb("G2", [128, 2, 256], BF16)
    TWi = sb("TWi", [128, 2, 128])
    G1 = sb("G1", [128, 2, 64], BF16)
    ident = sb("ident", [128, 128])
    for nm, t in (("F1", F1), ("F2", F2), ("G2", G2), ("G1", G1)):
        p.dma("pool", t[:], cd[nm][:], [cd[nm]], [t])
    p.dma("sp", TWf[:], cd["TWf"][:], [cd["TWf"]], [TWf])
    p.dma("sp", TWi[:], cd["TWi"][:], [cd["TWi"]], [TWi])
    p.dma("sp", ident[:], cd["ident"][:], [cd["ident"]], [ident])
    w1 = sb("fw1", [33, 64])
    w2 = sb("fw2", [64, 64])
    fb = sb("fb", [64, 3])
    frb = sb("frb", [64, 2])
    p.dma("sp", w1[:], wd["filt_w1"][l], [wd["filt_w1"]], [w1])
    p.dma("sp", w2[:], wd["filt_w2"][l], [wd["filt_w2"]], [w2])
    p.dma("sp", fb[:], wd["fb"][l], [wd["fb"]], [fb])
    for j in range(2):
        p.tt("dve", frb[:, j:j + 1], fb[:, 0:1], fb[:, 1 + j:2 + j], ALU.mult, [fb], [frb])
    PA, PX, PB, PY = PS[0:2], PS[2:4], PS[4:6], PS[6:8]
    argt = sb("argt", [64, 512])
    wrp = sb("wrp", [64, 512])
    h1c = sb("h1c", [64, 512])
    zch = [sb("zch%d" % i, [33, 512]) for i in range(2)]
    h2c = sb("h2c", [64, CTX])

    def wrap_sin(dst_t, dst_ap, ps_ap, ps_t, j, n):
        a = argt[0:64, 0:n]
        p.ts("dve", a, ps_ap, fb[:, 0:1], ALU.mult, [ps_t, fb, frb], [argt], s2=frb[:, j:j + 1], op1=ALU.add)
        w_ = wrp[0:64, 0:n]
        for bound, period in ((3 * PI, 4 * PI), (PI, 2 * PI)):
            p.ts("dve", w_, a, bound, ALU.is_gt, [argt], [wrp], s2=-period, op1=ALU.mult)
            p.tt("dve", a, a, w_, ALU.add, [argt, wrp], [argt])
            p.ts("dve", w_, a, -bound, ALU.is_lt, [argt], [wrp], s2=period, op1=ALU.mult)
            p.tt("dve", a, a, w_, ALU.add, [argt, wrp], [argt])
        p.act(dst_ap, a, AF.Sin, [argt], [dst_t])

    def mlp(z_dram, n, dst_t):
        for q in range(0, n, 512):
            m = min(512, n - q)
            zt = zch[(q // 512) % 2]
            p.dma("sp", zt[:, 0:m], z_dram[:, q:q + m], [z_dram], [zt])
            p.mm(PA[0][0:64, 0:m], w1[:], zt[:, 0:m], True, True, [w1, zt], [PA[0]])
            wrap_sin(h1c, h1c[:, 0:m], PA[0][0:64, 0:m], PA[0], 0, m)
            p.mm(PA[1][0:64, 0:m], w2[:], h1c[:, 0:m], True, True, [w2, h1c], [PA[1]])
            wrap_sin(dst_t, dst_t[:, q:q + m], PA[1][0:64, 0:m], PA[1], 1, m)

    mlp(cd["zl"], SEQ, H2)
    mlp(cd["zc"], CTX, h2c)

    cw = sb("cw", [64, 3, 4])
    hbr = sb("hbr", [64, 2, 64])
    hbc = sb("hbc", [64, 2])
    w3 = sb("fw3", [64, 4, 64], BF16)
    w3f = sb("fw3f", [64, 4, 64])
    wc = sb("wc", [64, CTX])
    uc = sb("uc", [64, 3, CTX])
    xct = sb("xct", [64, 3, CTX])
    hfc = sb("hfc", [64, 4, CTX])
    zc1 = sb("zc1", [64, CTX])
    yct = sb("yct", [64, CTX])
    yct2 = sb("yct2", [64, CTX])
    xin = [sb("xin%d" % i, [128, 64]) for i in range(3)]
    Af = [sb("Af%d" % i, [128, 512]) for i in range(2)]
    tm = [sb("tm%d" % i, [128, 512]) for i in range(4)]
    tms = [[sb("tms%d_%d" % (a, b), [128, 256]) for b in range(4)] for a in range(4)]
    Apr = [sb("Apr%d" % i, [128, 4, 128], BF16) for i in range(2)]
    Api = [sb("Api%d" % i, [128, 4, 128], BF16) for i in range(2)]
    Xf = [sb("Xf%d" % i, [128, 512]) for i in range(4)]
    Yr = [sb("Yr%d" % i, [128, 4, 128], BF16) for i in range(2)]
    Yi = [sb("Yi%d" % i, [128, 4, 128], BF16) for i in range(2)]
    zf = [sb("zf%d" % i, [64, 4, 128]) for i in range(2)]
    zb = [sb("zb%d" % i, [64, 4, 128], BF16) for i in range(2)]
    xgb = [sb("xgb%d" % i, [64, 4, 128]) for i in range(2)]
    cnt = {"f": 0, "e": 0, "x": 0}

    def eng():
        cnt["e"] += 1
        return "dve" if cnt["e"] % 2 else "pool"

    def stage12(src_t, lhs_fn, rhs_t, rhs0, rhs1, lhs2_fn, TW, dr_t, di_t, src2_t=None):
        cnt["f"] += 1
        twr = TW[:, 0, :].unsqueeze(1).broadcast_to([128, 2, 128])
        twi = TW[:, 1, :].unsqueeze(1).broadcast_to([128, 2, 128])
        v = []
        for hf in range(2):
            ps = (PA if rhs1 is None else PB)[hf]
            for k in range(2):
                c = 2 * hf + k
                cols = slice(k * 256, (k + 1) * 256)
                if rhs1 is None:
                    p.mm(ps[:, cols], lhs_fn(c), rhs0, True, True, [src_t, rhs_t], [ps])
                else:
                    p.mm(ps[:, cols], lhs_fn(c), rhs0, True, False, [src_t, rhs_t], [ps])
                    p.mm(ps[:, cols], lhs2_fn(c), rhs1, False, True, [src2_t, rhs_t], [ps])
            af = Af[hf]
            p.act(af[:], ps[:], AF.Identity, [ps], [af])
            a4 = af[:].rearrange("p (k r j) -> p k r j", k=2, r=2)
            tset = tms[(cnt["f"] % 2) * 2 + hf]
            T4 = [(t_[:].rearrange("p (k j) -> p k j", k=2), t_) for t_ in tset]
            v.append((af, a4[:, :, 0, :], a4[:, :, 1, :], T4, dr_t[:, 2 * hf:2 * hf + 2, :], di_t[:, 2 * hf:2 * hf + 2, :]))
        af0, sr0, si0, T40, dr0, di0 = v[0]
        af1, sr1, si1, T41, dr1, di1 = v[1]
        (a1_, A1), (b1_, B1), (a2_, A2), (b2_, B2) = T40
        (c1_, C1), (d1_, D1), (c2_, C2), (d2_, D2) = T41
        p.tt("dve", a1_, sr0, twr, ALU.mult, [af0, TW], [A1])
        p.tt("dve", b1_, si0, twi, ALU.mult, [af0, TW], [B1])
        p.tt("dve", a2_, sr0, twi, ALU.mult, [af0, TW], [A2])
        p.tt("dve", b2_, si0, twr, ALU.mult, [af0, TW], [B2])
        p.tt("dve", c1_, sr1, twr, ALU.mult, [af1, TW], [C1])
        p.tt("dve", d1_, si1, twi, ALU.mult, [af1, TW], [D1])
        p.tt("pool", c2_, sr1, twi, ALU.mult, [af1, TW], [C2])
        p.tt("pool", d2_, si1, twr, ALU.mult, [af1, TW], [D2])
        p.tt("dve", dr0, a1_, b1_, ALU.subtract, [A1, B1], [dr_t])
        p.tt("dve", di0, a2_, b2_, ALU.add, [A2, B2], [di_t])
        p.tt("dve", dr1, c1_, d1_, ALU.subtract, [C1, D1], [dr_t])
        p.tt("pool", di1, c2_, d2_, ALU.add, [C2, D2], [di_t])

    def fwd_batch(src_t, lhs_fn):
        b = cnt["f"] % 2
        ar, ai = Apr[b], Api[b]
        stage12(src_t, lhs_fn, F1, F1[:], None, None, TWf, ar, ai)
        arf = ar[:].rearrange("p c j -> p (c j)")
        aif = ai[:].rearrange("p c j -> p (c j)")
        p.mm(PX[0][:], F2[:, 0, :], arf, True, False, [F2, ar], [PX[0]])
        p.mm(PX[0][:], F2[:, 2, :], aif, False, True, [F2, ai], [PX[0]])
        p.mm(PX[1][:], F2[:, 0, :], aif, True, False, [F2, ai], [PX[1]])
        p.mm(PX[1][:], F2[:, 1, :], arf, False, True, [F2, ar], [PX[1]])

    def inv_batch(yr, yi, py):
        b = cnt["f"] % 2
        br, bi = Apr[b], Api[b]
        stage12(yr, lambda c: yr[:, c, :], G2, G2[:, 0, :], G2[:, 1, :], lambda c: yi[:, c, :], TWi, br, bi, src2_t=yi)
        p.mm(py[0:64, :], G1[:, 0, :], br[:].rearrange("p c j -> p (c j)"), True, False, [G1, br], [py])
        p.mm(py[0:64, :], G1[:, 1, :], bi[:].rearrange("p c j -> p (c j)"), False, True, [G1, bi], [py])

    def short_conv(xt, xap, ut, uap, g, n):
        p.act(uap, xap, AF.Identity, [xt, cw], [ut], scale=cw[:, g, 1:2], bias=cw[:, g, 3:4])
        p.stt(uap[:, 1:n], xap[:, 0:n - 1], cw[:, g, 0:1], uap[:, 1:n], ALU.mult, ALU.add, [xt, cw, ut], [ut])
        p.stt(uap[:, 0:n - 1], xap[:, 1:n], cw[:, g, 2:3], uap[:, 0:n - 1], ALU.mult, ALU.add, [xt, cw, ut], [ut])

    def ctx_conv(zt, zap, o, gate_ap, out_t, out_ap):
        p.ts("dve", yct[:], zap, hbc[:, o:o + 1], ALU.mult, [zt, hbc], [yct])
        p.op("dve", lambda e: e.memset(yct2[:], 0.0), [], [yct2])
        accs = (yct, yct2)
        k = 0
        for q in range(CTX):
            a_ = accs[k % 2]
            k += 1
            p.stt(a_[:, q:CTX], zap[:, 0:CTX - q], hfc[:, 2 * o, q:q + 1], a_[:, q:CTX], ALU.mult, ALU.add, [zt, hfc, a_], [a_])
        for q in range(1, CTX):
            a_ = accs[k % 2]
            k += 1
            p.stt(a_[:, 0:CTX - q], zap[:, q:CTX], hfc[:, 2 * o + 1, q:q + 1], a_[:, 0:CTX - q], ALU.mult, ALU.add, [zt, hfc, a_], [a_])
        p.tt("dve", yct[:], yct[:], yct2[:], ALU.add, [yct, yct2], [yct])
        p.tt("dve", out_ap, yct[:], gate_ap, ALU.mult, [yct, uc], [out_t])

    hfv = BGb[:].bitcast(BF16)[0:64, :].rearrange("p (s c j) -> p s c j", s=2, c=64)
    win3 = BGa[0:64, :].rearrange("p (c j) -> p c j", j=128)
    for cg in range(4):
        p.dma("sp", cw[:], wd["cw"][l, cg], [wd["cw"]], [cw])
        p.dma("sp", hbc[:], wd["hbc"][l, cg], [wd["hbc"]], [hbc])
        for o in range(2):
            p.dma("sp", hbr[:, o, :], wd["hb"][l, cg, o:o + 1, :].partition_broadcast(64)[:, 0, :], [wd["hb"]], [hbr])
        p.dma("sp", w3f[:], wd["fw3"][l, cg], [wd["fw3"]], [w3f])
        p.dma("pool", w3[:], wd["fw3"][l, cg], [wd["fw3"]], [w3])
        p.dma("sp", wc[:], cd["wc"][cg], [cd["wc"]], [wc])
        for g in range(3):
            c0 = 768 + g * 256 + cg * 64
            for t in range(NTT):
                xi = xin[cnt["x"] % 3]
                cnt["x"] += 1
                p.dma("sp", xi[:], S1[t * 128:(t + 1) * 128, c0:c0 + 64], [S1], [xi])
                pt = PA[(t // 4) % 2]
                p.tr(pt[0:64, (t % 4) * 128:(t % 4 + 1) * 128], xi[:], ident[:], [xi, ident], [pt])
                if t % 4 == 3 and t < 64:
                    p.act(BGa[0:64, (t - 3) * 128:(t + 1) * 128], pt[0:64, :], AF.Identity, [pt], [BGa])
                if t == 65:
                    p.act(xct[:, g, :], pt[0:64, 0:256], AF.Identity, [pt], [xct])
            short_conv(BGa, BGa[0:64, :], BGb, BGb[0:64, :], g, SEQ)
            p.dma("sp", scr[g], BGb[0:64, :], [BGb], [scr])
            short_conv(xct, xct[:, g, :], uc, uc[:, g, :], g, CTX)
        for blk in range(4):
            p.mm(PX[0][0:64, 0:CTX], w3f[:, blk, :], h2c[:], True, True, [w3f, h2c], [PX[0]])
            p.tt("dve", hfc[:, blk, :], PX[0][0:64, 0:CTX], wc[:], ALU.mult, [PX[0], wc], [hfc])
        ctx_conv(uc, uc[:, 0, :], 0, uc[:, 1, :], zc1, zc1[:])
        ctx_conv(zc1, zc1[:], 1, uc[:, 2, :], zc1, zc1[:])
        p.dma("sp", YC[cg], zc1[:], [zc1], [YC])
        p.dma("sp", BGa[0:64, :], cd["wl"][cg].rearrange("p c j -> p (c j)"), [cd["wl"]], [BGa])
        for o in range(2):
            for n2 in range(128):
                ps = PY[n2 % 2]
                p.mm(ps[0:64, 0:128], H2[:, n2:SEQ:128], w3[:, 2 * o:2 * o + 2, :].rearrange("p s c -> p (s c)"), True, True, [H2, w3], [ps])
                p.tt("dve", hfv[:, :, :, n2], ps[0:64, 0:128].rearrange("p (s c) -> p s c", s=2),
                     win3[:, :, n2].unsqueeze(1).broadcast_to([64, 2, 64]), ALU.mult, [ps, BGa], [BGb])
            p.op("dve", lambda e: e.memset(hfv[0:1, 1, :, 0], 0.0), [], [BGb])
            for bt in range(16):
                c0 = 4 * bt
                fwd_batch(BGb, lambda c: hfv[:, 0, c0 + c, :])
                xr, xi_ = Xf[0], Xf[1]
                p.act(xr[:], PX[0][:], AF.Identity, [PX[0]], [xr])
                p.act(xi_[:], PX[1][:], AF.Identity, [PX[1]], [xi_])
                fwd_batch(BGb, lambda c: hfv[:, 1, c0 + c, :])
                p.tt("dve", Hs[:, 0, c0:c0 + 4, :], xr[:].rearrange("p (c j) -> p c j", c=4),
                     PX[0][:].rearrange("p (c j) -> p c j", c=4), ALU.add, [xr, PX[0]], [Hs])
                p.tt("dve", Hs[:, 1, c0:c0 + 4, :], xi_[:].rearrange("p (c j) -> p c j", c=4),
                     PX[1][:].rearrange("p (c j) -> p c j", c=4), ALU.subtract, [xi_, PX[1]], [Hs])
            for bt in range(16):
                c0 = 4 * bt
                b = bt % 2
                z_, zb_, g_ = zf[b], zb[b], xgb[b]
                if o == 0:
                    p.dma("sp", z_[:], scr[0, c0:c0 + 4, :].rearrange("c (p j) -> p c j", j=128), [scr], [z_])
                else:
                    p.dma("sp", z_[:], scr2[:, c0:c0 + 4, :], [scr2], [z_])
                p.dma("sp", g_[:], scr[1 + o, c0:c0 + 4, :].rearrange("c (p j) -> p c j", j=128), [scr], [g_])
                p.op("pool", lambda e: e.tensor_copy(out=zb_[:], in_=z_[:]), [z_], [zb_])
                p.tt("pool", z_[:], z_[:], hbr[:, o, c0:c0 + 4].unsqueeze(2).broadcast_to([64, 4, 128]), ALU.mult, [z_, hbr], [z_])
                fwd_batch(zb_, lambda c: zb_[:, c, :])
                xr, xi_ = Xf[2], Xf[3]
                p.act(xr[:], PX[0][:], AF.Identity, [PX[0]], [xr])
                p.act(xi_[:], PX[1][:], AF.Identity, [PX[1]], [xi_])
                x4r = xr[:].rearrange("p (c j) -> p c j", c=4)
                x4i = xi_[:].rearrange("p (c j) -> p c j", c=4)
                ta = tm[0][:].rearrange("p (c j) -> p c j", c=4)
                tb = tm[1][:].rearrange("p (c j) -> p c j", c=4)
                tc_ = tm[2][:].rearrange("p (c j) -> p c j", c=4)
                td_ = tm[3][:].rearrange("p (c j) -> p c j", c=4)
                p.tt("dve", ta, x4r, Hs[:, 0, c0:c0 + 4, :], ALU.mult, [xr, Hs], [tm[0]])
                p.tt("dve", tb, x4i, Hs[:, 1, c0:c0 + 4, :], ALU.mult, [xi_, Hs], [tm[1]])
                p.tt("dve", Yr[b][:], ta, tb, ALU.subtract, [tm[0], tm[1]], [Yr[b]])
                p.tt("pool", tc_, x4r, Hs[:, 1, c0:c0 + 4, :], ALU.mult, [xr, Hs], [tm[2]])
                p.tt("pool", td_, x4i, Hs[:, 0, c0:c0 + 4, :], ALU.mult, [xi_, Hs], [tm[3]])
                p.tt("pool", Yi[b][:], tc_, td_, ALU.add, [tm[2], tm[3]], [Yi[b]])
                py = PY[bt % 2]
                inv_batch(Yr[b], Yi[b], py)
                p.tt("dve", z_[:], py[0:64, :].rearrange("p (c j) -> p c j", c=4), z_[:], ALU.add, [py, z_], [z_])
                p.tt("pool", z_[:], z_[:], g_[:], ALU.mult, [z_, g_], [z_])
                if o == 0:
                    p.dma("sp", scr2[:, c0:c0 + 4, :], z_[:], [z_], [scr2])
                else:
                    p.dma("sp", YL[cg, :, c0:c0 + 4, :], z_[:], [z_], [YL])
    st.close()


def emit_l4(p, PS, S1, GO, cd):
    st = Stage(p)
    sb = st.sb
    ident = sb("ident", [128, 128])
    mask = sb("mask", [128, 128])
    Jm = sb("Jm", [128, 128], BF16)
    pm = sb("pm", [128, 2])
    ones = sb("ones", [32, 1])
    p.dma("sp", ident[:], cd["ident"][:], [cd["ident"]], [ident])
    p.dma("sp", mask[:], cd["mask"][:], [cd["mask"]], [mask])
    p.dma("pool", Jm[:], cd["J"][:], [cd["J"]], [Jm])
    p.dma("sp", pm[:], cd["pm"][:], [cd["pm"]], [pm])
    p.op("dve", lambda e: e.memset(ones[:], 1.0), [], [ones])
    qd = sb("qd", [32, TG], BF16)
    kd = sb("kd", [32, TG], BF16)
    ktT = [sb("ktT%d" % i, [128, NPR, 32], BF16) for i in range(2)]
    vnat = sb("vnat", [128, NPR, 64], BF16)
    vt = sb("vt", [128, NPR, 64], BF16)
    STm = sb("STm", [128, NPR, 128], BF16)
    Sbf = sb("Sbf", [32, NCH + 1, 64], BF16)
    dc = sb("dc", [32, NCH])
    oT = sb("oT", [64, TG])
    Scur = [sb("Scur%d" % i, [32, 64]) for i in range(2)]
    seg3 = sb("seg3", [32, 3, SEGL])
    gin = [sb("gin%d" % i, [128, 3, 32]) for i in range(3)]
    Gs = [sb("Gs%d" % i, [32, SEGL]) for i in range(2)]
    At = sb("At", [32, SEGL])
    Bt = sb("Bt", [32, SEGL])
    Sc = sb("Sc", [32, 22])
    tmpc = sb("tmpc", [32, 22])
    psT, psS, psK, psO = PS[0:2], PS[2:4], PS[4:6], PS[6:8]
    ng = 0
    for hd in range(4):
        p.dma("pool", vnat[:], S1[:, 1792 + hd * 64:1856 + hd * 64].rearrange("(t p) c -> p t c", p=128), [S1], [vnat], max_dma_last_dim=4096)
        for d in range(2):
            def gtile(j):
                return (64 + j if j < 2 else j - 2) if d == 0 else 65 - j
            if d == 0:
                p.op("pool", lambda e: e.tensor_copy(out=vt[:, 0:2, :], in_=vnat[:, 64:66, :]), [vnat], [vt])
                p.op("pool", lambda e: e.tensor_copy(out=vt[:, 2:66, :], in_=vnat[:, 0:64, :]), [vnat], [vt])
            else:
                for j0 in range(0, NPR, 8):
                    n = min(8, NPR - j0)
                    ps = psK[(j0 // 8) % 2]
                    for j in range(n):
                        p.mm(ps[:, j * 64:(j + 1) * 64], Jm[:], vnat[:, gtile(j0 + j), :], True, True, [Jm, vnat], [ps])
                    p.act(vt[:, j0:j0 + n, :], ps[:, 0:n * 64].rearrange("p (a b) -> p a b", b=64), AF.Identity, [ps], [vt])
            gcol = (2304 if d == 0 else 2432) + hd * 32
            for s in range(NSEG):
                seg = slice(s * SEGL, (s + 1) * SEGL)
                for jj in range(11):
                    j = s * 11 + jj
                    gt = gtile(j)
                    gi = gin[ng % 3]
                    ps = psT[ng % 2]
                    ng += 1
                    rows = slice(gt * 128, (gt + 1) * 128)
                    p.dma("sp", gi[:, 0:2, :], S1[rows, 1536 + hd * 32:1536 + hd * 32 + 256].rearrange("p (a c) -> p a c", a=2)[:, :, 0:32], [S1], [gi])
                    p.dma("sp", gi[:, 2, :], S1[rows, gcol:gcol + 32], [S1], [gi])
                    for a in range(3):
                        p.tr(ps[0:32, a * 128:(a + 1) * 128], gi[:, a, :], ident[:], [gi, ident], [ps])
                    dst = seg3[:, :, jj * 128:(jj + 1) * 128]
                    if d == 1:
                        dst = dst[:, :, ::-1]
                    p.act(dst, ps[0:32, 0:384].rearrange("p (a j) -> p a j", a=3), AF.Identity, [ps], [seg3])
                qseg, kseg, gseg = seg3[:, 0, :], seg3[:, 1, :], seg3[:, 2, :]
                G = Gs[s % 2]
                Gp = Gs[(s + 1) % 2]
                init = 0.0 if s == 0 else Gp[:, SEGL - 1:SEGL]
                rd = [seg3, ones] + ([] if s == 0 else [Gp])
                p.op("dve", lambda e: e.tensor_tensor_scan(out=G[:], data0=ones[:, 0:1].broadcast_to([32, SEGL]), data1=gseg,
                                                           initial=init, op0=ALU.mult, op1=ALU.add), rd, [G])
                G3 = G[:].rearrange("p (c j) -> p c j", j=64)
                Ec = G3[:, :, 63]
                if s == 0:
                    p.op("dve", lambda e: e.memset(Sc[:, 0:1], 0.0), [], [Sc])
                else:
                    p.op("dve", lambda e: e.tensor_copy(out=Sc[:, 0:1], in_=Gp[:, SEGL - 1:SEGL]), [Gp], [Sc])
                p.op("dve", lambda e: e.tensor_copy(out=Sc[:, 1:22], in_=G3[:, 0:21, 63]), [G], [Sc])
                A3 = At[:].rearrange("p (c j) -> p c j", j=64)
                p.tt("dve", A3, G3, Sc[:].unsqueeze(2).broadcast_to([32, 22, 64]), ALU.subtract, [G, Sc], [At])
                p.act(Bt[:], At[:], AF.Exp, [At], [Bt])
                p.tt("dve", qd[:, seg], qseg, Bt[:], ALU.mult, [seg3, Bt], [qd])
                p.act(Bt[:], At[:], AF.Exp, [At], [Bt], scale=-1.0)
                p.tt("dve", kd[:, seg], kseg, Bt[:], ALU.mult, [seg3, Bt], [kd])
                p.tt("dve", tmpc[:], Ec, Sc[:], ALU.subtract, [G, Sc], [tmpc])
                p.act(dc[:, s * 22:(s + 1) * 22], tmpc[:], AF.Exp, [tmpc], [dc])
                p.tt("dve", A3, Ec.unsqueeze(2).broadcast_to([32, 22, 64]), G3, ALU.subtract, [G], [At])
                p.act(At[:], At[:], AF.Exp, [At], [At])
                p.tt("dve", Bt[:], kseg, At[:], ALU.mult, [seg3, At], [Bt])
                ps = psS[s % 2]
                for j in range(11):
                    p.tr(ps[:, j * 32:(j + 1) * 32], Bt[:, j * 128:(j + 1) * 128], ident[0:32, 0:32], [Bt, ident], [ps])
                for hf in range(2):
                    p.act(ktT[hf][:, s * 11:(s + 1) * 11, :], ps[:, 0:352].rearrange("p (a b) -> p a b", b=32), AF.Identity,
                          [ps, pm], [ktT[hf]], scale=pm[:, hf:hf + 1])
            for g0 in range(0, NPR, 4):
                n = min(4, NPR - g0)
                ps = psS[(g0 // 4) % 2]
                for j in range(n):
                    tok = slice((g0 + j) * 128, (g0 + j + 1) * 128)
                    p.mm(ps[:, j * 128:(j + 1) * 128], kd[:, tok], qd[:, tok], True, True, [kd, qd], [ps])
                p.tt("dve", STm[:, g0:g0 + n, :], ps[:, 0:n * 128].rearrange("p (a b) -> p a b", b=128),
                     mask[:].unsqueeze(1).broadcast_to([128, n, 128]), ALU.mult, [ps, mask], [STm])
            p.op("dve", lambda e: e.memset(Scur[0][:], 0.0), [], [Scur[0]])
            p.op("dve", lambda e: e.memset(Sbf[:, 0, :], 0.0), [], [Sbf])
            for c0 in range(0, NCH, 8):
                n = min(8, NCH - c0)
                ps = psK[(c0 // 8) % 2]
                for j in range(n):
                    pr, hf = divmod(c0 + j, 2)
                    p.mm(ps[0:32, j * 64:(j + 1) * 64], ktT[hf][:, pr, :], vt[:, pr, :], True, True, [ktT[hf], vt], [ps])
                for j in range(n):
                    c = c0 + j
                    sa, sb_ = Scur[c % 2], Scur[(c + 1) % 2]
                    p.stt(sb_[:], sa[:], dc[:, c:c + 1], ps[0:32, j * 64:(j + 1) * 64], ALU.mult, ALU.add, [sa, dc, ps], [sb_])
                    p.act(Sbf[:, c + 1, :], sb_[:], AF.Identity, [sb_], [Sbf])
            for g0 in range(0, NPR, 4):
                n = min(4, NPR - g0)
                ps = psO[(g0 // 4) % 2]
                for j in range(n):
                    pr = g0 + j
                    cs_ = slice(j * 128, (j + 1) * 128)
                    p.mm(ps[0:64, cs_], vt[:, pr, :], STm[:, pr, :], True, False, [vt, STm], [ps])
                    for hf in range(2):
                        c = 2 * pr + hf
                        tok = slice(c * 64, (c + 1) * 64)
                        p.mm(ps[0:64, j * 128 + hf * 64:j * 128 + (hf + 1) * 64], Sbf[:, c, :], qd[:, tok], False, hf == 1, [Sbf, qd], [ps])
                p.act(oT[:, g0 * 128:(g0 + n) * 128], ps[0:64, 0:n * 128], AF.Identity, [ps], [oT])
            p.dma("sp", GO[hd, d], oT[:], [oT], [GO])
    st.close()


def emit_l5a(p, PS, l, SY, YL, YC, GO, S1, Hin, H1, MOD, wd, ident_d):
    st = Stage(p)
    sb = st.sb
    W = sb("W", [128, 8, D], BF16)
    og = sb("og", [128, 8])
    g1 = [sb("g1_%d" % k, [128, D]) for k in range(2)]
    ident = sb("ident", [128, 128])
    epst = sb("eps", [128, 1])
    p.op("dve", lambda e: e.memset(epst[:], EPS), [], [epst])
    p.dma("sp", og[:], wd["out_norm_g"][l], [wd["out_norm_g"]], [og])
    p.dma("sp", ident[:], ident_d[:], [ident_d], [ident])
    for k in range(2):
        p.dma("sp", g1[k][:], MOD[l, k:k + 1, 2 * D:3 * D].partition_broadcast(128)[:, 0, :], [MOD], [g1[k]])
    for c in range(8):
        p.dma("pool", W[:, c, :], wd["w_out"][l, c * 128:(c + 1) * 128, :], [wd["w_out"]], [W], max_dma_last_dim=4096)
    NB = 2
    yt = [sb("yt%d" % i, [128, D]) for i in range(NB)]
    ht = [sb("ht%d" % i, [128, D]) for i in range(NB)]
    srt = [sb("srt%d" % i, [128, 256]) for i in range(NB)]
    hin = [sb("hin%d" % i, [64, 4, 128]) for i in range(NB)]
    gf = [sb("gf%d" % i, [64, 4, 128]) for i in range(NB)]
    gb = [sb("gb%d" % i, [64, 4, 128]) for i in range(NB)]
    ho = [sb("ho%d" % i, [128, D]) for i in range(NB)]
    yT = [sb("yT%d" % i, [128, 8, 128], BF16) for i in range(NB)]
    sq = sb("sq", [128, D])
    st_ = [sb("st%d" % i, [128, 16]) for i in range(NB)]
    pst, pso, psx = PS[0:2], PS[2:6], PS[6:8]
    def phaseA(i):
        bi = i % NB
        rows = slice(i * 128, (i + 1) * 128)
        y, h, sr, s = yt[bi], ht[bi], srt[bi], st_[bi]
        p.dma("sp", y[:, 0:512], SY[rows, :], [SY], [y])
        p.dma("sp", sr[:], S1[rows, 2048:2304], [S1], [sr])
        p.dma("sp", h[:], Hin[rows, :], [Hin], [h])
        if i < 64:
            p.dma("sp", hin[bi][:], YL[:, i, :, :].rearrange("g c j -> c g j"), [YL], [hin[bi]])
            f0, b0 = 256 + i * 128, 256 + (63 - i) * 128
        else:
            c = i - 64
            p.dma("sp", hin[bi][:], YC[:, :, c * 128:(c + 1) * 128].rearrange("g c j -> c g j"), [YC], [hin[bi]])
            f0, b0 = c * 128, (1 - c) * 128
        p.dma("sp", gf[bi][:], GO[:, 0, :, f0:f0 + 128].rearrange("h v j -> v h j"), [GO], [gf[bi]])
        p.dma("sp", gb[bi][:], GO[:, 1, :, b0:b0 + 128].rearrange("h v j -> v h j"), [GO], [gb[bi]])
        p.tt("pool", gf[bi][:], gf[bi][:], gb[bi][:, :, ::-1], ALU.add, [gf[bi], gb[bi]], [gf[bi]])
        for a in range(4):
            p.tr(psx[0][:, a * 64:(a + 1) * 64], hin[bi][:, a, :], ident[0:64, 0:64], [hin[bi], ident], [psx[0]])
            p.tr(psx[1][:, a * 64:(a + 1) * 64], gf[bi][:, a, :], ident[0:64, 0:64], [gf[bi], ident], [psx[1]])
        p.act(y[:, 512:768], psx[0][:, 0:256], AF.Identity, [psx[0]], [y])
        p.act(y[:, 768:1024], psx[1][:, 0:256], AF.Identity, [psx[1]], [y])
        p.tt("pool", sq[:], y[:], y[:], ALU.mult, [y], [sq])
        p.op("dve", lambda e: e.tensor_reduce(out=s[:], in_=sq[:].rearrange("p (h d) -> p h d", d=64), axis=AX.X, op=ALU.add), [sq], [s])
        p.rsqrt(s, s[:], s, s[:], 1.0 / 64, epst)
        y3 = y[:].rearrange("p (h d) -> p h d", d=64)
        p.tt("dve", y3, y3, s[:].unsqueeze(2).broadcast_to([128, 16, 64]), ALU.mult, [y, s], [y])
        p.tt("pool", y[:, 768:1024], y[:, 768:1024], sr[:], ALU.mult, [y, sr], [y])
        for c in range(8):
            ps = pst[c // 4]
            blk = ps[:, (c % 4) * 128:(c % 4 + 1) * 128]
            p.tr(blk, y[:, c * 128:(c + 1) * 128], ident[:], [y, ident], [ps])
            p.act(yT[bi][:, c, :], blk, AF.Identity, [ps, og], [yT[bi]], scale=og[:, c:c + 1])

    def phaseB(i):
        kind = 0 if i < 64 else 1
        bi = i % NB
        rows = slice(i * 128, (i + 1) * 128)
        h = ht[bi]
        for hf in range(2):
            ps = pso[(2 * i + hf) % 4]
            cols = slice(hf * 512, (hf + 1) * 512)
            for c in range(8):
                p.mm(ps[:], yT[bi][:, c, :], W[:, c, cols], c == 0, c == 7, [yT[bi], W], [ps])
            p.tt("dve", ho[bi][:, cols], ps[:], g1[kind][:, cols], ALU.mult, [ps, g1[kind]], [ho[bi]])
            p.tt("pool", ho[bi][:, cols], ho[bi][:, cols], h[:, cols], ALU.add, [ho[bi], h], [ho[bi]])
        p.dma("sp", H1[rows, :], ho[bi][:], [ho[bi]], [H1])

    phaseA(0)
    for i in range(NTT):
        if i + 1 < NTT:
            phaseA(i + 1)
        phaseB(i)
    st.close()


def emit_l5b(p, PS, l, H1, Hout, MOD, wd, ident_d, final):
    st = Stage(p)
    sb = st.sb
    W1 = sb("W1", [128, 8, FFN], BF16)
    W3 = sb("W3", [128, 8, FFN], BF16)
    W2 = sb("W2", [128, NF, D], BF16)
    A = sb("A", [128, 8, 2])
    ng = sb("ng", [128, 8])
    g2 = [sb("g2_%d" % k, [128, D]) for k in range(3 if final else 2)]
    ident = sb("ident", [128, 128])
    epst = sb("eps", [128, 1])
    p.op("dve", lambda e: e.memset(epst[:], EPS), [], [epst])
    p.dma("sp", ident[:], ident_d[:], [ident_d], [ident])
    p.dma("sp", ng[:], wd["norm2_g"][l], [wd["norm2_g"]], [ng])
    for k in range(2):
        p.dma("sp", g2[k][:], MOD[l, k:k + 1, 5 * D:6 * D].partition_broadcast(128)[:, 0, :], [MOD], [g2[k]])
    if final:
        p.dma("sp", g2[2][:], wd["final_norm_g"][:].partition_broadcast(128)[:, 0, :], [wd["final_norm_g"]], [g2[2]])
    for c in range(8):
        p.dma("pool", W1[:, c, :], wd["ffn_w1"][l, c * 128:(c + 1) * 128, :], [wd["ffn_w1"]], [W1], max_dma_last_dim=4096)
        p.dma("pool", W3[:, c, :], wd["ffn_w3"][l, c * 128:(c + 1) * 128, :], [wd["ffn_w3"]], [W3], max_dma_last_dim=4096)
    for f in range(NF):
        p.dma("pool", W2[:, f, :], wd["ffn_w2"][l, f * 128:(f + 1) * 128, :], [wd["ffn_w2"]], [W2], max_dma_last_dim=4096)
    modT = load_modT(p, st, PS, MOD, l, ident)
    for k in range(2):
        p.stt(A[:, :, k], modT[:, k * 48 + 32:k * 48 + 40], 1.0, ng[:], ALU.add, ALU.mult, [modT, ng], [A])
    G = 2
    hts = [sb("ht%d" % i, [128, D]) for i in range(2 * G)]
    xs = sb("xs", [128, D])
    junk = sb("junk", [128, D])
    u2T = [sb("u2T%d" % i, [128, 8, G * 128], BF16) for i in range(2)]
    hidT = sb("hidT", [128, NF, G * 128], BF16)
    sil = [sb("sil%d" % i, [128, G * 128]) for i in range(2)]
    ho = [sb("ho%d" % i, [128, D]) for i in range(2)]
    st_ = sb("st", [128, 8])
    pst, psu, psd = PS[0:2], PS[2:6], PS[6:8]
    groups = [list(range(a, a + G)) for a in range(0, 64, G)] + [[64, 65]]
    nd = 0
    for gi, tiles in enumerate(groups):
        kind = 0 if tiles[0] < 64 else 1
        N = len(tiles) * 128
        u = u2T[gi % 2]
        for j, i in enumerate(tiles):
            h = hts[(gi % 2) * G + j]
            rows = slice(i * 128, (i + 1) * 128)
            p.dma("sp", h[:], H1[rows, :], [H1], [h])
            sc = st_[:, 2 * j:2 * j + 1]
            sr_ = st_[:, 2 * j + 1:2 * j + 2]
            p.act(junk[:], h[:], AF.Square, [h], [junk, st_], accum_out=sc)
            p.rsqrt(st_, sr_, st_, sc, 1.0 / D, epst)
            p.act(xs[:], h[:], AF.Identity, [h, st_], [xs], scale=sr_)
            for c in range(8):
                ps = pst[c // 4]
                blk = ps[:, (c % 4) * 128:(c % 4 + 1) * 128]
                p.tr(blk, xs[:, c * 128:(c + 1) * 128], ident[:], [xs, ident], [ps])
                p.act(u[:, c, j * 128:(j + 1) * 128], blk, AF.Identity, [ps, A, modT], [u],
                      scale=A[:, c, kind:kind + 1], bias=modT[:, kind * 48 + 24 + c:kind * 48 + 25 + c])
        for f in range(NF):
            ps1 = psu[(2 * f) % 4]
            ps3 = psu[(2 * f + 1) % 4]
            fc = slice(f * 128, (f + 1) * 128)
            for c in range(8):
                p.mm(ps1[:, 0:N], W1[:, c, fc], u[:, c, 0:N], c == 0, c == 7, [W1, u], [ps1])
            for c in range(8):
                p.mm(ps3[:, 0:N], W3[:, c, fc], u[:, c, 0:N], c == 0, c == 7, [W3, u], [ps3])
            s_ = sil[f % 2]
            p.act(s_[:, 0:N], ps1[:, 0:N], AF.Silu, [ps1], [s_])
            p.tt("dve", hidT[:, f, 0:N], s_[:, 0:N], ps3[:, 0:N], ALU.mult, [s_, ps3], [hidT])
        for j, i in enumerate(tiles):
            h = hts[(gi % 2) * G + j]
            rows = slice(i * 128, (i + 1) * 128)
            o = ho[nd % 2]
            nd += 1
            for hf in range(2):
                ps = psd[hf]
                cols = slice(hf * 512, (hf + 1) * 512)
                for f in range(NF):
                    p.mm(ps[:], hidT[:, f, j * 128:(j + 1) * 128], W2[:, f, cols], f == 0, f == NF - 1, [hidT, W2], [ps])
                p.tt("dve", o[:, cols], ps[:], g2[kind][:, cols], ALU.mult, [ps, g2[kind]], [o])
                p.tt("pool", o[:, cols], o[:, cols], h[:, cols], ALU.add, [o, h], [o])
            if final:
                sc = st_[:, 4:5]
                sr_ = st_[:, 5:6]
                p.act(junk[:], o[:], AF.Square, [o], [junk, st_], accum_out=sc)
                p.rsqrt(st_, sr_, st_, sc, 1.0 / D, epst)
                p.stt(o[:], o[:], sr_, g2[2][:], ALU.mult, ALU.mult, [o, st_, g2[2]], [o])
            p.dma("sp", Hout[rows, :], o[:], [o], [Hout])
    st.close()


FUSED_W = ["ada_w", "ada_b", "w_in", "qkg", "gla_gate_w", "gla_gate_b", "norm1_g", "norm2_g", "out_norm_g", "w_out",
           "ffn_w1", "ffn_w3", "ffn_w2", "final_norm_g", "cw", "hbc", "hb", "fw3", "filt_w1", "filt_w2", "fb"]
FUSED_C = ["cs", "ident", "mask", "J", "pm", "zl", "zc", "wl", "wc", "F1", "TWf", "F2", "G2", "TWi", "G1"]


def fused_host_inputs(w):
    rope = rope_table()
    one = np.concatenate([np.ones((CTX, 32), np.float32), np.zeros((CTX, 32), np.float32)], axis=1)
    j = np.arange(128)
    zl, win_l = hy_features(SEQ)
    zc, win_c = hy_features(CTX)
    consts = hy_consts()
    consts.update({
        "cs": np.concatenate([rope, one], axis=0),
        "ident": np.eye(128, dtype=np.float32),
        "mask": ((j[:, None] // 64 == j[None, :] // 64) & (j[None, :] >= j[:, None])).astype(np.float32),
        "J": np.eye(128, dtype=np.float32)[::-1].copy(),
        "pm": L4_PM,
        "zl": zl, "zc": zc,
        "wl": win_l.reshape(64, 128, 4, 64).transpose(2, 0, 3, 1),
        "wc": win_c.reshape(CTX, 4, 64).transpose(1, 2, 0),
    })
    fmL = lambda a: np.stack([fm(a[l]) for l in range(DEPTH)])
    cwf = w["hy_conv_w"].reshape(DEPTH, 3, 3, 4, 64)
    cbf = w["hy_conv_b"].reshape(DEPTH, 3, 4, 64)
    cw = np.concatenate([cwf.transpose(0, 3, 4, 2, 1), cbf.transpose(0, 2, 3, 1)[..., None]], axis=-1)
    hb = w["hy_bias"].reshape(DEPTH, 2, 4, 64).transpose(0, 2, 1, 3)
    ws = {
        "ada_w": w["ada_w"], "ada_b": w["ada_b"], "w_in": w["w_in"],
        "qkg": np.concatenate([w["q_norm_g"], w["k_norm_g"]], axis=1),
        "gla_gate_w": w["gla_gate_w"], "gla_gate_b": w["gla_gate_b"].reshape(DEPTH, 256),
        "norm1_g": fmL(w["norm1_g"]), "norm2_g": fmL(w["norm2_g"]), "out_norm_g": fmL(w["out_norm_g"]),
        "w_out": w["w_out"], "ffn_w1": w["ffn_w1"], "ffn_w3": w["ffn_w3"], "ffn_w2": w["ffn_w2"],
        "final_norm_g": w["final_norm_g"].reshape(1, D),
        "cw": cw, "hbc": hb.transpose(0, 1, 3, 2), "hb": hb,
        "fw3": w["filt_w3"].reshape(DEPTH, 64, 4, 4, 64).transpose(0, 3, 1, 2, 4),
        "filt_w1": w["filt_w1"], "filt_w2": w["filt_w2"],
        "fb": np.stack([w["filt_freq"], w["filt_b1"], w["filt_b2"]], axis=-1),
    }
    shared = {k: np.ascontiguousarray(v, dtype=np.float32) for k, v in {**ws, **consts}.items()}
    in_maps = []
    for i in range(NCORES):
        b = i // 4
        m = dict(shared)
        m["H0"] = np.ascontiguousarray(np.concatenate([w["x"][b], w["ctx"][b]], axis=0))
        cvec = np.stack([w["c"][b], w["c_ctx"]], axis=0)
        m["cT"] = np.ascontiguousarray(cvec.T.reshape(8, 128, 2).transpose(1, 0, 2))
        in_maps.append(m)
    return in_maps


def build_fused(shapes, nlayers=DEPTH, final=True, dump=()):
    p = Prog()
    H0 = p.dram_in("H0", [TT, D])
    cT = p.dram_in("cT", [128, 8, 2])
    wd = {k: p.dram_in(k, shapes[k]) for k in FUSED_W}
    cd = {k: p.dram_in(k, shapes[k]) for k in FUSED_C}
    out = p.dram_out("out", [TT, D])
    MOD = scratch(p, "MOD", [DEPTH, 2, 6 * D])
    S1 = scratch(p, "S1", [TT, OUT1])
    SY = scratch(p, "SY", [TT, 512])
    YL = scratch(p, "YL", [4, 64, 64, 128])
    YC = scratch(p, "YC", [4, 64, CTX])
    GO = scratch(p, "GO", [4, 2, 64, TT])
    H1 = scratch(p, "H1", [TT, D])
    HA = scratch(p, "HA", [TT, D])
    scr = scratch(p, "scr", [3, 64, SEQ])
    scr2 = scratch(p, "scr2", [64, 64, 128])
    PS2 = [p.ps("Q%d" % i, (128, 1024)) for i in range(4)]
    PS = [Tile(PS2[i // 2].h[:, (i % 2) * 512:(i % 2 + 1) * 512]) for i in range(8)]
    emit_ada(p, PS, cT, wd["ada_w"], wd["ada_b"], MOD)
    Hin = H0
    for l in range(nlayers):
        last = l == nlayers - 1
        emit_l1(p, PS, l, Hin, S1, MOD, wd, cd["cs"], cd["ident"])
        emit_l2(p, PS, S1, SY, cd["ident"], PS2)
        emit_l3(p, PS, l, S1, YL, YC, wd, cd, scr, scr2)
        emit_l4(p, PS, S1, GO, cd)
        emit_l5a(p, PS, l, SY, YL, YC, GO, S1, Hin, H1, MOD, wd, cd["ident"])
        emit_l5b(p, PS, l, H1, out if last else HA, MOD, wd, cd["ident"], final and last)
        Hin = HA
    for name in dump:
        src = {"S1": S1, "SY": SY, "YL": YL, "YC": YC, "GO": GO, "H1": H1, "MOD": MOD}[name]
        shp = list(src.h.shape)
        d = p.dram_out("dump_" + name, shp)
        flat = lambda ap: ap.rearrange(" ".join("abcd"[:len(shp)]) + " -> " + ("(" + " ".join("abcd"[:len(shp) - 1]) + ") " + "abcd"[len(shp) - 1] if len(shp) > 2 else "a b"))
        p.dma("sp", flat(d[:]), flat(src[:]), [src], [d])
    return p


def kernel_fused(inputs, nlayers=DEPTH, final=True, dump=()):
    w = {k: np.ascontiguousarray(np.asarray(v, dtype=np.float32)) for k, v in inputs.items()}
    in_maps = fused_host_inputs(w)
    shapes = {k: list(v.shape) for k, v in in_maps[0].items()}
    res = run(build_fused(shapes, nlayers, final, dump), in_maps)
    return res


def kernel(**inputs):
    res = kernel_fused(inputs)
    return np.ascontiguousarray(np.stack([res[0]["out"][:SEQ], res[4]["out"][:SEQ]], axis=0), dtype=np.float32)
```
